# Optimizing a Trainium2 kernel written in Bass

```python
import math
import jax, jax.numpy as jnp
from jax import lax
import numpy as np

D_MODEL = 2048
BATCH = 4
SEQ = 2048
DEPTH = 4
DEC_BATCH = 32
DEC_SEQ = 4
PAST_LEN = 16384
PAGE_SIZE = 128

N_META = 16
GROUP_W = D_MODEL // 4
MIX_W = 4 * GROUP_W
CHUNK = 128
CONV_K = 4
NORM_EPS = 1e-6

SSD_HEAD_DIM = 64
SSD_HEADS = GROUP_W // SSD_HEAD_DIM
SSD_GROUPS = 2
SSD_STATE = 128
SSD_CONV_DIM = GROUP_W + 2 * SSD_GROUPS * SSD_STATE

WINDOW = 128
SWA_HEAD_DIM = 64
SWA_HEADS = GROUP_W // SWA_HEAD_DIM
SWA_KV_HEADS = 2

RET_HEADS = 4
RET_DV = GROUP_W // RET_HEADS
RET_DK = RET_DV // 2

GDN_HEADS = 4
GDN_DK = GROUP_W // GDN_HEADS
GDN_DV = GROUP_W // GDN_HEADS
GDN_CONV_DIM = 2 * GDN_HEADS * GDN_DK + GDN_HEADS * GDN_DV

SPLIT_SIZES = (
    GROUP_W, SSD_CONV_DIM, SSD_HEADS,
    SWA_HEADS * SWA_HEAD_DIM, SWA_KV_HEADS * SWA_HEAD_DIM,
    SWA_KV_HEADS * SWA_HEAD_DIM, GROUP_W,
    RET_HEADS * RET_DK, RET_HEADS * RET_DK, RET_HEADS * RET_DV, GROUP_W,
    GDN_CONV_DIM, GROUP_W, GDN_HEADS, GDN_HEADS,
)
IN_W = sum(SPLIT_SIZES)
SPLIT_POINTS = tuple(int(s) for s in np.cumsum(SPLIT_SIZES)[:-1])

kernel_name = 'hybrid_ssd_swa_ret_gdn_step'


def rmsnorm(x, g):
    xf = x.astype(jnp.float32)
    y = xf * lax.rsqrt(jnp.mean(xf * xf, axis=-1, keepdims=True) + NORM_EPS)
    return (y * g.astype(jnp.float32)).astype(x.dtype)


def group_rmsnorm(x, g, n_groups):
    shp = x.shape
    xg = x.reshape(shp[:-1] + (n_groups, shp[-1] // n_groups))
    return rmsnorm(xg, g.reshape(n_groups, -1)).reshape(shp)


def l2norm(x):
    return x * lax.rsqrt(jnp.sum(x * x, axis=-1, keepdims=True) + 1e-6)


def causal_conv(u, prev, w, b=None):
    ext = jnp.concatenate([prev.astype(u.dtype), u], axis=1)
    out = lax.conv_general_dilated(ext, w[:, None, :].astype(u.dtype), window_strides=(1,), padding='VALID',
                                   dimension_numbers=('NWC', 'WIO', 'NWC'), feature_group_count=u.shape[-1])
    if b is not None:
        out = out + b.astype(u.dtype)
    return out, ext[:, -(CONV_K - 1):]


def _chunks(a, c):
    b, l = a.shape[:2]
    return jnp.moveaxis(a.reshape((b, l // c, c) + a.shape[2:]), 1, 0)


def _unchunk(a):
    nc, b, c = a.shape[:3]
    return jnp.moveaxis(a, 0, 1).reshape((b, nc * c) + a.shape[3:])


def decay_linear_scan(q, k, v, log_a, s0, chunk):
    tri = jnp.tril(jnp.ones((chunk, chunk), bool))

    def step(s, inp):
        qc, kc, vc, la = inp
        cum = jnp.cumsum(la, axis=1).transpose(0, 2, 1)
        seg = cum[..., :, None] - cum[..., None, :]
        decay = jnp.where(tri, jnp.exp(jnp.where(tri, seg, 0.0)), 0.0)
        scores = jnp.einsum('bihd,bjhd->bhij', qc, kc) * decay
        o = (jnp.einsum('bhij,bjhe->bihe', scores, vc)
             + jnp.einsum('bihd,bhi,bhde->bihe', qc, jnp.exp(cum), s))
        s_new = (s * jnp.exp(cum[..., -1])[..., None, None]
                 + jnp.einsum('bjhd,bhj,bjhe->bhde', kc, jnp.exp(cum[..., -1:] - cum), vc))
        return s_new, o

    s, o = lax.scan(step, s0, (_chunks(q, chunk), _chunks(k, chunk), _chunks(v, chunk), _chunks(log_a, chunk)))
    return _unchunk(o), s


def gated_delta_scan(q, k, v, beta, log_a, s0, chunk):
    tri = jnp.tril(jnp.ones((chunk, chunk), bool))
    eye = jnp.eye(chunk, dtype=jnp.float32)

    def step(s, inp):
        qc, kc, vc, bc, la = inp
        cum = jnp.cumsum(la, axis=1).transpose(0, 2, 1)
        seg = cum[..., :, None] - cum[..., None, :]
        decay = jnp.where(tri, jnp.exp(jnp.where(tri, seg, 0.0)), 0.0)
        kb = kc * bc[..., None]
        a = jnp.einsum('bihd,bjhd->bhij', kb, kc) * decay * (1.0 - eye)
        t = lax.linalg.triangular_solve(eye + a, jnp.broadcast_to(eye, a.shape),
                                        left_side=True, lower=True, unit_diagonal=True)
        u = jnp.einsum('bhij,bjhe->bhie', t, vc * bc[..., None])
        w = jnp.einsum('bhij,bjhd->bhid', t, kb * jnp.exp(cum).transpose(0, 2, 1)[..., None])
        v_new = u - jnp.einsum('bhid,bhde->bhie', w, s)
        qk = jnp.einsum('bihd,bjhd->bhij', qc, kc) * decay
        o = (jnp.einsum('bihd,bhi,bhde->bihe', qc, jnp.exp(cum), s)
             + jnp.einsum('bhij,bhje->bihe', qk, v_new))
        s_new = (s * jnp.exp(cum[..., -1])[..., None, None]
                 + jnp.einsum('bjhd,bhj,bhje->bhde', kc, jnp.exp(cum[..., -1:] - cum), v_new))
        return s_new, o

    s, o = lax.scan(step, s0, (_chunks(q, chunk), _chunks(k, chunk), _chunks(v, chunk),
                               _chunks(beta, chunk), _chunks(log_a, chunk)))
    return _unchunk(o), s


def scan_with_meta(fn, seqs, s0, prompt):
    l = seqs[0].shape[1]
    if prompt:
        o_meta, s = fn(*[a[:, :N_META] for a in seqs], s0, N_META)
        o_tok, s = fn(*[a[:, N_META:] for a in seqs], s, math.gcd(l - N_META, CHUNK))
        return jnp.concatenate([o_meta, o_tok], axis=1), s
    return fn(*seqs, s0, math.gcd(l, CHUNK))


def sink_attention(q, k, v, q_pos, k_pos, sinks):
    g = q.shape[4]
    f32 = jnp.float32
    slopes = jnp.exp2(-8.0 * jnp.arange(1, SWA_HEADS + 1, dtype=f32) / SWA_HEADS).reshape(SWA_KV_HEADS, g)
    dist = q_pos[:, :, None] - k_pos[:, None, :]
    visible = (dist >= 0) & (dist <= WINDOW) & (k_pos[:, None, :] >= 0)
    s = jnp.einsum('bnqkgd,bnskd->bnkgqs', q, k).astype(f32) * (SWA_HEAD_DIM ** -0.5)
    s = s - slopes[None, None, :, :, None, None] * dist.astype(f32)[None, :, None, None]
    s = jnp.where(visible[None, :, None, None], s, jnp.finfo(f32).min)
    sink = jnp.broadcast_to(sinks.astype(f32).reshape(SWA_KV_HEADS, g)[None, None, :, :, None, None],
                            s.shape[:-1] + (1,))
    p = jax.nn.softmax(jnp.concatenate([s, sink], axis=-1), axis=-1)[..., :-1]
    return jnp.einsum('bnkgqs,bnskd->bnqkgd', p.astype(v.dtype), v)


def swa_prompt(q, k, v, sinks):
    b, l = q.shape[:2]
    nb = -(-l // WINDOW)
    pad = nb * WINDOW - l
    qb = jnp.pad(q, ((0, 0), (0, pad), (0, 0), (0, 0), (0, 0))).reshape((b, nb, WINDOW) + q.shape[2:])

    def key_blocks(a):
        ap = jnp.pad(a, ((0, 0), (WINDOW, pad), (0, 0), (0, 0)))
        prev = ap[:, :nb * WINDOW].reshape((b, nb, WINDOW) + a.shape[2:])
        cur = ap[:, WINDOW:].reshape((b, nb, WINDOW) + a.shape[2:])
        return jnp.concatenate([prev, cur], axis=2)

    start = jnp.arange(nb)[:, None] * WINDOW
    q_pos = start + jnp.arange(WINDOW)[None]
    k_pos = start - WINDOW + jnp.arange(2 * WINDOW)[None]
    o = sink_attention(qb, key_blocks(k), key_blocks(v), q_pos, k_pos, sinks)
    return o.reshape((b, nb * WINDOW) + q.shape[2:])[:, :l]


def hybrid_mixer(h, w_in, w_out, ssd_conv_w, ssd_conv_b, ssd_dt_bias, ssd_a_log, ssd_d, ssd_norm,
                 swa_sinks, ret_norm, gdn_conv_w, gdn_dt_bias, gdn_a_log, gdn_norm, state, prompt):
    b, l, _ = h.shape
    f32 = jnp.float32
    if prompt:
        state = (jnp.zeros((b, SSD_HEADS, SSD_STATE, SSD_HEAD_DIM), f32),
                 jnp.zeros((b, CONV_K - 1, SSD_CONV_DIM), h.dtype), None, None,
                 jnp.zeros((b, RET_HEADS, RET_DK, RET_DV), f32),
                 jnp.zeros((b, GDN_HEADS, GDN_DK, GDN_DV), f32),
                 jnp.zeros((b, CONV_K - 1, GDN_CONV_DIM), h.dtype))
    ssd_s0, ssd_c0, k_buf, v_buf, ret_s0, gdn_s0, gdn_c0 = state

    (z, xbc, dt_raw, qa, ka, va, ga, qr, kr, vr, gr, qkv_d, gd, bd, ad) = jnp.split(h @ w_in, SPLIT_POINTS, axis=-1)

    xbc, ssd_c1 = causal_conv(xbc, ssd_c0, ssd_conv_w, ssd_conv_b)
    xbc = jax.nn.silu(xbc)
    xs = xbc[..., :GROUP_W].reshape(b, l, SSD_HEADS, SSD_HEAD_DIM).astype(f32)
    bcm = xbc[..., GROUP_W:].reshape(b, l, 2, SSD_GROUPS, SSD_STATE).astype(f32)
    rep = SSD_HEADS // SSD_GROUPS
    b_in = jnp.repeat(bcm[:, :, 0], rep, axis=2)
    c_out = jnp.repeat(bcm[:, :, 1], rep, axis=2)
    dt = jax.nn.softplus(dt_raw.astype(f32) + ssd_dt_bias.astype(f32))
    log_a = -jnp.exp(ssd_a_log.astype(f32)) * dt
    o, ssd_s1 = scan_with_meta(decay_linear_scan, (c_out, b_in, xs * dt[..., None], log_a),
                               ssd_s0.astype(f32), prompt)
    y = (o + ssd_d.astype(f32)[:, None] * xs).reshape(b, l, GROUP_W).astype(h.dtype)
    y_ssd = group_rmsnorm(y * jax.nn.silu(z), ssd_norm, SSD_GROUPS)

    grp = SWA_HEADS // SWA_KV_HEADS
    q = qa.reshape(b, l, SWA_KV_HEADS, grp, SWA_HEAD_DIM)
    k = ka.reshape(b, l, SWA_KV_HEADS, SWA_HEAD_DIM)
    v = va.reshape(b, l, SWA_KV_HEADS, SWA_HEAD_DIM)
    if prompt:
        o = swa_prompt(q, k, v, swa_sinks)
        k_win, v_win = k[:, -WINDOW:], v[:, -WINDOW:]
    else:
        k_all = jnp.concatenate([k_buf.astype(k.dtype), k], axis=1)
        v_all = jnp.concatenate([v_buf.astype(v.dtype), v], axis=1)
        q_pos = PAST_LEN + jnp.arange(l)
        k_pos = PAST_LEN - WINDOW + jnp.arange(WINDOW + l)
        o = sink_attention(q[:, None], k_all[:, None], v_all[:, None], q_pos[None], k_pos[None], swa_sinks)[:, 0]
        k_win, v_win = k_all[:, -WINDOW:], v_all[:, -WINDOW:]
    y_swa = o.reshape(b, l, GROUP_W) * jax.nn.silu(ga)

    qr = qr.reshape(b, l, RET_HEADS, RET_DK).astype(f32)
    kr = kr.reshape(b, l, RET_HEADS, RET_DK).astype(f32) * (RET_DK ** -0.5)
    vr = vr.reshape(b, l, RET_HEADS, RET_DV).astype(f32)
    log_gamma = jnp.log1p(-jnp.exp2(-5.0 - jnp.arange(RET_HEADS, dtype=f32)))
    o, ret_s1 = scan_with_meta(decay_linear_scan, (qr, kr, vr, jnp.broadcast_to(log_gamma, (b, l, RET_HEADS))),
                               ret_s0.astype(f32), prompt)
    y_ret = group_rmsnorm(o.reshape(b, l, GROUP_W), ret_norm, RET_HEADS).astype(h.dtype) * jax.nn.silu(gr)

    qkv, gdn_c1 = causal_conv(qkv_d, gdn_c0, gdn_conv_w)
    qkv = jax.nn.silu(qkv).astype(f32)
    qd, kd, vd = jnp.split(qkv, [GDN_HEADS * GDN_DK, 2 * GDN_HEADS * GDN_DK], axis=-1)
    qd = l2norm(qd.reshape(b, l, GDN_HEADS, GDN_DK)) * (GDN_DK ** -0.5)
    kd = l2norm(kd.reshape(b, l, GDN_HEADS, GDN_DK))
    vd = vd.reshape(b, l, GDN_HEADS, GDN_DV)
    beta = jax.nn.sigmoid(bd.astype(f32))
    log_alpha = -jnp.exp(gdn_a_log.astype(f32)) * jax.nn.softplus(ad.astype(f32) + gdn_dt_bias.astype(f32))
    o, gdn_s1 = scan_with_meta(gated_delta_scan, (qd, kd, vd, beta, log_alpha), gdn_s0.astype(f32), prompt)
    y_gdn = rmsnorm(o, gdn_norm).reshape(b, l, GROUP_W).astype(h.dtype) * jax.nn.silu(gd)

    out = jnp.concatenate([y_ssd, y_swa, y_ret, y_gdn], axis=-1) @ w_out
    return out, (ssd_s1, ssd_c1, k_win, v_win, ret_s1, gdn_s1, gdn_c1)


def trunk(x, weights, states, prompt):
    pre_norm, post_norm = weights[0], weights[1]
    new = []
    for layer in range(DEPTH):
        st = None if prompt else tuple(s[layer] for s in states)
        h = rmsnorm(x, pre_norm[layer])
        m, st_new = hybrid_mixer(h, *[w[layer] for w in weights[2:]], st, prompt)
        x = x + rmsnorm(m, post_norm[layer])
        new.append(st_new)
    stacked = [jnp.stack([n[i] for n in new]) for i in range(7)]
    return x, stacked


def setup_inputs(seed: int = 0) -> dict:
    key = jax.random.key(seed)
    ks = iter(jax.random.split(key, 40))
    f32 = jnp.float32

    def nrm(shape, scale):
        return scale * jax.random.normal(next(ks), shape, f32)

    def gain(shape):
        return 1.0 + nrm(shape, 0.02)

    def dt_bias(shape):
        dt = jnp.exp(jax.random.uniform(next(ks), shape, f32, math.log(1e-3), math.log(1e-1)))
        return dt + jnp.log(-jnp.expm1(-dt))

    def a_log(shape):
        return jnp.log(jax.random.uniform(next(ks), shape, f32, 1.0, 16.0))

    return {
        'x_prompt': nrm((BATCH, SEQ, D_MODEL), 1.0),
        'x_sample': nrm((DEC_BATCH, DEC_SEQ, D_MODEL), 1.0),
        'state_ssd': nrm((DEPTH, DEC_BATCH, SSD_HEADS, SSD_STATE, SSD_HEAD_DIM), 0.1),
        'state_ssd_conv': nrm((DEPTH, DEC_BATCH, CONV_K - 1, SSD_CONV_DIM), 1.0),
        'cache_swa_k': nrm((DEPTH, DEC_BATCH, WINDOW, SWA_KV_HEADS, SWA_HEAD_DIM), 1.0),
        'cache_swa_v': nrm((DEPTH, DEC_BATCH, WINDOW, SWA_KV_HEADS, SWA_HEAD_DIM), 1.0),
        'state_ret': nrm((DEPTH, DEC_BATCH, RET_HEADS, RET_DK, RET_DV), 0.1),
        'state_gdn': nrm((DEPTH, DEC_BATCH, GDN_HEADS, GDN_DK, GDN_DV), 0.1),
        'state_gdn_conv': nrm((DEPTH, DEC_BATCH, CONV_K - 1, GDN_CONV_DIM), 1.0),
        'meta_tokens': nrm((N_META, D_MODEL), 1.0),
        'pre_norm': gain((DEPTH, D_MODEL)),
        'post_norm': gain((DEPTH, D_MODEL)),
        'w_in': nrm((DEPTH, D_MODEL, IN_W), D_MODEL ** -0.5),
        'w_out': nrm((DEPTH, MIX_W, D_MODEL), MIX_W ** -0.5),
        'ssd_conv_w': nrm((DEPTH, CONV_K, SSD_CONV_DIM), CONV_K ** -0.5),
        'ssd_conv_b': nrm((DEPTH, SSD_CONV_DIM), 0.02),
        'ssd_dt_bias': dt_bias((DEPTH, SSD_HEADS)),
        'ssd_a_log': a_log((DEPTH, SSD_HEADS)),
        'ssd_d': gain((DEPTH, SSD_HEADS)),
        'ssd_norm': gain((DEPTH, GROUP_W)),
        'swa_sinks': nrm((DEPTH, SWA_HEADS), 0.5),
        'ret_norm': gain((DEPTH, GROUP_W)),
        'gdn_conv_w': nrm((DEPTH, CONV_K, GDN_CONV_DIM), CONV_K ** -0.5),
        'gdn_dt_bias': dt_bias((DEPTH, GDN_HEADS)),
        'gdn_a_log': a_log((DEPTH, GDN_HEADS)),
        'gdn_norm': gain((DEPTH, GDN_DV)),
    }


def reference(x_prompt, x_sample, state_ssd, state_ssd_conv, cache_swa_k, cache_swa_v, state_ret, state_gdn,
              state_gdn_conv, meta_tokens, pre_norm, post_norm, w_in, w_out, ssd_conv_w, ssd_conv_b, ssd_dt_bias,
              ssd_a_log, ssd_d, ssd_norm, swa_sinks, ret_norm, gdn_conv_w, gdn_dt_bias, gdn_a_log, gdn_norm):
    weights = (pre_norm, post_norm, w_in, w_out, ssd_conv_w, ssd_conv_b, ssd_dt_bias, ssd_a_log, ssd_d, ssd_norm,
               swa_sinks, ret_norm, gdn_conv_w, gdn_dt_bias, gdn_a_log, gdn_norm)
    b = x_prompt.shape[0]
    meta = jnp.broadcast_to(meta_tokens[None].astype(x_prompt.dtype), (b, N_META, D_MODEL))
    xp = jnp.concatenate([meta, x_prompt], axis=1)
    yp, (p_ssd, p_ssd_conv, p_swa_k, p_swa_v, p_ret, p_gdn, p_gdn_conv) = trunk(xp, weights, None, True)
    samp_states = (state_ssd, state_ssd_conv, cache_swa_k, cache_swa_v, state_ret, state_gdn, state_gdn_conv)
    ys, (s_ssd, s_ssd_conv, s_swa_k, s_swa_v, s_ret, s_gdn, s_gdn_conv) = trunk(x_sample, weights, samp_states, False)
    return (yp[:, N_META:], ys, p_ssd, p_ssd_conv, p_swa_k, p_swa_v, p_ret, p_gdn, p_gdn_conv,
            s_ssd, s_ssd_conv, s_swa_k, s_swa_v, s_ret, s_gdn, s_gdn_conv)
```

```python
import math
from contextlib import ExitStack

import numpy as np
import concourse.bass as bass
import concourse.mybir as mybir
from concourse.bass_utils import run_bass_kernel_spmd

F32 = mybir.dt.float32
BF16 = mybir.dt.bfloat16
I32 = mybir.dt.int32
AF = mybir.ActivationFunctionType
ALU = mybir.AluOpType
AX = mybir.AxisListType

D = 2048
KC = 16
DEPTH = 4
SEQ = 2048
NMETA = 16
NPT = SEQ + NMETA
NSS = 4
LS = 4
NT = NPT + NSS * LS
IN_W = 6416
EPS = 1e-6
NEG = -30000.0

C_Z, C_XBC, C_DT, C_QA, C_KA, C_VA, C_GA = 0, 512, 1536, 1544, 2056, 2184, 2312
C_QR, C_KR, C_VR, C_GR, C_QKVD, C_GD, C_BD, C_AD = 2824, 3080, 3336, 3848, 4360, 5896, 6408, 6412


class Buf:
    __slots__ = ("t", "lw", "rd", "rd_dma", "name", "excl")

    def __init__(self, t, name="", excl=False):
        self.t = t
        self.excl = excl
        self.lw = None
        self.rd = {}
        self.rd_dma = []
        self.name = name

    def __getitem__(self, k):
        return self.t[k]


class View:
    __slots__ = ("p", "t")

    def __init__(self, parent, ap):
        self.p = parent
        self.t = ap

    def __getitem__(self, k):
        return self.t[k]

    lw = property(lambda self: self.p.lw, lambda self, v: setattr(self.p, "lw", v))
    rd = property(lambda self: self.p.rd, lambda self, v: setattr(self.p, "rd", v))
    rd_dma = property(lambda self: self.p.rd_dma, lambda self, v: setattr(self.p, "rd_dma", v))
    excl = property(lambda self: self.p.excl)


class Prog:
    def __init__(self, nc, n_dma_sems=60):
        self.nc = nc
        self.ops = []
        self.E = {"pe": nc.tensor, "act": nc.scalar, "dve": nc.vector, "pool": nc.gpsimd, "sp": nc.sync}
        self.n_dma_sems = n_dma_sems

    def op(self, eng, fn, r=(), w=(), dma=False):
        idx = len(self.ops)
        deps = set()
        for b in r:
            if b.lw is not None:
                deps.add(b.lw)
            if b.excl:
                deps.update(v for e, v in b.rd.items() if e != eng)
        for b in w:
            if b.lw is not None:
                deps.add(b.lw)
            deps.update(b.rd.values())
            deps.update(b.rd_dma)
        for b in r:
            if dma:
                b.rd_dma.append(idx)
            else:
                b.rd[eng] = idx
        for b in w:
            b.lw = idx
            b.rd = {}
            b.rd_dma = []
        deps.discard(idx)
        self.ops.append([eng, fn, deps, dma])
        return idx

    def emit(self, es):
        nc = self.nc
        ops = self.ops
        needed = set()
        for i, (eng, fn, deps, dma) in enumerate(ops):
            for d in deps:
                de, _, _, ddma = ops[d]
                if ddma:
                    continue
                if de == eng and eng == "pe":
                    continue
                needed.add(d)
        esem = {e: es.enter_context(nc.semaphore("sem_" + e)) for e in ("pe", "act", "dve", "pool")}
        dsem = [es.enter_context(nc.semaphore("dsem%d" % i)) for i in range(self.n_dma_sems)]
        dval = [0] * self.n_dma_sems
        ecount = {e: 0 for e in esem}
        sig = [None] * len(ops)
        known = {e: {} for e in self.E}
        ndma = 0
        nwait = 0
        dcnt = {}
        for i, (eng, fn, deps, dma) in enumerate(ops):
            waits = {}
            for d in deps:
                s = sig[d]
                if s is None:
                    continue
                if s[1] > waits.get(s[0], (None, 0))[1]:
                    waits[s[0]] = s
            if dma:
                base, cnt_ = {"sp": (0, 10), "pool": (10, 10), "act": (20, 40)}[eng]
                k = base + (dcnt.get(eng, 0) % cnt_)
                dcnt[eng] = dcnt.get(eng, 0) + 1
                ndma += 1
                if dval[k] > 0:
                    key = ("d", k)
                    if dval[k] > waits.get(key, (None, 0))[1]:
                        waits[key] = (key, dval[k])
            kn = known[eng]
            for key, (_, val) in waits.items():
                if kn.get(key, 0) >= val:
                    continue
                sem = esem[key] if isinstance(key, str) else dsem[key[1]]
                self.E[eng].wait_ge(sem, val)
                kn[key] = val
                nwait += 1
            ins = fn()
            if dma:
                dval[k] += 16
                ins.then_inc(dsem[k], 16)
                sig[i] = (("d", k), dval[k])
            elif i in needed:
                ecount[eng] += 1
                ins.then_inc(esem[eng], 1)
                sig[i] = (eng, ecount[eng])
        for k in range(self.n_dma_sems):
            if dval[k] > 0:
                nc.sync.wait_ge(dsem[k], dval[k])
        for e in esem:
            if ecount[e] > 0:
                nc.sync.wait_ge(esem[e], ecount[e])
        self.stats = dict(n_ops=len(ops), n_wait=nwait, n_dma=ndma, counts=dict(ecount))


def make_chunks():
    chunks = [dict(kind="p", L=NMETA, row0=0, ci=0)]
    for c in range(1, 17):
        chunks.append(dict(kind="p", L=128, row0=NMETA + 128 * (c - 1), ci=c))
    samples = [dict(kind="s", L=LS, row0=NPT + LS * s, seq=s) for s in range(NSS)]
    return chunks, samples


class StopBuild(Exception):
    pass


def build_program(n_layers=DEPTH, n_blocks=None, cpb=2, dbg=False, stop=None):
    nc = bass.Bass("TRN2", target_bir_lowering=False)
    es = ExitStack()
    P = Prog(nc)

    def dram(name, shape, dt=F32, kind="ExternalInput"):
        return nc.dram_tensor(name, list(shape), dt, kind=kind).ap()

    xin = dram("xin", [NT, D])
    st_ssd = dram("st_ssd", [DEPTH, NSS, 8, 128, 64])
    st_ssdc = dram("st_ssdc", [DEPTH, NSS, 3, 1024])
    st_k = dram("st_k", [DEPTH, NSS, 128, 128])
    st_v = dram("st_v", [DEPTH, NSS, 128, 128])
    st_ret = dram("st_ret", [DEPTH, NSS, 4, 64, 128])
    st_gdn = dram("st_gdn", [DEPTH, NSS, 4, 128, 128])
    st_gdnc = dram("st_gdnc", [DEPTH, NSS, 3, 1536])
    pre_norm = dram("pre_norm", [DEPTH, D])
    post_norm = dram("post_norm", [DEPTH, D])
    w_in = dram("w_in", [DEPTH, D, IN_W])
    w_out = dram("w_out", [DEPTH, D, D])
    ssd_conv_w = dram("ssd_conv_w", [DEPTH, 4, 1024])
    ssd_conv_b = dram("ssd_conv_b", [DEPTH, 1024])
    ssd_dt_bias = dram("ssd_dt_bias", [DEPTH, 8])
    ssd_a_log = dram("ssd_a_log", [DEPTH, 8])
    ssd_d = dram("ssd_d", [DEPTH, 8])
    ssd_norm = dram("ssd_norm", [DEPTH, 512])
    swa_sinks = dram("swa_sinks", [DEPTH, 8])
    ret_norm = dram("ret_norm", [DEPTH, 512])
    gdn_conv_w = dram("gdn_conv_w", [DEPTH, 4, 1536])
    gdn_dt_bias = dram("gdn_dt_bias", [DEPTH, 4])
    gdn_a_log = dram("gdn_a_log", [DEPTH, 4])
    gdn_norm = dram("gdn_norm", [DEPTH, 128])

    EO = "ExternalOutput"
    yout = dram("yout", [NT, D], kind=EO)
    o_p_ssd = dram("o_p_ssd", [DEPTH, 8, 128, 64], kind=EO)
    o_p_ssdc = dram("o_p_ssdc", [DEPTH, 3, 1024], kind=EO)
    o_p_k = dram("o_p_k", [DEPTH, 128, 128], kind=EO)
    o_p_v = dram("o_p_v", [DEPTH, 128, 128], kind=EO)
    o_p_ret = dram("o_p_ret", [DEPTH, 4, 64, 128], kind=EO)
    o_p_gdn = dram("o_p_gdn", [DEPTH, 4, 128, 128], kind=EO)
    o_p_gdnc = dram("o_p_gdnc", [DEPTH, 3, 1536], kind=EO)
    o_s_ssd = dram("o_s_ssd", [DEPTH, NSS, 8, 128, 64], kind=EO)
    o_s_ssdc = dram("o_s_ssdc", [DEPTH, NSS, 3, 1024], kind=EO)
    o_s_k = dram("o_s_k", [DEPTH, NSS, 128, 128], kind=EO)
    o_s_v = dram("o_s_v", [DEPTH, NSS, 128, 128], kind=EO)
    o_s_ret = dram("o_s_ret", [DEPTH, NSS, 4, 64, 128], kind=EO)
    o_s_gdn = dram("o_s_gdn", [DEPTH, NSS, 4, 128, 128], kind=EO)
    o_s_gdnc = dram("o_s_gdnc", [DEPTH, NSS, 3, 1536], kind=EO)
    xscr = dram("xscr", [NT, D], kind="Internal")
    wbf_in = dram("wbf_in", [DEPTH, D, IN_W], BF16, kind="Internal")
    wbf_out = dram("wbf_out", [DEPTH, D, D], BF16, kind="Internal")
    ydbg = dram("ydbg", [NT, D], BF16, kind=EO) if dbg else None

    def sb(name, shape, dt=F32):
        return Buf(es.enter_context(nc.sbuf_tensor(name, list(shape), dt)), name)

    def ps(name, shape, dt=F32):
        return Buf(es.enter_context(nc.psum_tensor(name, list(shape), dt)), name, excl=True)

    DRAMBUF = {}

    def dbuf(ap_name):
        if ap_name not in DRAMBUF:
            DRAMBUF[ap_name] = Buf(None, ap_name)
        return DRAMBUF[ap_name]

    def dma(out, in_, r=(), w=(), eng="pool", **kw):
        P.op(eng, lambda: P.E[eng].dma_start(out=out, in_=in_, **kw), r=r, w=w, dma=True)

    def V(fn, r=(), w=()):
        P.op("dve", fn, r=r, w=w)

    def A(fn, r=(), w=()):
        P.op("act", fn, r=r, w=w)

    def G(fn, r=(), w=()):
        P.op("pool", fn, r=r, w=w)

    def T(fn, r=(), w=()):
        P.op("pe", fn, r=r, w=w)

    def mm(out, lhsT, rhs, start, stop, r, w):
        T(lambda: nc.tensor.matmul(out, lhsT=lhsT, rhs=rhs, start=start, stop=stop), r=r, w=w)

    def tr(out, in_, ident, r, w):
        T(lambda: nc.tensor.transpose(out, in_, ident), r=r, w=w)

    def act(out, in_, func, r, w, bias=None, scale=None, accum_out=None):
        kw = {}
        if bias is not None:
            kw["bias"] = bias
        if scale is not None:
            kw["scale"] = scale
        if accum_out is not None:
            kw["accum_out"] = accum_out
        A(lambda: nc.scalar.activation(out=out, in_=in_, func=func, **kw), r=r, w=w)

    def tt(out, in0, in1, op, r, w, eng="dve"):
        e = nc.vector if eng == "dve" else nc.gpsimd
        P.op(eng, lambda: e.tensor_tensor(out=out, in0=in0, in1=in1, op=op), r=r, w=w)

    def ts(out, in0, s1, op0, r, w, s2=None, op1=None, eng="dve"):
        e = nc.vector if eng == "dve" else nc.gpsimd
        if op1 is None:
            P.op(eng, lambda: e.tensor_scalar(out=out, in0=in0, scalar1=s1, scalar2=None, op0=op0), r=r, w=w)
        else:
            P.op(eng, lambda: e.tensor_scalar(out=out, in0=in0, scalar1=s1, scalar2=s2, op0=op0, op1=op1), r=r, w=w)

    def stt(out, in0, scalar, in1, op0, op1, r, w):
        V(lambda: nc.vector.scalar_tensor_tensor(out=out, in0=in0, scalar=scalar, in1=in1, op0=op0, op1=op1), r=r, w=w)

    def cp(out, in_, r, w, eng="dve"):
        if eng == "act":
            A(lambda: nc.scalar.activation(out=out, in_=in_, func=AF.Copy), r=r, w=w)
        else:
            e = nc.vector if eng == "dve" else nc.gpsimd
            P.op(eng, lambda: e.tensor_copy(out=out, in_=in_), r=r, w=w)

    def memset(buf, ap, val, eng="pool"):
        e = nc.vector if eng == "dve" else nc.gpsimd
        P.op(eng, lambda: e.memset(ap, val), w=[buf])

    def bc(ap, shape):
        return ap.to_broadcast(list(shape))

    ident_f = sb("ident_f", [128, 128])
    ident_b = sb("ident_b", [128, 128], BF16)
    Umat = sb("Umat", [128, 128])
    NEGU = sb("NEGU", [128, 128])
    SUm = sb("SUm", [128, 128])
    NEGown = sb("NEGown", [128, 128], BF16)
    NEGprev = sb("NEGprev", [128, 128], BF16)
    NEGmeta = sb("NEGmeta", [128, 128], BF16)
    ones_b = sb("ones_b", [128, 128], BF16)
    f1 = sb("f1", [128, 512])
    f2 = sb("f2", [128, 512])
    C_f1 = sb("C_f1", [128, 512])
    zeros_f = View(f1, f1[:, 0:128])
    ones_f = View(f1, f1[:, 128:256])
    tmpc = View(f1, f1[:, 256:384])
    tmpc2 = View(f1, f1[:, 384:512])
    dmat = View(f2, f2[:, 0:128])
    iota_fr = View(f2, f2[:, 128:256])
    iota_q = sb("iota_q", [128, 1])

    memset(zeros_f, zeros_f[:], 0.0)
    memset(ones_f, ones_f[:], 1.0)
    memset(ones_b, ones_b[:], 1.0)
    G(lambda: nc.gpsimd.iota(iota_q[:], pattern=[[0, 1]], base=0, channel_multiplier=1,
                             allow_small_or_imprecise_dtypes=True), w=[iota_q])
    G(lambda: nc.gpsimd.iota(iota_fr[:], pattern=[[1, 128]], base=0, channel_multiplier=0,
                             allow_small_or_imprecise_dtypes=True), w=[iota_fr])

    def aff(dst, src, fill, base, cm, step, op):
        G(lambda: nc.gpsimd.affine_select(out=dst[:], in_=src[:], pattern=[[step, 128]], compare_op=op,
                                          fill=fill, base=base, channel_multiplier=cm), r=[src], w=[dst])

    aff(ident_f, ones_f, 0.0, 0, 1, -1, ALU.is_equal)
    cp(ident_b[:], ident_f[:], r=[ident_f], w=[ident_b], eng="pool")
    aff(Umat, ones_f, 0.0, 0, -1, 1, ALU.is_ge)
    aff(NEGU, zeros_f, NEG, 0, -1, 1, ALU.is_ge)
    aff(SUm, ones_f, 0.0, 0, -1, 1, ALU.is_gt)
    cp(NEGown[:], NEGU[:], r=[NEGU], w=[NEGown], eng="pool")
    aff(tmpc, zeros_f, NEG, 0, 1, -1, ALU.is_ge)
    cp(NEGprev[:], tmpc[:], r=[tmpc], w=[NEGprev], eng="pool")
    aff(tmpc2, zeros_f, NEG, 112, 1, -1, ALU.is_ge)
    cp(NEGmeta[:], tmpc2[:], r=[tmpc2], w=[NEGmeta], eng="pool")

    lg = [math.log1p(-2.0 ** (-5 - h)) for h in range(4)]
    RDEC = sb("RDEC", [128, 4, 128])
    GPOW = sb("GPOW", [128, 4])
    RW = {L: sb("RW%d" % L, [128, 4]) for L in (LS, NMETA, 128)}
    GL = {L: sb("GL%d" % L, [64, 4]) for L in (LS, NMETA, 128)}
    ts(dmat[:], iota_fr[:], iota_q[:, 0:1], ALU.subtract, r=[iota_fr, iota_q], w=[dmat])
    ip1 = sb("ip1", [128, 1])
    ts(ip1[:], iota_q[:], 1.0, ALU.add, r=[iota_q], w=[ip1])
    for h in range(4):
        t0 = View(C_f1, C_f1[:, h * 128:(h + 1) * 128])
        ts(t0[:], dmat[:], lg[h], ALU.mult, r=[dmat], w=[t0], s2=NEGU[:, 0:1] if False else None)
        tt(t0[:], t0[:], NEGU[:], ALU.add, r=[t0, NEGU], w=[t0])
        act(RDEC[:, h, :], t0[:], AF.Exp, r=[t0], w=[RDEC])
        act(GPOW[:, h:h + 1], ip1[:], AF.Exp, r=[ip1], w=[GPOW], scale=lg[h])
        for L in RW:
            tq = sb("rwt%d_%d" % (h, L), [128, 1])
            ts(tq[:], iota_q[:], -1.0, ALU.mult, r=[iota_q], w=[tq], s2=float(L - 1), op1=ALU.add)
            act(RW[L][:, h:h + 1], tq[:], AF.Exp, r=[tq], w=[RW[L]], scale=lg[h])
            memset(GL[L], GL[L][:, h:h + 1], math.exp(lg[h] * L), eng="dve")

    slopes = [2.0 ** (-(h + 1)) for h in range(8)]
    SLQ = sb("SLQ", [128, 8])
    for h in range(8):
        ts(SLQ[:, h:h + 1], iota_q[:], slopes[h], ALU.mult, r=[iota_q], w=[SLQ])


    pchunks, schunks = make_chunks()
    blocks = [[pchunks[0]] + schunks]
    for b0 in range(1, 17, cpb):
        blocks.append(pchunks[b0:b0 + cpb])
    if n_blocks is not None:
        blocks = blocks[:n_blocks]
    BT = 128 * cpb
    NSLOT = max(5, cpb)
    WG = 256

    xt = [sb("xt%d" % i, [128, D]) for i in range(2)]
    hb = sb("hb", [128, D], BF16)
    hT = sb("hT", [128, KC, BT], BF16)
    wb = [sb("wb%d" % i, [128, KC, WG], BF16) for i in range(2)]
    SLOTA = sb("SLOTA", [128, 4096], BF16)
    SLOTB = sb("SLOTB", [128, 4096], BF16)
    wb.append(View(SLOTA, SLOTA[:, :].rearrange("p (k n) -> p k n", k=KC, n=WG)))
    wb.append(View(SLOTB, SLOTB[:, :].rearrange("p (k n) -> p k n", k=KC, n=WG)))
    assert NSLOT == 5 and WG == 256
    Gs = [sb("Gs%d" % i, [128, 4, 512], BF16) for i in range(2)]
    Gs.append(View(SLOTA, SLOTA[:, 0:2048].rearrange("p (a b) -> p a b", a=4, b=512)))
    Gs.append(View(SLOTA, SLOTA[:, 2048:4096].rearrange("p (a b) -> p a b", a=4, b=512)))
    Gs.append(View(SLOTB, SLOTB[:, 0:2048].rearrange("p (a b) -> p a b", a=4, b=512)))
    SM = [sb("SM%d" % i, [128, 16]) for i in range(NSLOT)]
    KV = [sb("KV%d" % i, [128, 256]) for i in range(2)]
    KV.append(View(SLOTB, SLOTB[:, 3584:4096].bitcast(F32)))
    KV += [sb("KV%d" % i, [128, 256]) for i in range(3, NSLOT)]
    VAUG = [sb("VAUG%d" % i, [128, 2, 72], BF16) for i in range(NSLOT)]
    VR = [sb("VR%d" % i, [128, 4, 128], BF16) for i in range(2)]
    for i in range(3):
        VR.append(View(SLOTB, SLOTB[:, 2048 + 512 * i:2560 + 512 * i].rearrange("p (a b) -> p a b", a=4, b=128)))
    KR = [sb("KR%d" % i, [128, 4, 64], BF16) for i in range(NSLOT)]
    XBC = sb("XBC", [128, 8, BT], BF16)
    XBCt = [Buf(XBC.t[:, ft_, :], "XBC%d" % ft_) for ft_ in range(8)]
    QA = sb("QA", [65, 8, BT], BF16)
    KA = sb("KA", [65, 2, BT], BF16)
    QR = sb("QR", [64, 4, BT], BF16)
    KRF = sb("KRF", [64, 4, BT], BF16)
    QKVD = sb("QKVD", [128, 12, BT], BF16)
    QKVDt = [Buf(QKVD.t[:, ft_, :], "QKVD%d" % ft_) for ft_ in range(12)]
    ST = [sb("ST%d" % i, [128, BT + 4]) for i in range(2)]
    ACC = [sb("ACC%d" % i, [128, BT]) for i in range(2)]
    carS = sb("carS", [128, 8, 3])
    carG = sb("carG", [128, 12, 3])
    cstSs = [sb("cstS%d" % i, [128, 8, NSS, 3]) for i in range(2)]
    cstGs = [sb("cstG%d" % i, [128, 12, NSS, 3]) for i in range(2)]
    csoS = sb("csoS", [128, 8, NSS, 3])
    csoG = sb("csoG", [128, 12, NSS, 3])
    Yb = [sb("Y%d" % i, [128, D], BF16) for i in range(2)]
    ssq = sb("ssq", [128, 8])
    lnv = sb("lnv", [128, 8])
    rstd = sb("rstd", [128, 8])
    smalls = {}
    for _n in ("E0_ssq", "E1_ssq", "E_tot", "E_lnv", "E_rstd", "F0_ssq", "F1_ssq", "F_tot", "F_lnv", "F0_rstd", "F1_rstd"):
        smalls[_n] = sb(_n, [128, 8])
    for _p in "ACD":
        for _n in ("ssq", "lnv", "rstd", "negcum", "ecum", "ecl", "dtr", "dtv", "lav"):
            smalls[_p + "_" + _n] = sb(_p + "_" + _n, [128, 8])
    sqj = sb("sqj", [128, 512], BF16)
    gTs = [sb("gT%d" % i, [128, KC]) for i in range(2)]
    postg = sb("postg", [128, D])
    ssdn = sb("ssdn", [128, 512])
    retn = sb("retn", [128, 512])
    gdnn = sb("gdnn", [128, 4, 128])
    cwSs = [sb("cwS%d" % i, [128, 8, 4]) for i in range(2)]
    cbSs = [sb("cbS%d" % i, [128, 8]) for i in range(2)]
    cwGs = [sb("cwG%d" % i, [128, 12, 4]) for i in range(2)]
    dtbS = sb("dtbS", [128, 8])
    AnS = sb("AnS", [128, 8])
    Dss = sb("Dss", [128, 8])
    ESQ = sb("ESQ", [128, 8])
    dtbG = sb("dtbG", [128, 4])
    AnG = sb("AnG", [128, 4])
    Sssd = [sb("Sssd%d" % i, [128, 8, 64]) for i in range(2)]
    Sssdb = [sb("Sssdb%d" % i, [128, 8, 64], BF16) for i in range(2)]
    Sret = [sb("Sret%d" % i, [64, 4, 128]) for i in range(2)]
    Sretb = [sb("Sretb%d" % i, [64, 4, 128], BF16) for i in range(2)]
    Sgdn = [sb("Sgdn%d" % i, [128, 4, 128]) for i in range(2)]
    Sgdnb = [sb("Sgdnb%d" % i, [128, 4, 128], BF16) for i in range(2)]
    PKA = sb("PKA", [65, 2, 128], BF16)
    PVA = sb("PVA", [128, 2, 72], BF16)
    PKAm = sb("PKAm", [65, 2, NMETA], BF16)
    PVAm = sb("PVAm", [128, 2, 72], BF16)
    cKb = sb("cKb", [128, 128], BF16)
    junkA = sb("junkA", [128, 512], BF16)
    junkC = sb("junkC", [128, 512], BF16)
    junkD = sb("junkD", [128, 512], BF16)
    C_ng = sb("C_ng", [128, 512])
    D_ng = sb("D_ng", [128, 512])
    decT = sb("decT", [128, 8, 128])
    negcum = sb("negcum", [128, 8])
    ecum = sb("ecum", [128, 8])
    ecl = sb("ecl", [128, 8])
    dtr = sb("dtr", [128, 8])
    dtv = sb("dtv", [128, 8])
    lav = sb("lav", [128, 8])
    MT = sb("MT", [128, 8, 128], BF16)
    xs_tm = sb("xs_tm", [128, 512], BF16)
    xdt = sb("xdt", [128, 512], BF16)
    xdtw = sb("xdtw", [128, 512], BF16)
    Btm = sb("Btm", [128, 256], BF16)
    D_f1 = sb("D_f1", [128, 512])
    D_f3 = sb("D_f3", [128, 512])
    B_f2 = sb("B_f2", [128, 512])
    cK = View(B_f2, B_f2[:, 0:256])
    C_MT = sb("C_MT", [128, 4, 128], BF16)
    D_decT = sb("D_decT", [128, 4, 128])
    kw = sb("kw", [128, 4, 64], BF16)
    beta = sb("beta", [128, 4])
    nbeta = sb("nbeta", [128, 4])
    Pm = [sb("Pm%d" % i, [128, 4, 128]) for i in range(2)]
    PTm = [sb("PTm%d" % i, [128, 4, 128]) for i in range(2)]
    Rm = [sb("Rm%d" % i, [128, 4, 128]) for i in range(2)]
    QKd = sb("QKd", [128, 4, 128], BF16)
    Vtm = sb("Vtm", [128, 512], BF16)
    knw = sb("knw", [128, 512], BF16)
    vnew = sb("vnew", [128, 512], BF16)
    PTs = [sb("PTs%d" % i, [128, 4, 128], BF16) for i in range(4)]
    den = sb("den", [128, 8])

    banks = [ps("bank%d" % i, [128, 512]) for i in range(8)]
    bctr = [0]

    def nb():
        b = banks[bctr[0] % 8]
        bctr[0] += 1
        return b

    def bfv(bk):
        return bk.t[:].bitcast(BF16)

    def v3(ap, a, b):
        return ap.rearrange("p (a b) -> p a b", a=a, b=b)

    for h in range(8):
        memset(QA, QA[64:65, h, :], 8.0 * slopes[h])
    G(lambda: nc.gpsimd.iota(PKA[64:65, :, :], pattern=[[0, 2], [1, 128]], base=-128, channel_multiplier=0,
                             allow_small_or_imprecise_dtypes=True), w=[PKA])
    G(lambda: nc.gpsimd.iota(PKAm[64:65, :, :], pattern=[[0, 2], [1, NMETA]], base=-NMETA, channel_multiplier=0,
                             allow_small_or_imprecise_dtypes=True), w=[PKAm])
    for i in range(NSLOT):
        memset(VAUG[i], VAUG[i][:, :, 64:65], 1.0)
    memset(PVA, PVA[:, :, 64:65], 1.0)
    memset(PVAm, PVAm[:, :, 64:65], 1.0)

    xbufs = {}

    def xbuf(bi):
        if bi not in xbufs:
            xbufs[bi] = Buf(None, "x%d" % bi)
        return xbufs[bi]

    wview_in = [wbf_in[l].rearrange("(kc p) n -> p kc n", p=128) for l in range(DEPTH)]
    wview_out = [wbf_out[l].rearrange("(kc p) n -> p kc n", p=128) for l in range(DEPTH)]
    wscr_bufs = {l: [Buf(None, "wscr%d_%d" % (l, i)) for i in range(5)] for l in range(DEPTH)}
    wctr = [0]
    nwb = [2]
    cur_layer = [0]

    def convert_weights(l, piece=None):
        for i in range(4):
            if piece is None or piece == i:
                dma(wbf_in[l, i * 512:(i + 1) * 512, :], w_in[l, i * 512:(i + 1) * 512, :], w=[wscr_bufs[l][i]],
                    eng="pool")
        if piece is None or piece == 4:
            dma(wbf_out[l], w_out[l], w=[wscr_bufs[l][4]], eng="pool")

    def load_w(view, c0, width, off=0, buf=None):
        if buf is None:
            buf = wb[wctr[0] % nwb[0]]
            wctr[0] += 1
        dma(buf[:, :, off:off + width], view[:, :, c0:c0 + width], r=wscr_bufs[cur_layer[0]], w=[buf], eng="sp")
        return buf

    def rms_stats(src_ap, L, n, scale, col=0):
        pass

    def chk(tag):
        if stop == tag:
            raise StopBuild()

    XA = [[f1, f2, C_f1, D_f1], [D_f3, B_f2, C_ng, D_ng]]
    a1_done = set()

    def tiles_of(bi2):
        if bi2 == 0:
            return [dict(row0=0, L=NMETA, tok0=0), dict(row0=NPT, L=NSS * LS, tok0=NMETA)]
        return [dict(row0=ch_["row0"], L=128, tok0=128 * i_) for i_, ch_ in enumerate(blocks[bi2])]

    def stage_a1(l2, bi2):
        if (l2, bi2) in a1_done:
            return
        a1_done.add((l2, bi2))
        xs2 = xin if l2 == 0 else xscr
        F_tot, F_lnv = smalls["F_tot"], smalls["F_lnv"]
        for ti, tl in enumerate(tiles_of(bi2)):
            L, r0 = tl["L"], tl["row0"]
            xa = XA[ti % 2]
            F_ssq, F_rstd = smalls["F%d_ssq" % (ti % 2)], smalls["F%d_rstd" % (ti % 2)]
            for c in range(4):
                dma(xa[c][0:L, :], xs2[r0:r0 + L, c * 512:(c + 1) * 512], r=[xbuf(bi2)], w=[xa[c]])
                act(junkC[0:L, :], xa[c][0:L, :], AF.Square, r=[xa[c]], w=[junkC, F_ssq], accum_out=F_ssq[0:L, c:c + 1])
            V(lambda L=L, F_ssq=F_ssq: nc.vector.tensor_reduce(out=F_tot[0:L, 0:1], in_=F_ssq[0:L, 0:4], axis=AX.X,
                                                               op=ALU.add), r=[F_ssq], w=[F_tot])
            act(F_lnv[0:L, 0:1], F_tot[0:L, 0:1], AF.Ln, r=[F_tot], w=[F_lnv], scale=1.0 / D, bias=EPS)
            act(F_rstd[0:L, 0:1], F_lnv[0:L, 0:1], AF.Exp, r=[F_lnv], w=[F_rstd], scale=-0.5)

    pending_epi = []

    def flush_epi():
        while pending_epi:
            pending_epi.pop(0)()

    parts_of = {}

    def multi_load(parent, pairs):
        parts = []
        for i, (o_, i_) in enumerate(pairs):
            pb = parent if i == 0 else Buf(None, "part")
            dma(o_, i_, w=[pb], eng="act", allow_slow_non_contiguous=True)
            if i > 0:
                parts.append(pb)
        parts_of[id(parent)] = parts

    def RD(parent):
        return [parent] + parts_of.get(id(parent), [])

    def load_slow_params(l):
        p = l % 2
        multi_load(gTs[p], [(gTs[p][:], pre_norm[l].rearrange("(kc p) -> p kc", p=128))])
        multi_load(cwSs[p], [(cwSs[p][:, :, k_], ssd_conv_w[l, k_].rearrange("(ft p) -> p ft", p=128)) for k_ in range(4)])
        multi_load(cwGs[p], [(cwGs[p][:, :, k_], gdn_conv_w[l, k_].rearrange("(ft p) -> p ft", p=128)) for k_ in range(4)])
        multi_load(cbSs[p], [(cbSs[p][:], ssd_conv_b[l].rearrange("(ft p) -> p ft", p=128))])
        multi_load(cstSs[p], [(cstSs[p][:, :, s_, t_], st_ssdc[l, s_, t_].rearrange("(ft p) -> p ft", p=128))
                              for s_ in range(NSS) for t_ in range(3)])
        multi_load(cstGs[p], [(cstGs[p][:, :, s_, t_], st_gdnc[l, s_, t_].rearrange("(ft p) -> p ft", p=128))
                              for s_ in range(NSS) for t_ in range(3)])

    try:
        for l in range(n_layers):
            chk('const')
            cur_layer[0] = l
            if l == 0:
                convert_weights(0)
            xsrc = xin if l == 0 else xscr
            xdst = yout if l == n_layers - 1 else xscr
            if l == 0:
                load_slow_params(0)
            gT, cwS, cbS, cwG, cstS, cstG = [b[l % 2] for b in (gTs, cwSs, cbSs, cwGs, cstSs, cstGs)]
            dma(postg[:], bc(post_norm[l:l + 1, :], [128, D]), w=[postg])
            dma(ssdn[:], bc(ssd_norm[l:l + 1, :], [128, 512]), w=[ssdn])
            dma(retn[:], bc(ret_norm[l:l + 1, :], [128, 512]), w=[retn])
            for h in range(4):
                dma(gdnn[:, h, :], bc(gdn_norm[l:l + 1, :], [128, 128]), w=[gdnn])
            dma(dtbS[:], bc(ssd_dt_bias[l:l + 1, :], [128, 8]), w=[dtbS])
            dma(AnS[:], bc(ssd_a_log[l:l + 1, :], [128, 8]), w=[AnS])
            dma(Dss[:], bc(ssd_d[l:l + 1, :], [128, 8]), w=[Dss])
            dma(ESQ[:], bc(swa_sinks[l:l + 1, :], [128, 8]), w=[ESQ])
            dma(dtbG[:], bc(gdn_dt_bias[l:l + 1, :], [128, 4]), w=[dtbG])
            dma(AnG[:], bc(gdn_a_log[l:l + 1, :], [128, 4]), w=[AnG])
            act(AnS[:], AnS[:], AF.Exp, r=[AnS], w=[AnS])
            ts(AnS[:], AnS[:], -1.0, ALU.mult, r=[AnS], w=[AnS])
            act(AnG[:], AnG[:], AF.Exp, r=[AnG], w=[AnG])
            ts(AnG[:], AnG[:], -1.0, ALU.mult, r=[AnG], w=[AnG])
            tt(ESQ[:], ESQ[:], SLQ[:], ALU.add, r=[ESQ, SLQ], w=[ESQ])
            act(ESQ[:], ESQ[:], AF.Exp, r=[ESQ], w=[ESQ])
            memset(Sssd[0], Sssd[0][:], 0.0)
            memset(Sssdb[0], Sssdb[0][:], 0.0)
            memset(Sret[0], Sret[0][:], 0.0)
            memset(Sretb[0], Sretb[0][:], 0.0)
            memset(Sgdn[0], Sgdn[0][:], 0.0)
            memset(Sgdnb[0], Sgdnb[0][:], 0.0)
            memset(carS, carS[:], 0.0)
            memset(carG, carG[:], 0.0)

            for bi, blk in enumerate(blocks):
                is0 = (bi == 0)
                nwb[0] = 2 if is0 else 4
                tok = 0
                for si, ch in enumerate(blk):
                    ch["tok0"] = tok
                    ch["slot"] = si
                    tok += ch["L"]
                nbt = tok
                if is0:
                    tm_tiles = [dict(row0=0, L=NMETA, tok0=0), dict(row0=NPT, L=NSS * LS, tok0=NMETA)]
                else:
                    tm_tiles = [dict(row0=ch["row0"], L=128, tok0=ch["tok0"]) for ch in blk]
                if is0:
                    G(lambda: nc.gpsimd.iota(KA[64:65, :, 0:NMETA], pattern=[[0, 2], [1, NMETA]], base=0,
                                             channel_multiplier=0, allow_small_or_imprecise_dtypes=True), w=[KA])
                    G(lambda: nc.gpsimd.iota(KA[64:65, :, NMETA:NMETA + 16], pattern=[[0, 2], [0, NSS], [1, LS]], base=0,
                                             channel_multiplier=0, allow_small_or_imprecise_dtypes=True), w=[KA])
                elif bi == 1:
                    G(lambda: nc.gpsimd.iota(KA[64:65, :, :], pattern=[[0, 2], [0, cpb], [1, 128]], base=0,
                                             channel_multiplier=0, allow_small_or_imprecise_dtypes=True), w=[KA])

                chk('params')
                stage_a1(l, bi)
                for ti, tl in enumerate(tm_tiles):
                    L, r0, t0 = tl["L"], tl["row0"], tl["tok0"]
                    xa = XA[ti % 2]
                    F_rstd = smalls["F%d_rstd" % (ti % 2)]
                    for c in range(4):
                        ts(hb[0:L, c * 512:(c + 1) * 512], xa[c][0:L, :], F_rstd[0:L, 0:1], ALU.mult, r=[xa[c], F_rstd],
                           w=[hb])
                    for q in range(4):
                        bk = nb()
                        bv = bfv(bk)
                        for j in range(4):
                            kc = 4 * q + j
                            tr(bv[:, j * 128:j * 128 + L], hb[0:L, kc * 128:(kc + 1) * 128], ident_b[0:L, 0:L],
                               r=[hb, ident_b], w=[bk])
                        tt(hT[:, 4 * q:4 * q + 4, t0:t0 + L], v3(bv[:, 0:512], 4, 128)[:, :, 0:L],
                           bc(gT[:, 4 * q:4 * q + 4].unsqueeze(2), [128, 4, L]), ALU.mult, r=[bk] + RD(gT), w=[hT])
                flush_epi()

                def decay(la, L, H, decT_, negcum_, ecum_, ecl_):
                    bk = nb()
                    mm(bk[0:L, 0:H], Umat[0:L, 0:L], la[0:L, 0:H], True, True, r=[Umat, la], w=[bk])
                    ts(negcum_[0:L, 0:H], bk[0:L, 0:H], -1.0, ALU.mult, r=[bk], w=[negcum_])
                    act(ecum_[0:L, 0:H], bk[0:L, 0:H], AF.Exp, r=[bk], w=[ecum_])
                    yield
                    for hq in range(H // 4):
                        bk = nb()
                        for hh in range(4):
                            h = 4 * hq + hh
                            o = bk[:, hh * 128:hh * 128 + L]
                            mm(o, bc(la[0:L, h:h + 1], [L, 128]), Umat[0:L, 0:L], True, False, r=[la, Umat], w=[bk])
                            mm(o, ident_f[0:L, :], NEGU[0:L, 0:L], False, True, r=[ident_f, NEGU], w=[bk])
                        yield
                        for hh in range(4):
                            h = 4 * hq + hh
                            act(decT_[0:L, h, 0:L], bk[0:L, hh * 128:hh * 128 + L], AF.Exp, r=[bk, negcum_], w=[decT_],
                                bias=negcum_[0:L, h:h + 1])
                        act(ecl_[:, 4 * hq:4 * hq + 4], v3(bk[:, 0:512], 4, 128)[:, :, L - 1], AF.Exp, r=[bk], w=[ecl_])
                        yield

                def softplus_la(dst, src_ap, src_bufs, dtb, An, L, H, dtr_, keep_dt=None):
                    tt(dtr_[0:L, 0:H], src_ap, dtb[0:L, 0:H], ALU.add, r=src_bufs + [dtb], w=[dtr_])
                    act(dtr_[0:L, 0:H], dtr_[0:L, 0:H], AF.Exp, r=[dtr_], w=[dtr_])
                    tgt = keep_dt if keep_dt is not None else dtr_
                    act(tgt[0:L, 0:H], dtr_[0:L, 0:H], AF.Ln, r=[dtr_], w=[tgt], bias=1.0)
                    tt(dst[0:L, 0:H], tgt[0:L, 0:H], An[0:L, 0:H], ALU.mult, r=[tgt, An], w=[dst])

                def head_rmsnorm_gate(o_buf, junk_, ssq_, lnv_, rstd_, Yc, L, nh, hd, ng_, ycols):
                    n = nh * hd
                    for h in range(nh):
                        act(junk_[0:L, h * hd:(h + 1) * hd], o_buf[0:L, h * hd:(h + 1) * hd], AF.Square, r=[o_buf],
                            w=[junk_, ssq_], accum_out=ssq_[0:L, h:h + 1])
                    act(lnv_[0:L, 0:nh], ssq_[0:L, 0:nh], AF.Ln, r=[ssq_], w=[lnv_], scale=1.0 / hd, bias=EPS)
                    act(rstd_[0:L, 0:nh], lnv_[0:L, 0:nh], AF.Exp, r=[lnv_], w=[rstd_], scale=-0.5)
                    yield
                    tt(v3(o_buf[0:L, 0:n], nh, hd), v3(o_buf[0:L, 0:n], nh, hd), bc(rstd_[0:L, 0:nh].unsqueeze(2), [L, nh, hd]),
                       ALU.mult, r=[o_buf, rstd_], w=[o_buf])
                    tt(Yc[0:L, ycols:ycols + n], o_buf[0:L, 0:n], ng_[0:L, 0:n], ALU.mult, r=[o_buf, ng_], w=[Yc])

                def chunk_ctx(ch):
                    sid = 0 if ch["kind"] == "p" else 1
                    return ch["L"], ch["tok0"], ch["slot"], sid, ch.get("seq", None), Gs[ch["slot"]], Yb[ch["slot"] % 2]

                def ssd_thread():
                    A = lambda n: smalls["A_" + n]
                    ssq_, lnv_, rstd_, negcum_, ecum_, ecl_, dtr_, dtv_, lav_ = [A(n) for n in (
                        "ssq", "lnv", "rstd", "negcum", "ecum", "ecl", "dtr", "dtv", "lav")]
                    for ch in blk:
                        L, t0, slot, sid, seq, G_, Yc = chunk_ctx(ch)
                        while slot >= 2 and not fin.get(slot - 2):
                            yield
                        S1, S1b = Sssd[sid], Sssdb[sid]
                        if sid == 1:
                            dma(S1[:], st_ssd[l, seq].rearrange("h n e -> n h e"), w=[S1])
                            cp(S1b[:], S1[:], r=[S1], w=[S1b], eng="act")
                        softplus_la(lav_, SM[slot][0:L, 0:8], [SM[slot]], dtbS, AnS, L, 8, dtr_, keep_dt=dtv_)
                        yield
                        yield from decay(lav_, L, 8, decT, negcum_, ecum_, ecl_)
                        yield ("wait_proj",)
                        bk = nb()
                        bv = bfv(bk)
                        for ft in range(4):
                            tr(bv[0:L, ft * 128:(ft + 1) * 128], XBCt[ft][:, t0:t0 + L], ident_b[:, :], r=[XBCt[ft], ident_b], w=[bk])
                        yield
                        cp(xs_tm[0:L, :], bv[0:L, 0:512], r=[bk], w=[xs_tm], eng="act")
                        tt(v3(xdt[0:L, :], 8, 64), v3(bv[0:L, 0:512], 8, 64), bc(dtv_[0:L, 0:8].unsqueeze(2), [L, 8, 64]),
                           ALU.mult, r=[bk, dtv_], w=[xdt])
                        bk = nb()
                        bv = bfv(bk)
                        for g in range(2):
                            tr(bv[0:L, g * 128:(g + 1) * 128], XBCt[4 + g][:, t0:t0 + L], ident_b[:, :], r=[XBCt[4 + g], ident_b], w=[bk])
                        yield
                        cp(Btm[0:L, :], bv[0:L, 0:256], r=[bk], w=[Btm], eng="act")
                        bk = nb()
                        for g in range(2):
                            mm(bk[0:L, g * 128:g * 128 + L], XBCt[4 + g][:, t0:t0 + L], XBCt[6 + g][:, t0:t0 + L], True, True,
                               r=[XBCt[4 + g], XBCt[6 + g]], w=[bk])
                        yield
                        for g in range(2):
                            tt(MT[0:L, 4 * g:4 * g + 4, 0:L], bc(bk[0:L, g * 128:g * 128 + L].unsqueeze(1), [L, 4, L]),
                               decT[0:L, 4 * g:4 * g + 4, 0:L], ALU.mult, r=[bk, decT], w=[MT])
                        yield
                        bki = nb()
                        for h in range(8):
                            mm(bki[0:L, h * 64:(h + 1) * 64], MT[0:L, h, 0:L], xdt[0:L, h * 64:(h + 1) * 64], True, True,
                               r=[MT, xdt], w=[bki])
                        bks = nb()
                        for h in range(8):
                            mm(bks[0:L, h * 64:(h + 1) * 64], XBCt[6 + h // 4][:, t0:t0 + L], S1b[:, h, :], True, True,
                               r=[XBCt[6 + h // 4], S1b], w=[bks])
                        yield
                        tt(v3(f1[0:L, :], 8, 64), v3(bks[0:L, :], 8, 64), bc(ecum_[0:L, 0:8].unsqueeze(2), [L, 8, 64]), ALU.mult,
                           r=[bks, ecum_], w=[f1])
                        tt(f1[0:L, :], bki[0:L, :], f1[0:L, :], ALU.add, r=[bki, f1], w=[f1])
                        tt(v3(f2[0:L, :], 8, 64), v3(xs_tm[0:L, :], 8, 64), bc(Dss[0:L, 0:8].unsqueeze(2), [L, 8, 64]), ALU.mult,
                           r=[xs_tm, Dss], w=[f2], eng="pool")
                        yield
                        tt(f1[0:L, :], f1[0:L, :], f2[0:L, :], ALU.add, r=[f1, f2], w=[f1])
                        tt(f1[0:L, :], f1[0:L, :], G_[0:L, 0, :], ALU.mult, r=[f1, G_], w=[f1])
                        for g in range(2):
                            act(junkA[0:L, g * 256:(g + 1) * 256], f1[0:L, g * 256:(g + 1) * 256], AF.Square, r=[f1],
                                w=[junkA, ssq_], accum_out=ssq_[0:L, g:g + 1])
                        yield
                        act(lnv_[0:L, 0:2], ssq_[0:L, 0:2], AF.Ln, r=[ssq_], w=[lnv_], scale=1.0 / 256, bias=EPS)
                        act(rstd_[0:L, 0:2], lnv_[0:L, 0:2], AF.Exp, r=[lnv_], w=[rstd_], scale=-0.5)
                        yield
                        tt(v3(f2[0:L, :], 2, 256), v3(f1[0:L, :], 2, 256), bc(rstd_[0:L, 0:2].unsqueeze(2), [L, 2, 256]),
                           ALU.mult, r=[f1, rstd_], w=[f2])
                        tt(Yc[0:L, 0:512], f2[0:L, :], ssdn[0:L, :], ALU.mult, r=[f2, ssdn], w=[Yc])
                        yield
                        tt(v3(xdtw[0:L, :], 8, 64), v3(xdt[0:L, :], 8, 64), bc(decT[0:L, 0:8, L - 1:L], [L, 8, 64]), ALU.mult,
                           r=[xdt, decT], w=[xdtw])
                        bkn = nb()
                        for h in range(8):
                            mm(bkn[:, h * 64:(h + 1) * 64], Btm[0:L, (h // 4) * 128:(h // 4 + 1) * 128],
                               xdtw[0:L, h * 64:(h + 1) * 64], True, True, r=[Btm, xdtw], w=[bkn])
                        tt(S1[:], S1[:], bc(ecl_[:, 0:8].unsqueeze(2), [128, 8, 64]), ALU.mult, r=[S1, ecl_], w=[S1])
                        yield
                        tt(S1[:], v3(bkn[:, :], 8, 64), S1[:], ALU.add, r=[bkn, S1], w=[S1])
                        cp(S1b[:], S1[:], r=[S1], w=[S1b], eng="act")
                        if sid == 1:
                            dma(o_s_ssd[l, seq].rearrange("h n e -> n h e"), S1[:], r=[S1], w=[dbuf("o_s_ssd")])
                        yield ("done", ch["slot"])

                def ret_thread():
                    A = lambda n: smalls["C_" + n]
                    ssq_, lnv_, rstd_ = A("ssq"), A("lnv"), A("rstd")
                    for ch in blk:
                        L, t0, slot, sid, seq, G_, Yc = chunk_ctx(ch)
                        while slot >= 2 and not fin.get(slot - 2):
                            yield
                        S2, S2b = Sret[sid], Sretb[sid]
                        tt(C_ng[0:L, :], retn[0:L, :], G_[0:L, 2, :], ALU.mult, r=[retn, G_], w=[C_ng], eng="pool")
                        yield ("wait_proj",)
                        if sid == 1:
                            dma(S2[:], st_ret[l, seq].rearrange("h d e -> d h e"), w=[S2])
                            cp(S2b[:], S2[:], r=[S2], w=[S2b], eng="act")
                        bk = nb()
                        for h in range(4):
                            mm(bk[0:L, h * 128:h * 128 + L], KRF[0:64, h, t0:t0 + L], QR[0:64, h, t0:t0 + L], True, True,
                               r=[KRF, QR], w=[bk])
                        yield
                        tt(C_MT[0:L, 0:4, 0:L], v3(bk[0:L, :], 4, 128)[:, :, 0:L], RDEC[0:L, :, 0:L], ALU.mult, r=[bk, RDEC],
                           w=[C_MT])
                        yield
                        bki = nb()
                        for h in range(4):
                            mm(bki[0:L, h * 128:(h + 1) * 128], C_MT[0:L, h, 0:L], VR[slot][0:L, h, :], True, True,
                               r=[C_MT, VR[slot]], w=[bki])
                        bks = nb()
                        for h in range(4):
                            mm(bks[0:L, h * 128:(h + 1) * 128], QR[0:64, h, t0:t0 + L], S2b[:, h, :], True, True,
                               r=[QR, S2b], w=[bks])
                        yield
                        tt(v3(C_f1[0:L, :], 4, 128), v3(bks[0:L, :], 4, 128), bc(GPOW[0:L, 0:4].unsqueeze(2), [L, 4, 128]),
                           ALU.mult, r=[bks, GPOW], w=[C_f1])
                        tt(C_f1[0:L, :], bki[0:L, :], C_f1[0:L, :], ALU.add, r=[bki, C_f1], w=[C_f1])
                        yield
                        yield from head_rmsnorm_gate(C_f1, junkC, ssq_, lnv_, rstd_, Yc, L, 4, 128, C_ng, 1024)
                        yield
                        tt(kw[0:L, :, :], KR[slot][0:L, :, :], bc(RW[L][0:L, 0:4].unsqueeze(2), [L, 4, 64]), ALU.mult,
                           r=[KR[slot], RW[L]], w=[kw], eng="pool")
                        bkn = nb()
                        for h in range(4):
                            mm(bkn[0:64, h * 128:(h + 1) * 128], kw[0:L, h, :], VR[slot][0:L, h, :], True, True,
                               r=[kw, VR[slot]], w=[bkn])
                        tt(S2[:], S2[:], bc(GL[L][:, 0:4].unsqueeze(2), [64, 4, 128]), ALU.mult, r=[S2, GL[L]], w=[S2])
                        yield
                        tt(S2[:], v3(bkn[0:64, :], 4, 128), S2[:], ALU.add, r=[bkn, S2], w=[S2])
                        cp(S2b[:], S2[:], r=[S2], w=[S2b], eng="act")
                        if sid == 1:
                            dma(o_s_ret[l, seq].rearrange("h d e -> d h e"), S2[:], r=[S2], w=[dbuf("o_s_ret")])
                        yield ("done", ch["slot"])

                def gdn_thread():
                    A = lambda n: smalls["D_" + n]
                    ssq_, lnv_, rstd_, negcum_, ecum_, ecl_, dtr_, lav_ = [A(n) for n in (
                        "ssq", "lnv", "rstd", "negcum", "ecum", "ecl", "dtr", "lav")]
                    M1 = Pm[1]
                    for ch in blk:
                        L, t0, slot, sid, seq, G_, Yc = chunk_ctx(ch)
                        while slot >= 2 and not fin.get(slot - 2):
                            yield
                        S3, S3b = Sgdn[sid], Sgdnb[sid]
                        tt(D_ng[0:L, :], gdnn[0:L, :, :].rearrange("p a b -> p (a b)"), G_[0:L, 3, :], ALU.mult, r=[gdnn, G_],
                           w=[D_ng], eng="pool")
                        if sid == 1:
                            dma(S3[:], st_gdn[l, seq].rearrange("h d e -> d h e"), w=[S3])
                            cp(S3b[:], S3[:], r=[S3], w=[S3b], eng="act")
                        act(beta[0:L, :], SM[slot][0:L, 8:12], AF.Exp, r=[SM[slot]], w=[beta], scale=-1.0)
                        ts(beta[0:L, :], beta[0:L, :], 1.0, ALU.add, r=[beta], w=[beta])
                        V(lambda L=L: nc.vector.reciprocal(out=beta[0:L, :], in_=beta[0:L, :]), r=[beta], w=[beta])
                        ts(nbeta[0:L, :], beta[0:L, :], -1.0, ALU.mult, r=[beta], w=[nbeta])
                        yield
                        softplus_la(lav_, SM[slot][0:L, 12:16], [SM[slot]], dtbG, AnG, L, 4, dtr_)
                        yield
                        yield from decay(lav_, L, 4, D_decT, negcum_, ecum_, ecl_)
                        yield ("wait_proj",)
                        bk = nb()
                        bv = bfv(bk)
                        for h in range(4):
                            tr(bv[0:L, h * 128:(h + 1) * 128], QKVDt[4 + h][:, t0:t0 + L], ident_b[:, :], r=[QKVDt[4 + h], ident_b],
                               w=[bk])
                        yield
                        tt(v3(knw[0:L, :], 4, 128), v3(bv[0:L, 0:512], 4, 128), bc(D_decT[0:L, 0:4, L - 1:L], [L, 4, 128]),
                           ALU.mult, r=[bk, D_decT], w=[knw])
                        bk = nb()
                        bv = bfv(bk)
                        for h in range(4):
                            tr(bv[0:L, h * 128:(h + 1) * 128], QKVDt[8 + h][:, t0:t0 + L], ident_b[:, :], r=[QKVDt[8 + h], ident_b],
                               w=[bk])
                        yield
                        cp(Vtm[0:L, :], bv[0:L, 0:512], r=[bk], w=[Vtm], eng="act")
                        bkg = nb()
                        bkq = nb()
                        for h in range(4):
                            mm(bkg[0:L, h * 128:h * 128 + L], QKVDt[4 + h][:, t0:t0 + L], QKVDt[4 + h][:, t0:t0 + L], True, True,
                               r=[QKVDt[4 + h]], w=[bkg])
                        for h in range(4):
                            mm(bkq[0:L, h * 128:h * 128 + L], QKVDt[4 + h][:, t0:t0 + L], QKVDt[h][:, t0:t0 + L], True, True,
                               r=[QKVDt[4 + h], QKVDt[h]], w=[bkq])
                        yield
                        tt(M1[0:L, :, 0:L], v3(bkg[0:L, :], 4, 128)[:, :, 0:L], D_decT[0:L, 0:4, 0:L], ALU.mult,
                           r=[bkg, D_decT], w=[M1])
                        tt(QKd[0:L, :, 0:L], v3(bkq[0:L, :], 4, 128)[:, :, 0:L], D_decT[0:L, 0:4, 0:L], ALU.mult,
                           r=[bkq, D_decT], w=[QKd])
                        yield
                        tt(M1[0:L, :, 0:L], M1[0:L, :, 0:L], bc(nbeta[0:L, 0:4].unsqueeze(2), [L, 4, L]), ALU.mult,
                           r=[M1, nbeta], w=[M1])
                        P0_, PT0_ = Pm[0], PTm[0]
                        tt(P0_[0:L, :, 0:L], M1[0:L, :, 0:L], bc(SUm[0:L, 0:L].unsqueeze(1), [L, 4, L]), ALU.mult,
                           r=[M1, SUm], w=[P0_])
                        yield
                        bk = nb()
                        for h in range(4):
                            tr(bk[0:L, h * 128:h * 128 + L], P0_[0:L, h, 0:L], ident_f[0:L, 0:L], r=[P0_, ident_f], w=[bk])
                        yield
                        cp(PT0_[0:L, :, 0:L], v3(bk[0:L, :], 4, 128)[:, :, 0:L], r=[bk], w=[PT0_], eng="act")
                        R_ = Rm[0]
                        tt(R_[0:L, :, 0:L], P0_[0:L, :, 0:L], bc(ident_f[0:L, 0:L].unsqueeze(1), [L, 4, L]), ALU.add,
                           r=[P0_, ident_f], w=[R_], eng="pool")
                        yield
                        nlev = max(1, int(math.ceil(math.log2(L))))
                        cur = 0
                        for k in range(1, nlev):
                            Pc, PTc = Pm[cur], PTm[cur]
                            Pn, PTn = Pm[1 - cur], PTm[1 - cur]
                            last = (k == nlev - 1)
                            bkt = nb()
                            for h in range(4):
                                mm(bkt[0:L, h * 128:h * 128 + L], Pc[0:L, h, 0:L], PTc[0:L, h, 0:L], True, True,
                                   r=[Pc, PTc], w=[bkt])
                            if not last:
                                bkp = nb()
                                for h in range(4):
                                    mm(bkp[0:L, h * 128:h * 128 + L], PTc[0:L, h, 0:L], Pc[0:L, h, 0:L], True, True,
                                       r=[Pc, PTc], w=[bkp])
                            yield
                            cp(PTn[0:L, :, 0:L], v3(bkt[0:L, :], 4, 128)[:, :, 0:L], r=[bkt], w=[PTn], eng="act")
                            if not last:
                                cp(Pn[0:L, :, 0:L], v3(bkp[0:L, :], 4, 128)[:, :, 0:L], r=[bkp], w=[Pn])
                            yield
                            Rc, Rn = Rm[cur], Rm[1 - cur]
                            bkr = nb()
                            for h in range(4):
                                mm(bkr[0:L, h * 128:h * 128 + L], PTn[0:L, h, 0:L], Rc[0:L, h, 0:L], True, True,
                                   r=[PTn, Rc], w=[bkr])
                            yield
                            tt(Rn[0:L, :, 0:L], v3(bkr[0:L, :], 4, 128)[:, :, 0:L], Rc[0:L, :, 0:L], ALU.add, r=[bkr, Rc],
                               w=[Rn])
                            yield
                            cur = 1 - cur
                        Rf = Rm[cur]
                        bk = nb()
                        for h in range(4):
                            mm(bk[0:L, h * 128:(h + 1) * 128], QKVDt[4 + h][:, t0:t0 + L], S3b[:, h, :], True, True,
                               r=[QKVDt[4 + h], S3b], w=[bk])
                        yield
                        tt(v3(D_f1[0:L, :], 4, 128), v3(bk[0:L, :], 4, 128), bc(ecum_[0:L, 0:4].unsqueeze(2), [L, 4, 128]),
                           ALU.mult, r=[bk, ecum_], w=[D_f1])
                        tt(D_f3[0:L, :], Vtm[0:L, :], D_f1[0:L, :], ALU.subtract, r=[Vtm, D_f1], w=[D_f3])
                        yield
                        bk = nb()
                        for h in range(4):
                            mm(bk[0:L, h * 128:(h + 1) * 128], Rf[0:L, h, 0:L], D_f3[0:L, h * 128:(h + 1) * 128], True, True,
                               r=[Rf, D_f3], w=[bk])
                        yield
                        tt(v3(vnew[0:L, :], 4, 128), v3(bk[0:L, :], 4, 128), bc(beta[0:L, 0:4].unsqueeze(2), [L, 4, 128]),
                           ALU.mult, r=[bk, beta], w=[vnew])
                        yield
                        bks = nb()
                        for h in range(4):
                            mm(bks[0:L, h * 128:(h + 1) * 128], QKVDt[h][:, t0:t0 + L], S3b[:, h, :], True, True,
                               r=[QKVDt[h], S3b], w=[bks])
                        bki = nb()
                        for h in range(4):
                            mm(bki[0:L, h * 128:(h + 1) * 128], QKd[0:L, h, 0:L], vnew[0:L, h * 128:(h + 1) * 128], True, True,
                               r=[QKd, vnew], w=[bki])
                        yield
                        tt(v3(D_f1[0:L, :], 4, 128), v3(bks[0:L, :], 4, 128), bc(ecum_[0:L, 0:4].unsqueeze(2), [L, 4, 128]),
                           ALU.mult, r=[bks, ecum_], w=[D_f1])
                        tt(D_f1[0:L, :], bki[0:L, :], D_f1[0:L, :], ALU.add, r=[bki, D_f1], w=[D_f1])
                        yield
                        yield from head_rmsnorm_gate(D_f1, junkD, ssq_, lnv_, rstd_, Yc, L, 4, 128, D_ng, 1536)
                        yield
                        bkn = nb()
                        for h in range(4):
                            mm(bkn[:, h * 128:(h + 1) * 128], knw[0:L, h * 128:(h + 1) * 128],
                               vnew[0:L, h * 128:(h + 1) * 128], True, True, r=[knw, vnew], w=[bkn])
                        tt(S3[:], S3[:], bc(ecl_[:, 0:4].unsqueeze(2), [128, 4, 128]), ALU.mult, r=[S3, ecl_], w=[S3])
                        yield
                        tt(S3[:], v3(bkn[:, :], 4, 128), S3[:], ALU.add, r=[bkn, S3], w=[S3])
                        cp(S3b[:], S3[:], r=[S3], w=[S3b], eng="act")
                        if sid == 1:
                            dma(o_s_gdn[l, seq].rearrange("h d e -> d h e"), S3[:], r=[S3], w=[dbuf("o_s_gdn")])
                        yield ("done", ch["slot"])

                def swa_thread():
                    for ch in blk:
                        L, t0, slot, sid, seq, G_, Yc = chunk_ctx(ch)
                        while slot >= 2 and not fin.get(slot - 2):
                            yield
                        if sid == 1:
                            dma(cK[:, 0:128], st_k[l, seq], w=[cK])
                            dma(cK[:, 128:256], st_v[l, seq], w=[cK])
                            cp(cKb[:, :], cK[:, 0:128], r=[cK], w=[cKb])
                            bk = nb()
                            bv = bfv(bk)
                            for g in range(2):
                                tr(bv[0:64, g * 128:(g + 1) * 128], cKb[:, g * 64:(g + 1) * 64], ident_b[:, :],
                                   r=[cKb, ident_b], w=[bk])
                            yield
                            cp(PKA[0:64, :, :], v3(bv[0:64, 0:256], 2, 128), r=[bk], w=[PKA])
                            cp(PVA[:, :, 0:64], v3(cK[:, 128:256], 2, 64), r=[cK], w=[PVA])
                            prev = (PKA, PVA, 128, NEGprev)
                        elif ch["ci"] == 0:
                            prev = None
                        elif ch["ci"] == 1:
                            prev = (PKAm, PVAm, NMETA, NEGmeta)
                        else:
                            prev = (PKA, PVA, 128, NEGprev)
                        for g in range(2):
                            tiles = []
                            if prev is not None:
                                tiles.append((prev[0], prev[0][0:65, g, 0:prev[2]], prev[1], prev[1][0:prev[2], g, 0:65],
                                              prev[2], prev[3]))
                            tiles.append((KA, KA[0:65, g, t0:t0 + L], VAUG[slot], VAUG[slot][0:L, g, 0:65], L, NEGown))
                            pts = []
                            for ti, (kbuf, kap, vbuf, vap, Lk, negm) in enumerate(tiles):
                                bk = nb()
                                for hh in range(4):
                                    h = 4 * g + hh
                                    o = bk[0:Lk, hh * 128:hh * 128 + L]
                                    mm(o, kap, QA[0:65, h, t0:t0 + L], True, False, r=[kbuf, QA], w=[bk])
                                    mm(o, ident_b[0:Lk, 0:Lk], negm[0:Lk, 0:L], False, True, r=[ident_b, negm], w=[bk])
                                yield
                                pt = PTs[2 * g + ti] if len(tiles) == 2 else PTs[2 * g + 1]
                                act(pt[0:Lk, :, 0:L], v3(bk[0:Lk, :], 4, 128)[:, :, 0:L], AF.Exp, r=[bk], w=[pt], scale=0.125)
                                pts.append((pt, vbuf, vap, Lk))
                                yield
                            bko = nb()
                            for hh in range(4):
                                for ti, (pt, vbuf, vap, Lk) in enumerate(pts):
                                    mm(bko[0:L, hh * 72:hh * 72 + 65], pt[0:Lk, hh, 0:L], vap, ti == 0, ti == len(pts) - 1,
                                       r=[pt, vbuf], w=[bko])
                            yield
                            ov = v3(bko[0:L, 0:288], 4, 72)
                            tt(den[0:L, 4 * g:4 * g + 4], ov[:, :, 64], ESQ[0:L, 4 * g:4 * g + 4], ALU.add, r=[bko, ESQ],
                               w=[den])
                            V(lambda L=L, g=g: nc.vector.reciprocal(out=den[0:L, 4 * g:4 * g + 4],
                                                                    in_=den[0:L, 4 * g:4 * g + 4]), r=[den], w=[den])
                            tt(v3(B_f2[0:L, 256 * g:256 * g + 256], 4, 64), ov[:, :, 0:64],
                               bc(den[0:L, 4 * g:4 * g + 4].unsqueeze(2), [L, 4, 64]), ALU.mult, r=[bko, den], w=[B_f2])
                            yield
                        tt(Yc[0:L, 512:1024], B_f2[0:L, :], G_[0:L, 1, :], ALU.mult, r=[B_f2, G_], w=[Yc])
                        if sid == 0:
                            if ch["ci"] == 0:
                                cp(PKAm[0:64, :, 0:L], KA[0:64, :, t0:t0 + L], r=[KA], w=[PKAm], eng="pool")
                                cp(PVAm[0:L, :, 0:64], VAUG[slot][0:L, :, 0:64], r=[VAUG[slot]], w=[PVAm], eng="pool")
                            else:
                                cp(PKA[0:64, :, 0:L], KA[0:64, :, t0:t0 + L], r=[KA], w=[PKA], eng="pool")
                                cp(PVA[0:L, :, 0:64], VAUG[slot][0:L, :, 0:64], r=[VAUG[slot]], w=[PVA], eng="pool")
                        if sid == 1:
                            dma(o_s_k[l, seq, 0:128 - LS, :], st_k[l, seq, LS:128, :], w=[dbuf("o_s_k")])
                            dma(o_s_v[l, seq, 0:128 - LS, :], st_v[l, seq, LS:128, :], w=[dbuf("o_s_v")])
                            dma(o_s_k[l, seq, 128 - LS:128, :], KV[slot][0:LS, 0:128], r=[KV[slot]], w=[dbuf("o_s_k")])
                            dma(o_s_v[l, seq, 128 - LS:128, :], KV[slot][0:LS, 128:256], r=[KV[slot]], w=[dbuf("o_s_v")])
                        elif ch["ci"] == 16:
                            dma(o_p_k[l], KV[slot][:, 0:128], r=[KV[slot]], w=[dbuf("o_p_k")])
                            dma(o_p_v[l], KV[slot][:, 128:256], r=[KV[slot]], w=[dbuf("o_p_v")])
                        yield ("done", ch["slot"])

                def finish_chunk(ch):
                    L, t0, slot, sid, seq, G_, Yc = chunk_ctx(ch)
                    if dbg:
                        dma(ydbg[ch["row0"]:ch["row0"] + L, :], Yc[0:L, :], r=[Yc], w=[dbuf("ydbg")])
                    for q in range(4):
                        bk = nb()
                        bv = bfv(bk)
                        for j in range(4):
                            kc = 4 * q + j
                            tr(bv[:, j * 128:j * 128 + L], Yc[0:L, kc * 128:(kc + 1) * 128], ident_b[0:L, 0:L],
                               r=[Yc, ident_b], w=[bk])
                        cp(hT[:, 4 * q:4 * q + 4, t0:t0 + L], v3(bv[:, 0:512], 4, 128)[:, :, 0:L], r=[bk], w=[hT],
                           eng="act" if q % 2 else "dve")

                chk('A')
                def tm_group(wbuf, off, n, evac):
                    for ch in blk:
                        L, t0, slot = ch["L"], ch["tok0"], ch["slot"]
                        bk = nb()
                        for kc in range(KC):
                            mm(bk[0:L, 0:n], hT[:, kc, t0:t0 + L], wbuf[:, kc, off:off + n], kc == 0, kc == KC - 1,
                               r=[hT, wbuf], w=[bk])
                        evac(bk, ch)

                def fm_group(wbuf, off, M, evac):
                    bk = nb()
                    for kc in range(KC):
                        mm(bk[0:M, 0:nbt], wbuf[:, kc, off:off + M], hT[:, kc, 0:nbt], kc == 0, kc == KC - 1,
                           r=[hT, wbuf], w=[bk])
                    evac(bk)

                def gate_evac(gi, half):
                    def f(bk, ch):
                        L = ch["L"]
                        act(Gs[ch["slot"]][0:L, gi, half * 256:(half + 1) * 256], bk[0:L, 0:256], AF.Silu, r=[bk],
                            w=[Gs[ch["slot"]]])
                    return f

                cctr = [0]

                def conv_evac(dst, ft, cw, cb, car, cst, cso):
                    def f(bk):
                        k = cctr[0] % 2
                        cctr[0] += 1
                        S_, A_ = ST[k], ACC[k]
                        if not is0:
                            n = nbt
                            cp(S_[:, 0:3], car[:, ft, :], r=[car], w=[S_], eng="pool")
                            cp(S_[:, 3:3 + n], bk[:, 0:n], r=[bk], w=[S_], eng="act")
                            cp(car[:, ft, :], S_[:, n:n + 3], r=[S_], w=[car], eng="pool")
                            no = n
                        else:
                            memset(S_, S_[:, 0:3], 0.0)
                            sv = v3(S_[:, 19:47], NSS, 7)
                            cp(sv[:, :, 0:3], cst[:, ft, :, :], r=RD(cst), w=[S_], eng="pool")
                            cp(S_[:, 3:19], bk[:, 0:16], r=[bk], w=[S_], eng="act")
                            cp(sv[:, :, 3:7], v3(bk[:, 16:32], NSS, LS), r=[bk], w=[S_], eng="act")
                            cp(car[:, ft, :], S_[:, 16:19], r=[S_], w=[car], eng="pool")
                            cp(cso[:, ft, :, :], sv[:, :, 4:7], r=[S_], w=[cso], eng="pool")
                            no = 44
                        ts(A_[:, 0:no], S_[:, 0:no], cw[:, ft, 0:1], ALU.mult, r=[S_] + RD(cw), w=[A_])
                        for kk in range(1, 4):
                            stt(A_[:, 0:no], S_[:, kk:kk + no], cw[:, ft, kk:kk + 1], A_[:, 0:no], ALU.mult, ALU.add,
                                r=[S_, A_] + RD(cw), w=[A_])
                        bias = cb[:, ft:ft + 1] if cb is not None else None
                        rr = [A_] + (RD(cb) if cb is not None else [])
                        if not is0:
                            act(dst[ft][:, 0:no], A_[:, 0:no], AF.Silu, r=rr, w=[dst[ft]], bias=bias)
                        else:
                            act(dst[ft][:, 0:16], A_[:, 0:16], AF.Silu, r=rr, w=[dst[ft]], bias=bias)
                            act(v3(dst[ft][:, 16:32], NSS, LS), v3(A_[:, 19:47], NSS, 7)[:, :, 0:4], AF.Silu, r=rr,
                                w=[dst[ft]], bias=bias)
                    return f

                wv = wview_in[l]
                import os as _os
                _gl = ((0, C_Z), (1, C_GA), (2, C_GR), (3, C_GD))
                if _os.environ.get('KSKIPG'):
                    _gl = ()
                if _os.environ.get('KDUPG'):
                    _gl = _gl + _gl
                for gi, c0 in _gl:
                    for half in range(2):
                        wbuf = load_w(wv, c0 + half * 256, 256)
                        tm_group(wbuf, 0, 256, gate_evac(gi, half))
                chk('B1')
                wbuf = load_w(wv, C_DT, 8)
                load_w(wv, C_BD, 8, off=8, buf=wbuf)

                def small_evac(bk, ch):
                    L = ch["L"]
                    cp(SM[ch["slot"]][0:L, :], bk[0:L, 0:16], r=[bk], w=[SM[ch["slot"]]])
                tm_group(wbuf, 0, 16, small_evac)
                fin = {}
                done_cnt = {}
                th_ssd, th_gdn, th_swa, th_ret = ssd_thread(), gdn_thread(), swa_thread(), ret_thread()
                early = [th_ssd, th_gdn]
                while early:
                    for th in list(early):
                        v = next(th)
                        if v is not None and v[0] == "wait_proj":
                            early.remove(th)
                chk('B2')
                wbuf = load_w(wv, C_KA, 256)

                def kv_evac(bk, ch):
                    L, slot = ch["L"], ch["slot"]
                    _m = _os.environ.get('KVMODE', '0')
                    if _m in ('0', '1'):
                        cp(KV[slot][0:L, :], bk[0:L, 0:256], r=[bk], w=[KV[slot]], eng="act")
                    if _m in ('0', '2'):
                        cp(VAUG[slot][0:L, :, 0:64], v3(bk[0:L, 128:256], 2, 64), r=[bk] + ([KV[slot]] if _os.environ.get('KVSER') else []), w=[VAUG[slot]])
                tm_group(wbuf, 0, 256, kv_evac)
                chk('B2a')
                for g in range(2):
                    fm_group(wbuf, g * 64, 64,
                             lambda bk, g=g: cp(KA[0:64, g, 0:nbt], bk[0:64, 0:nbt], r=[bk], w=[KA]))
                chk('B3')
                for half in range(2):
                    wbuf = load_w(wv, C_QA + half * 256, 256)
                    for hh in range(4):
                        h = half * 4 + hh
                        fm_group(wbuf, hh * 64, 64,
                                 lambda bk, h=h: cp(QA[0:64, h, 0:nbt], bk[0:64, 0:nbt], r=[bk], w=[QA],
                                                    eng="act" if h % 2 else "dve"))
                chk('B4')
                wbuf = load_w(wv, C_QR, 256)
                for h in range(4):
                    fm_group(wbuf, h * 64, 64,
                             lambda bk, h=h: cp(QR[0:64, h, 0:nbt], bk[0:64, 0:nbt], r=[bk], w=[QR]))
                wbuf = load_w(wv, C_KR, 256)
                for h in range(4):
                    fm_group(wbuf, h * 64, 64,
                             lambda bk, h=h: ts(KRF[0:64, h, 0:nbt], bk[0:64, 0:nbt], 0.125, ALU.mult, r=[bk], w=[KRF]))

                def kr_evac(bk, ch):
                    L, slot = ch["L"], ch["slot"]
                    ts(KR[slot][0:L, :, :], v3(bk[0:L, 0:256], 4, 64), 0.125, ALU.mult, r=[bk], w=[KR[slot]])
                tm_group(wbuf, 0, 256, kr_evac)
                chk('B5')
                for half in range(2):
                    wbuf = load_w(wv, C_VR + half * 256, 256)

                    def vr_evac(bk, ch, half=half):
                        L, slot = ch["L"], ch["slot"]
                        cp(VR[slot][0:L, 2 * half:2 * half + 2, :], v3(bk[0:L, 0:256], 2, 128), r=[bk], w=[VR[slot]],
                           eng="act")
                    tm_group(wbuf, 0, 256, vr_evac)
                chk('B6')
                for q in range(4):
                    wbuf = load_w(wv, C_XBC + q * 256, 256)
                    for j in range(2):
                        ft = 2 * q + j
                        fm_group(wbuf, j * 128, 128, conv_evac(XBCt, ft, cwS, cbS, carS, cstS, csoS))
                chk('B7')
                for q in range(6):
                    wbuf = load_w(wv, C_QKVD + q * 256, 256)
                    for j in range(2):
                        ft = 2 * q + j
                        fm_group(wbuf, j * 128, 128, conv_evac(QKVDt, ft, cwG, None, carG, cstG, csoG))
                chk('B8')
                for ft in range(8):
                    k = ft % 2
                    S_, A_ = ST[k], ACC[k]
                    sq_ = (sqj, junkD)[k]
                    Q_ = QKVDt[ft]
                    tt(sq_[:, 0:nbt], Q_[:, 0:nbt], Q_[:, 0:nbt], ALU.mult, r=[Q_], w=[sq_], eng="pool")
                    bk = nb()
                    mm(bk[:, 0:nbt], ones_b[:, :], sq_[:, 0:nbt], True, True, r=[ones_b, sq_], w=[bk])
                    act(A_[:, 0:nbt], bk[:, 0:nbt], AF.Ln, r=[bk], w=[A_], bias=EPS)
                    act(A_[:, 0:nbt], A_[:, 0:nbt], AF.Exp, r=[A_], w=[A_], scale=-0.5,
                        bias=(math.log(128.0 ** -0.5) if ft < 4 else 0.0))
                    tt(Q_[:, 0:nbt], Q_[:, 0:nbt], A_[:, 0:nbt], ALU.mult, r=[Q_, A_], w=[Q_])

                chk('B')
                if l + 1 < n_layers and len(blocks) >= 7:
                    if 1 <= bi <= 5:
                        convert_weights(l + 1, piece=bi - 1)
                    if bi == 6:
                        load_slow_params(l + 1)
                elif l + 1 < n_layers and bi == len(blocks) - 1:
                    convert_weights(l + 1)
                    load_slow_params(l + 1)
                threads = [th_ssd, th_gdn, th_swa, th_ret]
                while threads:
                    for th in list(threads):
                        try:
                            v = next(th)
                        except StopIteration:
                            threads.remove(th)
                            continue
                        if v is not None and v[0] == "done":
                            done_cnt[v[1]] = done_cnt.get(v[1], 0) + 1
                            if done_cnt[v[1]] == 4:
                                finish_chunk(blk[v[1]])
                                fin[v[1]] = True

                chk('C')
                if bi + 1 < len(blocks):
                    stage_a1(l, bi + 1)
                elif l + 1 < n_layers:
                    stage_a1(l + 1, 0)
                wvo = wview_out[l]
                for ti, tl in enumerate(tm_tiles):
                    pass
                osb = xt[0]
                outbuf = {}
                for ti, tl in enumerate(tm_tiles):
                    outbuf[ti] = None
                E_ssq = [smalls["E0_ssq"], smalls["E1_ssq"]]
                E_tot, E_lnv, E_rstd = smalls["E_tot"], smalls["E_lnv"], smalls["E_rstd"]
                XR = [ST[0], ST[1], ACC[0], ACC[1]]
                for cg in range(D // WG):
                    wbuf = load_w(wvo, cg * WG, WG)
                    for ti, tl in enumerate(tm_tiles):
                        L, t0 = tl["L"], tl["tok0"]
                        bk = nb()
                        for kc in range(KC):
                            mm(bk[0:L, 0:WG], hT[:, kc, t0:t0 + L], wbuf[:, kc, 0:WG], kc == 0, kc == KC - 1,
                               r=[hT, wbuf], w=[bk])
                        cp(xt[ti][0:L, cg * WG:(cg + 1) * WG], bk[0:L, 0:WG], r=[bk], w=[xt[ti]],
                           eng="act" if cg % 2 else "dve")
                        act(junkA[0:L, 0:WG], xt[ti][0:L, cg * WG:(cg + 1) * WG], AF.Square, r=[xt[ti]],
                            w=[junkA, E_ssq[ti]], accum_out=E_ssq[ti][0:L, cg:cg + 1])
                def epilogue(tm_tiles=tm_tiles, xsrc=xsrc, xdst=xdst, bi=bi, E_ssq=E_ssq, XR=XR):
                    xrc = 0
                    for ti, tl in enumerate(tm_tiles):
                        L, r0 = tl["L"], tl["row0"]
                        o_ = xt[ti]
                        V(lambda L=L, ti=ti: nc.vector.tensor_reduce(out=E_tot[0:L, 0:1], in_=E_ssq[ti][0:L, 0:8], axis=AX.X,
                                                                     op=ALU.add), r=[E_ssq[ti]], w=[E_tot])
                        act(E_lnv[0:L, 0:1], E_tot[0:L, 0:1], AF.Ln, r=[E_tot], w=[E_lnv], scale=1.0 / D, bias=EPS)
                        act(E_rstd[0:L, 0:1], E_lnv[0:L, 0:1], AF.Exp, r=[E_lnv], w=[E_rstd], scale=-0.5)
                        for c in range(8):
                            c0 = c * 256
                            xb = XR[xrc % 4]
                            xrc += 1
                            dma(xb[0:L, 0:256], xsrc[r0:r0 + L, c0:c0 + 256], r=[xbuf(bi)], w=[xb])
                            stt(o_[0:L, c0:c0 + 256], o_[0:L, c0:c0 + 256], E_rstd[0:L, 0:1], postg[0:L, c0:c0 + 256],
                                ALU.mult, ALU.mult, r=[o_, E_rstd, postg], w=[o_])
                            tt(o_[0:L, c0:c0 + 256], o_[0:L, c0:c0 + 256], xb[0:L, 0:256], ALU.add, r=[o_, xb], w=[o_])
                        dma(xdst[r0:r0 + L, :], o_[0:L, :], r=[o_], w=[xbuf(bi)])
                pending_epi.append(epilogue)

            flush_epi()
            if n_blocks is None:
                dma(o_p_ssd[l].rearrange("h n e -> n h e"), Sssd[0][:], r=[Sssd[0]], w=[dbuf("o_p_ssd")])
                dma(o_p_ret[l].rearrange("h d e -> d h e"), Sret[0][:], r=[Sret[0]], w=[dbuf("o_p_ret")])
                dma(o_p_gdn[l].rearrange("h d e -> d h e"), Sgdn[0][:], r=[Sgdn[0]], w=[dbuf("o_p_gdn")])
                for t_ in range(3):
                    dma(o_p_ssdc[l, t_].rearrange("(ft p) -> p ft", p=128), carS[:, :, t_], r=[carS],
                        w=[Buf(None, "o_p_ssdc")], eng="act", allow_slow_non_contiguous=True)
                    dma(o_p_gdnc[l, t_].rearrange("(ft p) -> p ft", p=128), carG[:, :, t_], r=[carG],
                        w=[Buf(None, "o_p_gdnc")], eng="act", allow_slow_non_contiguous=True)
            for s in range(NSS):
                for t_ in range(3):
                    dma(o_s_ssdc[l, s, t_].rearrange("(ft p) -> p ft", p=128), csoS[:, :, s, t_], r=[csoS],
                        w=[Buf(None, "o_s_ssdc")], eng="act", allow_slow_non_contiguous=True)
                    dma(o_s_gdnc[l, s, t_].rearrange("(ft p) -> p ft", p=128), csoG[:, :, s, t_], r=[csoG],
                        w=[Buf(None, "o_s_gdnc")], eng="act", allow_slow_non_contiguous=True)

    except StopBuild:
        pass

    P.emit(es)
    es.close()
    return nc, P.stats


_CACHE = {}


def _in_maps(inp):
    f = lambda a: np.ascontiguousarray(np.asarray(a, dtype=np.float32))
    maps = []
    for c in range(8):
        b = c % 4
        sl = slice(NSS * c, NSS * c + NSS)
        xin = np.concatenate([inp["meta_tokens"], inp["x_prompt"][b], inp["x_sample"][sl].reshape(NSS * LS, D)], axis=0)
        m = {
            "xin": f(xin),
            "st_ssd": f(inp["state_ssd"][:, sl]),
            "st_ssdc": f(inp["state_ssd_conv"][:, sl]),
            "st_k": f(inp["cache_swa_k"][:, sl].reshape(DEPTH, NSS, 128, 128)),
            "st_v": f(inp["cache_swa_v"][:, sl].reshape(DEPTH, NSS, 128, 128)),
            "st_ret": f(inp["state_ret"][:, sl]),
            "st_gdn": f(inp["state_gdn"][:, sl]),
            "st_gdnc": f(inp["state_gdn_conv"][:, sl]),
        }
        for k in ("pre_norm", "post_norm", "w_in", "w_out", "ssd_conv_w", "ssd_conv_b", "ssd_dt_bias", "ssd_a_log",
                  "ssd_d", "ssd_norm", "swa_sinks", "ret_norm", "gdn_conv_w", "gdn_dt_bias", "gdn_a_log", "gdn_norm"):
            m[k] = f(inp[k])
        maps.append(m)
    return maps


def kernel(**inp):
    if "nc" not in _CACHE:
        _CACHE["nc"] = build_program()[0]
    nc = _CACHE["nc"]
    res = run_bass_kernel_spmd(nc, _in_maps(inp), core_ids=list(range(8)))
    R = res.results
    B = 4
    y_prompt = np.stack([R[b]["yout"][NMETA:NPT] for b in range(B)]).astype(np.float32)
    y_sample = np.concatenate([R[c]["yout"][NPT:NT].reshape(NSS, LS, D) for c in range(8)]).astype(np.float32)

    def pst(name, shape):
        return np.stack([np.stack([R[b][name][l] for b in range(B)]) for l in range(DEPTH)]).reshape(shape).astype(np.float32)

    def sst(name, shape):
        return np.concatenate([R[c][name] for c in range(8)], axis=1).reshape(shape).astype(np.float32)

    outs = (
        y_prompt, y_sample,
        pst("o_p_ssd", (DEPTH, B, 8, 128, 64)), pst("o_p_ssdc", (DEPTH, B, 3, 1024)),
        pst("o_p_k", (DEPTH, B, 128, 2, 64)), pst("o_p_v", (DEPTH, B, 128, 2, 64)),
        pst("o_p_ret", (DEPTH, B, 4, 64, 128)), pst("o_p_gdn", (DEPTH, B, 4, 128, 128)),
        pst("o_p_gdnc", (DEPTH, B, 3, 1536)),
        sst("o_s_ssd", (DEPTH, 32, 8, 128, 64)), sst("o_s_ssdc", (DEPTH, 32, 3, 1024)),
        sst("o_s_k", (DEPTH, 32, 128, 2, 64)), sst("o_s_v", (DEPTH, 32, 128, 2, 64)),
        sst("o_s_ret", (DEPTH, 32, 4, 64, 128)), sst("o_s_gdn", (DEPTH, 32, 4, 128, 128)),
        sst("o_s_gdnc", (DEPTH, 32, 3, 1536)),
    )
    return outs
```

```python
import math
from contextlib import ExitStack

import numpy as np
import concourse.bass as bass
import concourse.mybir as mybir
from concourse.bass_utils import run_bass_kernel_spmd

F32 = mybir.dt.float32
BF16 = mybir.dt.bfloat16
I32 = mybir.dt.int32
AF = mybir.ActivationFunctionType
ALU = mybir.AluOpType
AX = mybir.AxisListType

D = 2048
KC = 16
DEPTH = 4
SEQ = 2048
NMETA = 16
NPT = SEQ + NMETA
NSS = 4
LS = 4
NT = NPT + NSS * LS
IN_W = 6416
EPS = 1e-6
NEG = -30000.0

C_Z, C_XBC, C_DT, C_QA, C_KA, C_VA, C_GA = 0, 512, 1536, 1544, 2056, 2184, 2312
C_QR, C_KR, C_VR, C_GR, C_QKVD, C_GD, C_BD, C_AD = 2824, 3080, 3336, 3848, 4360, 5896, 6408, 6412


class Buf:
    __slots__ = ("t", "lw", "rd", "rd_dma", "name", "excl")

    def __init__(self, t, name="", excl=False):
        self.t = t
        self.excl = excl
        self.lw = None
        self.rd = {}
        self.rd_dma = []
        self.name = name

    def __getitem__(self, k):
        return self.t[k]


class View:
    __slots__ = ("p", "t")

    def __init__(self, parent, ap):
        self.p = parent
        self.t = ap

    def __getitem__(self, k):
        return self.t[k]

    lw = property(lambda self: self.p.lw, lambda self, v: setattr(self.p, "lw", v))
    rd = property(lambda self: self.p.rd, lambda self, v: setattr(self.p, "rd", v))
    rd_dma = property(lambda self: self.p.rd_dma, lambda self, v: setattr(self.p, "rd_dma", v))
    excl = property(lambda self: self.p.excl)


class Prog:
    def __init__(self, nc, n_dma_sems=60):
        self.nc = nc
        self.ops = []
        self.E = {"pe": nc.tensor, "act": nc.scalar, "dve": nc.vector, "pool": nc.gpsimd, "sp": nc.sync}
        self.n_dma_sems = n_dma_sems

    def op(self, eng, fn, r=(), w=(), dma=False):
        idx = len(self.ops)
        deps = set()
        for b in r:
            if b.lw is not None:
                deps.add(b.lw)
            if b.excl:
                deps.update(v for e, v in b.rd.items() if e != eng)
        for b in w:
            if b.lw is not None:
                deps.add(b.lw)
            deps.update(b.rd.values())
            deps.update(b.rd_dma)
        for b in r:
            if dma:
                b.rd_dma.append(idx)
            else:
                b.rd[eng] = idx
        for b in w:
            b.lw = idx
            b.rd = {}
            b.rd_dma = []
        deps.discard(idx)
        self.ops.append([eng, fn, deps, dma])
        return idx

    def emit(self, es):
        nc = self.nc
        ops = self.ops
        needed = set()
        for i, (eng, fn, deps, dma) in enumerate(ops):
            for d in deps:
                de, _, _, ddma = ops[d]
                if ddma:
                    continue
                if de == eng and eng == "pe":
                    continue
                needed.add(d)
        esem = {e: es.enter_context(nc.semaphore("sem_" + e)) for e in ("pe", "act", "dve", "pool")}
        dsem = [es.enter_context(nc.semaphore("dsem%d" % i)) for i in range(self.n_dma_sems)]
        dval = [0] * self.n_dma_sems
        ecount = {e: 0 for e in esem}
        sig = [None] * len(ops)
        known = {e: {} for e in self.E}
        ndma = 0
        nwait = 0
        dcnt = {}
        for i, (eng, fn, deps, dma) in enumerate(ops):
            waits = {}
            for d in deps:
                s = sig[d]
                if s is None:
                    continue
                if s[1] > waits.get(s[0], (None, 0))[1]:
                    waits[s[0]] = s
            if dma:
                base, cnt_ = {"sp": (0, 10), "pool": (10, 10), "act": (20, 40)}[eng]
                k = base + (dcnt.get(eng, 0) % cnt_)
                dcnt[eng] = dcnt.get(eng, 0) + 1
                ndma += 1
                if dval[k] > 0:
                    key = ("d", k)
                    if dval[k] > waits.get(key, (None, 0))[1]:
                        waits[key] = (key, dval[k])
            kn = known[eng]
            for key, (_, val) in waits.items():
                if kn.get(key, 0) >= val:
                    continue
                sem = esem[key] if isinstance(key, str) else dsem[key[1]]
                self.E[eng].wait_ge(sem, val)
                kn[key] = val
                nwait += 1
            ins = fn()
            if dma:
                dval[k] += 16
                ins.then_inc(dsem[k], 16)
                sig[i] = (("d", k), dval[k])
            elif i in needed:
                ecount[eng] += 1
                ins.then_inc(esem[eng], 1)
                sig[i] = (eng, ecount[eng])
        for k in range(self.n_dma_sems):
            if dval[k] > 0:
                nc.sync.wait_ge(dsem[k], dval[k])
        for e in esem:
            if ecount[e] > 0:
                nc.sync.wait_ge(esem[e], ecount[e])
        self.stats = dict(n_ops=len(ops), n_wait=nwait, n_dma=ndma, counts=dict(ecount))


def make_chunks():
    chunks = [dict(kind="p", L=NMETA, row0=0, ci=0)]
    for c in range(1, 17):
        chunks.append(dict(kind="p", L=128, row0=NMETA + 128 * (c - 1), ci=c))
    samples = [dict(kind="s", L=LS, row0=NPT + LS * s, seq=s) for s in range(NSS)]
    return chunks, samples


class StopBuild(Exception):
    pass


def build_program(n_layers=DEPTH, n_blocks=None, cpb=2, dbg=False, stop=None):
    nc = bass.Bass("TRN2", target_bir_lowering=False)
    es = ExitStack()
    P = Prog(nc)

    def dram(name, shape, dt=F32, kind="ExternalInput"):
        return nc.dram_tensor(name, list(shape), dt, kind=kind).ap()

    xin = dram("xin", [NT, D])
    st_ssd = dram("st_ssd", [DEPTH, NSS, 8, 128, 64])
    st_ssdc = dram("st_ssdc", [DEPTH, NSS, 3, 1024])
    st_k = dram("st_k", [DEPTH, NSS, 128, 128])
    st_v = dram("st_v", [DEPTH, NSS, 128, 128])
    st_ret = dram("st_ret", [DEPTH, NSS, 4, 64, 128])
    st_gdn = dram("st_gdn", [DEPTH, NSS, 4, 128, 128])
    st_gdnc = dram("st_gdnc", [DEPTH, NSS, 3, 1536])
    pre_norm = dram("pre_norm", [DEPTH, D])
    post_norm = dram("post_norm", [DEPTH, D])
    w_in = dram("w_in", [DEPTH, D, IN_W])
    w_out = dram("w_out", [DEPTH, D, D])
    ssd_conv_w = dram("ssd_conv_w", [DEPTH, 4, 1024])
    ssd_conv_b = dram("ssd_conv_b", [DEPTH, 1024])
    ssd_dt_bias = dram("ssd_dt_bias", [DEPTH, 8])
    ssd_a_log = dram("ssd_a_log", [DEPTH, 8])
    ssd_d = dram("ssd_d", [DEPTH, 8])
    ssd_norm = dram("ssd_norm", [DEPTH, 512])
    swa_sinks = dram("swa_sinks", [DEPTH, 8])
    ret_norm = dram("ret_norm", [DEPTH, 512])
    gdn_conv_w = dram("gdn_conv_w", [DEPTH, 4, 1536])
    gdn_dt_bias = dram("gdn_dt_bias", [DEPTH, 4])
    gdn_a_log = dram("gdn_a_log", [DEPTH, 4])
    gdn_norm = dram("gdn_norm", [DEPTH, 128])

    EO = "ExternalOutput"
    yout = dram("yout", [NT, D], kind=EO)
    o_p_ssd = dram("o_p_ssd", [DEPTH, 8, 128, 64], kind=EO)
    o_p_ssdc = dram("o_p_ssdc", [DEPTH, 3, 1024], kind=EO)
    o_p_k = dram("o_p_k", [DEPTH, 128, 128], kind=EO)
    o_p_v = dram("o_p_v", [DEPTH, 128, 128], kind=EO)
    o_p_ret = dram("o_p_ret", [DEPTH, 4, 64, 128], kind=EO)
    o_p_gdn = dram("o_p_gdn", [DEPTH, 4, 128, 128], kind=EO)
    o_p_gdnc = dram("o_p_gdnc", [DEPTH, 3, 1536], kind=EO)
    o_s_ssd = dram("o_s_ssd", [DEPTH, NSS, 8, 128, 64], kind=EO)
    o_s_ssdc = dram("o_s_ssdc", [DEPTH, NSS, 3, 1024], kind=EO)
    o_s_k = dram("o_s_k", [DEPTH, NSS, 128, 128], kind=EO)
    o_s_v = dram("o_s_v", [DEPTH, NSS, 128, 128], kind=EO)
    o_s_ret = dram("o_s_ret", [DEPTH, NSS, 4, 64, 128], kind=EO)
    o_s_gdn = dram("o_s_gdn", [DEPTH, NSS, 4, 128, 128], kind=EO)
    o_s_gdnc = dram("o_s_gdnc", [DEPTH, NSS, 3, 1536], kind=EO)
    xscr = dram("xscr", [NT, D], kind="Internal")
    wbf_in = dram("wbf_in", [DEPTH, D, IN_W], BF16, kind="Internal")
    wbf_out = dram("wbf_out", [DEPTH, D, D], BF16, kind="Internal")
    ydbg = dram("ydbg", [NT, D], BF16, kind=EO) if dbg else None

    def sb(name, shape, dt=F32):
        return Buf(es.enter_context(nc.sbuf_tensor(name, list(shape), dt)), name)

    def ps(name, shape, dt=F32):
        return Buf(es.enter_context(nc.psum_tensor(name, list(shape), dt)), name, excl=True)

    DRAMBUF = {}

    def dbuf(ap_name):
        if ap_name not in DRAMBUF:
            DRAMBUF[ap_name] = Buf(None, ap_name)
        return DRAMBUF[ap_name]

    def dma(out, in_, r=(), w=(), eng="pool", **kw):
        P.op(eng, lambda: P.E[eng].dma_start(out=out, in_=in_, **kw), r=r, w=w, dma=True)

    def V(fn, r=(), w=()):
        P.op("dve", fn, r=r, w=w)

    def A(fn, r=(), w=()):
        P.op("act", fn, r=r, w=w)

    def G(fn, r=(), w=()):
        P.op("pool", fn, r=r, w=w)

    def T(fn, r=(), w=()):
        P.op("pe", fn, r=r, w=w)

    def mm(out, lhsT, rhs, start, stop, r, w):
        T(lambda: nc.tensor.matmul(out, lhsT=lhsT, rhs=rhs, start=start, stop=stop), r=r, w=w)

    def tr(out, in_, ident, r, w):
        T(lambda: nc.tensor.transpose(out, in_, ident), r=r, w=w)

    def act(out, in_, func, r, w, bias=None, scale=None, accum_out=None):
        kw = {}
        if bias is not None:
            kw["bias"] = bias
        if scale is not None:
            kw["scale"] = scale
        if accum_out is not None:
            kw["accum_out"] = accum_out
        A(lambda: nc.scalar.activation(out=out, in_=in_, func=func, **kw), r=r, w=w)

    def tt(out, in0, in1, op, r, w, eng="dve"):
        e = nc.vector if eng == "dve" else nc.gpsimd
        P.op(eng, lambda: e.tensor_tensor(out=out, in0=in0, in1=in1, op=op), r=r, w=w)

    def ts(out, in0, s1, op0, r, w, s2=None, op1=None, eng="dve"):
        e = nc.vector if eng == "dve" else nc.gpsimd
        if op1 is None:
            P.op(eng, lambda: e.tensor_scalar(out=out, in0=in0, scalar1=s1, scalar2=None, op0=op0), r=r, w=w)
        else:
            P.op(eng, lambda: e.tensor_scalar(out=out, in0=in0, scalar1=s1, scalar2=s2, op0=op0, op1=op1), r=r, w=w)

    def stt(out, in0, scalar, in1, op0, op1, r, w):
        V(lambda: nc.vector.scalar_tensor_tensor(out=out, in0=in0, scalar=scalar, in1=in1, op0=op0, op1=op1), r=r, w=w)

    def cp(out, in_, r, w, eng="dve"):
        if eng == "act":
            A(lambda: nc.scalar.activation(out=out, in_=in_, func=AF.Copy), r=r, w=w)
        else:
            e = nc.vector if eng == "dve" else nc.gpsimd
            P.op(eng, lambda: e.tensor_copy(out=out, in_=in_), r=r, w=w)

    def memset(buf, ap, val, eng="pool"):
        e = nc.vector if eng == "dve" else nc.gpsimd
        P.op(eng, lambda: e.memset(ap, val), w=[buf])

    def bc(ap, shape):
        return ap.to_broadcast(list(shape))

    ident_f = sb("ident_f", [128, 128])
    ident_b = sb("ident_b", [128, 128], BF16)
    Umat = sb("Umat", [128, 128])
    NEGU = sb("NEGU", [128, 128])
    SUm = sb("SUm", [128, 128])
    NEGown = sb("NEGown", [128, 128], BF16)
    NEGprev = sb("NEGprev", [128, 128], BF16)
    NEGmeta = sb("NEGmeta", [128, 128], BF16)
    ones_b = sb("ones_b", [128, 128], BF16)
    f1 = sb("f1", [128, 512])
    f2 = sb("f2", [128, 512])
    C_f1 = sb("C_f1", [128, 512])
    zeros_f = View(f1, f1[:, 0:128])
    ones_f = View(f1, f1[:, 128:256])
    tmpc = View(f1, f1[:, 256:384])
    tmpc2 = View(f1, f1[:, 384:512])
    dmat = View(f2, f2[:, 0:128])
    iota_fr = View(f2, f2[:, 128:256])
    iota_q = sb("iota_q", [128, 1])

    memset(zeros_f, zeros_f[:], 0.0)
    memset(ones_f, ones_f[:], 1.0)
    memset(ones_b, ones_b[:], 1.0)
    G(lambda: nc.gpsimd.iota(iota_q[:], pattern=[[0, 1]], base=0, channel_multiplier=1,
                             allow_small_or_imprecise_dtypes=True), w=[iota_q])
    G(lambda: nc.gpsimd.iota(iota_fr[:], pattern=[[1, 128]], base=0, channel_multiplier=0,
                             allow_small_or_imprecise_dtypes=True), w=[iota_fr])

    def aff(dst, src, fill, base, cm, step, op):
        G(lambda: nc.gpsimd.affine_select(out=dst[:], in_=src[:], pattern=[[step, 128]], compare_op=op,
                                          fill=fill, base=base, channel_multiplier=cm), r=[src], w=[dst])

    aff(ident_f, ones_f, 0.0, 0, 1, -1, ALU.is_equal)
    cp(ident_b[:], ident_f[:], r=[ident_f], w=[ident_b], eng="pool")
    aff(Umat, ones_f, 0.0, 0, -1, 1, ALU.is_ge)
    aff(NEGU, zeros_f, NEG, 0, -1, 1, ALU.is_ge)
    aff(SUm, ones_f, 0.0, 0, -1, 1, ALU.is_gt)
    cp(NEGown[:], NEGU[:], r=[NEGU], w=[NEGown], eng="pool")
    aff(tmpc, zeros_f, NEG, 0, 1, -1, ALU.is_ge)
    cp(NEGprev[:], tmpc[:], r=[tmpc], w=[NEGprev], eng="pool")
    aff(tmpc2, zeros_f, NEG, 112, 1, -1, ALU.is_ge)
    cp(NEGmeta[:], tmpc2[:], r=[tmpc2], w=[NEGmeta], eng="pool")

    lg = [math.log1p(-2.0 ** (-5 - h)) for h in range(4)]
    RDEC = sb("RDEC", [128, 4, 128])
    GPOW = sb("GPOW", [128, 4])
    RW = {L: sb("RW%d" % L, [128, 4]) for L in (LS, NMETA, 128)}
    GL = {L: sb("GL%d" % L, [64, 4]) for L in (LS, NMETA, 128)}
    ts(dmat[:], iota_fr[:], iota_q[:, 0:1], ALU.subtract, r=[iota_fr, iota_q], w=[dmat])
    ip1 = sb("ip1", [128, 1])
    ts(ip1[:], iota_q[:], 1.0, ALU.add, r=[iota_q], w=[ip1])
    for h in range(4):
        t0 = View(C_f1, C_f1[:, h * 128:(h + 1) * 128])
        ts(t0[:], dmat[:], lg[h], ALU.mult, r=[dmat], w=[t0], s2=NEGU[:, 0:1] if False else None)
        tt(t0[:], t0[:], NEGU[:], ALU.add, r=[t0, NEGU], w=[t0])
        act(RDEC[:, h, :], t0[:], AF.Exp, r=[t0], w=[RDEC])
        act(GPOW[:, h:h + 1], ip1[:], AF.Exp, r=[ip1], w=[GPOW], scale=lg[h])
        for L in RW:
            tq = sb("rwt%d_%d" % (h, L), [128, 1])
            ts(tq[:], iota_q[:], -1.0, ALU.mult, r=[iota_q], w=[tq], s2=float(L - 1), op1=ALU.add)
            act(RW[L][:, h:h + 1], tq[:], AF.Exp, r=[tq], w=[RW[L]], scale=lg[h])
            memset(GL[L], GL[L][:, h:h + 1], math.exp(lg[h] * L), eng="dve")

    slopes = [2.0 ** (-(h + 1)) for h in range(8)]
    SLQ = sb("SLQ", [128, 8])
    for h in range(8):
        ts(SLQ[:, h:h + 1], iota_q[:], slopes[h], ALU.mult, r=[iota_q], w=[SLQ])


    pchunks, schunks = make_chunks()
    blocks = [[pchunks[0]] + schunks]
    for b0 in range(1, 17, cpb):
        blocks.append(pchunks[b0:b0 + cpb])
    if n_blocks is not None:
        blocks = blocks[:n_blocks]
    BT = 128 * cpb
    NSLOT = max(5, cpb)
    WG = 256

    xt = [sb("xt%d" % i, [128, D]) for i in range(2)]
    hb = sb("hb", [128, D], BF16)
    hT = sb("hT", [128, KC, BT], BF16)
    wb = [sb("wb%d" % i, [128, KC, WG], BF16) for i in range(2)]
    SLOTA = sb("SLOTA", [128, 4096], BF16)
    SLOTB = sb("SLOTB", [128, 4096], BF16)
    wb.append(View(SLOTA, SLOTA[:, :].rearrange("p (k n) -> p k n", k=KC, n=WG)))
    wb.append(View(SLOTB, SLOTB[:, :].rearrange("p (k n) -> p k n", k=KC, n=WG)))
    assert NSLOT == 5 and WG == 256
    Gs = [sb("Gs%d" % i, [128, 4, 512], BF16) for i in range(2)]
    Gs.append(View(SLOTA, SLOTA[:, 0:2048].rearrange("p (a b) -> p a b", a=4, b=512)))
    Gs.append(View(SLOTA, SLOTA[:, 2048:4096].rearrange("p (a b) -> p a b", a=4, b=512)))
    Gs.append(View(SLOTB, SLOTB[:, 0:2048].rearrange("p (a b) -> p a b", a=4, b=512)))
    SM = [sb("SM%d" % i, [128, 16]) for i in range(NSLOT)]
    KV = [sb("KV%d" % i, [128, 256]) for i in range(2)]
    KV.append(View(SLOTB, SLOTB[:, 3584:4096].bitcast(F32)))
    KV += [sb("KV%d" % i, [128, 256]) for i in range(3, NSLOT)]
    VAUG = [sb("VAUG%d" % i, [128, 2, 72], BF16) for i in range(NSLOT)]
    VR = [sb("VR%d" % i, [128, 4, 128], BF16) for i in range(2)]
    for i in range(3):
        VR.append(View(SLOTB, SLOTB[:, 2048 + 512 * i:2560 + 512 * i].rearrange("p (a b) -> p a b", a=4, b=128)))
    KR = [sb("KR%d" % i, [128, 4, 64], BF16) for i in range(NSLOT)]
    XBC = sb("XBC", [128, 8, BT], BF16)
    XBCt = [Buf(XBC.t[:, ft_, :], "XBC%d" % ft_) for ft_ in range(8)]
    QA = sb("QA", [65, 8, BT], BF16)
    KA = sb("KA", [65, 2, BT], BF16)
    QR = sb("QR", [64, 4, BT], BF16)
    KRF = sb("KRF", [64, 4, BT], BF16)
    QKVD = sb("QKVD", [128, 12, BT], BF16)
    QKVDt = [Buf(QKVD.t[:, ft_, :], "QKVD%d" % ft_) for ft_ in range(12)]
    ST = [sb("ST%d" % i, [128, BT + 4]) for i in range(2)]
    ACC = [sb("ACC%d" % i, [128, BT]) for i in range(2)]
    carS = sb("carS", [128, 8, 3])
    carG = sb("carG", [128, 12, 3])
    cstSs = [sb("cstS%d" % i, [128, 8, NSS, 3]) for i in range(2)]
    cstGs = [sb("cstG%d" % i, [128, 12, NSS, 3]) for i in range(2)]
    csoS = sb("csoS", [128, 8, NSS, 3])
    csoG = sb("csoG", [128, 12, NSS, 3])
    Yb = [sb("Y%d" % i, [128, D], BF16) for i in range(2)]
    ssq = sb("ssq", [128, 8])
    lnv = sb("lnv", [128, 8])
    rstd = sb("rstd", [128, 8])
    smalls = {}
    for _n in ("E0_ssq", "E1_ssq", "E_tot", "E_lnv", "E_rstd", "F0_ssq", "F1_ssq", "F_tot", "F_lnv", "F0_rstd", "F1_rstd"):
        smalls[_n] = sb(_n, [128, 8])
    for _p in "ACD":
        for _n in ("ssq", "lnv", "rstd", "negcum", "ecum", "ecl", "dtr", "dtv", "lav"):
            smalls[_p + "_" + _n] = sb(_p + "_" + _n, [128, 8])
    sqj = sb("sqj", [128, 512], BF16)
    gTs = [sb("gT%d" % i, [128, KC]) for i in range(2)]
    postg = sb("postg", [128, D])
    ssdn = sb("ssdn", [128, 512])
    retn = sb("retn", [128, 512])
    gdnn = sb("gdnn", [128, 4, 128])
    cwSs = [sb("cwS%d" % i, [128, 8, 4]) for i in range(2)]
    cbSs = [sb("cbS%d" % i, [128, 8]) for i in range(2)]
    cwGs = [sb("cwG%d" % i, [128, 12, 4]) for i in range(2)]
    dtbS = sb("dtbS", [128, 8])
    AnS = sb("AnS", [128, 8])
    Dss = sb("Dss", [128, 8])
    ESQ = sb("ESQ", [128, 8])
    dtbG = sb("dtbG", [128, 4])
    AnG = sb("AnG", [128, 4])
    Sssd = [sb("Sssd%d" % i, [128, 8, 64]) for i in range(2)]
    Sssdb = [sb("Sssdb%d" % i, [128, 8, 64], BF16) for i in range(2)]
    Sret = [sb("Sret%d" % i, [64, 4, 128]) for i in range(2)]
    Sretb = [sb("Sretb%d" % i, [64, 4, 128], BF16) for i in range(2)]
    Sgdn = [sb("Sgdn%d" % i, [128, 4, 128]) for i in range(2)]
    Sgdnb = [sb("Sgdnb%d" % i, [128, 4, 128], BF16) for i in range(2)]
    PKA = sb("PKA", [65, 2, 128], BF16)
    PVA = sb("PVA", [128, 2, 72], BF16)
    PKAm = sb("PKAm", [65, 2, NMETA], BF16)
    PVAm = sb("PVAm", [128, 2, 72], BF16)
    cKb = sb("cKb", [128, 128], BF16)
    junkA = sb("junkA", [128, 512], BF16)
    junkC = sb("junkC", [128, 512], BF16)
    junkD = sb("junkD", [128, 512], BF16)
    C_ng = sb("C_ng", [128, 512])
    D_ng = sb("D_ng", [128, 512])
    decT = sb("decT", [128, 8, 128])
    negcum = sb("negcum", [128, 8])
    ecum = sb("ecum", [128, 8])
    ecl = sb("ecl", [128, 8])
    dtr = sb("dtr", [128, 8])
    dtv = sb("dtv", [128, 8])
    lav = sb("lav", [128, 8])
    MT = sb("MT", [128, 8, 128], BF16)
    xs_tm = sb("xs_tm", [128, 512], BF16)
    xdt = sb("xdt", [128, 512], BF16)
    xdtw = sb("xdtw", [128, 512], BF16)
    Btm = sb("Btm", [128, 256], BF16)
    D_f1 = sb("D_f1", [128, 512])
    D_f3 = sb("D_f3", [128, 512])
    B_f2 = sb("B_f2", [128, 512])
    cK = View(B_f2, B_f2[:, 0:256])
    C_MT = sb("C_MT", [128, 4, 128], BF16)
    D_decT = sb("D_decT", [128, 4, 128])
    kw = sb("kw", [128, 4, 64], BF16)
    beta = sb("beta", [128, 4])
    nbeta = sb("nbeta", [128, 4])
    Pm = [sb("Pm%d" % i, [128, 4, 128]) for i in range(2)]
    PTm = [sb("PTm%d" % i, [128, 4, 128]) for i in range(2)]
    Rm = [sb("Rm%d" % i, [128, 4, 128]) for i in range(2)]
    QKd = sb("QKd", [128, 4, 128], BF16)
    Vtm = sb("Vtm", [128, 512], BF16)
    knw = sb("knw", [128, 512], BF16)
    vnew = sb("vnew", [128, 512], BF16)
    PTs = [sb("PTs%d" % i, [128, 4, 128], BF16) for i in range(4)]
    den = sb("den", [128, 8])

    banks = [ps("bank%d" % i, [128, 512]) for i in range(8)]
    bctr = [0]

    def nb():
        b = banks[bctr[0] % 8]
        bctr[0] += 1
        return b

    def bfv(bk):
        return bk.t[:].bitcast(BF16)

    def v3(ap, a, b):
        return ap.rearrange("p (a b) -> p a b", a=a, b=b)

    for h in range(8):
        memset(QA, QA[64:65, h, :], 8.0 * slopes[h])
    G(lambda: nc.gpsimd.iota(PKA[64:65, :, :], pattern=[[0, 2], [1, 128]], base=-128, channel_multiplier=0,
                             allow_small_or_imprecise_dtypes=True), w=[PKA])
    G(lambda: nc.gpsimd.iota(PKAm[64:65, :, :], pattern=[[0, 2], [1, NMETA]], base=-NMETA, channel_multiplier=0,
                             allow_small_or_imprecise_dtypes=True), w=[PKAm])
    for i in range(NSLOT):
        memset(VAUG[i], VAUG[i][:, :, 64:65], 1.0)
    memset(PVA, PVA[:, :, 64:65], 1.0)
    memset(PVAm, PVAm[:, :, 64:65], 1.0)

    xbufs = {}

    def xbuf(bi):
        if bi not in xbufs:
            xbufs[bi] = Buf(None, "x%d" % bi)
        return xbufs[bi]

    wview_in = [wbf_in[l].rearrange("(kc p) n -> p kc n", p=128) for l in range(DEPTH)]
    wview_out = [wbf_out[l].rearrange("(kc p) n -> p kc n", p=128) for l in range(DEPTH)]
    wscr_bufs = {l: [Buf(None, "wscr%d_%d" % (l, i)) for i in range(5)] for l in range(DEPTH)}
    wctr = [0]
    nwb = [2]
    cur_layer = [0]

    def convert_weights(l, piece=None):
        for i in range(4):
            if piece is None or piece == i:
                dma(wbf_in[l, i * 512:(i + 1) * 512, :], w_in[l, i * 512:(i + 1) * 512, :], w=[wscr_bufs[l][i]],
                    eng="pool")
        if piece is None or piece == 4:
            dma(wbf_out[l], w_out[l], w=[wscr_bufs[l][4]], eng="pool")

    def load_w(view, c0, width, off=0, buf=None):
        if buf is None:
            buf = wb[wctr[0] % nwb[0]]
            wctr[0] += 1
        dma(buf[:, :, off:off + width], view[:, :, c0:c0 + width], r=wscr_bufs[cur_layer[0]], w=[buf], eng="sp")
        return buf

    def rms_stats(src_ap, L, n, scale, col=0):
        pass

    def chk(tag):
        if stop == tag:
            raise StopBuild()

    XA = [[f1, f2, C_f1, D_f1], [D_f3, B_f2, C_ng, D_ng]]
    a1_done = set()

    def tiles_of(bi2):
        if bi2 == 0:
            return [dict(row0=0, L=NMETA, tok0=0), dict(row0=NPT, L=NSS * LS, tok0=NMETA)]
        return [dict(row0=ch_["row0"], L=128, tok0=128 * i_) for i_, ch_ in enumerate(blocks[bi2])]

    def stage_a1(l2, bi2):
        if (l2, bi2) in a1_done:
            return
        a1_done.add((l2, bi2))
        xs2 = xin if l2 == 0 else xscr
        F_tot, F_lnv = smalls["F_tot"], smalls["F_lnv"]
        for ti, tl in enumerate(tiles_of(bi2)):
            L, r0 = tl["L"], tl["row0"]
            xa = XA[ti % 2]
            F_ssq, F_rstd = smalls["F%d_ssq" % (ti % 2)], smalls["F%d_rstd" % (ti % 2)]
            for c in range(4):
                dma(xa[c][0:L, :], xs2[r0:r0 + L, c * 512:(c + 1) * 512], r=[xbuf(bi2)], w=[xa[c]])
                act(junkC[0:L, :], xa[c][0:L, :], AF.Square, r=[xa[c]], w=[junkC, F_ssq], accum_out=F_ssq[0:L, c:c + 1])
            V(lambda L=L, F_ssq=F_ssq: nc.vector.tensor_reduce(out=F_tot[0:L, 0:1], in_=F_ssq[0:L, 0:4], axis=AX.X,
                                                               op=ALU.add), r=[F_ssq], w=[F_tot])
            act(F_lnv[0:L, 0:1], F_tot[0:L, 0:1], AF.Ln, r=[F_tot], w=[F_lnv], scale=1.0 / D, bias=EPS)
            act(F_rstd[0:L, 0:1], F_lnv[0:L, 0:1], AF.Exp, r=[F_lnv], w=[F_rstd], scale=-0.5)

    pending_epi = []

    def flush_epi():
        while pending_epi:
            pending_epi.pop(0)()

    parts_of = {}

    def multi_load(parent, pairs):
        parts = []
        for i, (o_, i_) in enumerate(pairs):
            pb = parent if i == 0 else Buf(None, "part")
            dma(o_, i_, w=[pb], eng="act", allow_slow_non_contiguous=True)
            if i > 0:
                parts.append(pb)
        parts_of[id(parent)] = parts

    def RD(parent):
        return [parent] + parts_of.get(id(parent), [])

    def load_slow_params(l):
        p = l % 2
        multi_load(gTs[p], [(gTs[p][:], pre_norm[l].rearrange("(kc p) -> p kc", p=128))])
        multi_load(cwSs[p], [(cwSs[p][:, :, k_], ssd_conv_w[l, k_].rearrange("(ft p) -> p ft", p=128)) for k_ in range(4)])
        multi_load(cwGs[p], [(cwGs[p][:, :, k_], gdn_conv_w[l, k_].rearrange("(ft p) -> p ft", p=128)) for k_ in range(4)])
        multi_load(cbSs[p], [(cbSs[p][:], ssd_conv_b[l].rearrange("(ft p) -> p ft", p=128))])
        multi_load(cstSs[p], [(cstSs[p][:, :, s_, t_], st_ssdc[l, s_, t_].rearrange("(ft p) -> p ft", p=128))
                              for s_ in range(NSS) for t_ in range(3)])
        multi_load(cstGs[p], [(cstGs[p][:, :, s_, t_], st_gdnc[l, s_, t_].rearrange("(ft p) -> p ft", p=128))
                              for s_ in range(NSS) for t_ in range(3)])

    try:
        for l in range(n_layers):
            chk('const')
            cur_layer[0] = l
            if l == 0:
                convert_weights(0)
            xsrc = xin if l == 0 else xscr
            xdst = yout if l == n_layers - 1 else xscr
            if l == 0:
                load_slow_params(0)
            gT, cwS, cbS, cwG, cstS, cstG = [b[l % 2] for b in (gTs, cwSs, cbSs, cwGs, cstSs, cstGs)]
            dma(postg[:], bc(post_norm[l:l + 1, :], [128, D]), w=[postg])
            dma(ssdn[:], bc(ssd_norm[l:l + 1, :], [128, 512]), w=[ssdn])
            dma(retn[:], bc(ret_norm[l:l + 1, :], [128, 512]), w=[retn])
            for h in range(4):
                dma(gdnn[:, h, :], bc(gdn_norm[l:l + 1, :], [128, 128]), w=[gdnn])
            dma(dtbS[:], bc(ssd_dt_bias[l:l + 1, :], [128, 8]), w=[dtbS])
            dma(AnS[:], bc(ssd_a_log[l:l + 1, :], [128, 8]), w=[AnS])
            dma(Dss[:], bc(ssd_d[l:l + 1, :], [128, 8]), w=[Dss])
            dma(ESQ[:], bc(swa_sinks[l:l + 1, :], [128, 8]), w=[ESQ])
            dma(dtbG[:], bc(gdn_dt_bias[l:l + 1, :], [128, 4]), w=[dtbG])
            dma(AnG[:], bc(gdn_a_log[l:l + 1, :], [128, 4]), w=[AnG])
            act(AnS[:], AnS[:], AF.Exp, r=[AnS], w=[AnS])
            ts(AnS[:], AnS[:], -1.0, ALU.mult, r=[AnS], w=[AnS])
            act(AnG[:], AnG[:], AF.Exp, r=[AnG], w=[AnG])
            ts(AnG[:], AnG[:], -1.0, ALU.mult, r=[AnG], w=[AnG])
            tt(ESQ[:], ESQ[:], SLQ[:], ALU.add, r=[ESQ, SLQ], w=[ESQ])
            act(ESQ[:], ESQ[:], AF.Exp, r=[ESQ], w=[ESQ])
            memset(Sssd[0], Sssd[0][:], 0.0)
            memset(Sssdb[0], Sssdb[0][:], 0.0)
            memset(Sret[0], Sret[0][:], 0.0)
            memset(Sretb[0], Sretb[0][:], 0.0)
            memset(Sgdn[0], Sgdn[0][:], 0.0)
            memset(Sgdnb[0], Sgdnb[0][:], 0.0)
            memset(carS, carS[:], 0.0)
            memset(carG, carG[:], 0.0)

            for bi, blk in enumerate(blocks):
                is0 = (bi == 0)
                nwb[0] = 2 if is0 else 4
                tok = 0
                for si, ch in enumerate(blk):
                    ch["tok0"] = tok
                    ch["slot"] = si
                    tok += ch["L"]
                nbt = tok
                if is0:
                    tm_tiles = [dict(row0=0, L=NMETA, tok0=0), dict(row0=NPT, L=NSS * LS, tok0=NMETA)]
                else:
                    tm_tiles = [dict(row0=ch["row0"], L=128, tok0=ch["tok0"]) for ch in blk]
                if is0:
                    G(lambda: nc.gpsimd.iota(KA[64:65, :, 0:NMETA], pattern=[[0, 2], [1, NMETA]], base=0,
                                             channel_multiplier=0, allow_small_or_imprecise_dtypes=True), w=[KA])
                    G(lambda: nc.gpsimd.iota(KA[64:65, :, NMETA:NMETA + 16], pattern=[[0, 2], [0, NSS], [1, LS]], base=0,
                                             channel_multiplier=0, allow_small_or_imprecise_dtypes=True), w=[KA])
                elif bi == 1:
                    G(lambda: nc.gpsimd.iota(KA[64:65, :, :], pattern=[[0, 2], [0, cpb], [1, 128]], base=0,
                                             channel_multiplier=0, allow_small_or_imprecise_dtypes=True), w=[KA])

                chk('params')
                stage_a1(l, bi)
                for ti, tl in enumerate(tm_tiles):
                    L, r0, t0 = tl["L"], tl["row0"], tl["tok0"]
                    xa = XA[ti % 2]
                    F_rstd = smalls["F%d_rstd" % (ti % 2)]
                    for c in range(4):
                        ts(hb[0:L, c * 512:(c + 1) * 512], xa[c][0:L, :], F_rstd[0:L, 0:1], ALU.mult, r=[xa[c], F_rstd],
                           w=[hb])
                    for q in range(4):
                        bk = nb()
                        bv = bfv(bk)
                        for j in range(4):
                            kc = 4 * q + j
                            tr(bv[:, j * 128:j * 128 + L], hb[0:L, kc * 128:(kc + 1) * 128], ident_b[0:L, 0:L],
                               r=[hb, ident_b], w=[bk])
                        tt(hT[:, 4 * q:4 * q + 4, t0:t0 + L], v3(bv[:, 0:512], 4, 128)[:, :, 0:L],
                           bc(gT[:, 4 * q:4 * q + 4].unsqueeze(2), [128, 4, L]), ALU.mult, r=[bk] + RD(gT), w=[hT])
                flush_epi()

                def decay(la, L, H, decT_, negcum_, ecum_, ecl_):
                    bk = nb()
                    mm(bk[0:L, 0:H], Umat[0:L, 0:L], la[0:L, 0:H], True, True, r=[Umat, la], w=[bk])
                    ts(negcum_[0:L, 0:H], bk[0:L, 0:H], -1.0, ALU.mult, r=[bk], w=[negcum_])
                    act(ecum_[0:L, 0:H], bk[0:L, 0:H], AF.Exp, r=[bk], w=[ecum_])
                    yield
                    for hq in range(H // 4):
                        bk = nb()
                        for hh in range(4):
                            h = 4 * hq + hh
                            o = bk[:, hh * 128:hh * 128 + L]
                            mm(o, bc(la[0:L, h:h + 1], [L, 128]), Umat[0:L, 0:L], True, False, r=[la, Umat], w=[bk])
                            mm(o, ident_f[0:L, :], NEGU[0:L, 0:L], False, True, r=[ident_f, NEGU], w=[bk])
                        yield
                        for hh in range(4):
                            h = 4 * hq + hh
                            act(decT_[0:L, h, 0:L], bk[0:L, hh * 128:hh * 128 + L], AF.Exp, r=[bk, negcum_], w=[decT_],
                                bias=negcum_[0:L, h:h + 1])
                        act(ecl_[:, 4 * hq:4 * hq + 4], v3(bk[:, 0:512], 4, 128)[:, :, L - 1], AF.Exp, r=[bk], w=[ecl_])
                        yield

                def softplus_la(dst, src_ap, src_bufs, dtb, An, L, H, dtr_, keep_dt=None):
                    tt(dtr_[0:L, 0:H], src_ap, dtb[0:L, 0:H], ALU.add, r=src_bufs + [dtb], w=[dtr_])
                    act(dtr_[0:L, 0:H], dtr_[0:L, 0:H], AF.Exp, r=[dtr_], w=[dtr_])
                    tgt = keep_dt if keep_dt is not None else dtr_
                    act(tgt[0:L, 0:H], dtr_[0:L, 0:H], AF.Ln, r=[dtr_], w=[tgt], bias=1.0)
                    tt(dst[0:L, 0:H], tgt[0:L, 0:H], An[0:L, 0:H], ALU.mult, r=[tgt, An], w=[dst])

                def head_rmsnorm_gate(o_buf, junk_, ssq_, lnv_, rstd_, Yc, L, nh, hd, ng_, ycols):
                    n = nh * hd
                    for h in range(nh):
                        act(junk_[0:L, h * hd:(h + 1) * hd], o_buf[0:L, h * hd:(h + 1) * hd], AF.Square, r=[o_buf],
                            w=[junk_, ssq_], accum_out=ssq_[0:L, h:h + 1])
                    act(lnv_[0:L, 0:nh], ssq_[0:L, 0:nh], AF.Ln, r=[ssq_], w=[lnv_], scale=1.0 / hd, bias=EPS)
                    act(rstd_[0:L, 0:nh], lnv_[0:L, 0:nh], AF.Exp, r=[lnv_], w=[rstd_], scale=-0.5)
                    yield
                    tt(v3(o_buf[0:L, 0:n], nh, hd), v3(o_buf[0:L, 0:n], nh, hd), bc(rstd_[0:L, 0:nh].unsqueeze(2), [L, nh, hd]),
                       ALU.mult, r=[o_buf, rstd_], w=[o_buf])
                    tt(Yc[0:L, ycols:ycols + n], o_buf[0:L, 0:n], ng_[0:L, 0:n], ALU.mult, r=[o_buf, ng_], w=[Yc])

                def chunk_ctx(ch):
                    sid = 0 if ch["kind"] == "p" else 1
                    return ch["L"], ch["tok0"], ch["slot"], sid, ch.get("seq", None), Gs[ch["slot"]], Yb[ch["slot"] % 2]

                def ssd_thread():
                    A = lambda n: smalls["A_" + n]
                    ssq_, lnv_, rstd_, negcum_, ecum_, ecl_, dtr_, dtv_, lav_ = [A(n) for n in (
                        "ssq", "lnv", "rstd", "negcum", "ecum", "ecl", "dtr", "dtv", "lav")]
                    for ch in blk:
                        L, t0, slot, sid, seq, G_, Yc = chunk_ctx(ch)
                        while slot >= 2 and not fin.get(slot - 2):
                            yield
                        S1, S1b = Sssd[sid], Sssdb[sid]
                        if sid == 1:
                            dma(S1[:], st_ssd[l, seq].rearrange("h n e -> n h e"), w=[S1])
                            cp(S1b[:], S1[:], r=[S1], w=[S1b], eng="act")
                        softplus_la(lav_, SM[slot][0:L, 0:8], [SM[slot]], dtbS, AnS, L, 8, dtr_, keep_dt=dtv_)
                        yield
                        yield from decay(lav_, L, 8, decT, negcum_, ecum_, ecl_)
                        yield ("wait_proj",)
                        bk = nb()
                        bv = bfv(bk)
                        for ft in range(4):
                            tr(bv[0:L, ft * 128:(ft + 1) * 128], XBCt[ft][:, t0:t0 + L], ident_b[:, :], r=[XBCt[ft], ident_b], w=[bk])
                        yield
                        cp(xs_tm[0:L, :], bv[0:L, 0:512], r=[bk], w=[xs_tm], eng="act")
                        tt(v3(xdt[0:L, :], 8, 64), v3(bv[0:L, 0:512], 8, 64), bc(dtv_[0:L, 0:8].unsqueeze(2), [L, 8, 64]),
                           ALU.mult, r=[bk, dtv_], w=[xdt])
                        bk = nb()
                        bv = bfv(bk)
                        for g in range(2):
                            tr(bv[0:L, g * 128:(g + 1) * 128], XBCt[4 + g][:, t0:t0 + L], ident_b[:, :], r=[XBCt[4 + g], ident_b], w=[bk])
                        yield
                        cp(Btm[0:L, :], bv[0:L, 0:256], r=[bk], w=[Btm], eng="act")
                        bk = nb()
                        for g in range(2):
                            mm(bk[0:L, g * 128:g * 128 + L], XBCt[4 + g][:, t0:t0 + L], XBCt[6 + g][:, t0:t0 + L], True, True,
                               r=[XBCt[4 + g], XBCt[6 + g]], w=[bk])
                        yield
                        for g in range(2):
                            tt(MT[0:L, 4 * g:4 * g + 4, 0:L], bc(bk[0:L, g * 128:g * 128 + L].unsqueeze(1), [L, 4, L]),
                               decT[0:L, 4 * g:4 * g + 4, 0:L], ALU.mult, r=[bk, decT], w=[MT])
                        yield
                        bki = nb()
                        for h in range(8):
                            mm(bki[0:L, h * 64:(h + 1) * 64], MT[0:L, h, 0:L], xdt[0:L, h * 64:(h + 1) * 64], True, True,
                               r=[MT, xdt], w=[bki])
                        bks = nb()
                        for h in range(8):
                            mm(bks[0:L, h * 64:(h + 1) * 64], XBCt[6 + h // 4][:, t0:t0 + L], S1b[:, h, :], True, True,
                               r=[XBCt[6 + h // 4], S1b], w=[bks])
                        yield
                        tt(v3(f1[0:L, :], 8, 64), v3(bks[0:L, :], 8, 64), bc(ecum_[0:L, 0:8].unsqueeze(2), [L, 8, 64]), ALU.mult,
                           r=[bks, ecum_], w=[f1])
                        tt(f1[0:L, :], bki[0:L, :], f1[0:L, :], ALU.add, r=[bki, f1], w=[f1])
                        tt(v3(f2[0:L, :], 8, 64), v3(xs_tm[0:L, :], 8, 64), bc(Dss[0:L, 0:8].unsqueeze(2), [L, 8, 64]), ALU.mult,
                           r=[xs_tm, Dss], w=[f2], eng="pool")
                        yield
                        tt(f1[0:L, :], f1[0:L, :], f2[0:L, :], ALU.add, r=[f1, f2], w=[f1])
                        tt(f1[0:L, :], f1[0:L, :], G_[0:L, 0, :], ALU.mult, r=[f1, G_], w=[f1])
                        for g in range(2):
                            act(junkA[0:L, g * 256:(g + 1) * 256], f1[0:L, g * 256:(g + 1) * 256], AF.Square, r=[f1],
                                w=[junkA, ssq_], accum_out=ssq_[0:L, g:g + 1])
                        yield
                        act(lnv_[0:L, 0:2], ssq_[0:L, 0:2], AF.Ln, r=[ssq_], w=[lnv_], scale=1.0 / 256, bias=EPS)
                        act(rstd_[0:L, 0:2], lnv_[0:L, 0:2], AF.Exp, r=[lnv_], w=[rstd_], scale=-0.5)
                        yield
                        tt(v3(f2[0:L, :], 2, 256), v3(f1[0:L, :], 2, 256), bc(rstd_[0:L, 0:2].unsqueeze(2), [L, 2, 256]),
                           ALU.mult, r=[f1, rstd_], w=[f2])
                        tt(Yc[0:L, 0:512], f2[0:L, :], ssdn[0:L, :], ALU.mult, r=[f2, ssdn], w=[Yc])
                        yield
                        tt(v3(xdtw[0:L, :], 8, 64), v3(xdt[0:L, :], 8, 64), bc(decT[0:L, 0:8, L - 1:L], [L, 8, 64]), ALU.mult,
                           r=[xdt, decT], w=[xdtw])
                        bkn = nb()
                        for h in range(8):
                            mm(bkn[:, h * 64:(h + 1) * 64], Btm[0:L, (h // 4) * 128:(h // 4 + 1) * 128],
                               xdtw[0:L, h * 64:(h + 1) * 64], True, True, r=[Btm, xdtw], w=[bkn])
                        tt(S1[:], S1[:], bc(ecl_[:, 0:8].unsqueeze(2), [128, 8, 64]), ALU.mult, r=[S1, ecl_], w=[S1])
                        yield
                        tt(S1[:], v3(bkn[:, :], 8, 64), S1[:], ALU.add, r=[bkn, S1], w=[S1])
                        cp(S1b[:], S1[:], r=[S1], w=[S1b], eng="act")
                        if sid == 1:
                            dma(o_s_ssd[l, seq].rearrange("h n e -> n h e"), S1[:], r=[S1], w=[dbuf("o_s_ssd")])
                        yield ("done", ch["slot"])

                def ret_thread():
                    A = lambda n: smalls["C_" + n]
                    ssq_, lnv_, rstd_ = A("ssq"), A("lnv"), A("rstd")
                    for ch in blk:
                        L, t0, slot, sid, seq, G_, Yc = chunk_ctx(ch)
                        while slot >= 2 and not fin.get(slot - 2):
                            yield
                        S2, S2b = Sret[sid], Sretb[sid]
                        tt(C_ng[0:L, :], retn[0:L, :], G_[0:L, 2, :], ALU.mult, r=[retn, G_], w=[C_ng], eng="pool")
                        yield ("wait_proj",)
                        if sid == 1:
                            dma(S2[:], st_ret[l, seq].rearrange("h d e -> d h e"), w=[S2])
                            cp(S2b[:], S2[:], r=[S2], w=[S2b], eng="act")
                        bk = nb()
                        for h in range(4):
                            mm(bk[0:L, h * 128:h * 128 + L], KRF[0:64, h, t0:t0 + L], QR[0:64, h, t0:t0 + L], True, True,
                               r=[KRF, QR], w=[bk])
                        yield
                        tt(C_MT[0:L, 0:4, 0:L], v3(bk[0:L, :], 4, 128)[:, :, 0:L], RDEC[0:L, :, 0:L], ALU.mult, r=[bk, RDEC],
                           w=[C_MT])
                        yield
                        bki = nb()
                        for h in range(4):
                            mm(bki[0:L, h * 128:(h + 1) * 128], C_MT[0:L, h, 0:L], VR[slot][0:L, h, :], True, True,
                               r=[C_MT, VR[slot]], w=[bki])
                        bks = nb()
                        for h in range(4):
                            mm(bks[0:L, h * 128:(h + 1) * 128], QR[0:64, h, t0:t0 + L], S2b[:, h, :], True, True,
                               r=[QR, S2b], w=[bks])
                        yield
                        tt(v3(C_f1[0:L, :], 4, 128), v3(bks[0:L, :], 4, 128), bc(GPOW[0:L, 0:4].unsqueeze(2), [L, 4, 128]),
                           ALU.mult, r=[bks, GPOW], w=[C_f1])
                        tt(C_f1[0:L, :], bki[0:L, :], C_f1[0:L, :], ALU.add, r=[bki, C_f1], w=[C_f1])
                        yield
                        yield from head_rmsnorm_gate(C_f1, junkC, ssq_, lnv_, rstd_, Yc, L, 4, 128, C_ng, 1024)
                        yield
                        tt(kw[0:L, :, :], KR[slot][0:L, :, :], bc(RW[L][0:L, 0:4].unsqueeze(2), [L, 4, 64]), ALU.mult,
                           r=[KR[slot], RW[L]], w=[kw], eng="pool")
                        bkn = nb()
                        for h in range(4):
                            mm(bkn[0:64, h * 128:(h + 1) * 128], kw[0:L, h, :], VR[slot][0:L, h, :], True, True,
                               r=[kw, VR[slot]], w=[bkn])
                        tt(S2[:], S2[:], bc(GL[L][:, 0:4].unsqueeze(2), [64, 4, 128]), ALU.mult, r=[S2, GL[L]], w=[S2])
                        yield
                        tt(S2[:], v3(bkn[0:64, :], 4, 128), S2[:], ALU.add, r=[bkn, S2], w=[S2])
                        cp(S2b[:], S2[:], r=[S2], w=[S2b], eng="act")
                        if sid == 1:
                            dma(o_s_ret[l, seq].rearrange("h d e -> d h e"), S2[:], r=[S2], w=[dbuf("o_s_ret")])
                        yield ("done", ch["slot"])

                def gdn_thread():
                    A = lambda n: smalls["D_" + n]
                    ssq_, lnv_, rstd_, negcum_, ecum_, ecl_, dtr_, lav_ = [A(n) for n in (
                        "ssq", "lnv", "rstd", "negcum", "ecum", "ecl", "dtr", "lav")]
                    M1 = Pm[1]
                    for ch in blk:
                        L, t0, slot, sid, seq, G_, Yc = chunk_ctx(ch)
                        while slot >= 2 and not fin.get(slot - 2):
                            yield
                        S3, S3b = Sgdn[sid], Sgdnb[sid]
                        tt(D_ng[0:L, :], gdnn[0:L, :, :].rearrange("p a b -> p (a b)"), G_[0:L, 3, :], ALU.mult, r=[gdnn, G_],
                           w=[D_ng], eng="pool")
                        if sid == 1:
                            dma(S3[:], st_gdn[l, seq].rearrange("h d e -> d h e"), w=[S3])
                            cp(S3b[:], S3[:], r=[S3], w=[S3b], eng="act")
                        act(beta[0:L, :], SM[slot][0:L, 8:12], AF.Exp, r=[SM[slot]], w=[beta], scale=-1.0)
                        ts(beta[0:L, :], beta[0:L, :], 1.0, ALU.add, r=[beta], w=[beta])
                        V(lambda L=L: nc.vector.reciprocal(out=beta[0:L, :], in_=beta[0:L, :]), r=[beta], w=[beta])
                        ts(nbeta[0:L, :], beta[0:L, :], -1.0, ALU.mult, r=[beta], w=[nbeta])
                        yield
                        softplus_la(lav_, SM[slot][0:L, 12:16], [SM[slot]], dtbG, AnG, L, 4, dtr_)
                        yield
                        yield from decay(lav_, L, 4, D_decT, negcum_, ecum_, ecl_)
                        yield ("wait_proj",)
                        bk = nb()
                        bv = bfv(bk)
                        for h in range(4):
                            tr(bv[0:L, h * 128:(h + 1) * 128], QKVDt[4 + h][:, t0:t0 + L], ident_b[:, :], r=[QKVDt[4 + h], ident_b],
                               w=[bk])
                        yield
                        tt(v3(knw[0:L, :], 4, 128), v3(bv[0:L, 0:512], 4, 128), bc(D_decT[0:L, 0:4, L - 1:L], [L, 4, 128]),
                           ALU.mult, r=[bk, D_decT], w=[knw])
                        bk = nb()
                        bv = bfv(bk)
                        for h in range(4):
                            tr(bv[0:L, h * 128:(h + 1) * 128], QKVDt[8 + h][:, t0:t0 + L], ident_b[:, :], r=[QKVDt[8 + h], ident_b],
                               w=[bk])
                        yield
                        cp(Vtm[0:L, :], bv[0:L, 0:512], r=[bk], w=[Vtm], eng="act")
                        bkg = nb()
                        bkq = nb()
                        for h in range(4):
                            mm(bkg[0:L, h * 128:h * 128 + L], QKVDt[4 + h][:, t0:t0 + L], QKVDt[4 + h][:, t0:t0 + L], True, True,
                               r=[QKVDt[4 + h]], w=[bkg])
                        for h in range(4):
                            mm(bkq[0:L, h * 128:h * 128 + L], QKVDt[4 + h][:, t0:t0 + L], QKVDt[h][:, t0:t0 + L], True, True,
                               r=[QKVDt[4 + h], QKVDt[h]], w=[bkq])
                        yield
                        tt(M1[0:L, :, 0:L], v3(bkg[0:L, :], 4, 128)[:, :, 0:L], D_decT[0:L, 0:4, 0:L], ALU.mult,
                           r=[bkg, D_decT], w=[M1])
                        tt(QKd[0:L, :, 0:L], v3(bkq[0:L, :], 4, 128)[:, :, 0:L], D_decT[0:L, 0:4, 0:L], ALU.mult,
                           r=[bkq, D_decT], w=[QKd])
                        yield
                        tt(M1[0:L, :, 0:L], M1[0:L, :, 0:L], bc(nbeta[0:L, 0:4].unsqueeze(2), [L, 4, L]), ALU.mult,
                           r=[M1, nbeta], w=[M1])
                        P0_, PT0_ = Pm[0], PTm[0]
                        tt(P0_[0:L, :, 0:L], M1[0:L, :, 0:L], bc(SUm[0:L, 0:L].unsqueeze(1), [L, 4, L]), ALU.mult,
                           r=[M1, SUm], w=[P0_])
                        yield
                        bk = nb()
                        for h in range(4):
                            tr(bk[0:L, h * 128:h * 128 + L], P0_[0:L, h, 0:L], ident_f[0:L, 0:L], r=[P0_, ident_f], w=[bk])
                        yield
                        cp(PT0_[0:L, :, 0:L], v3(bk[0:L, :], 4, 128)[:, :, 0:L], r=[bk], w=[PT0_], eng="act")
                        R_ = Rm[0]
                        tt(R_[0:L, :, 0:L], P0_[0:L, :, 0:L], bc(ident_f[0:L, 0:L].unsqueeze(1), [L, 4, L]), ALU.add,
                           r=[P0_, ident_f], w=[R_], eng="pool")
                        yield
                        nlev = max(1, int(math.ceil(math.log2(L))))
                        cur = 0
                        for k in range(1, nlev):
                            Pc, PTc = Pm[cur], PTm[cur]
                            Pn, PTn = Pm[1 - cur], PTm[1 - cur]
                            last = (k == nlev - 1)
                            bkt = nb()
                            for h in range(4):
                                mm(bkt[0:L, h * 128:h * 128 + L], Pc[0:L, h, 0:L], PTc[0:L, h, 0:L], True, True,
                                   r=[Pc, PTc], w=[bkt])
                            if not last:
                                bkp = nb()
                                for h in range(4):
                                    mm(bkp[0:L, h * 128:h * 128 + L], PTc[0:L, h, 0:L], Pc[0:L, h, 0:L], True, True,
                                       r=[Pc, PTc], w=[bkp])
                            yield
                            cp(PTn[0:L, :, 0:L], v3(bkt[0:L, :], 4, 128)[:, :, 0:L], r=[bkt], w=[PTn], eng="act")
                            if not last:
                                cp(Pn[0:L, :, 0:L], v3(bkp[0:L, :], 4, 128)[:, :, 0:L], r=[bkp], w=[Pn])
                            yield
                            Rc, Rn = Rm[cur], Rm[1 - cur]
                            bkr = nb()
                            for h in range(4):
                                mm(bkr[0:L, h * 128:h * 128 + L], PTn[0:L, h, 0:L], Rc[0:L, h, 0:L], True, True,
                                   r=[PTn, Rc], w=[bkr])
                            yield
                            tt(Rn[0:L, :, 0:L], v3(bkr[0:L, :], 4, 128)[:, :, 0:L], Rc[0:L, :, 0:L], ALU.add, r=[bkr, Rc],
                               w=[Rn])
                            yield
                            cur = 1 - cur
                        Rf = Rm[cur]
                        bk = nb()
                        for h in range(4):
                            mm(bk[0:L, h * 128:(h + 1) * 128], QKVDt[4 + h][:, t0:t0 + L], S3b[:, h, :], True, True,
                               r=[QKVDt[4 + h], S3b], w=[bk])
                        yield
                        tt(v3(D_f1[0:L, :], 4, 128), v3(bk[0:L, :], 4, 128), bc(ecum_[0:L, 0:4].unsqueeze(2), [L, 4, 128]),
                           ALU.mult, r=[bk, ecum_], w=[D_f1])
                        tt(D_f3[0:L, :], Vtm[0:L, :], D_f1[0:L, :], ALU.subtract, r=[Vtm, D_f1], w=[D_f3])
                        yield
                        bk = nb()
                        for h in range(4):
                            mm(bk[0:L, h * 128:(h + 1) * 128], Rf[0:L, h, 0:L], D_f3[0:L, h * 128:(h + 1) * 128], True, True,
                               r=[Rf, D_f3], w=[bk])
                        yield
                        tt(v3(vnew[0:L, :], 4, 128), v3(bk[0:L, :], 4, 128), bc(beta[0:L, 0:4].unsqueeze(2), [L, 4, 128]),
                           ALU.mult, r=[bk, beta], w=[vnew])
                        yield
                        bks = nb()
                        for h in range(4):
                            mm(bks[0:L, h * 128:(h + 1) * 128], QKVDt[h][:, t0:t0 + L], S3b[:, h, :], True, True,
                               r=[QKVDt[h], S3b], w=[bks])
                        bki = nb()
                        for h in range(4):
                            mm(bki[0:L, h * 128:(h + 1) * 128], QKd[0:L, h, 0:L], vnew[0:L, h * 128:(h + 1) * 128], True, True,
                               r=[QKd, vnew], w=[bki])
                        yield
                        tt(v3(D_f1[0:L, :], 4, 128), v3(bks[0:L, :], 4, 128), bc(ecum_[0:L, 0:4].unsqueeze(2), [L, 4, 128]),
                           ALU.mult, r=[bks, ecum_], w=[D_f1])
                        tt(D_f1[0:L, :], bki[0:L, :], D_f1[0:L, :], ALU.add, r=[bki, D_f1], w=[D_f1])
                        yield
                        yield from head_rmsnorm_gate(D_f1, junkD, ssq_, lnv_, rstd_, Yc, L, 4, 128, D_ng, 1536)
                        yield
                        bkn = nb()
                        for h in range(4):
                            mm(bkn[:, h * 128:(h + 1) * 128], knw[0:L, h * 128:(h + 1) * 128],
                               vnew[0:L, h * 128:(h + 1) * 128], True, True, r=[knw, vnew], w=[bkn])
                        tt(S3[:], S3[:], bc(ecl_[:, 0:4].unsqueeze(2), [128, 4, 128]), ALU.mult, r=[S3, ecl_], w=[S3])
                        yield
                        tt(S3[:], v3(bkn[:, :], 4, 128), S3[:], ALU.add, r=[bkn, S3], w=[S3])
                        cp(S3b[:], S3[:], r=[S3], w=[S3b], eng="act")
                        if sid == 1:
                            dma(o_s_gdn[l, seq].rearrange("h d e -> d h e"), S3[:], r=[S3], w=[dbuf("o_s_gdn")])
                        yield ("done", ch["slot"])

                def swa_thread():
                    for ch in blk:
                        L, t0, slot, sid, seq, G_, Yc = chunk_ctx(ch)
                        while slot >= 2 and not fin.get(slot - 2):
                            yield
                        if sid == 1:
                            dma(cK[:, 0:128], st_k[l, seq], w=[cK])
                            dma(cK[:, 128:256], st_v[l, seq], w=[cK])
                            cp(cKb[:, :], cK[:, 0:128], r=[cK], w=[cKb])
                            bk = nb()
                            bv = bfv(bk)
                            for g in range(2):
                                tr(bv[0:64, g * 128:(g + 1) * 128], cKb[:, g * 64:(g + 1) * 64], ident_b[:, :],
                                   r=[cKb, ident_b], w=[bk])
                            yield
                            cp(PKA[0:64, :, :], v3(bv[0:64, 0:256], 2, 128), r=[bk], w=[PKA])
                            cp(PVA[:, :, 0:64], v3(cK[:, 128:256], 2, 64), r=[cK], w=[PVA])
                            prev = (PKA, PVA, 128, NEGprev)
                        elif ch["ci"] == 0:
                            prev = None
                        elif ch["ci"] == 1:
                            prev = (PKAm, PVAm, NMETA, NEGmeta)
                        else:
                            prev = (PKA, PVA, 128, NEGprev)
                        for g in range(2):
                            tiles = []
                            if prev is not None:
                                tiles.append((prev[0], prev[0][0:65, g, 0:prev[2]], prev[1], prev[1][0:prev[2], g, 0:65],
                                              prev[2], prev[3]))
                            tiles.append((KA, KA[0:65, g, t0:t0 + L], VAUG[slot], VAUG[slot][0:L, g, 0:65], L, NEGown))
                            pts = []
                            for ti, (kbuf, kap, vbuf, vap, Lk, negm) in enumerate(tiles):
                                bk = nb()
                                for hh in range(4):
                                    h = 4 * g + hh
                                    o = bk[0:Lk, hh * 128:hh * 128 + L]
                                    mm(o, kap, QA[0:65, h, t0:t0 + L], True, False, r=[kbuf, QA], w=[bk])
                                    mm(o, ident_b[0:Lk, 0:Lk], negm[0:Lk, 0:L], False, True, r=[ident_b, negm], w=[bk])
                                yield
                                pt = PTs[2 * g + ti] if len(tiles) == 2 else PTs[2 * g + 1]
                                act(pt[0:Lk, :, 0:L], v3(bk[0:Lk, :], 4, 128)[:, :, 0:L], AF.Exp, r=[bk], w=[pt], scale=0.125)
                                pts.append((pt, vbuf, vap, Lk))
                                yield
                            bko = nb()
                            for hh in range(4):
                                for ti, (pt, vbuf, vap, Lk) in enumerate(pts):
                                    mm(bko[0:L, hh * 72:hh * 72 + 65], pt[0:Lk, hh, 0:L], vap, ti == 0, ti == len(pts) - 1,
                                       r=[pt, vbuf], w=[bko])
                            yield
                            ov = v3(bko[0:L, 0:288], 4, 72)
                            tt(den[0:L, 4 * g:4 * g + 4], ov[:, :, 64], ESQ[0:L, 4 * g:4 * g + 4], ALU.add, r=[bko, ESQ],
                               w=[den])
                            V(lambda L=L, g=g: nc.vector.reciprocal(out=den[0:L, 4 * g:4 * g + 4],
                                                                    in_=den[0:L, 4 * g:4 * g + 4]), r=[den], w=[den])
                            tt(v3(B_f2[0:L, 256 * g:256 * g + 256], 4, 64), ov[:, :, 0:64],
                               bc(den[0:L, 4 * g:4 * g + 4].unsqueeze(2), [L, 4, 64]), ALU.mult, r=[bko, den], w=[B_f2])
                            yield
                        tt(Yc[0:L, 512:1024], B_f2[0:L, :], G_[0:L, 1, :], ALU.mult, r=[B_f2, G_], w=[Yc])
                        if sid == 0:
                            if ch["ci"] == 0:
                                cp(PKAm[0:64, :, 0:L], KA[0:64, :, t0:t0 + L], r=[KA], w=[PKAm], eng="pool")
                                cp(PVAm[0:L, :, 0:64], VAUG[slot][0:L, :, 0:64], r=[VAUG[slot]], w=[PVAm], eng="pool")
                            else:
                                cp(PKA[0:64, :, 0:L], KA[0:64, :, t0:t0 + L], r=[KA], w=[PKA], eng="pool")
                                cp(PVA[0:L, :, 0:64], VAUG[slot][0:L, :, 0:64], r=[VAUG[slot]], w=[PVA], eng="pool")
                        if sid == 1:
                            dma(o_s_k[l, seq, 0:128 - LS, :], st_k[l, seq, LS:128, :], w=[dbuf("o_s_k")])
                            dma(o_s_v[l, seq, 0:128 - LS, :], st_v[l, seq, LS:128, :], w=[dbuf("o_s_v")])
                            dma(o_s_k[l, seq, 128 - LS:128, :], KV[slot][0:LS, 0:128], r=[KV[slot]], w=[dbuf("o_s_k")])
                            dma(o_s_v[l, seq, 128 - LS:128, :], KV[slot][0:LS, 128:256], r=[KV[slot]], w=[dbuf("o_s_v")])
                        elif ch["ci"] == 16:
                            dma(o_p_k[l], KV[slot][:, 0:128], r=[KV[slot]], w=[dbuf("o_p_k")])
                            dma(o_p_v[l], KV[slot][:, 128:256], r=[KV[slot]], w=[dbuf("o_p_v")])
                        yield ("done", ch["slot"])

                def finish_chunk(ch):
                    L, t0, slot, sid, seq, G_, Yc = chunk_ctx(ch)
                    if dbg:
                        dma(ydbg[ch["row0"]:ch["row0"] + L, :], Yc[0:L, :], r=[Yc], w=[dbuf("ydbg")])
                    for q in range(4):
                        bk = nb()
                        bv = bfv(bk)
                        for j in range(4):
                            kc = 4 * q + j
                            tr(bv[:, j * 128:j * 128 + L], Yc[0:L, kc * 128:(kc + 1) * 128], ident_b[0:L, 0:L],
                               r=[Yc, ident_b], w=[bk])
                        cp(hT[:, 4 * q:4 * q + 4, t0:t0 + L], v3(bv[:, 0:512], 4, 128)[:, :, 0:L], r=[bk], w=[hT],
                           eng="act" if q % 2 else "dve")

                chk('A')
                def tm_group(wbuf, off, n, evac):
                    for ch in blk:
                        L, t0, slot = ch["L"], ch["tok0"], ch["slot"]
                        bk = nb()
                        for kc in range(KC):
                            mm(bk[0:L, 0:n], hT[:, kc, t0:t0 + L], wbuf[:, kc, off:off + n], kc == 0, kc == KC - 1,
                               r=[hT, wbuf], w=[bk])
                        evac(bk, ch)

                def fm_group(wbuf, off, M, evac):
                    bk = nb()
                    for kc in range(KC):
                        mm(bk[0:M, 0:nbt], wbuf[:, kc, off:off + M], hT[:, kc, 0:nbt], kc == 0, kc == KC - 1,
                           r=[hT, wbuf], w=[bk])
                    evac(bk)

                def gate_evac(gi, half):
                    def f(bk, ch):
                        L = ch["L"]
                        act(Gs[ch["slot"]][0:L, gi, half * 256:(half + 1) * 256], bk[0:L, 0:256], AF.Silu, r=[bk],
                            w=[Gs[ch["slot"]]])
                    return f

                cctr = [0]

                def conv_evac(dst, ft, cw, cb, car, cst, cso):
                    def f(bk):
                        k = cctr[0] % 2
                        cctr[0] += 1
                        S_, A_ = ST[k], ACC[k]
                        if not is0:
                            n = nbt
                            cp(S_[:, 0:3], car[:, ft, :], r=[car], w=[S_], eng="pool")
                            cp(S_[:, 3:3 + n], bk[:, 0:n], r=[bk], w=[S_], eng="act")
                            cp(car[:, ft, :], S_[:, n:n + 3], r=[S_], w=[car], eng="pool")
                            no = n
                        else:
                            memset(S_, S_[:, 0:3], 0.0)
                            sv = v3(S_[:, 19:47], NSS, 7)
                            cp(sv[:, :, 0:3], cst[:, ft, :, :], r=RD(cst), w=[S_], eng="pool")
                            cp(S_[:, 3:19], bk[:, 0:16], r=[bk], w=[S_], eng="act")
                            cp(sv[:, :, 3:7], v3(bk[:, 16:32], NSS, LS), r=[bk], w=[S_], eng="act")
                            cp(car[:, ft, :], S_[:, 16:19], r=[S_], w=[car], eng="pool")
                            cp(cso[:, ft, :, :], sv[:, :, 4:7], r=[S_], w=[cso], eng="pool")
                            no = 44
                        ts(A_[:, 0:no], S_[:, 0:no], cw[:, ft, 0:1], ALU.mult, r=[S_] + RD(cw), w=[A_])
                        for kk in range(1, 4):
                            stt(A_[:, 0:no], S_[:, kk:kk + no], cw[:, ft, kk:kk + 1], A_[:, 0:no], ALU.mult, ALU.add,
                                r=[S_, A_] + RD(cw), w=[A_])
                        bias = cb[:, ft:ft + 1] if cb is not None else None
                        rr = [A_] + (RD(cb) if cb is not None else [])
                        if not is0:
                            act(dst[ft][:, 0:no], A_[:, 0:no], AF.Silu, r=rr, w=[dst[ft]], bias=bias)
                        else:
                            act(dst[ft][:, 0:16], A_[:, 0:16], AF.Silu, r=rr, w=[dst[ft]], bias=bias)
                            act(v3(dst[ft][:, 16:32], NSS, LS), v3(A_[:, 19:47], NSS, 7)[:, :, 0:4], AF.Silu, r=rr,
                                w=[dst[ft]], bias=bias)
                    return f

                wv = wview_in[l]
                import os as _os
                _gl = ((0, C_Z), (1, C_GA), (2, C_GR), (3, C_GD))
                if _os.environ.get('KSKIPG'):
                    _gl = ()
                if _os.environ.get('KDUPG'):
                    _gl = _gl + _gl
                for gi, c0 in _gl:
                    for half in range(2):
                        wbuf = load_w(wv, c0 + half * 256, 256)
                        tm_group(wbuf, 0, 256, gate_evac(gi, half))
                chk('B1')
                wbuf = load_w(wv, C_DT, 8)
                load_w(wv, C_BD, 8, off=8, buf=wbuf)

                def small_evac(bk, ch):
                    L = ch["L"]
                    cp(SM[ch["slot"]][0:L, :], bk[0:L, 0:16], r=[bk], w=[SM[ch["slot"]]])
                tm_group(wbuf, 0, 16, small_evac)
                fin = {}
                done_cnt = {}
                th_ssd, th_gdn, th_swa, th_ret = ssd_thread(), gdn_thread(), swa_thread(), ret_thread()
                early = [th_ssd, th_gdn]
                while early:
                    for th in list(early):
                        v = next(th)
                        if v is not None and v[0] == "wait_proj":
                            early.remove(th)
                for q in range(4):
                    wbuf = load_w(wv, C_XBC + q * 256, 256)
                    for j in range(2):
                        ft = 2 * q + j
                        fm_group(wbuf, j * 128, 128, conv_evac(XBCt, ft, cwS, cbS, carS, cstS, csoS))
                chk('B7')
                for q in range(6):
                    wbuf = load_w(wv, C_QKVD + q * 256, 256)
                    for j in range(2):
                        ft = 2 * q + j
                        fm_group(wbuf, j * 128, 128, conv_evac(QKVDt, ft, cwG, None, carG, cstG, csoG))
                chk('B8')
                for ft in range(8):
                    k = ft % 2
                    S_, A_ = ST[k], ACC[k]
                    sq_ = (sqj, junkD)[k]
                    Q_ = QKVDt[ft]
                    tt(sq_[:, 0:nbt], Q_[:, 0:nbt], Q_[:, 0:nbt], ALU.mult, r=[Q_], w=[sq_], eng="pool")
                    bk = nb()
                    mm(bk[:, 0:nbt], ones_b[:, :], sq_[:, 0:nbt], True, True, r=[ones_b, sq_], w=[bk])
                    act(A_[:, 0:nbt], bk[:, 0:nbt], AF.Ln, r=[bk], w=[A_], bias=EPS)
                    act(A_[:, 0:nbt], A_[:, 0:nbt], AF.Exp, r=[A_], w=[A_], scale=-0.5,
                        bias=(math.log(128.0 ** -0.5) if ft < 4 else 0.0))
                    tt(Q_[:, 0:nbt], Q_[:, 0:nbt], A_[:, 0:nbt], ALU.mult, r=[Q_, A_], w=[Q_])

                chk('B2')
                wbuf = load_w(wv, C_KA, 256)

                def kv_evac(bk, ch):
                    L, slot = ch["L"], ch["slot"]
                    _m = _os.environ.get('KVMODE', '0')
                    if _m in ('0', '1'):
                        cp(KV[slot][0:L, :], bk[0:L, 0:256], r=[bk], w=[KV[slot]], eng="act")
                    if _m in ('0', '2'):
                        cp(VAUG[slot][0:L, :, 0:64], v3(bk[0:L, 128:256], 2, 64), r=[bk] + ([KV[slot]] if _os.environ.get('KVSER') else []), w=[VAUG[slot]])
                tm_group(wbuf, 0, 256, kv_evac)
                chk('B2a')
                for g in range(2):
                    fm_group(wbuf, g * 64, 64,
                             lambda bk, g=g: cp(KA[0:64, g, 0:nbt], bk[0:64, 0:nbt], r=[bk], w=[KA]))
                chk('B3')
                for half in range(2):
                    wbuf = load_w(wv, C_QA + half * 256, 256)
                    for hh in range(4):
                        h = half * 4 + hh
                        fm_group(wbuf, hh * 64, 64,
                                 lambda bk, h=h: cp(QA[0:64, h, 0:nbt], bk[0:64, 0:nbt], r=[bk], w=[QA],
                                                    eng="act" if h % 2 else "dve"))
                chk('B4')
                wbuf = load_w(wv, C_QR, 256)
                for h in range(4):
                    fm_group(wbuf, h * 64, 64,
                             lambda bk, h=h: cp(QR[0:64, h, 0:nbt], bk[0:64, 0:nbt], r=[bk], w=[QR]))
                wbuf = load_w(wv, C_KR, 256)
                for h in range(4):
                    fm_group(wbuf, h * 64, 64,
                             lambda bk, h=h: ts(KRF[0:64, h, 0:nbt], bk[0:64, 0:nbt], 0.125, ALU.mult, r=[bk], w=[KRF]))

                def kr_evac(bk, ch):
                    L, slot = ch["L"], ch["slot"]
                    ts(KR[slot][0:L, :, :], v3(bk[0:L, 0:256], 4, 64), 0.125, ALU.mult, r=[bk], w=[KR[slot]])
                tm_group(wbuf, 0, 256, kr_evac)
                chk('B5')
                for half in range(2):
                    wbuf = load_w(wv, C_VR + half * 256, 256)

                    def vr_evac(bk, ch, half=half):
                        L, slot = ch["L"], ch["slot"]
                        cp(VR[slot][0:L, 2 * half:2 * half + 2, :], v3(bk[0:L, 0:256], 2, 128), r=[bk], w=[VR[slot]],
                           eng="act")
                    tm_group(wbuf, 0, 256, vr_evac)
                chk('B6')
                chk('B')
                if l + 1 < n_layers and len(blocks) >= 7:
                    if 1 <= bi <= 5:
                        convert_weights(l + 1, piece=bi - 1)
                    if bi == 6:
                        load_slow_params(l + 1)
                elif l + 1 < n_layers and bi == len(blocks) - 1:
                    convert_weights(l + 1)
                    load_slow_params(l + 1)
                threads = [th_ssd, th_gdn, th_swa, th_ret]
                while threads:
                    for th in list(threads):
                        try:
                            v = next(th)
                        except StopIteration:
                            threads.remove(th)
                            continue
                        if v is not None and v[0] == "done":
                            done_cnt[v[1]] = done_cnt.get(v[1], 0) + 1
                            if done_cnt[v[1]] == 4:
                                finish_chunk(blk[v[1]])
                                fin[v[1]] = True

                chk('C')
                if bi + 1 < len(blocks):
                    stage_a1(l, bi + 1)
                elif l + 1 < n_layers:
                    stage_a1(l + 1, 0)
                wvo = wview_out[l]
                for ti, tl in enumerate(tm_tiles):
                    pass
                osb = xt[0]
                outbuf = {}
                for ti, tl in enumerate(tm_tiles):
                    outbuf[ti] = None
                E_ssq = [smalls["E0_ssq"], smalls["E1_ssq"]]
                E_tot, E_lnv, E_rstd = smalls["E_tot"], smalls["E_lnv"], smalls["E_rstd"]
                XR = [ST[0], ST[1], ACC[0], ACC[1]]
                for cg in range(D // WG):
                    wbuf = load_w(wvo, cg * WG, WG)
                    for ti, tl in enumerate(tm_tiles):
                        L, t0 = tl["L"], tl["tok0"]
                        bk = nb()
                        for kc in range(KC):
                            mm(bk[0:L, 0:WG], hT[:, kc, t0:t0 + L], wbuf[:, kc, 0:WG], kc == 0, kc == KC - 1,
                               r=[hT, wbuf], w=[bk])
                        cp(xt[ti][0:L, cg * WG:(cg + 1) * WG], bk[0:L, 0:WG], r=[bk], w=[xt[ti]],
                           eng="act" if cg % 2 else "dve")
                        act(junkA[0:L, 0:WG], xt[ti][0:L, cg * WG:(cg + 1) * WG], AF.Square, r=[xt[ti]],
                            w=[junkA, E_ssq[ti]], accum_out=E_ssq[ti][0:L, cg:cg + 1])
                def epilogue(tm_tiles=tm_tiles, xsrc=xsrc, xdst=xdst, bi=bi, E_ssq=E_ssq, XR=XR):
                    xrc = 0
                    for ti, tl in enumerate(tm_tiles):
                        L, r0 = tl["L"], tl["row0"]
                        o_ = xt[ti]
                        V(lambda L=L, ti=ti: nc.vector.tensor_reduce(out=E_tot[0:L, 0:1], in_=E_ssq[ti][0:L, 0:8], axis=AX.X,
                                                                     op=ALU.add), r=[E_ssq[ti]], w=[E_tot])
                        act(E_lnv[0:L, 0:1], E_tot[0:L, 0:1], AF.Ln, r=[E_tot], w=[E_lnv], scale=1.0 / D, bias=EPS)
                        act(E_rstd[0:L, 0:1], E_lnv[0:L, 0:1], AF.Exp, r=[E_lnv], w=[E_rstd], scale=-0.5)
                        for c in range(8):
                            c0 = c * 256
                            xb = XR[xrc % 4]
                            xrc += 1
                            dma(xb[0:L, 0:256], xsrc[r0:r0 + L, c0:c0 + 256], r=[xbuf(bi)], w=[xb])
                            stt(o_[0:L, c0:c0 + 256], o_[0:L, c0:c0 + 256], E_rstd[0:L, 0:1], postg[0:L, c0:c0 + 256],
                                ALU.mult, ALU.mult, r=[o_, E_rstd, postg], w=[o_])
                            tt(o_[0:L, c0:c0 + 256], o_[0:L, c0:c0 + 256], xb[0:L, 0:256], ALU.add, r=[o_, xb], w=[o_])
                        dma(xdst[r0:r0 + L, :], o_[0:L, :], r=[o_], w=[xbuf(bi)])
                pending_epi.append(epilogue)

            flush_epi()
            if n_blocks is None:
                dma(o_p_ssd[l].rearrange("h n e -> n h e"), Sssd[0][:], r=[Sssd[0]], w=[dbuf("o_p_ssd")])
                dma(o_p_ret[l].rearrange("h d e -> d h e"), Sret[0][:], r=[Sret[0]], w=[dbuf("o_p_ret")])
                dma(o_p_gdn[l].rearrange("h d e -> d h e"), Sgdn[0][:], r=[Sgdn[0]], w=[dbuf("o_p_gdn")])
                for t_ in range(3):
                    dma(o_p_ssdc[l, t_].rearrange("(ft p) -> p ft", p=128), carS[:, :, t_], r=[carS],
                        w=[Buf(None, "o_p_ssdc")], eng="act", allow_slow_non_contiguous=True)
                    dma(o_p_gdnc[l, t_].rearrange("(ft p) -> p ft", p=128), carG[:, :, t_], r=[carG],
                        w=[Buf(None, "o_p_gdnc")], eng="act", allow_slow_non_contiguous=True)
            for s in range(NSS):
                for t_ in range(3):
                    dma(o_s_ssdc[l, s, t_].rearrange("(ft p) -> p ft", p=128), csoS[:, :, s, t_], r=[csoS],
                        w=[Buf(None, "o_s_ssdc")], eng="act", allow_slow_non_contiguous=True)
                    dma(o_s_gdnc[l, s, t_].rearrange("(ft p) -> p ft", p=128), csoG[:, :, s, t_], r=[csoG],
                        w=[Buf(None, "o_s_gdnc")], eng="act", allow_slow_non_contiguous=True)

    except StopBuild:
        pass

    P.emit(es)
    es.close()
    return nc, P.stats


_CACHE = {}


def _in_maps(inp):
    f = lambda a: np.ascontiguousarray(np.asarray(a, dtype=np.float32))
    maps = []
    for c in range(8):
        b = c % 4
        sl = slice(NSS * c, NSS * c + NSS)
        xin = np.concatenate([inp["meta_tokens"], inp["x_prompt"][b], inp["x_sample"][sl].reshape(NSS * LS, D)], axis=0)
        m = {
            "xin": f(xin),
            "st_ssd": f(inp["state_ssd"][:, sl]),
            "st_ssdc": f(inp["state_ssd_conv"][:, sl]),
            "st_k": f(inp["cache_swa_k"][:, sl].reshape(DEPTH, NSS, 128, 128)),
            "st_v": f(inp["cache_swa_v"][:, sl].reshape(DEPTH, NSS, 128, 128)),
            "st_ret": f(inp["state_ret"][:, sl]),
            "st_gdn": f(inp["state_gdn"][:, sl]),
            "st_gdnc": f(inp["state_gdn_conv"][:, sl]),
        }
        for k in ("pre_norm", "post_norm", "w_in", "w_out", "ssd_conv_w", "ssd_conv_b", "ssd_dt_bias", "ssd_a_log",
                  "ssd_d", "ssd_norm", "swa_sinks", "ret_norm", "gdn_conv_w", "gdn_dt_bias", "gdn_a_log", "gdn_norm"):
            m[k] = f(inp[k])
        maps.append(m)
    return maps


def kernel(**inp):
    if "nc" not in _CACHE:
        _CACHE["nc"] = build_program()[0]
    nc = _CACHE["nc"]
    res = run_bass_kernel_spmd(nc, _in_maps(inp), core_ids=list(range(8)))
    R = res.results
    B = 4
    y_prompt = np.stack([R[b]["yout"][NMETA:NPT] for b in range(B)]).astype(np.float32)
    y_sample = np.concatenate([R[c]["yout"][NPT:NT].reshape(NSS, LS, D) for c in range(8)]).astype(np.float32)

    def pst(name, shape):
        return np.stack([np.stack([R[b][name][l] for b in range(B)]) for l in range(DEPTH)]).reshape(shape).astype(np.float32)

    def sst(name, shape):
        return np.concatenate([R[c][name] for c in range(8)], axis=1).reshape(shape).astype(np.float32)

    outs = (
        y_prompt, y_sample,
        pst("o_p_ssd", (DEPTH, B, 8, 128, 64)), pst("o_p_ssdc", (DEPTH, B, 3, 1024)),
        pst("o_p_k", (DEPTH, B, 128, 2, 64)), pst("o_p_v", (DEPTH, B, 128, 2, 64)),
        pst("o_p_ret", (DEPTH, B, 4, 64, 128)), pst("o_p_gdn", (DEPTH, B, 4, 128, 128)),
        pst("o_p_gdnc", (DEPTH, B, 3, 1536)),
        sst("o_s_ssd", (DEPTH, 32, 8, 128, 64)), sst("o_s_ssdc", (DEPTH, 32, 3, 1024)),
        sst("o_s_k", (DEPTH, 32, 128, 2, 64)), sst("o_s_v", (DEPTH, 32, 128, 2, 64)),
        sst("o_s_ret", (DEPTH, 32, 4, 64, 128)), sst("o_s_gdn", (DEPTH, 32, 4, 128, 128)),
        sst("o_s_gdnc", (DEPTH, 32, 3, 1536)),
    )
    return outs
```

```python
import math
from contextlib import ExitStack

import numpy as np
import concourse.bass as bass
import concourse.mybir as mybir
from concourse.bass_utils import run_bass_kernel_spmd

F32 = mybir.dt.float32
BF16 = mybir.dt.bfloat16
I32 = mybir.dt.int32
AF = mybir.ActivationFunctionType
ALU = mybir.AluOpType
AX = mybir.AxisListType

D = 2048
KC = 16
DEPTH = 4
SEQ = 2048
NMETA = 16
NPT = SEQ + NMETA
NSS = 4
LS = 4
NT = NPT + NSS * LS
IN_W = 6416
EPS = 1e-6
NEG = -30000.0

C_Z, C_XBC, C_DT, C_QA, C_KA, C_VA, C_GA = 0, 512, 1536, 1544, 2056, 2184, 2312
C_QR, C_KR, C_VR, C_GR, C_QKVD, C_GD, C_BD, C_AD = 2824, 3080, 3336, 3848, 4360, 5896, 6408, 6412


class Buf:
    __slots__ = ("t", "lw", "rd", "rd_dma", "name", "excl")

    def __init__(self, t, name="", excl=False):
        self.t = t
        self.excl = excl
        self.lw = None
        self.rd = {}
        self.rd_dma = []
        self.name = name

    def __getitem__(self, k):
        return self.t[k]


class View:
    __slots__ = ("p", "t")

    def __init__(self, parent, ap):
        self.p = parent
        self.t = ap

    def __getitem__(self, k):
        return self.t[k]

    lw = property(lambda self: self.p.lw, lambda self, v: setattr(self.p, "lw", v))
    rd = property(lambda self: self.p.rd, lambda self, v: setattr(self.p, "rd", v))
    rd_dma = property(lambda self: self.p.rd_dma, lambda self, v: setattr(self.p, "rd_dma", v))
    excl = property(lambda self: self.p.excl)


class Prog:
    def __init__(self, nc, n_dma_sems=24):
        self.nc = nc
        self.ops = []
        self.E = {"pe": nc.tensor, "act": nc.scalar, "dve": nc.vector, "pool": nc.gpsimd, "sp": nc.sync}
        self.n_dma_sems = n_dma_sems

    def op(self, eng, fn, r=(), w=(), dma=False):
        idx = len(self.ops)
        deps = set()
        for b in r:
            if b.lw is not None:
                deps.add(b.lw)
            if b.excl:
                deps.update(v for e, v in b.rd.items() if e != eng)
        for b in w:
            if b.lw is not None:
                deps.add(b.lw)
            deps.update(b.rd.values())
            deps.update(b.rd_dma)
        for b in r:
            if dma:
                b.rd_dma.append(idx)
            else:
                b.rd[eng] = idx
        for b in w:
            b.lw = idx
            b.rd = {}
            b.rd_dma = []
        deps.discard(idx)
        self.ops.append([eng, fn, deps, dma])
        return idx

    def emit(self, es):
        nc = self.nc
        ops = self.ops
        needed = set()
        for i, (eng, fn, deps, dma) in enumerate(ops):
            for d in deps:
                de, _, _, ddma = ops[d]
                if ddma:
                    continue
                if de == eng and eng == "pe":
                    continue
                needed.add(d)
        esem = {e: es.enter_context(nc.semaphore("sem_" + e)) for e in ("pe", "act", "dve", "pool")}
        dsem = [es.enter_context(nc.semaphore("dsem%d" % i)) for i in range(self.n_dma_sems)]
        dval = [0] * self.n_dma_sems
        ecount = {e: 0 for e in esem}
        sig = [None] * len(ops)
        known = {e: {} for e in self.E}
        ndma = 0
        nwait = 0
        dcnt = {}
        for i, (eng, fn, deps, dma) in enumerate(ops):
            waits = {}
            for d in deps:
                s = sig[d]
                if s is None:
                    continue
                if s[1] > waits.get(s[0], (None, 0))[1]:
                    waits[s[0]] = s
            if dma:
                half = self.n_dma_sems // 2
                base = 0 if eng == "sp" else half
                k = base + (dcnt.get(eng, 0) % half)
                dcnt[eng] = dcnt.get(eng, 0) + 1
                ndma += 1
                if dval[k] > 0:
                    key = ("d", k)
                    if dval[k] > waits.get(key, (None, 0))[1]:
                        waits[key] = (key, dval[k])
            kn = known[eng]
            for key, (_, val) in waits.items():
                if kn.get(key, 0) >= val:
                    continue
                sem = esem[key] if isinstance(key, str) else dsem[key[1]]
                self.E[eng].wait_ge(sem, val)
                kn[key] = val
                nwait += 1
            ins = fn()
            if dma:
                dval[k] += 16
                ins.then_inc(dsem[k], 16)
                sig[i] = (("d", k), dval[k])
            elif i in needed:
                ecount[eng] += 1
                ins.then_inc(esem[eng], 1)
                sig[i] = (eng, ecount[eng])
        for k in range(self.n_dma_sems):
            if dval[k] > 0:
                nc.sync.wait_ge(dsem[k], dval[k])
        for e in esem:
            if ecount[e] > 0:
                nc.sync.wait_ge(esem[e], ecount[e])
        self.stats = dict(n_ops=len(ops), n_wait=nwait, n_dma=ndma, counts=dict(ecount))


def make_chunks():
    chunks = [dict(kind="p", L=NMETA, row0=0, ci=0)]
    for c in range(1, 17):
        chunks.append(dict(kind="p", L=128, row0=NMETA + 128 * (c - 1), ci=c))
    samples = [dict(kind="s", L=LS, row0=NPT + LS * s, seq=s) for s in range(NSS)]
    return chunks, samples


class StopBuild(Exception):
    pass


def build_program(n_layers=DEPTH, n_blocks=None, cpb=2, dbg=False, stop=None):
    nc = bass.Bass("TRN2", target_bir_lowering=False)
    es = ExitStack()
    P = Prog(nc)

    def dram(name, shape, dt=F32, kind="ExternalInput"):
        return nc.dram_tensor(name, list(shape), dt, kind=kind).ap()

    xin = dram("xin", [NT, D])
    st_ssd = dram("st_ssd", [DEPTH, NSS, 8, 128, 64])
    st_ssdc = dram("st_ssdc", [DEPTH, NSS, 3, 1024])
    st_k = dram("st_k", [DEPTH, NSS, 128, 128])
    st_v = dram("st_v", [DEPTH, NSS, 128, 128])
    st_ret = dram("st_ret", [DEPTH, NSS, 4, 64, 128])
    st_gdn = dram("st_gdn", [DEPTH, NSS, 4, 128, 128])
    st_gdnc = dram("st_gdnc", [DEPTH, NSS, 3, 1536])
    pre_norm = dram("pre_norm", [DEPTH, D])
    post_norm = dram("post_norm", [DEPTH, D])
    w_in = dram("w_in", [DEPTH, D, IN_W])
    w_out = dram("w_out", [DEPTH, D, D])
    ssd_conv_w = dram("ssd_conv_w", [DEPTH, 4, 1024])
    ssd_conv_b = dram("ssd_conv_b", [DEPTH, 1024])
    ssd_dt_bias = dram("ssd_dt_bias", [DEPTH, 8])
    ssd_a_log = dram("ssd_a_log", [DEPTH, 8])
    ssd_d = dram("ssd_d", [DEPTH, 8])
    ssd_norm = dram("ssd_norm", [DEPTH, 512])
    swa_sinks = dram("swa_sinks", [DEPTH, 8])
    ret_norm = dram("ret_norm", [DEPTH, 512])
    gdn_conv_w = dram("gdn_conv_w", [DEPTH, 4, 1536])
    gdn_dt_bias = dram("gdn_dt_bias", [DEPTH, 4])
    gdn_a_log = dram("gdn_a_log", [DEPTH, 4])
    gdn_norm = dram("gdn_norm", [DEPTH, 128])

    EO = "ExternalOutput"
    yout = dram("yout", [NT, D], kind=EO)
    o_p_ssd = dram("o_p_ssd", [DEPTH, 8, 128, 64], kind=EO)
    o_p_ssdc = dram("o_p_ssdc", [DEPTH, 3, 1024], kind=EO)
    o_p_k = dram("o_p_k", [DEPTH, 128, 128], kind=EO)
    o_p_v = dram("o_p_v", [DEPTH, 128, 128], kind=EO)
    o_p_ret = dram("o_p_ret", [DEPTH, 4, 64, 128], kind=EO)
    o_p_gdn = dram("o_p_gdn", [DEPTH, 4, 128, 128], kind=EO)
    o_p_gdnc = dram("o_p_gdnc", [DEPTH, 3, 1536], kind=EO)
    o_s_ssd = dram("o_s_ssd", [DEPTH, NSS, 8, 128, 64], kind=EO)
    o_s_ssdc = dram("o_s_ssdc", [DEPTH, NSS, 3, 1024], kind=EO)
    o_s_k = dram("o_s_k", [DEPTH, NSS, 128, 128], kind=EO)
    o_s_v = dram("o_s_v", [DEPTH, NSS, 128, 128], kind=EO)
    o_s_ret = dram("o_s_ret", [DEPTH, NSS, 4, 64, 128], kind=EO)
    o_s_gdn = dram("o_s_gdn", [DEPTH, NSS, 4, 128, 128], kind=EO)
    o_s_gdnc = dram("o_s_gdnc", [DEPTH, NSS, 3, 1536], kind=EO)
    xscr = dram("xscr", [NT, D], kind="Internal")
    wbf_in = dram("wbf_in", [DEPTH, D, IN_W], BF16, kind="Internal")
    wbf_out = dram("wbf_out", [DEPTH, D, D], BF16, kind="Internal")
    ydbg = dram("ydbg", [NT, D], BF16, kind=EO) if dbg else None

    def sb(name, shape, dt=F32):
        return Buf(es.enter_context(nc.sbuf_tensor(name, list(shape), dt)), name)

    def ps(name, shape, dt=F32):
        return Buf(es.enter_context(nc.psum_tensor(name, list(shape), dt)), name, excl=True)

    DRAMBUF = {}

    def dbuf(ap_name):
        if ap_name not in DRAMBUF:
            DRAMBUF[ap_name] = Buf(None, ap_name)
        return DRAMBUF[ap_name]

    def dma(out, in_, r=(), w=(), eng="pool", **kw):
        P.op(eng, lambda: P.E[eng].dma_start(out=out, in_=in_, **kw), r=r, w=w, dma=True)

    def V(fn, r=(), w=()):
        P.op("dve", fn, r=r, w=w)

    def A(fn, r=(), w=()):
        P.op("act", fn, r=r, w=w)

    def G(fn, r=(), w=()):
        P.op("pool", fn, r=r, w=w)

    def T(fn, r=(), w=()):
        P.op("pe", fn, r=r, w=w)

    def mm(out, lhsT, rhs, start, stop, r, w):
        T(lambda: nc.tensor.matmul(out, lhsT=lhsT, rhs=rhs, start=start, stop=stop), r=r, w=w)

    def tr(out, in_, ident, r, w):
        T(lambda: nc.tensor.transpose(out, in_, ident), r=r, w=w)

    def act(out, in_, func, r, w, bias=None, scale=None, accum_out=None):
        kw = {}
        if bias is not None:
            kw["bias"] = bias
        if scale is not None:
            kw["scale"] = scale
        if accum_out is not None:
            kw["accum_out"] = accum_out
        A(lambda: nc.scalar.activation(out=out, in_=in_, func=func, **kw), r=r, w=w)

    def tt(out, in0, in1, op, r, w, eng="dve"):
        e = nc.vector if eng == "dve" else nc.gpsimd
        P.op(eng, lambda: e.tensor_tensor(out=out, in0=in0, in1=in1, op=op), r=r, w=w)

    def ts(out, in0, s1, op0, r, w, s2=None, op1=None, eng="dve"):
        e = nc.vector if eng == "dve" else nc.gpsimd
        if op1 is None:
            P.op(eng, lambda: e.tensor_scalar(out=out, in0=in0, scalar1=s1, scalar2=None, op0=op0), r=r, w=w)
        else:
            P.op(eng, lambda: e.tensor_scalar(out=out, in0=in0, scalar1=s1, scalar2=s2, op0=op0, op1=op1), r=r, w=w)

    def stt(out, in0, scalar, in1, op0, op1, r, w):
        V(lambda: nc.vector.scalar_tensor_tensor(out=out, in0=in0, scalar=scalar, in1=in1, op0=op0, op1=op1), r=r, w=w)

    def cp(out, in_, r, w, eng="dve"):
        if eng == "act":
            A(lambda: nc.scalar.activation(out=out, in_=in_, func=AF.Copy), r=r, w=w)
        else:
            e = nc.vector if eng == "dve" else nc.gpsimd
            P.op(eng, lambda: e.tensor_copy(out=out, in_=in_), r=r, w=w)

    def memset(buf, ap, val, eng="pool"):
        e = nc.vector if eng == "dve" else nc.gpsimd
        P.op(eng, lambda: e.memset(ap, val), w=[buf])

    def bc(ap, shape):
        return ap.to_broadcast(list(shape))

    ident_f = sb("ident_f", [128, 128])
    ident_b = sb("ident_b", [128, 128], BF16)
    Umat = sb("Umat", [128, 128])
    NEGU = sb("NEGU", [128, 128])
    SUm = sb("SUm", [128, 128])
    NEGown = sb("NEGown", [128, 128], BF16)
    NEGprev = sb("NEGprev", [128, 128], BF16)
    NEGmeta = sb("NEGmeta", [128, 128], BF16)
    ones_b = sb("ones_b", [128, 128], BF16)
    f1 = sb("f1", [128, 512])
    f2 = sb("f2", [128, 512])
    C_f1 = sb("C_f1", [128, 512])
    zeros_f = View(f1, f1[:, 0:128])
    ones_f = View(f1, f1[:, 128:256])
    tmpc = View(f1, f1[:, 256:384])
    tmpc2 = View(f1, f1[:, 384:512])
    dmat = View(f2, f2[:, 0:128])
    iota_fr = View(f2, f2[:, 128:256])
    iota_q = sb("iota_q", [128, 1])

    memset(zeros_f, zeros_f[:], 0.0)
    memset(ones_f, ones_f[:], 1.0)
    memset(ones_b, ones_b[:], 1.0)
    G(lambda: nc.gpsimd.iota(iota_q[:], pattern=[[0, 1]], base=0, channel_multiplier=1,
                             allow_small_or_imprecise_dtypes=True), w=[iota_q])
    G(lambda: nc.gpsimd.iota(iota_fr[:], pattern=[[1, 128]], base=0, channel_multiplier=0,
                             allow_small_or_imprecise_dtypes=True), w=[iota_fr])

    def aff(dst, src, fill, base, cm, step, op):
        G(lambda: nc.gpsimd.affine_select(out=dst[:], in_=src[:], pattern=[[step, 128]], compare_op=op,
                                          fill=fill, base=base, channel_multiplier=cm), r=[src], w=[dst])

    aff(ident_f, ones_f, 0.0, 0, 1, -1, ALU.is_equal)
    cp(ident_b[:], ident_f[:], r=[ident_f], w=[ident_b], eng="pool")
    aff(Umat, ones_f, 0.0, 0, -1, 1, ALU.is_ge)
    aff(NEGU, zeros_f, NEG, 0, -1, 1, ALU.is_ge)
    aff(SUm, ones_f, 0.0, 0, -1, 1, ALU.is_gt)
    cp(NEGown[:], NEGU[:], r=[NEGU], w=[NEGown], eng="pool")
    aff(tmpc, zeros_f, NEG, 0, 1, -1, ALU.is_ge)
    cp(NEGprev[:], tmpc[:], r=[tmpc], w=[NEGprev], eng="pool")
    aff(tmpc2, zeros_f, NEG, 112, 1, -1, ALU.is_ge)
    cp(NEGmeta[:], tmpc2[:], r=[tmpc2], w=[NEGmeta], eng="pool")

    lg = [math.log1p(-2.0 ** (-5 - h)) for h in range(4)]
    RDEC = sb("RDEC", [128, 4, 128])
    GPOW = sb("GPOW", [128, 4])
    RW = {L: sb("RW%d" % L, [128, 4]) for L in (LS, NMETA, 128)}
    GL = {L: sb("GL%d" % L, [64, 4]) for L in (LS, NMETA, 128)}
    ts(dmat[:], iota_fr[:], iota_q[:, 0:1], ALU.subtract, r=[iota_fr, iota_q], w=[dmat])
    ip1 = sb("ip1", [128, 1])
    ts(ip1[:], iota_q[:], 1.0, ALU.add, r=[iota_q], w=[ip1])
    for h in range(4):
        t0 = View(C_f1, C_f1[:, h * 128:(h + 1) * 128])
        ts(t0[:], dmat[:], lg[h], ALU.mult, r=[dmat], w=[t0], s2=NEGU[:, 0:1] if False else None)
        tt(t0[:], t0[:], NEGU[:], ALU.add, r=[t0, NEGU], w=[t0])
        act(RDEC[:, h, :], t0[:], AF.Exp, r=[t0], w=[RDEC])
        act(GPOW[:, h:h + 1], ip1[:], AF.Exp, r=[ip1], w=[GPOW], scale=lg[h])
        for L in RW:
            tq = sb("rwt%d_%d" % (h, L), [128, 1])
            ts(tq[:], iota_q[:], -1.0, ALU.mult, r=[iota_q], w=[tq], s2=float(L - 1), op1=ALU.add)
            act(RW[L][:, h:h + 1], tq[:], AF.Exp, r=[tq], w=[RW[L]], scale=lg[h])
            memset(GL[L], GL[L][:, h:h + 1], math.exp(lg[h] * L), eng="dve")

    slopes = [2.0 ** (-(h + 1)) for h in range(8)]
    SLQ = sb("SLQ", [128, 8])
    for h in range(8):
        ts(SLQ[:, h:h + 1], iota_q[:], slopes[h], ALU.mult, r=[iota_q], w=[SLQ])


    pchunks, schunks = make_chunks()
    blocks = [[pchunks[0]] + schunks]
    for b0 in range(1, 17, cpb):
        blocks.append(pchunks[b0:b0 + cpb])
    if n_blocks is not None:
        blocks = blocks[:n_blocks]
    BT = 128 * cpb
    NSLOT = max(5, cpb)
    WG = 256

    xt = [sb("xt%d" % i, [128, D]) for i in range(2)]
    hb = sb("hb", [128, D], BF16)
    hT = sb("hT", [128, KC, BT], BF16)
    wb = [sb("wb%d" % i, [128, KC, WG], BF16) for i in range(2)]
    SLOTA = sb("SLOTA", [128, 4096], BF16)
    SLOTB = sb("SLOTB", [128, 4096], BF16)
    wb.append(View(SLOTA, SLOTA[:, :].rearrange("p (k n) -> p k n", k=KC, n=WG)))
    wb.append(View(SLOTB, SLOTB[:, :].rearrange("p (k n) -> p k n", k=KC, n=WG)))
    assert NSLOT == 5 and WG == 256
    Gs = [sb("Gs%d" % i, [128, 4, 512], BF16) for i in range(2)]
    Gs.append(View(SLOTA, SLOTA[:, 0:2048].rearrange("p (a b) -> p a b", a=4, b=512)))
    Gs.append(View(SLOTA, SLOTA[:, 2048:4096].rearrange("p (a b) -> p a b", a=4, b=512)))
    Gs.append(View(SLOTB, SLOTB[:, 0:2048].rearrange("p (a b) -> p a b", a=4, b=512)))
    SM = [sb("SM%d" % i, [128, 16]) for i in range(NSLOT)]
    KV = [sb("KV%d" % i, [128, 256]) for i in range(2)]
    KV.append(View(SLOTB, SLOTB[:, 3584:4096].bitcast(F32)))
    KV += [sb("KV%d" % i, [128, 256]) for i in range(3, NSLOT)]
    VAUG = [sb("VAUG%d" % i, [128, 2, 72], BF16) for i in range(NSLOT)]
    VR = [sb("VR%d" % i, [128, 4, 128], BF16) for i in range(2)]
    for i in range(3):
        VR.append(View(SLOTB, SLOTB[:, 2048 + 512 * i:2560 + 512 * i].rearrange("p (a b) -> p a b", a=4, b=128)))
    KR = [sb("KR%d" % i, [128, 4, 64], BF16) for i in range(NSLOT)]
    XBC = sb("XBC", [128, 8, BT], BF16)
    XBCt = [Buf(XBC.t[:, ft_, :], "XBC%d" % ft_) for ft_ in range(8)]
    QA = sb("QA", [65, 8, BT], BF16)
    KA = sb("KA", [65, 2, BT], BF16)
    QR = sb("QR", [64, 4, BT], BF16)
    KRF = sb("KRF", [64, 4, BT], BF16)
    QKVD = sb("QKVD", [128, 12, BT], BF16)
    QKVDt = [Buf(QKVD.t[:, ft_, :], "QKVD%d" % ft_) for ft_ in range(12)]
    ST = [sb("ST%d" % i, [128, BT + 4]) for i in range(2)]
    ACC = [sb("ACC%d" % i, [128, BT]) for i in range(2)]
    carS = sb("carS", [128, 8, 3])
    carG = sb("carG", [128, 12, 3])
    cstSs = [sb("cstS%d" % i, [128, 8, NSS, 3]) for i in range(2)]
    cstGs = [sb("cstG%d" % i, [128, 12, NSS, 3]) for i in range(2)]
    csoS = sb("csoS", [128, 8, NSS, 3])
    csoG = sb("csoG", [128, 12, NSS, 3])
    Yb = [sb("Y%d" % i, [128, D], BF16) for i in range(2)]
    ssq = sb("ssq", [128, 8])
    lnv = sb("lnv", [128, 8])
    rstd = sb("rstd", [128, 8])
    smalls = {}
    for _n in ("E0_ssq", "E1_ssq", "E_tot", "E_lnv", "E_rstd", "F0_ssq", "F1_ssq", "F_tot", "F_lnv", "F0_rstd", "F1_rstd"):
        smalls[_n] = sb(_n, [128, 8])
    for _p in "ACD":
        for _n in ("ssq", "lnv", "rstd", "negcum", "ecum", "ecl", "dtr", "dtv", "lav"):
            smalls[_p + "_" + _n] = sb(_p + "_" + _n, [128, 8])
    sqj = sb("sqj", [128, 512], BF16)
    gTs = [sb("gT%d" % i, [128, KC]) for i in range(2)]
    postg = sb("postg", [128, D])
    ssdn = sb("ssdn", [128, 512])
    retn = sb("retn", [128, 512])
    gdnn = sb("gdnn", [128, 4, 128])
    cwSs = [sb("cwS%d" % i, [128, 8, 4]) for i in range(2)]
    cbSs = [sb("cbS%d" % i, [128, 8]) for i in range(2)]
    cwGs = [sb("cwG%d" % i, [128, 12, 4]) for i in range(2)]
    dtbS = sb("dtbS", [128, 8])
    AnS = sb("AnS", [128, 8])
    Dss = sb("Dss", [128, 8])
    ESQ = sb("ESQ", [128, 8])
    dtbG = sb("dtbG", [128, 4])
    AnG = sb("AnG", [128, 4])
    Sssd = [sb("Sssd%d" % i, [128, 8, 64]) for i in range(2)]
    Sssdb = [sb("Sssdb%d" % i, [128, 8, 64], BF16) for i in range(2)]
    Sret = [sb("Sret%d" % i, [64, 4, 128]) for i in range(2)]
    Sretb = [sb("Sretb%d" % i, [64, 4, 128], BF16) for i in range(2)]
    Sgdn = [sb("Sgdn%d" % i, [128, 4, 128]) for i in range(2)]
    Sgdnb = [sb("Sgdnb%d" % i, [128, 4, 128], BF16) for i in range(2)]
    PKA = sb("PKA", [65, 2, 128], BF16)
    PVA = sb("PVA", [128, 2, 72], BF16)
    PKAm = sb("PKAm", [65, 2, NMETA], BF16)
    PVAm = sb("PVAm", [128, 2, 72], BF16)
    cKb = sb("cKb", [128, 128], BF16)
    junkA = sb("junkA", [128, 512], BF16)
    junkC = sb("junkC", [128, 512], BF16)
    junkD = sb("junkD", [128, 512], BF16)
    C_ng = sb("C_ng", [128, 512])
    D_ng = sb("D_ng", [128, 512])
    decT = sb("decT", [128, 8, 128])
    negcum = sb("negcum", [128, 8])
    ecum = sb("ecum", [128, 8])
    ecl = sb("ecl", [128, 8])
    dtr = sb("dtr", [128, 8])
    dtv = sb("dtv", [128, 8])
    lav = sb("lav", [128, 8])
    MT = sb("MT", [128, 8, 128], BF16)
    xs_tm = sb("xs_tm", [128, 512], BF16)
    xdt = sb("xdt", [128, 512], BF16)
    xdtw = sb("xdtw", [128, 512], BF16)
    Btm = sb("Btm", [128, 256], BF16)
    D_f1 = sb("D_f1", [128, 512])
    D_f3 = sb("D_f3", [128, 512])
    B_f2 = sb("B_f2", [128, 512])
    cK = View(B_f2, B_f2[:, 0:256])
    C_MT = sb("C_MT", [128, 4, 128], BF16)
    D_decT = sb("D_decT", [128, 4, 128])
    kw = sb("kw", [128, 4, 64], BF16)
    beta = sb("beta", [128, 4])
    nbeta = sb("nbeta", [128, 4])
    Pm = [sb("Pm%d" % i, [128, 4, 128]) for i in range(2)]
    PTm = [sb("PTm%d" % i, [128, 4, 128]) for i in range(2)]
    Rm = [sb("Rm%d" % i, [128, 4, 128]) for i in range(2)]
    QKd = sb("QKd", [128, 4, 128], BF16)
    Vtm = sb("Vtm", [128, 512], BF16)
    knw = sb("knw", [128, 512], BF16)
    vnew = sb("vnew", [128, 512], BF16)
    PTs = [sb("PTs%d" % i, [128, 4, 128], BF16) for i in range(4)]
    den = sb("den", [128, 8])

    banks = [ps("bank%d" % i, [128, 512]) for i in range(8)]
    bctr = [0]

    def nb():
        b = banks[bctr[0] % 8]
        bctr[0] += 1
        return b

    def bfv(bk):
        return bk.t[:].bitcast(BF16)

    def v3(ap, a, b):
        return ap.rearrange("p (a b) -> p a b", a=a, b=b)

    for h in range(8):
        memset(QA, QA[64:65, h, :], 8.0 * slopes[h])
    G(lambda: nc.gpsimd.iota(PKA[64:65, :, :], pattern=[[0, 2], [1, 128]], base=-128, channel_multiplier=0,
                             allow_small_or_imprecise_dtypes=True), w=[PKA])
    G(lambda: nc.gpsimd.iota(PKAm[64:65, :, :], pattern=[[0, 2], [1, NMETA]], base=-NMETA, channel_multiplier=0,
                             allow_small_or_imprecise_dtypes=True), w=[PKAm])
    for i in range(NSLOT):
        memset(VAUG[i], VAUG[i][:, :, 64:65], 1.0)
    memset(PVA, PVA[:, :, 64:65], 1.0)
    memset(PVAm, PVAm[:, :, 64:65], 1.0)

    xbufs = {}

    def xbuf(bi):
        if bi not in xbufs:
            xbufs[bi] = Buf(None, "x%d" % bi)
        return xbufs[bi]

    wview_in = [wbf_in[l].rearrange("(kc p) n -> p kc n", p=128) for l in range(DEPTH)]
    wview_out = [wbf_out[l].rearrange("(kc p) n -> p kc n", p=128) for l in range(DEPTH)]
    wscr_bufs = {l: [Buf(None, "wscr%d_%d" % (l, i)) for i in range(5)] for l in range(DEPTH)}
    wctr = [0]
    nwb = [2]
    cur_layer = [0]

    def convert_weights(l, piece=None):
        for i in range(4):
            if piece is None or piece == i:
                dma(wbf_in[l, i * 512:(i + 1) * 512, :], w_in[l, i * 512:(i + 1) * 512, :], w=[wscr_bufs[l][i]],
                    eng="pool")
        if piece is None or piece == 4:
            dma(wbf_out[l], w_out[l], w=[wscr_bufs[l][4]], eng="pool")

    def load_w(view, c0, width, off=0, buf=None):
        if buf is None:
            buf = wb[wctr[0] % nwb[0]]
            wctr[0] += 1
        dma(buf[:, :, off:off + width], view[:, :, c0:c0 + width], r=wscr_bufs[cur_layer[0]], w=[buf], eng="sp")
        return buf

    def rms_stats(src_ap, L, n, scale, col=0):
        pass

    def chk(tag):
        if stop == tag:
            raise StopBuild()

    XA = [[f1, f2, C_f1, D_f1], [D_f3, B_f2, C_ng, D_ng]]
    a1_done = set()

    def tiles_of(bi2):
        if bi2 == 0:
            return [dict(row0=0, L=NMETA, tok0=0), dict(row0=NPT, L=NSS * LS, tok0=NMETA)]
        return [dict(row0=ch_["row0"], L=128, tok0=128 * i_) for i_, ch_ in enumerate(blocks[bi2])]

    def stage_a1(l2, bi2):
        if (l2, bi2) in a1_done:
            return
        a1_done.add((l2, bi2))
        xs2 = xin if l2 == 0 else xscr
        F_tot, F_lnv = smalls["F_tot"], smalls["F_lnv"]
        for ti, tl in enumerate(tiles_of(bi2)):
            L, r0 = tl["L"], tl["row0"]
            xa = XA[ti % 2]
            F_ssq, F_rstd = smalls["F%d_ssq" % (ti % 2)], smalls["F%d_rstd" % (ti % 2)]
            for c in range(4):
                dma(xa[c][0:L, :], xs2[r0:r0 + L, c * 512:(c + 1) * 512], r=[xbuf(bi2)], w=[xa[c]])
                act(junkC[0:L, :], xa[c][0:L, :], AF.Square, r=[xa[c]], w=[junkC, F_ssq], accum_out=F_ssq[0:L, c:c + 1])
            V(lambda L=L, F_ssq=F_ssq: nc.vector.tensor_reduce(out=F_tot[0:L, 0:1], in_=F_ssq[0:L, 0:4], axis=AX.X,
                                                               op=ALU.add), r=[F_ssq], w=[F_tot])
            act(F_lnv[0:L, 0:1], F_tot[0:L, 0:1], AF.Ln, r=[F_tot], w=[F_lnv], scale=1.0 / D, bias=EPS)
            act(F_rstd[0:L, 0:1], F_lnv[0:L, 0:1], AF.Exp, r=[F_lnv], w=[F_rstd], scale=-0.5)

    pending_epi = []

    def flush_epi():
        while pending_epi:
            pending_epi.pop(0)()

    parts_of = {}

    def multi_load(parent, pairs):
        parts = []
        for i, (o_, i_) in enumerate(pairs):
            pb = parent if i == 0 else Buf(None, "part")
            dma(o_, i_, w=[pb], eng="sp", allow_slow_non_contiguous=True)
            if i > 0:
                parts.append(pb)
        parts_of[id(parent)] = parts

    def RD(parent):
        return [parent] + parts_of.get(id(parent), [])

    def store_fm_rows(src_fn, srcbuf, nft, rows, dst2d, stg):
        for f0 in range(0, nft, 4):
            bk = nb()
            for j in range(4):
                tr(bk[0:rows, j * 128:(j + 1) * 128], src_fn(f0 + j), ident_f[:, :], r=[srcbuf, ident_f], w=[bk])
            cp(stg[0:rows, f0 * 128:(f0 + 4) * 128], bk[0:rows, 0:512], r=[bk], w=[stg])
        dma(dst2d, stg[0:rows, 0:nft * 128], r=[stg], w=[Buf(None, "fmrows")])

    def load_slow_params(l):
        p = l % 2
        multi_load(gTs[p], [(gTs[p][:], pre_norm[l].rearrange("(kc p) -> p kc", p=128))])
        multi_load(cwSs[p], [(cwSs[p][:, :, k_], ssd_conv_w[l, k_].rearrange("(ft p) -> p ft", p=128)) for k_ in range(4)])
        multi_load(cwGs[p], [(cwGs[p][:, :, k_], gdn_conv_w[l, k_].rearrange("(ft p) -> p ft", p=128)) for k_ in range(4)])
        multi_load(cbSs[p], [(cbSs[p][:], ssd_conv_b[l].rearrange("(ft p) -> p ft", p=128))])
        multi_load(cstSs[p], [(cstSs[p][:, :, s_, t_], st_ssdc[l, s_, t_].rearrange("(ft p) -> p ft", p=128))
                              for s_ in range(NSS) for t_ in range(3)])
        multi_load(cstGs[p], [(cstGs[p][:, :, s_, t_], st_gdnc[l, s_, t_].rearrange("(ft p) -> p ft", p=128))
                              for s_ in range(NSS) for t_ in range(3)])

    try:
        for l in range(n_layers):
            chk('const')
            cur_layer[0] = l
            if l == 0:
                convert_weights(0)
            xsrc = xin if l == 0 else xscr
            xdst = yout if l == n_layers - 1 else xscr
            if l == 0:
                load_slow_params(0)
            gT, cwS, cbS, cwG, cstS, cstG = [b[l % 2] for b in (gTs, cwSs, cbSs, cwGs, cstSs, cstGs)]
            dma(postg[:], bc(post_norm[l:l + 1, :], [128, D]), w=[postg])
            dma(ssdn[:], bc(ssd_norm[l:l + 1, :], [128, 512]), w=[ssdn])
            dma(retn[:], bc(ret_norm[l:l + 1, :], [128, 512]), w=[retn])
            for h in range(4):
                dma(gdnn[:, h, :], bc(gdn_norm[l:l + 1, :], [128, 128]), w=[gdnn])
            dma(dtbS[:], bc(ssd_dt_bias[l:l + 1, :], [128, 8]), w=[dtbS])
            dma(AnS[:], bc(ssd_a_log[l:l + 1, :], [128, 8]), w=[AnS])
            dma(Dss[:], bc(ssd_d[l:l + 1, :], [128, 8]), w=[Dss])
            dma(ESQ[:], bc(swa_sinks[l:l + 1, :], [128, 8]), w=[ESQ])
            dma(dtbG[:], bc(gdn_dt_bias[l:l + 1, :], [128, 4]), w=[dtbG])
            dma(AnG[:], bc(gdn_a_log[l:l + 1, :], [128, 4]), w=[AnG])
            act(AnS[:], AnS[:], AF.Exp, r=[AnS], w=[AnS])
            ts(AnS[:], AnS[:], -1.0, ALU.mult, r=[AnS], w=[AnS])
            act(AnG[:], AnG[:], AF.Exp, r=[AnG], w=[AnG])
            ts(AnG[:], AnG[:], -1.0, ALU.mult, r=[AnG], w=[AnG])
            tt(ESQ[:], ESQ[:], SLQ[:], ALU.add, r=[ESQ, SLQ], w=[ESQ])
            act(ESQ[:], ESQ[:], AF.Exp, r=[ESQ], w=[ESQ])
            memset(Sssd[0], Sssd[0][:], 0.0)
            memset(Sssdb[0], Sssdb[0][:], 0.0)
            memset(Sret[0], Sret[0][:], 0.0)
            memset(Sretb[0], Sretb[0][:], 0.0)
            memset(Sgdn[0], Sgdn[0][:], 0.0)
            memset(Sgdnb[0], Sgdnb[0][:], 0.0)
            memset(carS, carS[:], 0.0)
            memset(carG, carG[:], 0.0)

            for bi, blk in enumerate(blocks):
                is0 = (bi == 0)
                nwb[0] = 2 if is0 else 4
                tok = 0
                for si, ch in enumerate(blk):
                    ch["tok0"] = tok
                    ch["slot"] = si
                    tok += ch["L"]
                nbt = tok
                if is0:
                    tm_tiles = [dict(row0=0, L=NMETA, tok0=0), dict(row0=NPT, L=NSS * LS, tok0=NMETA)]
                else:
                    tm_tiles = [dict(row0=ch["row0"], L=128, tok0=ch["tok0"]) for ch in blk]
                if is0:
                    G(lambda: nc.gpsimd.iota(KA[64:65, :, 0:NMETA], pattern=[[0, 2], [1, NMETA]], base=0,
                                             channel_multiplier=0, allow_small_or_imprecise_dtypes=True), w=[KA])
                    G(lambda: nc.gpsimd.iota(KA[64:65, :, NMETA:NMETA + 16], pattern=[[0, 2], [0, NSS], [1, LS]], base=0,
                                             channel_multiplier=0, allow_small_or_imprecise_dtypes=True), w=[KA])
                elif bi == 1:
                    G(lambda: nc.gpsimd.iota(KA[64:65, :, :], pattern=[[0, 2], [0, cpb], [1, 128]], base=0,
                                             channel_multiplier=0, allow_small_or_imprecise_dtypes=True), w=[KA])

                chk('params')
                stage_a1(l, bi)
                for ti, tl in enumerate(tm_tiles):
                    L, r0, t0 = tl["L"], tl["row0"], tl["tok0"]
                    xa = XA[ti % 2]
                    F_rstd = smalls["F%d_rstd" % (ti % 2)]
                    for c in range(4):
                        ts(hb[0:L, c * 512:(c + 1) * 512], xa[c][0:L, :], F_rstd[0:L, 0:1], ALU.mult, r=[xa[c], F_rstd],
                           w=[hb])
                    for q in range(4):
                        bk = nb()
                        bv = bfv(bk)
                        for j in range(4):
                            kc = 4 * q + j
                            tr(bv[:, j * 128:j * 128 + L], hb[0:L, kc * 128:(kc + 1) * 128], ident_b[0:L, 0:L],
                               r=[hb, ident_b], w=[bk])
                        tt(hT[:, 4 * q:4 * q + 4, t0:t0 + L], v3(bv[:, 0:512], 4, 128)[:, :, 0:L],
                           bc(gT[:, 4 * q:4 * q + 4].unsqueeze(2), [128, 4, L]), ALU.mult, r=[bk] + RD(gT), w=[hT])
                flush_epi()

                def decay(la, L, H, decT_, negcum_, ecum_, ecl_):
                    bk = nb()
                    mm(bk[0:L, 0:H], Umat[0:L, 0:L], la[0:L, 0:H], True, True, r=[Umat, la], w=[bk])
                    ts(negcum_[0:L, 0:H], bk[0:L, 0:H], -1.0, ALU.mult, r=[bk], w=[negcum_])
                    act(ecum_[0:L, 0:H], bk[0:L, 0:H], AF.Exp, r=[bk], w=[ecum_])
                    yield
                    for hq in range(H // 4):
                        bk = nb()
                        for hh in range(4):
                            h = 4 * hq + hh
                            o = bk[:, hh * 128:hh * 128 + L]
                            mm(o, bc(la[0:L, h:h + 1], [L, 128]), Umat[0:L, 0:L], True, False, r=[la, Umat], w=[bk])
                            mm(o, ident_f[0:L, :], NEGU[0:L, 0:L], False, True, r=[ident_f, NEGU], w=[bk])
                        yield
                        for hh in range(4):
                            h = 4 * hq + hh
                            act(decT_[0:L, h, 0:L], bk[0:L, hh * 128:hh * 128 + L], AF.Exp, r=[bk, negcum_], w=[decT_],
                                bias=negcum_[0:L, h:h + 1])
                        act(ecl_[:, 4 * hq:4 * hq + 4], v3(bk[:, 0:512], 4, 128)[:, :, L - 1], AF.Exp, r=[bk], w=[ecl_])
                        yield

                def softplus_la(dst, src_ap, src_bufs, dtb, An, L, H, dtr_, keep_dt=None):
                    tt(dtr_[0:L, 0:H], src_ap, dtb[0:L, 0:H], ALU.add, r=src_bufs + [dtb], w=[dtr_])
                    act(dtr_[0:L, 0:H], dtr_[0:L, 0:H], AF.Exp, r=[dtr_], w=[dtr_])
                    tgt = keep_dt if keep_dt is not None else dtr_
                    act(tgt[0:L, 0:H], dtr_[0:L, 0:H], AF.Ln, r=[dtr_], w=[tgt], bias=1.0)
                    tt(dst[0:L, 0:H], tgt[0:L, 0:H], An[0:L, 0:H], ALU.mult, r=[tgt, An], w=[dst])

                def head_rmsnorm_gate(o_buf, junk_, ssq_, lnv_, rstd_, Yc, L, nh, hd, ng_, ycols):
                    n = nh * hd
                    for h in range(nh):
                        act(junk_[0:L, h * hd:(h + 1) * hd], o_buf[0:L, h * hd:(h + 1) * hd], AF.Square, r=[o_buf],
                            w=[junk_, ssq_], accum_out=ssq_[0:L, h:h + 1])
                    act(lnv_[0:L, 0:nh], ssq_[0:L, 0:nh], AF.Ln, r=[ssq_], w=[lnv_], scale=1.0 / hd, bias=EPS)
                    act(rstd_[0:L, 0:nh], lnv_[0:L, 0:nh], AF.Exp, r=[lnv_], w=[rstd_], scale=-0.5)
                    yield
                    tt(v3(o_buf[0:L, 0:n], nh, hd), v3(o_buf[0:L, 0:n], nh, hd), bc(rstd_[0:L, 0:nh].unsqueeze(2), [L, nh, hd]),
                       ALU.mult, r=[o_buf, rstd_], w=[o_buf])
                    tt(Yc[0:L, ycols:ycols + n], o_buf[0:L, 0:n], ng_[0:L, 0:n], ALU.mult, r=[o_buf, ng_], w=[Yc])

                def chunk_ctx(ch):
                    sid = 0 if ch["kind"] == "p" else 1
                    return ch["L"], ch["tok0"], ch["slot"], sid, ch.get("seq", None), Gs[ch["slot"]], Yb[ch["slot"] % 2]

                def ssd_thread():
                    A = lambda n: smalls["A_" + n]
                    ssq_, lnv_, rstd_, negcum_, ecum_, ecl_, dtr_, dtv_, lav_ = [A(n) for n in (
                        "ssq", "lnv", "rstd", "negcum", "ecum", "ecl", "dtr", "dtv", "lav")]
                    for ch in blk:
                        L, t0, slot, sid, seq, G_, Yc = chunk_ctx(ch)
                        while slot >= 2 and not fin.get(slot - 2):
                            yield
                        S1, S1b = Sssd[sid], Sssdb[sid]
                        if sid == 1:
                            dma(S1[:], st_ssd[l, seq].rearrange("h n e -> n h e"), w=[S1])
                            cp(S1b[:], S1[:], r=[S1], w=[S1b], eng="act")
                        softplus_la(lav_, SM[slot][0:L, 0:8], [SM[slot]], dtbS, AnS, L, 8, dtr_, keep_dt=dtv_)
                        yield
                        yield from decay(lav_, L, 8, decT, negcum_, ecum_, ecl_)
                        yield ("wait_proj",)
                        bk = nb()
                        bv = bfv(bk)
                        for ft in range(4):
                            tr(bv[0:L, ft * 128:(ft + 1) * 128], XBCt[ft][:, t0:t0 + L], ident_b[:, :], r=[XBCt[ft], ident_b], w=[bk])
                        yield
                        cp(xs_tm[0:L, :], bv[0:L, 0:512], r=[bk], w=[xs_tm], eng="act")
                        tt(v3(xdt[0:L, :], 8, 64), v3(bv[0:L, 0:512], 8, 64), bc(dtv_[0:L, 0:8].unsqueeze(2), [L, 8, 64]),
                           ALU.mult, r=[bk, dtv_], w=[xdt])
                        bk = nb()
                        bv = bfv(bk)
                        for g in range(2):
                            tr(bv[0:L, g * 128:(g + 1) * 128], XBCt[4 + g][:, t0:t0 + L], ident_b[:, :], r=[XBCt[4 + g], ident_b], w=[bk])
                        yield
                        cp(Btm[0:L, :], bv[0:L, 0:256], r=[bk], w=[Btm], eng="act")
                        bk = nb()
                        for g in range(2):
                            mm(bk[0:L, g * 128:g * 128 + L], XBCt[4 + g][:, t0:t0 + L], XBCt[6 + g][:, t0:t0 + L], True, True,
                               r=[XBCt[4 + g], XBCt[6 + g]], w=[bk])
                        yield
                        for g in range(2):
                            tt(MT[0:L, 4 * g:4 * g + 4, 0:L], bc(bk[0:L, g * 128:g * 128 + L].unsqueeze(1), [L, 4, L]),
                               decT[0:L, 4 * g:4 * g + 4, 0:L], ALU.mult, r=[bk, decT], w=[MT])
                        yield
                        bki = nb()
                        for h in range(8):
                            mm(bki[0:L, h * 64:(h + 1) * 64], MT[0:L, h, 0:L], xdt[0:L, h * 64:(h + 1) * 64], True, True,
                               r=[MT, xdt], w=[bki])
                        bks = nb()
                        for h in range(8):
                            mm(bks[0:L, h * 64:(h + 1) * 64], XBCt[6 + h // 4][:, t0:t0 + L], S1b[:, h, :], True, True,
                               r=[XBCt[6 + h // 4], S1b], w=[bks])
                        yield
                        tt(v3(f1[0:L, :], 8, 64), v3(bks[0:L, :], 8, 64), bc(ecum_[0:L, 0:8].unsqueeze(2), [L, 8, 64]), ALU.mult,
                           r=[bks, ecum_], w=[f1])
                        tt(f1[0:L, :], bki[0:L, :], f1[0:L, :], ALU.add, r=[bki, f1], w=[f1])
                        tt(v3(f2[0:L, :], 8, 64), v3(xs_tm[0:L, :], 8, 64), bc(Dss[0:L, 0:8].unsqueeze(2), [L, 8, 64]), ALU.mult,
                           r=[xs_tm, Dss], w=[f2], eng="pool")
                        yield
                        tt(f1[0:L, :], f1[0:L, :], f2[0:L, :], ALU.add, r=[f1, f2], w=[f1])
                        tt(f1[0:L, :], f1[0:L, :], G_[0:L, 0, :], ALU.mult, r=[f1, G_], w=[f1])
                        for g in range(2):
                            act(junkA[0:L, g * 256:(g + 1) * 256], f1[0:L, g * 256:(g + 1) * 256], AF.Square, r=[f1],
                                w=[junkA, ssq_], accum_out=ssq_[0:L, g:g + 1])
                        yield
                        act(lnv_[0:L, 0:2], ssq_[0:L, 0:2], AF.Ln, r=[ssq_], w=[lnv_], scale=1.0 / 256, bias=EPS)
                        act(rstd_[0:L, 0:2], lnv_[0:L, 0:2], AF.Exp, r=[lnv_], w=[rstd_], scale=-0.5)
                        yield
                        tt(v3(f2[0:L, :], 2, 256), v3(f1[0:L, :], 2, 256), bc(rstd_[0:L, 0:2].unsqueeze(2), [L, 2, 256]),
                           ALU.mult, r=[f1, rstd_], w=[f2])
                        tt(Yc[0:L, 0:512], f2[0:L, :], ssdn[0:L, :], ALU.mult, r=[f2, ssdn], w=[Yc])
                        yield
                        tt(v3(xdtw[0:L, :], 8, 64), v3(xdt[0:L, :], 8, 64), bc(decT[0:L, 0:8, L - 1:L], [L, 8, 64]), ALU.mult,
                           r=[xdt, decT], w=[xdtw])
                        bkn = nb()
                        for h in range(8):
                            mm(bkn[:, h * 64:(h + 1) * 64], Btm[0:L, (h // 4) * 128:(h // 4 + 1) * 128],
                               xdtw[0:L, h * 64:(h + 1) * 64], True, True, r=[Btm, xdtw], w=[bkn])
                        tt(S1[:], S1[:], bc(ecl_[:, 0:8].unsqueeze(2), [128, 8, 64]), ALU.mult, r=[S1, ecl_], w=[S1])
                        yield
                        tt(S1[:], v3(bkn[:, :], 8, 64), S1[:], ALU.add, r=[bkn, S1], w=[S1])
                        cp(S1b[:], S1[:], r=[S1], w=[S1b], eng="act")
                        if sid == 1:
                            dma(o_s_ssd[l, seq].rearrange("h n e -> n h e"), S1[:], r=[S1], w=[dbuf("o_s_ssd")])
                        yield ("done", ch["slot"])

                def ret_thread():
                    A = lambda n: smalls["C_" + n]
                    ssq_, lnv_, rstd_ = A("ssq"), A("lnv"), A("rstd")
                    for ch in blk:
                        L, t0, slot, sid, seq, G_, Yc = chunk_ctx(ch)
                        while slot >= 2 and not fin.get(slot - 2):
                            yield
                        S2, S2b = Sret[sid], Sretb[sid]
                        tt(C_ng[0:L, :], retn[0:L, :], G_[0:L, 2, :], ALU.mult, r=[retn, G_], w=[C_ng], eng="pool")
                        yield ("wait_proj",)
                        if sid == 1:
                            dma(S2[:], st_ret[l, seq].rearrange("h d e -> d h e"), w=[S2])
                            cp(S2b[:], S2[:], r=[S2], w=[S2b], eng="act")
                        bk = nb()
                        for h in range(4):
                            mm(bk[0:L, h * 128:h * 128 + L], KRF[0:64, h, t0:t0 + L], QR[0:64, h, t0:t0 + L], True, True,
                               r=[KRF, QR], w=[bk])
                        yield
                        tt(C_MT[0:L, 0:4, 0:L], v3(bk[0:L, :], 4, 128)[:, :, 0:L], RDEC[0:L, :, 0:L], ALU.mult, r=[bk, RDEC],
                           w=[C_MT])
                        yield
                        bki = nb()
                        for h in range(4):
                            mm(bki[0:L, h * 128:(h + 1) * 128], C_MT[0:L, h, 0:L], VR[slot][0:L, h, :], True, True,
                               r=[C_MT, VR[slot]], w=[bki])
                        bks = nb()
                        for h in range(4):
                            mm(bks[0:L, h * 128:(h + 1) * 128], QR[0:64, h, t0:t0 + L], S2b[:, h, :], True, True,
                               r=[QR, S2b], w=[bks])
                        yield
                        tt(v3(C_f1[0:L, :], 4, 128), v3(bks[0:L, :], 4, 128), bc(GPOW[0:L, 0:4].unsqueeze(2), [L, 4, 128]),
                           ALU.mult, r=[bks, GPOW], w=[C_f1])
                        tt(C_f1[0:L, :], bki[0:L, :], C_f1[0:L, :], ALU.add, r=[bki, C_f1], w=[C_f1])
                        yield
                        yield from head_rmsnorm_gate(C_f1, junkC, ssq_, lnv_, rstd_, Yc, L, 4, 128, C_ng, 1024)
                        yield
                        tt(kw[0:L, :, :], KR[slot][0:L, :, :], bc(RW[L][0:L, 0:4].unsqueeze(2), [L, 4, 64]), ALU.mult,
                           r=[KR[slot], RW[L]], w=[kw], eng="pool")
                        bkn = nb()
                        for h in range(4):
                            mm(bkn[0:64, h * 128:(h + 1) * 128], kw[0:L, h, :], VR[slot][0:L, h, :], True, True,
                               r=[kw, VR[slot]], w=[bkn])
                        tt(S2[:], S2[:], bc(GL[L][:, 0:4].unsqueeze(2), [64, 4, 128]), ALU.mult, r=[S2, GL[L]], w=[S2])
                        yield
                        tt(S2[:], v3(bkn[0:64, :], 4, 128), S2[:], ALU.add, r=[bkn, S2], w=[S2])
                        cp(S2b[:], S2[:], r=[S2], w=[S2b], eng="act")
                        if sid == 1:
                            dma(o_s_ret[l, seq].rearrange("h d e -> d h e"), S2[:], r=[S2], w=[dbuf("o_s_ret")])
                        yield ("done", ch["slot"])

                def gdn_thread():
                    A = lambda n: smalls["D_" + n]
                    ssq_, lnv_, rstd_, negcum_, ecum_, ecl_, dtr_, lav_ = [A(n) for n in (
                        "ssq", "lnv", "rstd", "negcum", "ecum", "ecl", "dtr", "lav")]
                    M1 = Pm[1]
                    for ch in blk:
                        L, t0, slot, sid, seq, G_, Yc = chunk_ctx(ch)
                        while slot >= 2 and not fin.get(slot - 2):
                            yield
                        S3, S3b = Sgdn[sid], Sgdnb[sid]
                        tt(D_ng[0:L, :], gdnn[0:L, :, :].rearrange("p a b -> p (a b)"), G_[0:L, 3, :], ALU.mult, r=[gdnn, G_],
                           w=[D_ng], eng="pool")
                        if sid == 1:
                            dma(S3[:], st_gdn[l, seq].rearrange("h d e -> d h e"), w=[S3])
                            cp(S3b[:], S3[:], r=[S3], w=[S3b], eng="act")
                        act(beta[0:L, :], SM[slot][0:L, 8:12], AF.Exp, r=[SM[slot]], w=[beta], scale=-1.0)
                        ts(beta[0:L, :], beta[0:L, :], 1.0, ALU.add, r=[beta], w=[beta])
                        V(lambda L=L: nc.vector.reciprocal(out=beta[0:L, :], in_=beta[0:L, :]), r=[beta], w=[beta])
                        ts(nbeta[0:L, :], beta[0:L, :], -1.0, ALU.mult, r=[beta], w=[nbeta])
                        yield
                        softplus_la(lav_, SM[slot][0:L, 12:16], [SM[slot]], dtbG, AnG, L, 4, dtr_)
                        yield
                        yield from decay(lav_, L, 4, D_decT, negcum_, ecum_, ecl_)
                        yield ("wait_proj",)
                        bk = nb()
                        bv = bfv(bk)
                        for h in range(4):
                            tr(bv[0:L, h * 128:(h + 1) * 128], QKVDt[4 + h][:, t0:t0 + L], ident_b[:, :], r=[QKVDt[4 + h], ident_b],
                               w=[bk])
                        yield
                        tt(v3(knw[0:L, :], 4, 128), v3(bv[0:L, 0:512], 4, 128), bc(D_decT[0:L, 0:4, L - 1:L], [L, 4, 128]),
                           ALU.mult, r=[bk, D_decT], w=[knw])
                        bk = nb()
                        bv = bfv(bk)
                        for h in range(4):
                            tr(bv[0:L, h * 128:(h + 1) * 128], QKVDt[8 + h][:, t0:t0 + L], ident_b[:, :], r=[QKVDt[8 + h], ident_b],
                               w=[bk])
                        yield
                        cp(Vtm[0:L, :], bv[0:L, 0:512], r=[bk], w=[Vtm], eng="act")
                        bkg = nb()
                        bkq = nb()
                        for h in range(4):
                            mm(bkg[0:L, h * 128:h * 128 + L], QKVDt[4 + h][:, t0:t0 + L], QKVDt[4 + h][:, t0:t0 + L], True, True,
                               r=[QKVDt[4 + h]], w=[bkg])
                        for h in range(4):
                            mm(bkq[0:L, h * 128:h * 128 + L], QKVDt[4 + h][:, t0:t0 + L], QKVDt[h][:, t0:t0 + L], True, True,
                               r=[QKVDt[4 + h], QKVDt[h]], w=[bkq])
                        yield
                        tt(M1[0:L, :, 0:L], v3(bkg[0:L, :], 4, 128)[:, :, 0:L], D_decT[0:L, 0:4, 0:L], ALU.mult,
                           r=[bkg, D_decT], w=[M1])
                        tt(QKd[0:L, :, 0:L], v3(bkq[0:L, :], 4, 128)[:, :, 0:L], D_decT[0:L, 0:4, 0:L], ALU.mult,
                           r=[bkq, D_decT], w=[QKd])
                        yield
                        tt(M1[0:L, :, 0:L], M1[0:L, :, 0:L], bc(nbeta[0:L, 0:4].unsqueeze(2), [L, 4, L]), ALU.mult,
                           r=[M1, nbeta], w=[M1])
                        P0_, PT0_ = Pm[0], PTm[0]
                        tt(P0_[0:L, :, 0:L], M1[0:L, :, 0:L], bc(SUm[0:L, 0:L].unsqueeze(1), [L, 4, L]), ALU.mult,
                           r=[M1, SUm], w=[P0_])
                        yield
                        bk = nb()
                        for h in range(4):
                            tr(bk[0:L, h * 128:h * 128 + L], P0_[0:L, h, 0:L], ident_f[0:L, 0:L], r=[P0_, ident_f], w=[bk])
                        yield
                        cp(PT0_[0:L, :, 0:L], v3(bk[0:L, :], 4, 128)[:, :, 0:L], r=[bk], w=[PT0_], eng="act")
                        R_ = Rm[0]
                        tt(R_[0:L, :, 0:L], P0_[0:L, :, 0:L], bc(ident_f[0:L, 0:L].unsqueeze(1), [L, 4, L]), ALU.add,
                           r=[P0_, ident_f], w=[R_], eng="pool")
                        yield
                        nlev = max(1, int(math.ceil(math.log2(L))))
                        cur = 0
                        for k in range(1, nlev):
                            Pc, PTc = Pm[cur], PTm[cur]
                            Pn, PTn = Pm[1 - cur], PTm[1 - cur]
                            last = (k == nlev - 1)
                            bkt = nb()
                            for h in range(4):
                                mm(bkt[0:L, h * 128:h * 128 + L], Pc[0:L, h, 0:L], PTc[0:L, h, 0:L], True, True,
                                   r=[Pc, PTc], w=[bkt])
                            if not last:
                                bkp = nb()
                                for h in range(4):
                                    mm(bkp[0:L, h * 128:h * 128 + L], PTc[0:L, h, 0:L], Pc[0:L, h, 0:L], True, True,
                                       r=[Pc, PTc], w=[bkp])
                            yield
                            cp(PTn[0:L, :, 0:L], v3(bkt[0:L, :], 4, 128)[:, :, 0:L], r=[bkt], w=[PTn], eng="act")
                            if not last:
                                cp(Pn[0:L, :, 0:L], v3(bkp[0:L, :], 4, 128)[:, :, 0:L], r=[bkp], w=[Pn])
                            yield
                            Rc, Rn = Rm[cur], Rm[1 - cur]
                            bkr = nb()
                            for h in range(4):
                                mm(bkr[0:L, h * 128:h * 128 + L], PTn[0:L, h, 0:L], Rc[0:L, h, 0:L], True, True,
                                   r=[PTn, Rc], w=[bkr])
                            yield
                            tt(Rn[0:L, :, 0:L], v3(bkr[0:L, :], 4, 128)[:, :, 0:L], Rc[0:L, :, 0:L], ALU.add, r=[bkr, Rc],
                               w=[Rn])
                            yield
                            cur = 1 - cur
                        Rf = Rm[cur]
                        bk = nb()
                        for h in range(4):
                            mm(bk[0:L, h * 128:(h + 1) * 128], QKVDt[4 + h][:, t0:t0 + L], S3b[:, h, :], True, True,
                               r=[QKVDt[4 + h], S3b], w=[bk])
                        yield
                        tt(v3(D_f1[0:L, :], 4, 128), v3(bk[0:L, :], 4, 128), bc(ecum_[0:L, 0:4].unsqueeze(2), [L, 4, 128]),
                           ALU.mult, r=[bk, ecum_], w=[D_f1])
                        tt(D_f3[0:L, :], Vtm[0:L, :], D_f1[0:L, :], ALU.subtract, r=[Vtm, D_f1], w=[D_f3])
                        yield
                        bk = nb()
                        for h in range(4):
                            mm(bk[0:L, h * 128:(h + 1) * 128], Rf[0:L, h, 0:L], D_f3[0:L, h * 128:(h + 1) * 128], True, True,
                               r=[Rf, D_f3], w=[bk])
                        yield
                        tt(v3(vnew[0:L, :], 4, 128), v3(bk[0:L, :], 4, 128), bc(beta[0:L, 0:4].unsqueeze(2), [L, 4, 128]),
                           ALU.mult, r=[bk, beta], w=[vnew])
                        yield
                        bks = nb()
                        for h in range(4):
                            mm(bks[0:L, h * 128:(h + 1) * 128], QKVDt[h][:, t0:t0 + L], S3b[:, h, :], True, True,
                               r=[QKVDt[h], S3b], w=[bks])
                        bki = nb()
                        for h in range(4):
                            mm(bki[0:L, h * 128:(h + 1) * 128], QKd[0:L, h, 0:L], vnew[0:L, h * 128:(h + 1) * 128], True, True,
                               r=[QKd, vnew], w=[bki])
                        yield
                        tt(v3(D_f1[0:L, :], 4, 128), v3(bks[0:L, :], 4, 128), bc(ecum_[0:L, 0:4].unsqueeze(2), [L, 4, 128]),
                           ALU.mult, r=[bks, ecum_], w=[D_f1])
                        tt(D_f1[0:L, :], bki[0:L, :], D_f1[0:L, :], ALU.add, r=[bki, D_f1], w=[D_f1])
                        yield
                        yield from head_rmsnorm_gate(D_f1, junkD, ssq_, lnv_, rstd_, Yc, L, 4, 128, D_ng, 1536)
                        yield
                        bkn = nb()
                        for h in range(4):
                            mm(bkn[:, h * 128:(h + 1) * 128], knw[0:L, h * 128:(h + 1) * 128],
                               vnew[0:L, h * 128:(h + 1) * 128], True, True, r=[knw, vnew], w=[bkn])
                        tt(S3[:], S3[:], bc(ecl_[:, 0:4].unsqueeze(2), [128, 4, 128]), ALU.mult, r=[S3, ecl_], w=[S3])
                        yield
                        tt(S3[:], v3(bkn[:, :], 4, 128), S3[:], ALU.add, r=[bkn, S3], w=[S3])
                        cp(S3b[:], S3[:], r=[S3], w=[S3b], eng="act")
                        if sid == 1:
                            dma(o_s_gdn[l, seq].rearrange("h d e -> d h e"), S3[:], r=[S3], w=[dbuf("o_s_gdn")])
                        yield ("done", ch["slot"])

                def swa_thread():
                    for ch in blk:
                        L, t0, slot, sid, seq, G_, Yc = chunk_ctx(ch)
                        while slot >= 2 and not fin.get(slot - 2):
                            yield
                        if sid == 1:
                            dma(cK[:, 0:128], st_k[l, seq], w=[cK])
                            dma(cK[:, 128:256], st_v[l, seq], w=[cK])
                            cp(cKb[:, :], cK[:, 0:128], r=[cK], w=[cKb])
                            bk = nb()
                            bv = bfv(bk)
                            for g in range(2):
                                tr(bv[0:64, g * 128:(g + 1) * 128], cKb[:, g * 64:(g + 1) * 64], ident_b[:, :],
                                   r=[cKb, ident_b], w=[bk])
                            yield
                            cp(PKA[0:64, :, :], v3(bv[0:64, 0:256], 2, 128), r=[bk], w=[PKA])
                            cp(PVA[:, :, 0:64], v3(cK[:, 128:256], 2, 64), r=[cK], w=[PVA])
                            prev = (PKA, PVA, 128, NEGprev)
                        elif ch["ci"] == 0:
                            prev = None
                        elif ch["ci"] == 1:
                            prev = (PKAm, PVAm, NMETA, NEGmeta)
                        else:
                            prev = (PKA, PVA, 128, NEGprev)
                        for g in range(2):
                            tiles = []
                            if prev is not None:
                                tiles.append((prev[0], prev[0][0:65, g, 0:prev[2]], prev[1], prev[1][0:prev[2], g, 0:65],
                                              prev[2], prev[3]))
                            tiles.append((KA, KA[0:65, g, t0:t0 + L], VAUG[slot], VAUG[slot][0:L, g, 0:65], L, NEGown))
                            pts = []
                            for ti, (kbuf, kap, vbuf, vap, Lk, negm) in enumerate(tiles):
                                bk = nb()
                                for hh in range(4):
                                    h = 4 * g + hh
                                    o = bk[0:Lk, hh * 128:hh * 128 + L]
                                    mm(o, kap, QA[0:65, h, t0:t0 + L], True, False, r=[kbuf, QA], w=[bk])
                                    mm(o, ident_b[0:Lk, 0:Lk], negm[0:Lk, 0:L], False, True, r=[ident_b, negm], w=[bk])
                                yield
                                pt = PTs[2 * g + ti] if len(tiles) == 2 else PTs[2 * g + 1]
                                act(pt[0:Lk, :, 0:L], v3(bk[0:Lk, :], 4, 128)[:, :, 0:L], AF.Exp, r=[bk], w=[pt], scale=0.125)
                                pts.append((pt, vbuf, vap, Lk))
                                yield
                            bko = nb()
                            for hh in range(4):
                                for ti, (pt, vbuf, vap, Lk) in enumerate(pts):
                                    mm(bko[0:L, hh * 72:hh * 72 + 65], pt[0:Lk, hh, 0:L], vap, ti == 0, ti == len(pts) - 1,
                                       r=[pt, vbuf], w=[bko])
                            yield
                            ov = v3(bko[0:L, 0:288], 4, 72)
                            tt(den[0:L, 4 * g:4 * g + 4], ov[:, :, 64], ESQ[0:L, 4 * g:4 * g + 4], ALU.add, r=[bko, ESQ],
                               w=[den])
                            V(lambda L=L, g=g: nc.vector.reciprocal(out=den[0:L, 4 * g:4 * g + 4],
                                                                    in_=den[0:L, 4 * g:4 * g + 4]), r=[den], w=[den])
                            tt(v3(B_f2[0:L, 256 * g:256 * g + 256], 4, 64), ov[:, :, 0:64],
                               bc(den[0:L, 4 * g:4 * g + 4].unsqueeze(2), [L, 4, 64]), ALU.mult, r=[bko, den], w=[B_f2])
                            yield
                        tt(Yc[0:L, 512:1024], B_f2[0:L, :], G_[0:L, 1, :], ALU.mult, r=[B_f2, G_], w=[Yc])
                        if sid == 0:
                            if ch["ci"] == 0:
                                cp(PKAm[0:64, :, 0:L], KA[0:64, :, t0:t0 + L], r=[KA], w=[PKAm], eng="pool")
                                cp(PVAm[0:L, :, 0:64], VAUG[slot][0:L, :, 0:64], r=[VAUG[slot]], w=[PVAm], eng="pool")
                            else:
                                cp(PKA[0:64, :, 0:L], KA[0:64, :, t0:t0 + L], r=[KA], w=[PKA], eng="pool")
                                cp(PVA[0:L, :, 0:64], VAUG[slot][0:L, :, 0:64], r=[VAUG[slot]], w=[PVA], eng="pool")
                        if sid == 1:
                            dma(o_s_k[l, seq, 0:128 - LS, :], st_k[l, seq, LS:128, :], w=[dbuf("o_s_k")])
                            dma(o_s_v[l, seq, 0:128 - LS, :], st_v[l, seq, LS:128, :], w=[dbuf("o_s_v")])
                            dma(o_s_k[l, seq, 128 - LS:128, :], KV[slot][0:LS, 0:128], r=[KV[slot]], w=[dbuf("o_s_k")])
                            dma(o_s_v[l, seq, 128 - LS:128, :], KV[slot][0:LS, 128:256], r=[KV[slot]], w=[dbuf("o_s_v")])
                        elif ch["ci"] == 16:
                            dma(o_p_k[l], KV[slot][:, 0:128], r=[KV[slot]], w=[dbuf("o_p_k")])
                            dma(o_p_v[l], KV[slot][:, 128:256], r=[KV[slot]], w=[dbuf("o_p_v")])
                        yield ("done", ch["slot"])

                def finish_chunk(ch):
                    L, t0, slot, sid, seq, G_, Yc = chunk_ctx(ch)
                    if dbg:
                        dma(ydbg[ch["row0"]:ch["row0"] + L, :], Yc[0:L, :], r=[Yc], w=[dbuf("ydbg")])
                    for q in range(4):
                        bk = nb()
                        bv = bfv(bk)
                        for j in range(4):
                            kc = 4 * q + j
                            tr(bv[:, j * 128:j * 128 + L], Yc[0:L, kc * 128:(kc + 1) * 128], ident_b[0:L, 0:L],
                               r=[Yc, ident_b], w=[bk])
                        cp(hT[:, 4 * q:4 * q + 4, t0:t0 + L], v3(bv[:, 0:512], 4, 128)[:, :, 0:L], r=[bk], w=[hT],
                           eng="act" if q % 2 else "dve")

                chk('A')
                def tm_group(wbuf, off, n, evac):
                    for ch in blk:
                        L, t0, slot = ch["L"], ch["tok0"], ch["slot"]
                        bk = nb()
                        for kc in range(KC):
                            mm(bk[0:L, 0:n], hT[:, kc, t0:t0 + L], wbuf[:, kc, off:off + n], kc == 0, kc == KC - 1,
                               r=[hT, wbuf], w=[bk])
                        evac(bk, ch)

                def fm_group(wbuf, off, M, evac):
                    bk = nb()
                    for kc in range(KC):
                        mm(bk[0:M, 0:nbt], wbuf[:, kc, off:off + M], hT[:, kc, 0:nbt], kc == 0, kc == KC - 1,
                           r=[hT, wbuf], w=[bk])
                    evac(bk)

                def gate_evac(gi, half):
                    def f(bk, ch):
                        L = ch["L"]
                        act(Gs[ch["slot"]][0:L, gi, half * 256:(half + 1) * 256], bk[0:L, 0:256], AF.Silu, r=[bk],
                            w=[Gs[ch["slot"]]])
                    return f

                cctr = [0]

                def conv_evac(dst, ft, cw, cb, car, cst, cso):
                    def f(bk):
                        k = cctr[0] % 2
                        cctr[0] += 1
                        S_, A_ = ST[k], ACC[k]
                        if not is0:
                            n = nbt
                            cp(S_[:, 0:3], car[:, ft, :], r=[car], w=[S_], eng="pool")
                            cp(S_[:, 3:3 + n], bk[:, 0:n], r=[bk], w=[S_], eng="act")
                            cp(car[:, ft, :], S_[:, n:n + 3], r=[S_], w=[car], eng="pool")
                            no = n
                        else:
                            memset(S_, S_[:, 0:3], 0.0)
                            sv = v3(S_[:, 19:47], NSS, 7)
                            cp(sv[:, :, 0:3], cst[:, ft, :, :], r=RD(cst), w=[S_], eng="pool")
                            cp(S_[:, 3:19], bk[:, 0:16], r=[bk], w=[S_], eng="act")
                            cp(sv[:, :, 3:7], v3(bk[:, 16:32], NSS, LS), r=[bk], w=[S_], eng="act")
                            cp(car[:, ft, :], S_[:, 16:19], r=[S_], w=[car], eng="pool")
                            cp(cso[:, ft, :, :], sv[:, :, 4:7], r=[S_], w=[cso], eng="pool")
                            no = 44
                        ts(A_[:, 0:no], S_[:, 0:no], cw[:, ft, 0:1], ALU.mult, r=[S_] + RD(cw), w=[A_])
                        for kk in range(1, 4):
                            stt(A_[:, 0:no], S_[:, kk:kk + no], cw[:, ft, kk:kk + 1], A_[:, 0:no], ALU.mult, ALU.add,
                                r=[S_, A_] + RD(cw), w=[A_])
                        bias = cb[:, ft:ft + 1] if cb is not None else None
                        rr = [A_] + (RD(cb) if cb is not None else [])
                        if not is0:
                            act(dst[ft][:, 0:no], A_[:, 0:no], AF.Silu, r=rr, w=[dst[ft]], bias=bias)
                        else:
                            act(dst[ft][:, 0:16], A_[:, 0:16], AF.Silu, r=rr, w=[dst[ft]], bias=bias)
                            act(v3(dst[ft][:, 16:32], NSS, LS), v3(A_[:, 19:47], NSS, 7)[:, :, 0:4], AF.Silu, r=rr,
                                w=[dst[ft]], bias=bias)
                    return f

                wv = wview_in[l]
                import os as _os
                _gl = ((0, C_Z), (1, C_GA), (2, C_GR), (3, C_GD))
                if _os.environ.get('KSKIPG'):
                    _gl = ()
                if _os.environ.get('KDUPG'):
                    _gl = _gl + _gl
                for gi, c0 in _gl:
                    for half in range(2):
                        wbuf = load_w(wv, c0 + half * 256, 256)
                        tm_group(wbuf, 0, 256, gate_evac(gi, half))
                chk('B1')
                wbuf = load_w(wv, C_DT, 8)
                load_w(wv, C_BD, 8, off=8, buf=wbuf)

                def small_evac(bk, ch):
                    L = ch["L"]
                    cp(SM[ch["slot"]][0:L, :], bk[0:L, 0:16], r=[bk], w=[SM[ch["slot"]]])
                tm_group(wbuf, 0, 16, small_evac)
                fin = {}
                done_cnt = {}
                th_ssd, th_gdn, th_swa, th_ret = ssd_thread(), gdn_thread(), swa_thread(), ret_thread()
                early = [th_ssd, th_gdn]
                while early:
                    for th in list(early):
                        v = next(th)
                        if v is not None and v[0] == "wait_proj":
                            early.remove(th)
                chk('B2')
                wbuf = load_w(wv, C_KA, 256)

                def kv_evac(bk, ch):
                    L, slot = ch["L"], ch["slot"]
                    _m = _os.environ.get('KVMODE', '0')
                    if _m in ('0', '1'):
                        cp(KV[slot][0:L, :], bk[0:L, 0:256], r=[bk], w=[KV[slot]], eng="act")
                    if _m in ('0', '2'):
                        cp(VAUG[slot][0:L, :, 0:64], v3(bk[0:L, 128:256], 2, 64), r=[bk] + ([KV[slot]] if _os.environ.get('KVSER') else []), w=[VAUG[slot]])
                tm_group(wbuf, 0, 256, kv_evac)
                chk('B2a')
                for g in range(2):
                    fm_group(wbuf, g * 64, 64,
                             lambda bk, g=g: cp(KA[0:64, g, 0:nbt], bk[0:64, 0:nbt], r=[bk], w=[KA]))
                chk('B3')
                for half in range(2):
                    wbuf = load_w(wv, C_QA + half * 256, 256)
                    for hh in range(4):
                        h = half * 4 + hh
                        fm_group(wbuf, hh * 64, 64,
                                 lambda bk, h=h: cp(QA[0:64, h, 0:nbt], bk[0:64, 0:nbt], r=[bk], w=[QA],
                                                    eng="act" if h % 2 else "dve"))
                chk('B4')
                wbuf = load_w(wv, C_QR, 256)
                for h in range(4):
                    fm_group(wbuf, h * 64, 64,
                             lambda bk, h=h: cp(QR[0:64, h, 0:nbt], bk[0:64, 0:nbt], r=[bk], w=[QR]))
                wbuf = load_w(wv, C_KR, 256)
                for h in range(4):
                    fm_group(wbuf, h * 64, 64,
                             lambda bk, h=h: ts(KRF[0:64, h, 0:nbt], bk[0:64, 0:nbt], 0.125, ALU.mult, r=[bk], w=[KRF]))

                def kr_evac(bk, ch):
                    L, slot = ch["L"], ch["slot"]
                    ts(KR[slot][0:L, :, :], v3(bk[0:L, 0:256], 4, 64), 0.125, ALU.mult, r=[bk], w=[KR[slot]])
                tm_group(wbuf, 0, 256, kr_evac)
                chk('B5')
                for half in range(2):
                    wbuf = load_w(wv, C_VR + half * 256, 256)

                    def vr_evac(bk, ch, half=half):
                        L, slot = ch["L"], ch["slot"]
                        cp(VR[slot][0:L, 2 * half:2 * half + 2, :], v3(bk[0:L, 0:256], 2, 128), r=[bk], w=[VR[slot]],
                           eng="act")
                    tm_group(wbuf, 0, 256, vr_evac)
                chk('B6')
                for q in range(4):
                    wbuf = load_w(wv, C_XBC + q * 256, 256)
                    for j in range(2):
                        ft = 2 * q + j
                        fm_group(wbuf, j * 128, 128, conv_evac(XBCt, ft, cwS, cbS, carS, cstS, csoS))
                chk('B7')
                for q in range(6):
                    wbuf = load_w(wv, C_QKVD + q * 256, 256)
                    for j in range(2):
                        ft = 2 * q + j
                        fm_group(wbuf, j * 128, 128, conv_evac(QKVDt, ft, cwG, None, carG, cstG, csoG))
                chk('B8')
                for ft in range(8):
                    k = ft % 2
                    S_, A_ = ST[k], ACC[k]
                    sq_ = (sqj, junkD)[k]
                    Q_ = QKVDt[ft]
                    tt(sq_[:, 0:nbt], Q_[:, 0:nbt], Q_[:, 0:nbt], ALU.mult, r=[Q_], w=[sq_], eng="pool")
                    bk = nb()
                    mm(bk[:, 0:nbt], ones_b[:, :], sq_[:, 0:nbt], True, True, r=[ones_b, sq_], w=[bk])
                    act(A_[:, 0:nbt], bk[:, 0:nbt], AF.Ln, r=[bk], w=[A_], bias=EPS)
                    act(A_[:, 0:nbt], A_[:, 0:nbt], AF.Exp, r=[A_], w=[A_], scale=-0.5,
                        bias=(math.log(128.0 ** -0.5) if ft < 4 else 0.0))
                    tt(Q_[:, 0:nbt], Q_[:, 0:nbt], A_[:, 0:nbt], ALU.mult, r=[Q_, A_], w=[Q_])

                chk('B')
                if l + 1 < n_layers and len(blocks) >= 7:
                    for pc_ in {0: (0, 1), 1: (2,), 2: (3,), 3: (4,)}.get(bi, ()):
                        convert_weights(l + 1, piece=pc_)
                    if bi == 6:
                        load_slow_params(l + 1)
                elif l + 1 < n_layers and bi == len(blocks) - 1:
                    convert_weights(l + 1)
                    load_slow_params(l + 1)
                threads = [th_ssd, th_gdn, th_swa, th_ret]
                while threads:
                    for th in list(threads):
                        try:
                            v = next(th)
                        except StopIteration:
                            threads.remove(th)
                            continue
                        if v is not None and v[0] == "done":
                            done_cnt[v[1]] = done_cnt.get(v[1], 0) + 1
                            if done_cnt[v[1]] == 4:
                                finish_chunk(blk[v[1]])
                                fin[v[1]] = True

                chk('C')
                if bi + 1 < len(blocks):
                    stage_a1(l, bi + 1)
                elif l + 1 < n_layers:
                    stage_a1(l + 1, 0)
                wvo = wview_out[l]
                for ti, tl in enumerate(tm_tiles):
                    pass
                osb = xt[0]
                outbuf = {}
                for ti, tl in enumerate(tm_tiles):
                    outbuf[ti] = None
                E_ssq = [smalls["E0_ssq"], smalls["E1_ssq"]]
                E_tot, E_lnv, E_rstd = smalls["E_tot"], smalls["E_lnv"], smalls["E_rstd"]
                XR = [ST[0], ST[1], ACC[0], ACC[1]]
                for cg in range(D // WG):
                    wbuf = load_w(wvo, cg * WG, WG)
                    for ti, tl in enumerate(tm_tiles):
                        L, t0 = tl["L"], tl["tok0"]
                        bk = nb()
                        for kc in range(KC):
                            mm(bk[0:L, 0:WG], hT[:, kc, t0:t0 + L], wbuf[:, kc, 0:WG], kc == 0, kc == KC - 1,
                               r=[hT, wbuf], w=[bk])
                        cp(xt[ti][0:L, cg * WG:(cg + 1) * WG], bk[0:L, 0:WG], r=[bk], w=[xt[ti]],
                           eng="act" if cg % 2 else "dve")
                        act(junkA[0:L, 0:WG], xt[ti][0:L, cg * WG:(cg + 1) * WG], AF.Square, r=[xt[ti]],
                            w=[junkA, E_ssq[ti]], accum_out=E_ssq[ti][0:L, cg:cg + 1])
                def epilogue(tm_tiles=tm_tiles, xsrc=xsrc, xdst=xdst, bi=bi, E_ssq=E_ssq, XR=XR):
                    xrc = 0
                    for ti, tl in enumerate(tm_tiles):
                        L, r0 = tl["L"], tl["row0"]
                        o_ = xt[ti]
                        V(lambda L=L, ti=ti: nc.vector.tensor_reduce(out=E_tot[0:L, 0:1], in_=E_ssq[ti][0:L, 0:8], axis=AX.X,
                                                                     op=ALU.add), r=[E_ssq[ti]], w=[E_tot])
                        act(E_lnv[0:L, 0:1], E_tot[0:L, 0:1], AF.Ln, r=[E_tot], w=[E_lnv], scale=1.0 / D, bias=EPS)
                        act(E_rstd[0:L, 0:1], E_lnv[0:L, 0:1], AF.Exp, r=[E_lnv], w=[E_rstd], scale=-0.5)
                        for c in range(8):
                            c0 = c * 256
                            xb = XR[xrc % 4]
                            xrc += 1
                            dma(xb[0:L, 0:256], xsrc[r0:r0 + L, c0:c0 + 256], r=[xbuf(bi)], w=[xb])
                            stt(o_[0:L, c0:c0 + 256], o_[0:L, c0:c0 + 256], E_rstd[0:L, 0:1], postg[0:L, c0:c0 + 256],
                                ALU.mult, ALU.mult, r=[o_, E_rstd, postg], w=[o_])
                            tt(o_[0:L, c0:c0 + 256], o_[0:L, c0:c0 + 256], xb[0:L, 0:256], ALU.add, r=[o_, xb], w=[o_])
                        dma(xdst[r0:r0 + L, :], o_[0:L, :], r=[o_], w=[xbuf(bi)])
                pending_epi.append(epilogue)

            flush_epi()
            if n_blocks is None:
                dma(o_p_ssd[l].rearrange("h n e -> n h e"), Sssd[0][:], r=[Sssd[0]], w=[dbuf("o_p_ssd")])
                dma(o_p_ret[l].rearrange("h d e -> d h e"), Sret[0][:], r=[Sret[0]], w=[dbuf("o_p_ret")])
                dma(o_p_gdn[l].rearrange("h d e -> d h e"), Sgdn[0][:], r=[Sgdn[0]], w=[dbuf("o_p_gdn")])
                store_fm_rows(lambda ft: carS[:, ft, :], carS, 8, 3, o_p_ssdc[l], xt[0])
                store_fm_rows(lambda ft: carG[:, ft, :], carG, 12, 3, o_p_gdnc[l], xt[1])
            store_fm_rows(lambda ft: csoS[:, ft, :, :].rearrange("p s t -> p (s t)"), csoS, 8, NSS * 3,
                          o_s_ssdc[l].rearrange("s t c -> (s t) c"), xt[0])
            store_fm_rows(lambda ft: csoG[:, ft, :, :].rearrange("p s t -> p (s t)"), csoG, 12, NSS * 3,
                          o_s_gdnc[l].rearrange("s t c -> (s t) c"), xt[1])

    except StopBuild:
        pass

    P.emit(es)
    es.close()
    return nc, P.stats


_CACHE = {}


def _in_maps(inp):
    f = lambda a: np.ascontiguousarray(np.asarray(a, dtype=np.float32))
    maps = []
    for c in range(8):
        b = c % 4
        sl = slice(NSS * c, NSS * c + NSS)
        xin = np.concatenate([inp["meta_tokens"], inp["x_prompt"][b], inp["x_sample"][sl].reshape(NSS * LS, D)], axis=0)
        m = {
            "xin": f(xin),
            "st_ssd": f(inp["state_ssd"][:, sl]),
            "st_ssdc": f(inp["state_ssd_conv"][:, sl]),
            "st_k": f(inp["cache_swa_k"][:, sl].reshape(DEPTH, NSS, 128, 128)),
            "st_v": f(inp["cache_swa_v"][:, sl].reshape(DEPTH, NSS, 128, 128)),
            "st_ret": f(inp["state_ret"][:, sl]),
            "st_gdn": f(inp["state_gdn"][:, sl]),
            "st_gdnc": f(inp["state_gdn_conv"][:, sl]),
        }
        for k in ("pre_norm", "post_norm", "w_in", "w_out", "ssd_conv_w", "ssd_conv_b", "ssd_dt_bias", "ssd_a_log",
                  "ssd_d", "ssd_norm", "swa_sinks", "ret_norm", "gdn_conv_w", "gdn_dt_bias", "gdn_a_log", "gdn_norm"):
            m[k] = f(inp[k])
        maps.append(m)
    return maps


def kernel(**inp):
    if "nc" not in _CACHE:
        _CACHE["nc"] = build_program()[0]
    nc = _CACHE["nc"]
    res = run_bass_kernel_spmd(nc, _in_maps(inp), core_ids=list(range(8)))
    R = res.results
    B = 4
    y_prompt = np.stack([R[b]["yout"][NMETA:NPT] for b in range(B)]).astype(np.float32)
    y_sample = np.concatenate([R[c]["yout"][NPT:NT].reshape(NSS, LS, D) for c in range(8)]).astype(np.float32)

    def pst(name, shape):
        return np.stack([np.stack([R[b][name][l] for b in range(B)]) for l in range(DEPTH)]).reshape(shape).astype(np.float32)

    def sst(name, shape):
        return np.concatenate([R[c][name] for c in range(8)], axis=1).reshape(shape).astype(np.float32)

    outs = (
        y_prompt, y_sample,
        pst("o_p_ssd", (DEPTH, B, 8, 128, 64)), pst("o_p_ssdc", (DEPTH, B, 3, 1024)),
        pst("o_p_k", (DEPTH, B, 128, 2, 64)), pst("o_p_v", (DEPTH, B, 128, 2, 64)),
        pst("o_p_ret", (DEPTH, B, 4, 64, 128)), pst("o_p_gdn", (DEPTH, B, 4, 128, 128)),
        pst("o_p_gdnc", (DEPTH, B, 3, 1536)),
        sst("o_s_ssd", (DEPTH, 32, 8, 128, 64)), sst("o_s_ssdc", (DEPTH, 32, 3, 1024)),
        sst("o_s_k", (DEPTH, 32, 128, 2, 64)), sst("o_s_v", (DEPTH, 32, 128, 2, 64)),
        sst("o_s_ret", (DEPTH, 32, 4, 64, 128)), sst("o_s_gdn", (DEPTH, 32, 4, 128, 128)),
        sst("o_s_gdnc", (DEPTH, 32, 3, 1536)),
    )
    return outs
```

```python
import math
from contextlib import ExitStack

import numpy as np
import concourse.bass as bass
import concourse.mybir as mybir
from concourse.bass_utils import run_bass_kernel_spmd

F32 = mybir.dt.float32
BF16 = mybir.dt.bfloat16
I32 = mybir.dt.int32
AF = mybir.ActivationFunctionType
ALU = mybir.AluOpType
AX = mybir.AxisListType

D = 2048
KC = 16
DEPTH = 4
SEQ = 2048
NMETA = 16
NPT = SEQ + NMETA
NSS = 4
LS = 4
NT = NPT + NSS * LS
IN_W = 6416
EPS = 1e-6
NEG = -30000.0

C_Z, C_XBC, C_DT, C_QA, C_KA, C_VA, C_GA = 0, 512, 1536, 1544, 2056, 2184, 2312
C_QR, C_KR, C_VR, C_GR, C_QKVD, C_GD, C_BD, C_AD = 2824, 3080, 3336, 3848, 4360, 5896, 6408, 6412


class Buf:
    __slots__ = ("t", "lw", "rd", "rd_dma", "name", "excl")

    def __init__(self, t, name="", excl=False):
        self.t = t
        self.excl = excl
        self.lw = None
        self.rd = {}
        self.rd_dma = []
        self.name = name

    def __getitem__(self, k):
        return self.t[k]


class View:
    __slots__ = ("p", "t")

    def __init__(self, parent, ap):
        self.p = parent
        self.t = ap

    def __getitem__(self, k):
        return self.t[k]

    lw = property(lambda self: self.p.lw, lambda self, v: setattr(self.p, "lw", v))
    rd = property(lambda self: self.p.rd, lambda self, v: setattr(self.p, "rd", v))
    rd_dma = property(lambda self: self.p.rd_dma, lambda self, v: setattr(self.p, "rd_dma", v))
    excl = property(lambda self: self.p.excl)


class Prog:
    def __init__(self, nc, n_dma_sems=24):
        self.nc = nc
        self.ops = []
        self.E = {"pe": nc.tensor, "act": nc.scalar, "dve": nc.vector, "pool": nc.gpsimd, "sp": nc.sync}
        self.n_dma_sems = n_dma_sems
        self.embed_waits = True

    def op(self, eng, fn, r=(), w=(), dma=False):
        idx = len(self.ops)
        deps = set()
        for b in r:
            if b.lw is not None:
                deps.add(b.lw)
            if b.excl:
                deps.update(v for e, v in b.rd.items() if e != eng)
        for b in w:
            if b.lw is not None:
                deps.add(b.lw)
            deps.update(b.rd.values())
            deps.update(b.rd_dma)
        for b in r:
            if dma:
                b.rd_dma.append(idx)
            else:
                b.rd[eng] = idx
        for b in w:
            b.lw = idx
            b.rd = {}
            b.rd_dma = []
        deps.discard(idx)
        self.ops.append([eng, fn, deps, dma])
        return idx

    def emit(self, es):
        nc = self.nc
        ops = self.ops
        needed = set()
        for i, (eng, fn, deps, dma) in enumerate(ops):
            for d in deps:
                de, _, _, ddma = ops[d]
                if ddma:
                    continue
                if de == eng and eng == "pe":
                    continue
                needed.add(d)
        esem = {e: es.enter_context(nc.semaphore("sem_" + e)) for e in ("pe", "act", "dve", "pool")}
        dsem = [es.enter_context(nc.semaphore("dsem%d" % i)) for i in range(self.n_dma_sems)]
        dval = [0] * self.n_dma_sems
        ecount = {e: 0 for e in esem}
        sig = [None] * len(ops)
        known = {e: {} for e in self.E}
        ndma = 0
        nwait = 0
        dcnt = {}
        for i, (eng, fn, deps, dma) in enumerate(ops):
            waits = {}
            for d in deps:
                s = sig[d]
                if s is None:
                    continue
                if s[1] > waits.get(s[0], (None, 0))[1]:
                    waits[s[0]] = s
            if dma:
                half = self.n_dma_sems // 2
                base = 0 if eng == "sp" else half
                k = base + (dcnt.get(eng, 0) % half)
                dcnt[eng] = dcnt.get(eng, 0) + 1
                ndma += 1
                if dval[k] > 0:
                    key = ("d", k)
                    if dval[k] > waits.get(key, (None, 0))[1]:
                        waits[key] = (key, dval[k])
            kn = known[eng]
            todo = []
            for key, (_, val) in waits.items():
                if kn.get(key, 0) >= val:
                    continue
                sem = esem[key] if isinstance(key, str) else dsem[key[1]]
                todo.append((sem, val))
                kn[key] = val
                nwait += 1
            embed = None
            if todo and eng != "pe" and self.embed_waits:
                embed = todo.pop()
            for sem, val in todo:
                self.E[eng].wait_ge(sem, val)
            ins = fn()
            if embed is not None:
                ins._wait_ge(embed[0], embed[1])
            if dma:
                dval[k] += 16
                ins.then_inc(dsem[k], 16)
                sig[i] = (("d", k), dval[k])
            elif i in needed:
                ecount[eng] += 1
                ins.then_inc(esem[eng], 1)
                sig[i] = (eng, ecount[eng])
        for k in range(self.n_dma_sems):
            if dval[k] > 0:
                nc.sync.wait_ge(dsem[k], dval[k])
        for e in esem:
            if ecount[e] > 0:
                nc.sync.wait_ge(esem[e], ecount[e])
        self.stats = dict(n_ops=len(ops), n_wait=nwait, n_dma=ndma, counts=dict(ecount))


def make_chunks():
    chunks = [dict(kind="p", L=NMETA, row0=0, ci=0)]
    for c in range(1, 17):
        chunks.append(dict(kind="p", L=128, row0=NMETA + 128 * (c - 1), ci=c))
    samples = [dict(kind="s", L=LS, row0=NPT + LS * s, seq=s) for s in range(NSS)]
    return chunks, samples


class StopBuild(Exception):
    pass


def build_program(n_layers=DEPTH, n_blocks=None, cpb=2, dbg=False, stop=None):
    nc = bass.Bass("TRN2", target_bir_lowering=False)
    es = ExitStack()
    P = Prog(nc)

    def dram(name, shape, dt=F32, kind="ExternalInput"):
        return nc.dram_tensor(name, list(shape), dt, kind=kind).ap()

    xin = dram("xin", [NT, D])
    st_ssd = dram("st_ssd", [DEPTH, NSS, 8, 128, 64])
    st_ssdc = dram("st_ssdc", [DEPTH, NSS, 3, 1024])
    st_k = dram("st_k", [DEPTH, NSS, 128, 128])
    st_v = dram("st_v", [DEPTH, NSS, 128, 128])
    st_ret = dram("st_ret", [DEPTH, NSS, 4, 64, 128])
    st_gdn = dram("st_gdn", [DEPTH, NSS, 4, 128, 128])
    st_gdnc = dram("st_gdnc", [DEPTH, NSS, 3, 1536])
    pre_norm = dram("pre_norm", [DEPTH, D])
    post_norm = dram("post_norm", [DEPTH, D])
    w_in = dram("w_in", [DEPTH, D, IN_W])
    w_out = dram("w_out", [DEPTH, D, D])
    ssd_conv_w = dram("ssd_conv_w", [DEPTH, 4, 1024])
    ssd_conv_b = dram("ssd_conv_b", [DEPTH, 1024])
    ssd_dt_bias = dram("ssd_dt_bias", [DEPTH, 8])
    ssd_a_log = dram("ssd_a_log", [DEPTH, 8])
    ssd_d = dram("ssd_d", [DEPTH, 8])
    ssd_norm = dram("ssd_norm", [DEPTH, 512])
    swa_sinks = dram("swa_sinks", [DEPTH, 8])
    ret_norm = dram("ret_norm", [DEPTH, 512])
    gdn_conv_w = dram("gdn_conv_w", [DEPTH, 4, 1536])
    gdn_dt_bias = dram("gdn_dt_bias", [DEPTH, 4])
    gdn_a_log = dram("gdn_a_log", [DEPTH, 4])
    gdn_norm = dram("gdn_norm", [DEPTH, 128])

    EO = "ExternalOutput"
    yout = dram("yout", [NT, D], kind=EO)
    o_p_ssd = dram("o_p_ssd", [DEPTH, 8, 128, 64], kind=EO)
    o_p_ssdc = dram("o_p_ssdc", [DEPTH, 3, 1024], kind=EO)
    o_p_k = dram("o_p_k", [DEPTH, 128, 128], kind=EO)
    o_p_v = dram("o_p_v", [DEPTH, 128, 128], kind=EO)
    o_p_ret = dram("o_p_ret", [DEPTH, 4, 64, 128], kind=EO)
    o_p_gdn = dram("o_p_gdn", [DEPTH, 4, 128, 128], kind=EO)
    o_p_gdnc = dram("o_p_gdnc", [DEPTH, 3, 1536], kind=EO)
    o_s_ssd = dram("o_s_ssd", [DEPTH, NSS, 8, 128, 64], kind=EO)
    o_s_ssdc = dram("o_s_ssdc", [DEPTH, NSS, 3, 1024], kind=EO)
    o_s_k = dram("o_s_k", [DEPTH, NSS, 128, 128], kind=EO)
    o_s_v = dram("o_s_v", [DEPTH, NSS, 128, 128], kind=EO)
    o_s_ret = dram("o_s_ret", [DEPTH, NSS, 4, 64, 128], kind=EO)
    o_s_gdn = dram("o_s_gdn", [DEPTH, NSS, 4, 128, 128], kind=EO)
    o_s_gdnc = dram("o_s_gdnc", [DEPTH, NSS, 3, 1536], kind=EO)
    xscr = dram("xscr", [NT, D], kind="Internal")
    wbf_in = dram("wbf_in", [DEPTH, D, IN_W], BF16, kind="Internal")
    wbf_out = dram("wbf_out", [DEPTH, D, D], BF16, kind="Internal")
    ydbg = dram("ydbg", [NT, D], BF16, kind=EO) if dbg else None

    def sb(name, shape, dt=F32):
        return Buf(es.enter_context(nc.sbuf_tensor(name, list(shape), dt)), name)

    def ps(name, shape, dt=F32):
        return Buf(es.enter_context(nc.psum_tensor(name, list(shape), dt)), name, excl=True)

    DRAMBUF = {}

    def dbuf(ap_name):
        if ap_name not in DRAMBUF:
            DRAMBUF[ap_name] = Buf(None, ap_name)
        return DRAMBUF[ap_name]

    def dma(out, in_, r=(), w=(), eng="pool", **kw):
        P.op(eng, lambda: P.E[eng].dma_start(out=out, in_=in_, **kw), r=r, w=w, dma=True)

    def V(fn, r=(), w=()):
        P.op("dve", fn, r=r, w=w)

    def A(fn, r=(), w=()):
        P.op("act", fn, r=r, w=w)

    def G(fn, r=(), w=()):
        P.op("pool", fn, r=r, w=w)

    def T(fn, r=(), w=()):
        P.op("pe", fn, r=r, w=w)

    def mm(out, lhsT, rhs, start, stop, r, w):
        T(lambda: nc.tensor.matmul(out, lhsT=lhsT, rhs=rhs, start=start, stop=stop), r=r, w=w)

    def tr(out, in_, ident, r, w):
        T(lambda: nc.tensor.transpose(out, in_, ident), r=r, w=w)

    def act(out, in_, func, r, w, bias=None, scale=None, accum_out=None):
        kw = {}
        if bias is not None:
            kw["bias"] = bias
        if scale is not None:
            kw["scale"] = scale
        if accum_out is not None:
            kw["accum_out"] = accum_out
        A(lambda: nc.scalar.activation(out=out, in_=in_, func=func, **kw), r=r, w=w)

    def tt(out, in0, in1, op, r, w, eng="dve"):
        e = nc.vector if eng == "dve" else nc.gpsimd
        P.op(eng, lambda: e.tensor_tensor(out=out, in0=in0, in1=in1, op=op), r=r, w=w)

    def ts(out, in0, s1, op0, r, w, s2=None, op1=None, eng="dve"):
        e = nc.vector if eng == "dve" else nc.gpsimd
        if op1 is None:
            P.op(eng, lambda: e.tensor_scalar(out=out, in0=in0, scalar1=s1, scalar2=None, op0=op0), r=r, w=w)
        else:
            P.op(eng, lambda: e.tensor_scalar(out=out, in0=in0, scalar1=s1, scalar2=s2, op0=op0, op1=op1), r=r, w=w)

    def stt(out, in0, scalar, in1, op0, op1, r, w):
        V(lambda: nc.vector.scalar_tensor_tensor(out=out, in0=in0, scalar=scalar, in1=in1, op0=op0, op1=op1), r=r, w=w)

    def cp(out, in_, r, w, eng="dve"):
        if eng == "act":
            A(lambda: nc.scalar.activation(out=out, in_=in_, func=AF.Copy), r=r, w=w)
        else:
            e = nc.vector if eng == "dve" else nc.gpsimd
            P.op(eng, lambda: e.tensor_copy(out=out, in_=in_), r=r, w=w)

    def memset(buf, ap, val, eng="pool"):
        e = nc.vector if eng == "dve" else nc.gpsimd
        P.op(eng, lambda: e.memset(ap, val), w=[buf])

    def bc(ap, shape):
        return ap.to_broadcast(list(shape))

    ident_f = sb("ident_f", [128, 128])
    ident_b = sb("ident_b", [128, 128], BF16)
    Umat = sb("Umat", [128, 128])
    NEGU = sb("NEGU", [128, 128])
    SUm = sb("SUm", [128, 128])
    NEGown = sb("NEGown", [128, 128], BF16)
    NEGprev = sb("NEGprev", [128, 128], BF16)
    NEGmeta = sb("NEGmeta", [128, 128], BF16)
    ones_b = sb("ones_b", [128, 128], BF16)
    f1 = sb("f1", [128, 512])
    f2 = sb("f2", [128, 512])
    C_f1 = sb("C_f1", [128, 512])
    zeros_f = View(f1, f1[:, 0:128])
    ones_f = View(f1, f1[:, 128:256])
    tmpc = View(f1, f1[:, 256:384])
    tmpc2 = View(f1, f1[:, 384:512])
    dmat = View(f2, f2[:, 0:128])
    iota_fr = View(f2, f2[:, 128:256])
    iota_q = sb("iota_q", [128, 1])

    memset(zeros_f, zeros_f[:], 0.0)
    memset(ones_f, ones_f[:], 1.0)
    memset(ones_b, ones_b[:], 1.0)
    G(lambda: nc.gpsimd.iota(iota_q[:], pattern=[[0, 1]], base=0, channel_multiplier=1,
                             allow_small_or_imprecise_dtypes=True), w=[iota_q])
    G(lambda: nc.gpsimd.iota(iota_fr[:], pattern=[[1, 128]], base=0, channel_multiplier=0,
                             allow_small_or_imprecise_dtypes=True), w=[iota_fr])

    def aff(dst, src, fill, base, cm, step, op):
        G(lambda: nc.gpsimd.affine_select(out=dst[:], in_=src[:], pattern=[[step, 128]], compare_op=op,
                                          fill=fill, base=base, channel_multiplier=cm), r=[src], w=[dst])

    aff(ident_f, ones_f, 0.0, 0, 1, -1, ALU.is_equal)
    cp(ident_b[:], ident_f[:], r=[ident_f], w=[ident_b], eng="pool")
    aff(Umat, ones_f, 0.0, 0, -1, 1, ALU.is_ge)
    aff(NEGU, zeros_f, NEG, 0, -1, 1, ALU.is_ge)
    aff(SUm, ones_f, 0.0, 0, -1, 1, ALU.is_gt)
    cp(NEGown[:], NEGU[:], r=[NEGU], w=[NEGown], eng="pool")
    aff(tmpc, zeros_f, NEG, 0, 1, -1, ALU.is_ge)
    cp(NEGprev[:], tmpc[:], r=[tmpc], w=[NEGprev], eng="pool")
    aff(tmpc2, zeros_f, NEG, 112, 1, -1, ALU.is_ge)
    cp(NEGmeta[:], tmpc2[:], r=[tmpc2], w=[NEGmeta], eng="pool")

    lg = [math.log1p(-2.0 ** (-5 - h)) for h in range(4)]
    RDEC = sb("RDEC", [128, 4, 128])
    GPOW = sb("GPOW", [128, 4])
    RW = {L: sb("RW%d" % L, [128, 4]) for L in (LS, NMETA, 128)}
    GL = {L: sb("GL%d" % L, [64, 4]) for L in (LS, NMETA, 128)}
    ts(dmat[:], iota_fr[:], iota_q[:, 0:1], ALU.subtract, r=[iota_fr, iota_q], w=[dmat])
    ip1 = sb("ip1", [128, 1])
    ts(ip1[:], iota_q[:], 1.0, ALU.add, r=[iota_q], w=[ip1])
    for h in range(4):
        t0 = View(C_f1, C_f1[:, h * 128:(h + 1) * 128])
        ts(t0[:], dmat[:], lg[h], ALU.mult, r=[dmat], w=[t0], s2=NEGU[:, 0:1] if False else None)
        tt(t0[:], t0[:], NEGU[:], ALU.add, r=[t0, NEGU], w=[t0])
        act(RDEC[:, h, :], t0[:], AF.Exp, r=[t0], w=[RDEC])
        act(GPOW[:, h:h + 1], ip1[:], AF.Exp, r=[ip1], w=[GPOW], scale=lg[h])
        for L in RW:
            tq = sb("rwt%d_%d" % (h, L), [128, 1])
            ts(tq[:], iota_q[:], -1.0, ALU.mult, r=[iota_q], w=[tq], s2=float(L - 1), op1=ALU.add)
            act(RW[L][:, h:h + 1], tq[:], AF.Exp, r=[tq], w=[RW[L]], scale=lg[h])
            memset(GL[L], GL[L][:, h:h + 1], math.exp(lg[h] * L), eng="dve")

    slopes = [2.0 ** (-(h + 1)) for h in range(8)]
    SLQ = sb("SLQ", [128, 8])
    for h in range(8):
        ts(SLQ[:, h:h + 1], iota_q[:], slopes[h], ALU.mult, r=[iota_q], w=[SLQ])


    pchunks, schunks = make_chunks()
    blocks = [[pchunks[0]] + schunks]
    for b0 in range(1, 17, cpb):
        blocks.append(pchunks[b0:b0 + cpb])
    if n_blocks is not None:
        blocks = blocks[:n_blocks]
    BT = 128 * cpb
    NSLOT = max(5, cpb)
    WG = 256

    xt = [sb("xt%d" % i, [128, D]) for i in range(2)]
    hb = sb("hb", [128, D], BF16)
    hT = sb("hT", [128, KC, BT], BF16)
    wb = [sb("wb%d" % i, [128, KC, WG], BF16) for i in range(2)]
    SLOTA = sb("SLOTA", [128, 4096], BF16)
    SLOTB = sb("SLOTB", [128, 4096], BF16)
    wb.append(View(SLOTA, SLOTA[:, :].rearrange("p (k n) -> p k n", k=KC, n=WG)))
    wb.append(View(SLOTB, SLOTB[:, :].rearrange("p (k n) -> p k n", k=KC, n=WG)))
    assert NSLOT == 5 and WG == 256
    Gs = [sb("Gs%d" % i, [128, 4, 512], BF16) for i in range(2)]
    Gs.append(View(SLOTA, SLOTA[:, 0:2048].rearrange("p (a b) -> p a b", a=4, b=512)))
    Gs.append(View(SLOTA, SLOTA[:, 2048:4096].rearrange("p (a b) -> p a b", a=4, b=512)))
    Gs.append(View(SLOTB, SLOTB[:, 0:2048].rearrange("p (a b) -> p a b", a=4, b=512)))
    SM = [sb("SM%d" % i, [128, 16]) for i in range(NSLOT)]
    KV = [sb("KV%d" % i, [128, 256]) for i in range(2)]
    KV.append(View(SLOTB, SLOTB[:, 3584:4096].bitcast(F32)))
    KV += [sb("KV%d" % i, [128, 256]) for i in range(3, NSLOT)]
    VAUG = [sb("VAUG%d" % i, [128, 2, 72], BF16) for i in range(NSLOT)]
    VR = [sb("VR%d" % i, [128, 4, 128], BF16) for i in range(2)]
    for i in range(3):
        VR.append(View(SLOTB, SLOTB[:, 2048 + 512 * i:2560 + 512 * i].rearrange("p (a b) -> p a b", a=4, b=128)))
    KR = [sb("KR%d" % i, [128, 4, 64], BF16) for i in range(NSLOT)]
    XBC = sb("XBC", [128, 8, BT], BF16)
    XBCt = [Buf(XBC.t[:, ft_, :], "XBC%d" % ft_) for ft_ in range(8)]
    QA = sb("QA", [65, 8, BT], BF16)
    KA = sb("KA", [65, 2, BT], BF16)
    QR = sb("QR", [64, 4, BT], BF16)
    KRF = sb("KRF", [64, 4, BT], BF16)
    QKVD = sb("QKVD", [128, 12, BT], BF16)
    QKVDt = [Buf(QKVD.t[:, ft_, :], "QKVD%d" % ft_) for ft_ in range(12)]
    ST = [sb("ST%d" % i, [128, BT + 4]) for i in range(2)]
    ACC = [sb("ACC%d" % i, [128, BT]) for i in range(2)]
    carS = sb("carS", [128, 8, 3])
    carG = sb("carG", [128, 12, 3])
    cstSs = [sb("cstS%d" % i, [128, 8, NSS, 3]) for i in range(2)]
    cstGs = [sb("cstG%d" % i, [128, 12, NSS, 3]) for i in range(2)]
    csoS = sb("csoS", [128, 8, NSS, 3])
    csoG = sb("csoG", [128, 12, NSS, 3])
    Yb = [sb("Y%d" % i, [128, D], BF16) for i in range(2)]
    ssq = sb("ssq", [128, 8])
    lnv = sb("lnv", [128, 8])
    rstd = sb("rstd", [128, 8])
    smalls = {}
    for _n in ("E0_ssq", "E1_ssq", "E_tot", "E_lnv", "E_rstd", "F0_ssq", "F1_ssq", "F_tot", "F_lnv", "F0_rstd", "F1_rstd"):
        smalls[_n] = sb(_n, [128, 8])
    for _p in "ACD":
        for _n in ("ssq", "lnv", "rstd", "negcum", "ecum", "ecl", "dtr", "dtv", "lav"):
            smalls[_p + "_" + _n] = sb(_p + "_" + _n, [128, 8])
    sqj = sb("sqj", [128, 512], BF16)
    gTs = [sb("gT%d" % i, [128, KC]) for i in range(2)]
    postg = sb("postg", [128, D])
    ssdn = sb("ssdn", [128, 512])
    retn = sb("retn", [128, 512])
    gdnn = sb("gdnn", [128, 4, 128])
    cwSs = [sb("cwS%d" % i, [128, 8, 4]) for i in range(2)]
    cbSs = [sb("cbS%d" % i, [128, 8]) for i in range(2)]
    cwGs = [sb("cwG%d" % i, [128, 12, 4]) for i in range(2)]
    dtbS = sb("dtbS", [128, 8])
    AnS = sb("AnS", [128, 8])
    Dss = sb("Dss", [128, 8])
    ESQ = sb("ESQ", [128, 8])
    dtbG = sb("dtbG", [128, 4])
    AnG = sb("AnG", [128, 4])
    Sssd = [sb("Sssd%d" % i, [128, 8, 64]) for i in range(2)]
    Sssdb = [sb("Sssdb%d" % i, [128, 8, 64], BF16) for i in range(2)]
    Sret = [sb("Sret%d" % i, [64, 4, 128]) for i in range(2)]
    Sretb = [sb("Sretb%d" % i, [64, 4, 128], BF16) for i in range(2)]
    Sgdn = [sb("Sgdn%d" % i, [128, 4, 128]) for i in range(2)]
    Sgdnb = [sb("Sgdnb%d" % i, [128, 4, 128], BF16) for i in range(2)]
    PKA = sb("PKA", [65, 2, 128], BF16)
    PVA = sb("PVA", [128, 2, 72], BF16)
    PKAm = sb("PKAm", [65, 2, NMETA], BF16)
    PVAm = sb("PVAm", [128, 2, 72], BF16)
    cKb = sb("cKb", [128, 128], BF16)
    junkA = sb("junkA", [128, 512], BF16)
    junkC = sb("junkC", [128, 512], BF16)
    junkD = sb("junkD", [128, 512], BF16)
    C_ng = sb("C_ng", [128, 512])
    D_ng = sb("D_ng", [128, 512])
    decT = sb("decT", [128, 8, 128])
    negcum = sb("negcum", [128, 8])
    ecum = sb("ecum", [128, 8])
    ecl = sb("ecl", [128, 8])
    dtr = sb("dtr", [128, 8])
    dtv = sb("dtv", [128, 8])
    lav = sb("lav", [128, 8])
    MT = sb("MT", [128, 8, 128], BF16)
    xs_tm = sb("xs_tm", [128, 512], BF16)
    xdt = sb("xdt", [128, 512], BF16)
    xdtw = sb("xdtw", [128, 512], BF16)
    Btm = sb("Btm", [128, 256], BF16)
    D_f1 = sb("D_f1", [128, 512])
    D_f3 = sb("D_f3", [128, 512])
    B_f2 = sb("B_f2", [128, 512])
    cK = View(B_f2, B_f2[:, 0:256])
    C_MT = sb("C_MT", [128, 4, 128], BF16)
    D_decT = sb("D_decT", [128, 4, 128])
    kw = sb("kw", [128, 4, 64], BF16)
    beta = sb("beta", [128, 4])
    nbeta = sb("nbeta", [128, 4])
    Pm = [sb("Pm%d" % i, [128, 4, 128]) for i in range(2)]
    PTm = [sb("PTm%d" % i, [128, 4, 128]) for i in range(2)]
    Rm = [sb("Rm%d" % i, [128, 4, 128]) for i in range(2)]
    QKd = sb("QKd", [128, 4, 128], BF16)
    Vtm = sb("Vtm", [128, 512], BF16)
    knw = sb("knw", [128, 512], BF16)
    vnew = sb("vnew", [128, 512], BF16)
    PTs = [sb("PTs%d" % i, [128, 4, 128], BF16) for i in range(4)]
    den = sb("den", [128, 8])

    banks = [ps("bank%d" % i, [128, 512]) for i in range(8)]
    bctr = [0]

    def nb():
        b = banks[bctr[0] % 8]
        bctr[0] += 1
        return b

    def bfv(bk):
        return bk.t[:].bitcast(BF16)

    def v3(ap, a, b):
        return ap.rearrange("p (a b) -> p a b", a=a, b=b)

    for h in range(8):
        memset(QA, QA[64:65, h, :], 8.0 * slopes[h])
    G(lambda: nc.gpsimd.iota(PKA[64:65, :, :], pattern=[[0, 2], [1, 128]], base=-128, channel_multiplier=0,
                             allow_small_or_imprecise_dtypes=True), w=[PKA])
    G(lambda: nc.gpsimd.iota(PKAm[64:65, :, :], pattern=[[0, 2], [1, NMETA]], base=-NMETA, channel_multiplier=0,
                             allow_small_or_imprecise_dtypes=True), w=[PKAm])
    for i in range(NSLOT):
        memset(VAUG[i], VAUG[i][:, :, 64:65], 1.0)
    memset(PVA, PVA[:, :, 64:65], 1.0)
    memset(PVAm, PVAm[:, :, 64:65], 1.0)

    xbufs = {}

    def xbuf(bi):
        if bi not in xbufs:
            xbufs[bi] = Buf(None, "x%d" % bi)
        return xbufs[bi]

    wview_in = [wbf_in[l].rearrange("(kc p) n -> p kc n", p=128) for l in range(DEPTH)]
    wview_out = [wbf_out[l].rearrange("(kc p) n -> p kc n", p=128) for l in range(DEPTH)]
    wscr_bufs = {l: [Buf(None, "wscr%d_%d" % (l, i)) for i in range(5)] for l in range(DEPTH)}
    wctr = [0]
    nwb = [2]
    cur_layer = [0]

    def convert_weights(l, piece=None):
        for i in range(4):
            if piece is None or piece == i:
                dma(wbf_in[l, i * 512:(i + 1) * 512, :], w_in[l, i * 512:(i + 1) * 512, :], w=[wscr_bufs[l][i]],
                    eng="pool")
        if piece is None or piece == 4:
            dma(wbf_out[l], w_out[l], w=[wscr_bufs[l][4]], eng="pool")

    def load_w(view, c0, width, off=0, buf=None):
        if buf is None:
            buf = wb[wctr[0] % nwb[0]]
            wctr[0] += 1
        dma(buf[:, :, off:off + width], view[:, :, c0:c0 + width], r=wscr_bufs[cur_layer[0]], w=[buf], eng="sp")
        return buf

    def rms_stats(src_ap, L, n, scale, col=0):
        pass

    def chk(tag):
        if stop == tag:
            raise StopBuild()

    XA = [[f1, f2, C_f1, D_f1], [D_f3, B_f2, C_ng, D_ng]]
    a1_done = set()

    def tiles_of(bi2):
        if bi2 == 0:
            return [dict(row0=0, L=NMETA, tok0=0), dict(row0=NPT, L=NSS * LS, tok0=NMETA)]
        return [dict(row0=ch_["row0"], L=128, tok0=128 * i_) for i_, ch_ in enumerate(blocks[bi2])]

    def stage_a1(l2, bi2):
        if (l2, bi2) in a1_done:
            return
        a1_done.add((l2, bi2))
        xs2 = xin if l2 == 0 else xscr
        F_tot, F_lnv = smalls["F_tot"], smalls["F_lnv"]
        for ti, tl in enumerate(tiles_of(bi2)):
            L, r0 = tl["L"], tl["row0"]
            xa = XA[ti % 2]
            F_ssq, F_rstd = smalls["F%d_ssq" % (ti % 2)], smalls["F%d_rstd" % (ti % 2)]
            for c in range(4):
                dma(xa[c][0:L, :], xs2[r0:r0 + L, c * 512:(c + 1) * 512], r=[xbuf(bi2)], w=[xa[c]])
                act(junkC[0:L, :], xa[c][0:L, :], AF.Square, r=[xa[c]], w=[junkC, F_ssq], accum_out=F_ssq[0:L, c:c + 1])
            V(lambda L=L, F_ssq=F_ssq: nc.vector.tensor_reduce(out=F_tot[0:L, 0:1], in_=F_ssq[0:L, 0:4], axis=AX.X,
                                                               op=ALU.add), r=[F_ssq], w=[F_tot])
            act(F_lnv[0:L, 0:1], F_tot[0:L, 0:1], AF.Ln, r=[F_tot], w=[F_lnv], scale=1.0 / D, bias=EPS)
            act(F_rstd[0:L, 0:1], F_lnv[0:L, 0:1], AF.Exp, r=[F_lnv], w=[F_rstd], scale=-0.5)

    pending_epi = []

    def flush_epi():
        while pending_epi:
            pending_epi.pop(0)()

    parts_of = {}

    def multi_load(parent, pairs):
        parts = []
        for i, (o_, i_) in enumerate(pairs):
            pb = parent if i == 0 else Buf(None, "part")
            dma(o_, i_, w=[pb], eng="sp", allow_slow_non_contiguous=True)
            if i > 0:
                parts.append(pb)
        parts_of[id(parent)] = parts

    def RD(parent):
        return [parent] + parts_of.get(id(parent), [])

    def store_fm_rows(src_fn, srcbuf, nft, rows, dst2d, stg):
        for f0 in range(0, nft, 4):
            bk = nb()
            for j in range(4):
                tr(bk[0:rows, j * 128:(j + 1) * 128], src_fn(f0 + j), ident_f[:, :], r=[srcbuf, ident_f], w=[bk])
            cp(stg[0:rows, f0 * 128:(f0 + 4) * 128], bk[0:rows, 0:512], r=[bk], w=[stg])
        dma(dst2d, stg[0:rows, 0:nft * 128], r=[stg], w=[Buf(None, "fmrows")])

    def load_slow_params(l):
        p = l % 2
        multi_load(gTs[p], [(gTs[p][:], pre_norm[l].rearrange("(kc p) -> p kc", p=128))])
        multi_load(cwSs[p], [(cwSs[p][:, :, k_], ssd_conv_w[l, k_].rearrange("(ft p) -> p ft", p=128)) for k_ in range(4)])
        multi_load(cwGs[p], [(cwGs[p][:, :, k_], gdn_conv_w[l, k_].rearrange("(ft p) -> p ft", p=128)) for k_ in range(4)])
        multi_load(cbSs[p], [(cbSs[p][:], ssd_conv_b[l].rearrange("(ft p) -> p ft", p=128))])
        multi_load(cstSs[p], [(cstSs[p][:, :, s_, t_], st_ssdc[l, s_, t_].rearrange("(ft p) -> p ft", p=128))
                              for s_ in range(NSS) for t_ in range(3)])
        multi_load(cstGs[p], [(cstGs[p][:, :, s_, t_], st_gdnc[l, s_, t_].rearrange("(ft p) -> p ft", p=128))
                              for s_ in range(NSS) for t_ in range(3)])

    try:
        for l in range(n_layers):
            chk('const')
            cur_layer[0] = l
            if l == 0:
                convert_weights(0)
            xsrc = xin if l == 0 else xscr
            xdst = yout if l == n_layers - 1 else xscr
            if l == 0:
                load_slow_params(0)
            gT, cwS, cbS, cwG, cstS, cstG = [b[l % 2] for b in (gTs, cwSs, cbSs, cwGs, cstSs, cstGs)]
            dma(postg[:], bc(post_norm[l:l + 1, :], [128, D]), w=[postg])
            dma(ssdn[:], bc(ssd_norm[l:l + 1, :], [128, 512]), w=[ssdn])
            dma(retn[:], bc(ret_norm[l:l + 1, :], [128, 512]), w=[retn])
            for h in range(4):
                dma(gdnn[:, h, :], bc(gdn_norm[l:l + 1, :], [128, 128]), w=[gdnn])
            dma(dtbS[:], bc(ssd_dt_bias[l:l + 1, :], [128, 8]), w=[dtbS])
            dma(AnS[:], bc(ssd_a_log[l:l + 1, :], [128, 8]), w=[AnS])
            dma(Dss[:], bc(ssd_d[l:l + 1, :], [128, 8]), w=[Dss])
            dma(ESQ[:], bc(swa_sinks[l:l + 1, :], [128, 8]), w=[ESQ])
            dma(dtbG[:], bc(gdn_dt_bias[l:l + 1, :], [128, 4]), w=[dtbG])
            dma(AnG[:], bc(gdn_a_log[l:l + 1, :], [128, 4]), w=[AnG])
            act(AnS[:], AnS[:], AF.Exp, r=[AnS], w=[AnS])
            ts(AnS[:], AnS[:], -1.0, ALU.mult, r=[AnS], w=[AnS])
            act(AnG[:], AnG[:], AF.Exp, r=[AnG], w=[AnG])
            ts(AnG[:], AnG[:], -1.0, ALU.mult, r=[AnG], w=[AnG])
            tt(ESQ[:], ESQ[:], SLQ[:], ALU.add, r=[ESQ, SLQ], w=[ESQ])
            act(ESQ[:], ESQ[:], AF.Exp, r=[ESQ], w=[ESQ])
            memset(Sssd[0], Sssd[0][:], 0.0)
            memset(Sssdb[0], Sssdb[0][:], 0.0)
            memset(Sret[0], Sret[0][:], 0.0)
            memset(Sretb[0], Sretb[0][:], 0.0)
            memset(Sgdn[0], Sgdn[0][:], 0.0)
            memset(Sgdnb[0], Sgdnb[0][:], 0.0)
            memset(carS, carS[:], 0.0)
            memset(carG, carG[:], 0.0)

            for bi, blk in enumerate(blocks):
                is0 = (bi == 0)
                nwb[0] = 2 if is0 else 4
                tok = 0
                for si, ch in enumerate(blk):
                    ch["tok0"] = tok
                    ch["slot"] = si
                    tok += ch["L"]
                nbt = tok
                if is0:
                    tm_tiles = [dict(row0=0, L=NMETA, tok0=0), dict(row0=NPT, L=NSS * LS, tok0=NMETA)]
                else:
                    tm_tiles = [dict(row0=ch["row0"], L=128, tok0=ch["tok0"]) for ch in blk]
                if is0:
                    G(lambda: nc.gpsimd.iota(KA[64:65, :, 0:NMETA], pattern=[[0, 2], [1, NMETA]], base=0,
                                             channel_multiplier=0, allow_small_or_imprecise_dtypes=True), w=[KA])
                    G(lambda: nc.gpsimd.iota(KA[64:65, :, NMETA:NMETA + 16], pattern=[[0, 2], [0, NSS], [1, LS]], base=0,
                                             channel_multiplier=0, allow_small_or_imprecise_dtypes=True), w=[KA])
                elif bi == 1:
                    G(lambda: nc.gpsimd.iota(KA[64:65, :, :], pattern=[[0, 2], [0, cpb], [1, 128]], base=0,
                                             channel_multiplier=0, allow_small_or_imprecise_dtypes=True), w=[KA])

                chk('params')
                stage_a1(l, bi)
                for ti, tl in enumerate(tm_tiles):
                    L, r0, t0 = tl["L"], tl["row0"], tl["tok0"]
                    xa = XA[ti % 2]
                    F_rstd = smalls["F%d_rstd" % (ti % 2)]
                    for c in range(4):
                        ts(hb[0:L, c * 512:(c + 1) * 512], xa[c][0:L, :], F_rstd[0:L, 0:1], ALU.mult, r=[xa[c], F_rstd],
                           w=[hb])
                    for q in range(4):
                        bk = nb()
                        bv = bfv(bk)
                        for j in range(4):
                            kc = 4 * q + j
                            tr(bv[:, j * 128:j * 128 + L], hb[0:L, kc * 128:(kc + 1) * 128], ident_b[0:L, 0:L],
                               r=[hb, ident_b], w=[bk])
                        tt(hT[:, 4 * q:4 * q + 4, t0:t0 + L], v3(bv[:, 0:512], 4, 128)[:, :, 0:L],
                           bc(gT[:, 4 * q:4 * q + 4].unsqueeze(2), [128, 4, L]), ALU.mult, r=[bk] + RD(gT), w=[hT])
                flush_epi()

                def decay(la, L, H, decT_, negcum_, ecum_, ecl_):
                    bk = nb()
                    mm(bk[0:L, 0:H], Umat[0:L, 0:L], la[0:L, 0:H], True, True, r=[Umat, la], w=[bk])
                    ts(negcum_[0:L, 0:H], bk[0:L, 0:H], -1.0, ALU.mult, r=[bk], w=[negcum_])
                    act(ecum_[0:L, 0:H], bk[0:L, 0:H], AF.Exp, r=[bk], w=[ecum_])
                    yield
                    for hq in range(H // 4):
                        bk = nb()
                        for hh in range(4):
                            h = 4 * hq + hh
                            o = bk[:, hh * 128:hh * 128 + L]
                            mm(o, bc(la[0:L, h:h + 1], [L, 128]), Umat[0:L, 0:L], True, False, r=[la, Umat], w=[bk])
                            mm(o, ident_f[0:L, :], NEGU[0:L, 0:L], False, True, r=[ident_f, NEGU], w=[bk])
                        yield
                        for hh in range(4):
                            h = 4 * hq + hh
                            act(decT_[0:L, h, 0:L], bk[0:L, hh * 128:hh * 128 + L], AF.Exp, r=[bk, negcum_], w=[decT_],
                                bias=negcum_[0:L, h:h + 1])
                        act(ecl_[:, 4 * hq:4 * hq + 4], v3(bk[:, 0:512], 4, 128)[:, :, L - 1], AF.Exp, r=[bk], w=[ecl_])
                        yield

                def softplus_la(dst, src_ap, src_bufs, dtb, An, L, H, dtr_, keep_dt=None):
                    tt(dtr_[0:L, 0:H], src_ap, dtb[0:L, 0:H], ALU.add, r=src_bufs + [dtb], w=[dtr_])
                    act(dtr_[0:L, 0:H], dtr_[0:L, 0:H], AF.Exp, r=[dtr_], w=[dtr_])
                    tgt = keep_dt if keep_dt is not None else dtr_
                    act(tgt[0:L, 0:H], dtr_[0:L, 0:H], AF.Ln, r=[dtr_], w=[tgt], bias=1.0)
                    tt(dst[0:L, 0:H], tgt[0:L, 0:H], An[0:L, 0:H], ALU.mult, r=[tgt, An], w=[dst])

                def head_rmsnorm_gate(o_buf, junk_, ssq_, lnv_, rstd_, Yc, L, nh, hd, ng_, ycols):
                    n = nh * hd
                    for h in range(nh):
                        act(junk_[0:L, h * hd:(h + 1) * hd], o_buf[0:L, h * hd:(h + 1) * hd], AF.Square, r=[o_buf],
                            w=[junk_, ssq_], accum_out=ssq_[0:L, h:h + 1])
                    act(lnv_[0:L, 0:nh], ssq_[0:L, 0:nh], AF.Ln, r=[ssq_], w=[lnv_], scale=1.0 / hd, bias=EPS)
                    act(rstd_[0:L, 0:nh], lnv_[0:L, 0:nh], AF.Exp, r=[lnv_], w=[rstd_], scale=-0.5)
                    yield
                    tt(v3(o_buf[0:L, 0:n], nh, hd), v3(o_buf[0:L, 0:n], nh, hd), bc(rstd_[0:L, 0:nh].unsqueeze(2), [L, nh, hd]),
                       ALU.mult, r=[o_buf, rstd_], w=[o_buf])
                    tt(Yc[0:L, ycols:ycols + n], o_buf[0:L, 0:n], ng_[0:L, 0:n], ALU.mult, r=[o_buf, ng_], w=[Yc])

                def chunk_ctx(ch):
                    sid = 0 if ch["kind"] == "p" else 1
                    return ch["L"], ch["tok0"], ch["slot"], sid, ch.get("seq", None), Gs[ch["slot"]], Yb[ch["slot"] % 2]

                def ssd_thread():
                    A = lambda n: smalls["A_" + n]
                    ssq_, lnv_, rstd_, negcum_, ecum_, ecl_, dtr_, dtv_, lav_ = [A(n) for n in (
                        "ssq", "lnv", "rstd", "negcum", "ecum", "ecl", "dtr", "dtv", "lav")]
                    for ch in blk:
                        L, t0, slot, sid, seq, G_, Yc = chunk_ctx(ch)
                        while slot >= 2 and not fin.get(slot - 2):
                            yield
                        S1, S1b = Sssd[sid], Sssdb[sid]
                        if sid == 1:
                            dma(S1[:], st_ssd[l, seq].rearrange("h n e -> n h e"), w=[S1])
                            cp(S1b[:], S1[:], r=[S1], w=[S1b], eng="act")
                        softplus_la(lav_, SM[slot][0:L, 0:8], [SM[slot]], dtbS, AnS, L, 8, dtr_, keep_dt=dtv_)
                        yield
                        yield from decay(lav_, L, 8, decT, negcum_, ecum_, ecl_)
                        yield ("wait_proj",)
                        bk = nb()
                        bv = bfv(bk)
                        for ft in range(4):
                            tr(bv[0:L, ft * 128:(ft + 1) * 128], XBCt[ft][:, t0:t0 + L], ident_b[:, :], r=[XBCt[ft], ident_b], w=[bk])
                        yield
                        cp(xs_tm[0:L, :], bv[0:L, 0:512], r=[bk], w=[xs_tm], eng="act")
                        tt(v3(xdt[0:L, :], 8, 64), v3(bv[0:L, 0:512], 8, 64), bc(dtv_[0:L, 0:8].unsqueeze(2), [L, 8, 64]),
                           ALU.mult, r=[bk, dtv_], w=[xdt])
                        bk = nb()
                        bv = bfv(bk)
                        for g in range(2):
                            tr(bv[0:L, g * 128:(g + 1) * 128], XBCt[4 + g][:, t0:t0 + L], ident_b[:, :], r=[XBCt[4 + g], ident_b], w=[bk])
                        yield
                        cp(Btm[0:L, :], bv[0:L, 0:256], r=[bk], w=[Btm], eng="act")
                        bk = nb()
                        for g in range(2):
                            mm(bk[0:L, g * 128:g * 128 + L], XBCt[4 + g][:, t0:t0 + L], XBCt[6 + g][:, t0:t0 + L], True, True,
                               r=[XBCt[4 + g], XBCt[6 + g]], w=[bk])
                        yield
                        for g in range(2):
                            tt(MT[0:L, 4 * g:4 * g + 4, 0:L], bc(bk[0:L, g * 128:g * 128 + L].unsqueeze(1), [L, 4, L]),
                               decT[0:L, 4 * g:4 * g + 4, 0:L], ALU.mult, r=[bk, decT], w=[MT])
                        yield
                        bki = nb()
                        for h in range(8):
                            mm(bki[0:L, h * 64:(h + 1) * 64], MT[0:L, h, 0:L], xdt[0:L, h * 64:(h + 1) * 64], True, True,
                               r=[MT, xdt], w=[bki])
                        bks = nb()
                        for h in range(8):
                            mm(bks[0:L, h * 64:(h + 1) * 64], XBCt[6 + h // 4][:, t0:t0 + L], S1b[:, h, :], True, True,
                               r=[XBCt[6 + h // 4], S1b], w=[bks])
                        yield
                        tt(v3(f1[0:L, :], 8, 64), v3(bks[0:L, :], 8, 64), bc(ecum_[0:L, 0:8].unsqueeze(2), [L, 8, 64]), ALU.mult,
                           r=[bks, ecum_], w=[f1])
                        tt(f1[0:L, :], bki[0:L, :], f1[0:L, :], ALU.add, r=[bki, f1], w=[f1])
                        tt(v3(f2[0:L, :], 8, 64), v3(xs_tm[0:L, :], 8, 64), bc(Dss[0:L, 0:8].unsqueeze(2), [L, 8, 64]), ALU.mult,
                           r=[xs_tm, Dss], w=[f2], eng="pool")
                        yield
                        tt(f1[0:L, :], f1[0:L, :], f2[0:L, :], ALU.add, r=[f1, f2], w=[f1])
                        tt(f1[0:L, :], f1[0:L, :], G_[0:L, 0, :], ALU.mult, r=[f1, G_], w=[f1])
                        for g in range(2):
                            act(junkA[0:L, g * 256:(g + 1) * 256], f1[0:L, g * 256:(g + 1) * 256], AF.Square, r=[f1],
                                w=[junkA, ssq_], accum_out=ssq_[0:L, g:g + 1])
                        yield
                        act(lnv_[0:L, 0:2], ssq_[0:L, 0:2], AF.Ln, r=[ssq_], w=[lnv_], scale=1.0 / 256, bias=EPS)
                        act(rstd_[0:L, 0:2], lnv_[0:L, 0:2], AF.Exp, r=[lnv_], w=[rstd_], scale=-0.5)
                        yield
                        tt(v3(f2[0:L, :], 2, 256), v3(f1[0:L, :], 2, 256), bc(rstd_[0:L, 0:2].unsqueeze(2), [L, 2, 256]),
                           ALU.mult, r=[f1, rstd_], w=[f2])
                        tt(Yc[0:L, 0:512], f2[0:L, :], ssdn[0:L, :], ALU.mult, r=[f2, ssdn], w=[Yc])
                        yield
                        tt(v3(xdtw[0:L, :], 8, 64), v3(xdt[0:L, :], 8, 64), bc(decT[0:L, 0:8, L - 1:L], [L, 8, 64]), ALU.mult,
                           r=[xdt, decT], w=[xdtw])
                        bkn = nb()
                        for h in range(8):
                            mm(bkn[:, h * 64:(h + 1) * 64], Btm[0:L, (h // 4) * 128:(h // 4 + 1) * 128],
                               xdtw[0:L, h * 64:(h + 1) * 64], True, True, r=[Btm, xdtw], w=[bkn])
                        tt(S1[:], S1[:], bc(ecl_[:, 0:8].unsqueeze(2), [128, 8, 64]), ALU.mult, r=[S1, ecl_], w=[S1])
                        yield
                        tt(S1[:], v3(bkn[:, :], 8, 64), S1[:], ALU.add, r=[bkn, S1], w=[S1])
                        cp(S1b[:], S1[:], r=[S1], w=[S1b], eng="act")
                        if sid == 1:
                            dma(o_s_ssd[l, seq].rearrange("h n e -> n h e"), S1[:], r=[S1], w=[dbuf("o_s_ssd")])
                        yield ("done", ch["slot"])

                def ret_thread():
                    A = lambda n: smalls["C_" + n]
                    ssq_, lnv_, rstd_ = A("ssq"), A("lnv"), A("rstd")
                    for ch in blk:
                        L, t0, slot, sid, seq, G_, Yc = chunk_ctx(ch)
                        while slot >= 2 and not fin.get(slot - 2):
                            yield
                        S2, S2b = Sret[sid], Sretb[sid]
                        tt(C_ng[0:L, :], retn[0:L, :], G_[0:L, 2, :], ALU.mult, r=[retn, G_], w=[C_ng], eng="pool")
                        yield ("wait_proj",)
                        if sid == 1:
                            dma(S2[:], st_ret[l, seq].rearrange("h d e -> d h e"), w=[S2])
                            cp(S2b[:], S2[:], r=[S2], w=[S2b], eng="act")
                        bk = nb()
                        for h in range(4):
                            mm(bk[0:L, h * 128:h * 128 + L], KRF[0:64, h, t0:t0 + L], QR[0:64, h, t0:t0 + L], True, True,
                               r=[KRF, QR], w=[bk])
                        yield
                        tt(C_MT[0:L, 0:4, 0:L], v3(bk[0:L, :], 4, 128)[:, :, 0:L], RDEC[0:L, :, 0:L], ALU.mult, r=[bk, RDEC],
                           w=[C_MT])
                        yield
                        bki = nb()
                        for h in range(4):
                            mm(bki[0:L, h * 128:(h + 1) * 128], C_MT[0:L, h, 0:L], VR[slot][0:L, h, :], True, True,
                               r=[C_MT, VR[slot]], w=[bki])
                        bks = nb()
                        for h in range(4):
                            mm(bks[0:L, h * 128:(h + 1) * 128], QR[0:64, h, t0:t0 + L], S2b[:, h, :], True, True,
                               r=[QR, S2b], w=[bks])
                        yield
                        tt(v3(C_f1[0:L, :], 4, 128), v3(bks[0:L, :], 4, 128), bc(GPOW[0:L, 0:4].unsqueeze(2), [L, 4, 128]),
                           ALU.mult, r=[bks, GPOW], w=[C_f1])
                        tt(C_f1[0:L, :], bki[0:L, :], C_f1[0:L, :], ALU.add, r=[bki, C_f1], w=[C_f1])
                        yield
                        yield from head_rmsnorm_gate(C_f1, junkC, ssq_, lnv_, rstd_, Yc, L, 4, 128, C_ng, 1024)
                        yield
                        tt(kw[0:L, :, :], KR[slot][0:L, :, :], bc(RW[L][0:L, 0:4].unsqueeze(2), [L, 4, 64]), ALU.mult,
                           r=[KR[slot], RW[L]], w=[kw], eng="pool")
                        bkn = nb()
                        for h in range(4):
                            mm(bkn[0:64, h * 128:(h + 1) * 128], kw[0:L, h, :], VR[slot][0:L, h, :], True, True,
                               r=[kw, VR[slot]], w=[bkn])
                        tt(S2[:], S2[:], bc(GL[L][:, 0:4].unsqueeze(2), [64, 4, 128]), ALU.mult, r=[S2, GL[L]], w=[S2])
                        yield
                        tt(S2[:], v3(bkn[0:64, :], 4, 128), S2[:], ALU.add, r=[bkn, S2], w=[S2])
                        cp(S2b[:], S2[:], r=[S2], w=[S2b], eng="act")
                        if sid == 1:
                            dma(o_s_ret[l, seq].rearrange("h d e -> d h e"), S2[:], r=[S2], w=[dbuf("o_s_ret")])
                        yield ("done", ch["slot"])

                def gdn_thread():
                    A = lambda n: smalls["D_" + n]
                    ssq_, lnv_, rstd_, negcum_, ecum_, ecl_, dtr_, lav_ = [A(n) for n in (
                        "ssq", "lnv", "rstd", "negcum", "ecum", "ecl", "dtr", "lav")]
                    M1 = Pm[1]
                    for ch in blk:
                        L, t0, slot, sid, seq, G_, Yc = chunk_ctx(ch)
                        while slot >= 2 and not fin.get(slot - 2):
                            yield
                        S3, S3b = Sgdn[sid], Sgdnb[sid]
                        tt(D_ng[0:L, :], gdnn[0:L, :, :].rearrange("p a b -> p (a b)"), G_[0:L, 3, :], ALU.mult, r=[gdnn, G_],
                           w=[D_ng], eng="pool")
                        if sid == 1:
                            dma(S3[:], st_gdn[l, seq].rearrange("h d e -> d h e"), w=[S3])
                            cp(S3b[:], S3[:], r=[S3], w=[S3b], eng="act")
                        act(beta[0:L, :], SM[slot][0:L, 8:12], AF.Exp, r=[SM[slot]], w=[beta], scale=-1.0)
                        ts(beta[0:L, :], beta[0:L, :], 1.0, ALU.add, r=[beta], w=[beta])
                        V(lambda L=L: nc.vector.reciprocal(out=beta[0:L, :], in_=beta[0:L, :]), r=[beta], w=[beta])
                        ts(nbeta[0:L, :], beta[0:L, :], -1.0, ALU.mult, r=[beta], w=[nbeta])
                        yield
                        softplus_la(lav_, SM[slot][0:L, 12:16], [SM[slot]], dtbG, AnG, L, 4, dtr_)
                        yield
                        yield from decay(lav_, L, 4, D_decT, negcum_, ecum_, ecl_)
                        yield ("wait_proj",)
                        bk = nb()
                        bv = bfv(bk)
                        for h in range(4):
                            tr(bv[0:L, h * 128:(h + 1) * 128], QKVDt[4 + h][:, t0:t0 + L], ident_b[:, :], r=[QKVDt[4 + h], ident_b],
                               w=[bk])
                        yield
                        tt(v3(knw[0:L, :], 4, 128), v3(bv[0:L, 0:512], 4, 128), bc(D_decT[0:L, 0:4, L - 1:L], [L, 4, 128]),
                           ALU.mult, r=[bk, D_decT], w=[knw])
                        bk = nb()
                        bv = bfv(bk)
                        for h in range(4):
                            tr(bv[0:L, h * 128:(h + 1) * 128], QKVDt[8 + h][:, t0:t0 + L], ident_b[:, :], r=[QKVDt[8 + h], ident_b],
                               w=[bk])
                        yield
                        cp(Vtm[0:L, :], bv[0:L, 0:512], r=[bk], w=[Vtm], eng="act")
                        bkg = nb()
                        bkq = nb()
                        for h in range(4):
                            mm(bkg[0:L, h * 128:h * 128 + L], QKVDt[4 + h][:, t0:t0 + L], QKVDt[4 + h][:, t0:t0 + L], True, True,
                               r=[QKVDt[4 + h]], w=[bkg])
                        for h in range(4):
                            mm(bkq[0:L, h * 128:h * 128 + L], QKVDt[4 + h][:, t0:t0 + L], QKVDt[h][:, t0:t0 + L], True, True,
                               r=[QKVDt[4 + h], QKVDt[h]], w=[bkq])
                        yield
                        tt(M1[0:L, :, 0:L], v3(bkg[0:L, :], 4, 128)[:, :, 0:L], D_decT[0:L, 0:4, 0:L], ALU.mult,
                           r=[bkg, D_decT], w=[M1])
                        tt(QKd[0:L, :, 0:L], v3(bkq[0:L, :], 4, 128)[:, :, 0:L], D_decT[0:L, 0:4, 0:L], ALU.mult,
                           r=[bkq, D_decT], w=[QKd])
                        yield
                        tt(M1[0:L, :, 0:L], M1[0:L, :, 0:L], bc(nbeta[0:L, 0:4].unsqueeze(2), [L, 4, L]), ALU.mult,
                           r=[M1, nbeta], w=[M1])
                        P0_, PT0_ = Pm[0], PTm[0]
                        tt(P0_[0:L, :, 0:L], M1[0:L, :, 0:L], bc(SUm[0:L, 0:L].unsqueeze(1), [L, 4, L]), ALU.mult,
                           r=[M1, SUm], w=[P0_])
                        yield
                        bk = nb()
                        for h in range(4):
                            tr(bk[0:L, h * 128:h * 128 + L], P0_[0:L, h, 0:L], ident_f[0:L, 0:L], r=[P0_, ident_f], w=[bk])
                        yield
                        cp(PT0_[0:L, :, 0:L], v3(bk[0:L, :], 4, 128)[:, :, 0:L], r=[bk], w=[PT0_], eng="act")
                        R_ = Rm[0]
                        tt(R_[0:L, :, 0:L], P0_[0:L, :, 0:L], bc(ident_f[0:L, 0:L].unsqueeze(1), [L, 4, L]), ALU.add,
                           r=[P0_, ident_f], w=[R_], eng="pool")
                        yield
                        nlev = max(1, int(math.ceil(math.log2(L))))
                        cur = 0
                        for k in range(1, nlev):
                            Pc, PTc = Pm[cur], PTm[cur]
                            Pn, PTn = Pm[1 - cur], PTm[1 - cur]
                            last = (k == nlev - 1)
                            bkt = nb()
                            for h in range(4):
                                mm(bkt[0:L, h * 128:h * 128 + L], Pc[0:L, h, 0:L], PTc[0:L, h, 0:L], True, True,
                                   r=[Pc, PTc], w=[bkt])
                            if not last:
                                bkp = nb()
                                for h in range(4):
                                    mm(bkp[0:L, h * 128:h * 128 + L], PTc[0:L, h, 0:L], Pc[0:L, h, 0:L], True, True,
                                       r=[Pc, PTc], w=[bkp])
                            yield
                            cp(PTn[0:L, :, 0:L], v3(bkt[0:L, :], 4, 128)[:, :, 0:L], r=[bkt], w=[PTn], eng="act")
                            if not last:
                                cp(Pn[0:L, :, 0:L], v3(bkp[0:L, :], 4, 128)[:, :, 0:L], r=[bkp], w=[Pn])
                            yield
                            Rc, Rn = Rm[cur], Rm[1 - cur]
                            bkr = nb()
                            for h in range(4):
                                mm(bkr[0:L, h * 128:h * 128 + L], PTn[0:L, h, 0:L], Rc[0:L, h, 0:L], True, True,
                                   r=[PTn, Rc], w=[bkr])
                            yield
                            tt(Rn[0:L, :, 0:L], v3(bkr[0:L, :], 4, 128)[:, :, 0:L], Rc[0:L, :, 0:L], ALU.add, r=[bkr, Rc],
                               w=[Rn])
                            yield
                            cur = 1 - cur
                        Rf = Rm[cur]
                        bk = nb()
                        for h in range(4):
                            mm(bk[0:L, h * 128:(h + 1) * 128], QKVDt[4 + h][:, t0:t0 + L], S3b[:, h, :], True, True,
                               r=[QKVDt[4 + h], S3b], w=[bk])
                        yield
                        tt(v3(D_f1[0:L, :], 4, 128), v3(bk[0:L, :], 4, 128), bc(ecum_[0:L, 0:4].unsqueeze(2), [L, 4, 128]),
                           ALU.mult, r=[bk, ecum_], w=[D_f1])
                        tt(D_f3[0:L, :], Vtm[0:L, :], D_f1[0:L, :], ALU.subtract, r=[Vtm, D_f1], w=[D_f3])
                        yield
                        bk = nb()
                        for h in range(4):
                            mm(bk[0:L, h * 128:(h + 1) * 128], Rf[0:L, h, 0:L], D_f3[0:L, h * 128:(h + 1) * 128], True, True,
                               r=[Rf, D_f3], w=[bk])
                        yield
                        tt(v3(vnew[0:L, :], 4, 128), v3(bk[0:L, :], 4, 128), bc(beta[0:L, 0:4].unsqueeze(2), [L, 4, 128]),
                           ALU.mult, r=[bk, beta], w=[vnew])
                        yield
                        bks = nb()
                        for h in range(4):
                            mm(bks[0:L, h * 128:(h + 1) * 128], QKVDt[h][:, t0:t0 + L], S3b[:, h, :], True, True,
                               r=[QKVDt[h], S3b], w=[bks])
                        bki = nb()
                        for h in range(4):
                            mm(bki[0:L, h * 128:(h + 1) * 128], QKd[0:L, h, 0:L], vnew[0:L, h * 128:(h + 1) * 128], True, True,
                               r=[QKd, vnew], w=[bki])
                        yield
                        tt(v3(D_f1[0:L, :], 4, 128), v3(bks[0:L, :], 4, 128), bc(ecum_[0:L, 0:4].unsqueeze(2), [L, 4, 128]),
                           ALU.mult, r=[bks, ecum_], w=[D_f1])
                        tt(D_f1[0:L, :], bki[0:L, :], D_f1[0:L, :], ALU.add, r=[bki, D_f1], w=[D_f1])
                        yield
                        yield from head_rmsnorm_gate(D_f1, junkD, ssq_, lnv_, rstd_, Yc, L, 4, 128, D_ng, 1536)
                        yield
                        bkn = nb()
                        for h in range(4):
                            mm(bkn[:, h * 128:(h + 1) * 128], knw[0:L, h * 128:(h + 1) * 128],
                               vnew[0:L, h * 128:(h + 1) * 128], True, True, r=[knw, vnew], w=[bkn])
                        tt(S3[:], S3[:], bc(ecl_[:, 0:4].unsqueeze(2), [128, 4, 128]), ALU.mult, r=[S3, ecl_], w=[S3])
                        yield
                        tt(S3[:], v3(bkn[:, :], 4, 128), S3[:], ALU.add, r=[bkn, S3], w=[S3])
                        cp(S3b[:], S3[:], r=[S3], w=[S3b], eng="act")
                        if sid == 1:
                            dma(o_s_gdn[l, seq].rearrange("h d e -> d h e"), S3[:], r=[S3], w=[dbuf("o_s_gdn")])
                        yield ("done", ch["slot"])

                def swa_thread():
                    for ch in blk:
                        L, t0, slot, sid, seq, G_, Yc = chunk_ctx(ch)
                        while slot >= 2 and not fin.get(slot - 2):
                            yield
                        if sid == 1:
                            dma(cK[:, 0:128], st_k[l, seq], w=[cK])
                            dma(cK[:, 128:256], st_v[l, seq], w=[cK])
                            cp(cKb[:, :], cK[:, 0:128], r=[cK], w=[cKb])
                            bk = nb()
                            bv = bfv(bk)
                            for g in range(2):
                                tr(bv[0:64, g * 128:(g + 1) * 128], cKb[:, g * 64:(g + 1) * 64], ident_b[:, :],
                                   r=[cKb, ident_b], w=[bk])
                            yield
                            cp(PKA[0:64, :, :], v3(bv[0:64, 0:256], 2, 128), r=[bk], w=[PKA])
                            cp(PVA[:, :, 0:64], v3(cK[:, 128:256], 2, 64), r=[cK], w=[PVA])
                            prev = (PKA, PVA, 128, NEGprev)
                        elif ch["ci"] == 0:
                            prev = None
                        elif ch["ci"] == 1:
                            prev = (PKAm, PVAm, NMETA, NEGmeta)
                        else:
                            prev = (PKA, PVA, 128, NEGprev)
                        for g in range(2):
                            tiles = []
                            if prev is not None:
                                tiles.append((prev[0], prev[0][0:65, g, 0:prev[2]], prev[1], prev[1][0:prev[2], g, 0:65],
                                              prev[2], prev[3]))
                            tiles.append((KA, KA[0:65, g, t0:t0 + L], VAUG[slot], VAUG[slot][0:L, g, 0:65], L, NEGown))
                            pts = []
                            for ti, (kbuf, kap, vbuf, vap, Lk, negm) in enumerate(tiles):
                                bk = nb()
                                for hh in range(4):
                                    h = 4 * g + hh
                                    o = bk[0:Lk, hh * 128:hh * 128 + L]
                                    mm(o, kap, QA[0:65, h, t0:t0 + L], True, False, r=[kbuf, QA], w=[bk])
                                    mm(o, ident_b[0:Lk, 0:Lk], negm[0:Lk, 0:L], False, True, r=[ident_b, negm], w=[bk])
                                yield
                                pt = PTs[2 * g + ti] if len(tiles) == 2 else PTs[2 * g + 1]
                                act(pt[0:Lk, :, 0:L], v3(bk[0:Lk, :], 4, 128)[:, :, 0:L], AF.Exp, r=[bk], w=[pt], scale=0.125)
                                pts.append((pt, vbuf, vap, Lk))
                                yield
                            bko = nb()
                            for hh in range(4):
                                for ti, (pt, vbuf, vap, Lk) in enumerate(pts):
                                    mm(bko[0:L, hh * 72:hh * 72 + 65], pt[0:Lk, hh, 0:L], vap, ti == 0, ti == len(pts) - 1,
                                       r=[pt, vbuf], w=[bko])
                            yield
                            ov = v3(bko[0:L, 0:288], 4, 72)
                            tt(den[0:L, 4 * g:4 * g + 4], ov[:, :, 64], ESQ[0:L, 4 * g:4 * g + 4], ALU.add, r=[bko, ESQ],
                               w=[den])
                            V(lambda L=L, g=g: nc.vector.reciprocal(out=den[0:L, 4 * g:4 * g + 4],
                                                                    in_=den[0:L, 4 * g:4 * g + 4]), r=[den], w=[den])
                            tt(v3(B_f2[0:L, 256 * g:256 * g + 256], 4, 64), ov[:, :, 0:64],
                               bc(den[0:L, 4 * g:4 * g + 4].unsqueeze(2), [L, 4, 64]), ALU.mult, r=[bko, den], w=[B_f2])
                            yield
                        tt(Yc[0:L, 512:1024], B_f2[0:L, :], G_[0:L, 1, :], ALU.mult, r=[B_f2, G_], w=[Yc])
                        if sid == 0:
                            if ch["ci"] == 0:
                                cp(PKAm[0:64, :, 0:L], KA[0:64, :, t0:t0 + L], r=[KA], w=[PKAm], eng="pool")
                                cp(PVAm[0:L, :, 0:64], VAUG[slot][0:L, :, 0:64], r=[VAUG[slot]], w=[PVAm], eng="pool")
                            else:
                                cp(PKA[0:64, :, 0:L], KA[0:64, :, t0:t0 + L], r=[KA], w=[PKA], eng="pool")
                                cp(PVA[0:L, :, 0:64], VAUG[slot][0:L, :, 0:64], r=[VAUG[slot]], w=[PVA], eng="pool")
                        if sid == 1:
                            dma(o_s_k[l, seq, 0:128 - LS, :], st_k[l, seq, LS:128, :], w=[dbuf("o_s_k")])
                            dma(o_s_v[l, seq, 0:128 - LS, :], st_v[l, seq, LS:128, :], w=[dbuf("o_s_v")])
                            dma(o_s_k[l, seq, 128 - LS:128, :], KV[slot][0:LS, 0:128], r=[KV[slot]], w=[dbuf("o_s_k")])
                            dma(o_s_v[l, seq, 128 - LS:128, :], KV[slot][0:LS, 128:256], r=[KV[slot]], w=[dbuf("o_s_v")])
                        elif ch["ci"] == 16:
                            dma(o_p_k[l], KV[slot][:, 0:128], r=[KV[slot]], w=[dbuf("o_p_k")])
                            dma(o_p_v[l], KV[slot][:, 128:256], r=[KV[slot]], w=[dbuf("o_p_v")])
                        yield ("done", ch["slot"])

                def finish_chunk(ch):
                    L, t0, slot, sid, seq, G_, Yc = chunk_ctx(ch)
                    if dbg:
                        dma(ydbg[ch["row0"]:ch["row0"] + L, :], Yc[0:L, :], r=[Yc], w=[dbuf("ydbg")])
                    for q in range(4):
                        bk = nb()
                        bv = bfv(bk)
                        for j in range(4):
                            kc = 4 * q + j
                            tr(bv[:, j * 128:j * 128 + L], Yc[0:L, kc * 128:(kc + 1) * 128], ident_b[0:L, 0:L],
                               r=[Yc, ident_b], w=[bk])
                        cp(hT[:, 4 * q:4 * q + 4, t0:t0 + L], v3(bv[:, 0:512], 4, 128)[:, :, 0:L], r=[bk], w=[hT],
                           eng="act" if q % 2 else "dve")

                chk('A')
                def tm_group(wbuf, off, n, evac):
                    for ch in blk:
                        L, t0, slot = ch["L"], ch["tok0"], ch["slot"]
                        bk = nb()
                        for kc in range(KC):
                            mm(bk[0:L, 0:n], hT[:, kc, t0:t0 + L], wbuf[:, kc, off:off + n], kc == 0, kc == KC - 1,
                               r=[hT, wbuf], w=[bk])
                        evac(bk, ch)

                def fm_group(wbuf, off, M, evac):
                    bk = nb()
                    for kc in range(KC):
                        mm(bk[0:M, 0:nbt], wbuf[:, kc, off:off + M], hT[:, kc, 0:nbt], kc == 0, kc == KC - 1,
                           r=[hT, wbuf], w=[bk])
                    evac(bk)

                def gate_evac(gi, half):
                    def f(bk, ch):
                        L = ch["L"]
                        act(Gs[ch["slot"]][0:L, gi, half * 256:(half + 1) * 256], bk[0:L, 0:256], AF.Silu, r=[bk],
                            w=[Gs[ch["slot"]]])
                    return f

                cctr = [0]

                def conv_evac(dst, ft, cw, cb, car, cst, cso):
                    def f(bk):
                        k = cctr[0] % 2
                        cctr[0] += 1
                        S_, A_ = ST[k], ACC[k]
                        if not is0:
                            n = nbt
                            cp(S_[:, 0:3], car[:, ft, :], r=[car], w=[S_], eng="pool")
                            cp(S_[:, 3:3 + n], bk[:, 0:n], r=[bk], w=[S_], eng="act")
                            cp(car[:, ft, :], S_[:, n:n + 3], r=[S_], w=[car], eng="pool")
                            no = n
                        else:
                            memset(S_, S_[:, 0:3], 0.0)
                            sv = v3(S_[:, 19:47], NSS, 7)
                            cp(sv[:, :, 0:3], cst[:, ft, :, :], r=RD(cst), w=[S_], eng="pool")
                            cp(S_[:, 3:19], bk[:, 0:16], r=[bk], w=[S_], eng="act")
                            cp(sv[:, :, 3:7], v3(bk[:, 16:32], NSS, LS), r=[bk], w=[S_], eng="act")
                            cp(car[:, ft, :], S_[:, 16:19], r=[S_], w=[car], eng="pool")
                            cp(cso[:, ft, :, :], sv[:, :, 4:7], r=[S_], w=[cso], eng="pool")
                            no = 44
                        ts(A_[:, 0:no], S_[:, 0:no], cw[:, ft, 0:1], ALU.mult, r=[S_] + RD(cw), w=[A_])
                        for kk in range(1, 4):
                            stt(A_[:, 0:no], S_[:, kk:kk + no], cw[:, ft, kk:kk + 1], A_[:, 0:no], ALU.mult, ALU.add,
                                r=[S_, A_] + RD(cw), w=[A_])
                        bias = cb[:, ft:ft + 1] if cb is not None else None
                        rr = [A_] + (RD(cb) if cb is not None else [])
                        if not is0:
                            act(dst[ft][:, 0:no], A_[:, 0:no], AF.Silu, r=rr, w=[dst[ft]], bias=bias)
                        else:
                            act(dst[ft][:, 0:16], A_[:, 0:16], AF.Silu, r=rr, w=[dst[ft]], bias=bias)
                            act(v3(dst[ft][:, 16:32], NSS, LS), v3(A_[:, 19:47], NSS, 7)[:, :, 0:4], AF.Silu, r=rr,
                                w=[dst[ft]], bias=bias)
                    return f

                wv = wview_in[l]
                import os as _os
                _gl = ((0, C_Z), (1, C_GA), (2, C_GR), (3, C_GD))
                if _os.environ.get('KSKIPG'):
                    _gl = ()
                if _os.environ.get('KDUPG'):
                    _gl = _gl + _gl
                for gi, c0 in _gl:
                    for half in range(2):
                        wbuf = load_w(wv, c0 + half * 256, 256)
                        tm_group(wbuf, 0, 256, gate_evac(gi, half))
                chk('B1')
                wbuf = load_w(wv, C_DT, 8)
                load_w(wv, C_BD, 8, off=8, buf=wbuf)

                def small_evac(bk, ch):
                    L = ch["L"]
                    cp(SM[ch["slot"]][0:L, :], bk[0:L, 0:16], r=[bk], w=[SM[ch["slot"]]])
                tm_group(wbuf, 0, 16, small_evac)
                fin = {}
                done_cnt = {}
                th_ssd, th_gdn, th_swa, th_ret = ssd_thread(), gdn_thread(), swa_thread(), ret_thread()
                early = [th_ssd, th_gdn]
                while early:
                    for th in list(early):
                        v = next(th)
                        if v is not None and v[0] == "wait_proj":
                            early.remove(th)
                chk('B2')
                wbuf = load_w(wv, C_KA, 256)

                def kv_evac(bk, ch):
                    L, slot = ch["L"], ch["slot"]
                    _m = _os.environ.get('KVMODE', '0')
                    if _m in ('0', '1'):
                        cp(KV[slot][0:L, :], bk[0:L, 0:256], r=[bk], w=[KV[slot]], eng="act")
                    if _m in ('0', '2'):
                        cp(VAUG[slot][0:L, :, 0:64], v3(bk[0:L, 128:256], 2, 64), r=[bk] + ([KV[slot]] if _os.environ.get('KVSER') else []), w=[VAUG[slot]])
                tm_group(wbuf, 0, 256, kv_evac)
                chk('B2a')
                for g in range(2):
                    fm_group(wbuf, g * 64, 64,
                             lambda bk, g=g: cp(KA[0:64, g, 0:nbt], bk[0:64, 0:nbt], r=[bk], w=[KA]))
                chk('B3')
                for half in range(2):
                    wbuf = load_w(wv, C_QA + half * 256, 256)
                    for hh in range(4):
                        h = half * 4 + hh
                        fm_group(wbuf, hh * 64, 64,
                                 lambda bk, h=h: cp(QA[0:64, h, 0:nbt], bk[0:64, 0:nbt], r=[bk], w=[QA],
                                                    eng="act" if h % 2 else "dve"))
                chk('B4')
                wbuf = load_w(wv, C_QR, 256)
                for h in range(4):
                    fm_group(wbuf, h * 64, 64,
                             lambda bk, h=h: cp(QR[0:64, h, 0:nbt], bk[0:64, 0:nbt], r=[bk], w=[QR]))
                wbuf = load_w(wv, C_KR, 256)
                for h in range(4):
                    fm_group(wbuf, h * 64, 64,
                             lambda bk, h=h: ts(KRF[0:64, h, 0:nbt], bk[0:64, 0:nbt], 0.125, ALU.mult, r=[bk], w=[KRF]))

                def kr_evac(bk, ch):
                    L, slot = ch["L"], ch["slot"]
                    ts(KR[slot][0:L, :, :], v3(bk[0:L, 0:256], 4, 64), 0.125, ALU.mult, r=[bk], w=[KR[slot]])
                tm_group(wbuf, 0, 256, kr_evac)
                chk('B5')
                for half in range(2):
                    wbuf = load_w(wv, C_VR + half * 256, 256)

                    def vr_evac(bk, ch, half=half):
                        L, slot = ch["L"], ch["slot"]
                        cp(VR[slot][0:L, 2 * half:2 * half + 2, :], v3(bk[0:L, 0:256], 2, 128), r=[bk], w=[VR[slot]],
                           eng="act")
                    tm_group(wbuf, 0, 256, vr_evac)
                chk('B6')
                for q in range(4):
                    wbuf = load_w(wv, C_XBC + q * 256, 256)
                    for j in range(2):
                        ft = 2 * q + j
                        fm_group(wbuf, j * 128, 128, conv_evac(XBCt, ft, cwS, cbS, carS, cstS, csoS))
                chk('B7')
                for q in range(6):
                    wbuf = load_w(wv, C_QKVD + q * 256, 256)
                    for j in range(2):
                        ft = 2 * q + j
                        fm_group(wbuf, j * 128, 128, conv_evac(QKVDt, ft, cwG, None, carG, cstG, csoG))
                chk('B8')
                for ft in range(8):
                    k = ft % 2
                    S_, A_ = ST[k], ACC[k]
                    sq_ = (sqj, junkD)[k]
                    Q_ = QKVDt[ft]
                    tt(sq_[:, 0:nbt], Q_[:, 0:nbt], Q_[:, 0:nbt], ALU.mult, r=[Q_], w=[sq_], eng="pool")
                    bk = nb()
                    mm(bk[:, 0:nbt], ones_b[:, :], sq_[:, 0:nbt], True, True, r=[ones_b, sq_], w=[bk])
                    act(A_[:, 0:nbt], bk[:, 0:nbt], AF.Ln, r=[bk], w=[A_], bias=EPS)
                    act(A_[:, 0:nbt], A_[:, 0:nbt], AF.Exp, r=[A_], w=[A_], scale=-0.5,
                        bias=(math.log(128.0 ** -0.5) if ft < 4 else 0.0))
                    tt(Q_[:, 0:nbt], Q_[:, 0:nbt], A_[:, 0:nbt], ALU.mult, r=[Q_, A_], w=[Q_])

                chk('B')
                if l + 1 < n_layers and len(blocks) >= 7:
                    for pc_ in {0: (0, 1), 1: (2,), 2: (3,), 3: (4,)}.get(bi, ()):
                        convert_weights(l + 1, piece=pc_)
                    if bi == 6:
                        load_slow_params(l + 1)
                elif l + 1 < n_layers and bi == len(blocks) - 1:
                    convert_weights(l + 1)
                    load_slow_params(l + 1)
                threads = [th_ssd, th_gdn, th_swa, th_ret]
                while threads:
                    for th in list(threads):
                        try:
                            v = next(th)
                        except StopIteration:
                            threads.remove(th)
                            continue
                        if v is not None and v[0] == "done":
                            done_cnt[v[1]] = done_cnt.get(v[1], 0) + 1
                            if done_cnt[v[1]] == 4:
                                finish_chunk(blk[v[1]])
                                fin[v[1]] = True

                chk('C')
                if bi + 1 < len(blocks):
                    stage_a1(l, bi + 1)
                elif l + 1 < n_layers:
                    stage_a1(l + 1, 0)
                wvo = wview_out[l]
                for ti, tl in enumerate(tm_tiles):
                    pass
                osb = xt[0]
                outbuf = {}
                for ti, tl in enumerate(tm_tiles):
                    outbuf[ti] = None
                E_ssq = [smalls["E0_ssq"], smalls["E1_ssq"]]
                E_tot, E_lnv, E_rstd = smalls["E_tot"], smalls["E_lnv"], smalls["E_rstd"]
                XR = [ST[0], ST[1], ACC[0], ACC[1]]
                for cg in range(D // WG):
                    wbuf = load_w(wvo, cg * WG, WG)
                    for ti, tl in enumerate(tm_tiles):
                        L, t0 = tl["L"], tl["tok0"]
                        bk = nb()
                        for kc in range(KC):
                            mm(bk[0:L, 0:WG], hT[:, kc, t0:t0 + L], wbuf[:, kc, 0:WG], kc == 0, kc == KC - 1,
                               r=[hT, wbuf], w=[bk])
                        cp(xt[ti][0:L, cg * WG:(cg + 1) * WG], bk[0:L, 0:WG], r=[bk], w=[xt[ti]],
                           eng="act" if cg % 2 else "dve")
                        act(junkA[0:L, 0:WG], xt[ti][0:L, cg * WG:(cg + 1) * WG], AF.Square, r=[xt[ti]],
                            w=[junkA, E_ssq[ti]], accum_out=E_ssq[ti][0:L, cg:cg + 1])
                def epilogue(tm_tiles=tm_tiles, xsrc=xsrc, xdst=xdst, bi=bi, E_ssq=E_ssq, XR=XR):
                    xrc = 0
                    for ti, tl in enumerate(tm_tiles):
                        L, r0 = tl["L"], tl["row0"]
                        o_ = xt[ti]
                        V(lambda L=L, ti=ti: nc.vector.tensor_reduce(out=E_tot[0:L, 0:1], in_=E_ssq[ti][0:L, 0:8], axis=AX.X,
                                                                     op=ALU.add), r=[E_ssq[ti]], w=[E_tot])
                        act(E_lnv[0:L, 0:1], E_tot[0:L, 0:1], AF.Ln, r=[E_tot], w=[E_lnv], scale=1.0 / D, bias=EPS)
                        act(E_rstd[0:L, 0:1], E_lnv[0:L, 0:1], AF.Exp, r=[E_lnv], w=[E_rstd], scale=-0.5)
                        for c in range(8):
                            c0 = c * 256
                            xb = XR[xrc % 4]
                            xrc += 1
                            dma(xb[0:L, 0:256], xsrc[r0:r0 + L, c0:c0 + 256], r=[xbuf(bi)], w=[xb])
                            stt(o_[0:L, c0:c0 + 256], o_[0:L, c0:c0 + 256], E_rstd[0:L, 0:1], postg[0:L, c0:c0 + 256],
                                ALU.mult, ALU.mult, r=[o_, E_rstd, postg], w=[o_])
                            tt(o_[0:L, c0:c0 + 256], o_[0:L, c0:c0 + 256], xb[0:L, 0:256], ALU.add, r=[o_, xb], w=[o_])
                        dma(xdst[r0:r0 + L, :], o_[0:L, :], r=[o_], w=[xbuf(bi)])
                pending_epi.append(epilogue)

            flush_epi()
            if n_blocks is None:
                dma(o_p_ssd[l].rearrange("h n e -> n h e"), Sssd[0][:], r=[Sssd[0]], w=[dbuf("o_p_ssd")])
                dma(o_p_ret[l].rearrange("h d e -> d h e"), Sret[0][:], r=[Sret[0]], w=[dbuf("o_p_ret")])
                dma(o_p_gdn[l].rearrange("h d e -> d h e"), Sgdn[0][:], r=[Sgdn[0]], w=[dbuf("o_p_gdn")])
                store_fm_rows(lambda ft: carS[:, ft, :], carS, 8, 3, o_p_ssdc[l], xt[0])
                store_fm_rows(lambda ft: carG[:, ft, :], carG, 12, 3, o_p_gdnc[l], xt[1])
            store_fm_rows(lambda ft: csoS[:, ft, :, :].rearrange("p s t -> p (s t)"), csoS, 8, NSS * 3,
                          o_s_ssdc[l].rearrange("s t c -> (s t) c"), xt[0])
            store_fm_rows(lambda ft: csoG[:, ft, :, :].rearrange("p s t -> p (s t)"), csoG, 12, NSS * 3,
                          o_s_gdnc[l].rearrange("s t c -> (s t) c"), xt[1])

    except StopBuild:
        pass

    P.emit(es)
    es.close()
    return nc, P.stats


_CACHE = {}


def _in_maps(inp):
    f = lambda a: np.ascontiguousarray(np.asarray(a, dtype=np.float32))
    maps = []
    for c in range(8):
        b = c % 4
        sl = slice(NSS * c, NSS * c + NSS)
        xin = np.concatenate([inp["meta_tokens"], inp["x_prompt"][b], inp["x_sample"][sl].reshape(NSS * LS, D)], axis=0)
        m = {
            "xin": f(xin),
            "st_ssd": f(inp["state_ssd"][:, sl]),
            "st_ssdc": f(inp["state_ssd_conv"][:, sl]),
            "st_k": f(inp["cache_swa_k"][:, sl].reshape(DEPTH, NSS, 128, 128)),
            "st_v": f(inp["cache_swa_v"][:, sl].reshape(DEPTH, NSS, 128, 128)),
            "st_ret": f(inp["state_ret"][:, sl]),
            "st_gdn": f(inp["state_gdn"][:, sl]),
            "st_gdnc": f(inp["state_gdn_conv"][:, sl]),
        }
        for k in ("pre_norm", "post_norm", "w_in", "w_out", "ssd_conv_w", "ssd_conv_b", "ssd_dt_bias", "ssd_a_log",
                  "ssd_d", "ssd_norm", "swa_sinks", "ret_norm", "gdn_conv_w", "gdn_dt_bias", "gdn_a_log", "gdn_norm"):
            m[k] = f(inp[k])
        maps.append(m)
    return maps


def kernel(**inp):
    if "nc" not in _CACHE:
        _CACHE["nc"] = build_program()[0]
    nc = _CACHE["nc"]
    res = run_bass_kernel_spmd(nc, _in_maps(inp), core_ids=list(range(8)))
    R = res.results
    B = 4
    y_prompt = np.stack([R[b]["yout"][NMETA:NPT] for b in range(B)]).astype(np.float32)
    y_sample = np.concatenate([R[c]["yout"][NPT:NT].reshape(NSS, LS, D) for c in range(8)]).astype(np.float32)

    def pst(name, shape):
        return np.stack([np.stack([R[b][name][l] for b in range(B)]) for l in range(DEPTH)]).reshape(shape).astype(np.float32)

    def sst(name, shape):
        return np.concatenate([R[c][name] for c in range(8)], axis=1).reshape(shape).astype(np.float32)

    outs = (
        y_prompt, y_sample,
        pst("o_p_ssd", (DEPTH, B, 8, 128, 64)), pst("o_p_ssdc", (DEPTH, B, 3, 1024)),
        pst("o_p_k", (DEPTH, B, 128, 2, 64)), pst("o_p_v", (DEPTH, B, 128, 2, 64)),
        pst("o_p_ret", (DEPTH, B, 4, 64, 128)), pst("o_p_gdn", (DEPTH, B, 4, 128, 128)),
        pst("o_p_gdnc", (DEPTH, B, 3, 1536)),
        sst("o_s_ssd", (DEPTH, 32, 8, 128, 64)), sst("o_s_ssdc", (DEPTH, 32, 3, 1024)),
        sst("o_s_k", (DEPTH, 32, 128, 2, 64)), sst("o_s_v", (DEPTH, 32, 128, 2, 64)),
        sst("o_s_ret", (DEPTH, 32, 4, 64, 128)), sst("o_s_gdn", (DEPTH, 32, 4, 128, 128)),
        sst("o_s_gdnc", (DEPTH, 32, 3, 1536)),
    )
    return outs
```

```python
import math
from contextlib import ExitStack

import numpy as np
import concourse.bass as bass
import concourse.mybir as mybir
from concourse.bass_utils import run_bass_kernel_spmd

F32 = mybir.dt.float32
BF16 = mybir.dt.bfloat16
I32 = mybir.dt.int32
AF = mybir.ActivationFunctionType
ALU = mybir.AluOpType
AX = mybir.AxisListType

D = 2048
KC = 16
DEPTH = 4
SEQ = 2048
NMETA = 16
NPT = SEQ + NMETA
NSS = 4
LS = 4
NT = NPT + NSS * LS
IN_W = 6416
EPS = 1e-6
NEG = -30000.0

C_Z, C_XBC, C_DT, C_QA, C_KA, C_VA, C_GA = 0, 512, 1536, 1544, 2056, 2184, 2312
C_QR, C_KR, C_VR, C_GR, C_QKVD, C_GD, C_BD, C_AD = 2824, 3080, 3336, 3848, 4360, 5896, 6408, 6412


class Buf:
    __slots__ = ("t", "lw", "rd", "rd_dma", "name", "excl")

    def __init__(self, t, name="", excl=False):
        self.t = t
        self.excl = excl
        self.lw = None
        self.rd = {}
        self.rd_dma = []
        self.name = name

    def __getitem__(self, k):
        return self.t[k]


class View:
    __slots__ = ("p", "t")

    def __init__(self, parent, ap):
        self.p = parent
        self.t = ap

    def __getitem__(self, k):
        return self.t[k]

    lw = property(lambda self: self.p.lw, lambda self, v: setattr(self.p, "lw", v))
    rd = property(lambda self: self.p.rd, lambda self, v: setattr(self.p, "rd", v))
    rd_dma = property(lambda self: self.p.rd_dma, lambda self, v: setattr(self.p, "rd_dma", v))
    excl = property(lambda self: self.p.excl)


class Prog:
    def __init__(self, nc, n_dma_sems=24):
        self.nc = nc
        self.ops = []
        self.E = {"pe": nc.tensor, "act": nc.scalar, "dve": nc.vector, "pool": nc.gpsimd, "sp": nc.sync}
        self.n_dma_sems = n_dma_sems
        self.embed_waits = True

    def op(self, eng, fn, r=(), w=(), dma=False):
        idx = len(self.ops)
        deps = set()
        for b in r:
            if b.lw is not None:
                deps.add(b.lw)
            if b.excl:
                deps.update(v for e, v in b.rd.items() if e != eng)
        for b in w:
            if b.lw is not None:
                deps.add(b.lw)
            deps.update(b.rd.values())
            deps.update(b.rd_dma)
        for b in r:
            if dma:
                b.rd_dma.append(idx)
            else:
                b.rd[eng] = idx
        for b in w:
            b.lw = idx
            b.rd = {}
            b.rd_dma = []
        deps.discard(idx)
        self.ops.append([eng, fn, deps, dma])
        return idx

    def emit(self, es):
        nc = self.nc
        ops = self.ops
        needed = set()
        for i, (eng, fn, deps, dma) in enumerate(ops):
            for d in deps:
                de, _, _, ddma = ops[d]
                if ddma:
                    continue
                if de == eng and eng == "pe":
                    continue
                needed.add(d)
        esem = {e: es.enter_context(nc.semaphore("sem_" + e)) for e in ("pe", "act", "dve", "pool")}
        dsem = [es.enter_context(nc.semaphore("dsem%d" % i)) for i in range(self.n_dma_sems)]
        dval = [0] * self.n_dma_sems
        ecount = {e: 0 for e in esem}
        sig = [None] * len(ops)
        known = {e: {} for e in self.E}
        ndma = 0
        nwait = 0
        dcnt = {}
        for i, (eng, fn, deps, dma) in enumerate(ops):
            waits = {}
            for d in deps:
                s = sig[d]
                if s is None:
                    continue
                if s[1] > waits.get(s[0], (None, 0))[1]:
                    waits[s[0]] = s
            if dma:
                half = self.n_dma_sems // 2
                base = 0 if eng == "sp" else half
                k = base + (dcnt.get(eng, 0) % half)
                dcnt[eng] = dcnt.get(eng, 0) + 1
                ndma += 1
                if dval[k] > 0:
                    key = ("d", k)
                    if dval[k] > waits.get(key, (None, 0))[1]:
                        waits[key] = (key, dval[k])
            kn = known[eng]
            todo = []
            for key, (_, val) in waits.items():
                if kn.get(key, 0) >= val:
                    continue
                sem = esem[key] if isinstance(key, str) else dsem[key[1]]
                todo.append((sem, val))
                kn[key] = val
                nwait += 1
            embed = None
            if todo and eng != "pe" and self.embed_waits:
                embed = todo.pop()
            for sem, val in todo:
                self.E[eng].wait_ge(sem, val)
            ins = fn()
            if embed is not None:
                ins._wait_ge(embed[0], embed[1])
            if dma:
                dval[k] += 16
                ins.then_inc(dsem[k], 16)
                sig[i] = (("d", k), dval[k])
            elif i in needed:
                ecount[eng] += 1
                ins.then_inc(esem[eng], 1)
                sig[i] = (eng, ecount[eng])
        for k in range(self.n_dma_sems):
            if dval[k] > 0:
                nc.sync.wait_ge(dsem[k], dval[k])
        for e in esem:
            if ecount[e] > 0:
                nc.sync.wait_ge(esem[e], ecount[e])
        self.stats = dict(n_ops=len(ops), n_wait=nwait, n_dma=ndma, counts=dict(ecount))


def make_chunks():
    chunks = [dict(kind="p", L=NMETA, row0=0, ci=0)]
    for c in range(1, 17):
        chunks.append(dict(kind="p", L=128, row0=NMETA + 128 * (c - 1), ci=c))
    samples = [dict(kind="s", L=LS, row0=NPT + LS * s, seq=s) for s in range(NSS)]
    return chunks, samples


class StopBuild(Exception):
    pass


def build_program(n_layers=DEPTH, n_blocks=None, cpb=2, dbg=False, stop=None):
    nc = bass.Bass("TRN2", target_bir_lowering=False)
    es = ExitStack()
    P = Prog(nc)

    def dram(name, shape, dt=F32, kind="ExternalInput"):
        return nc.dram_tensor(name, list(shape), dt, kind=kind).ap()

    xin = dram("xin", [NT, D])
    st_ssd = dram("st_ssd", [DEPTH, NSS, 8, 128, 64])
    st_ssdc = dram("st_ssdc", [DEPTH, NSS, 3, 1024])
    st_k = dram("st_k", [DEPTH, NSS, 128, 128])
    st_v = dram("st_v", [DEPTH, NSS, 128, 128])
    st_ret = dram("st_ret", [DEPTH, NSS, 4, 64, 128])
    st_gdn = dram("st_gdn", [DEPTH, NSS, 4, 128, 128])
    st_gdnc = dram("st_gdnc", [DEPTH, NSS, 3, 1536])
    pre_norm = dram("pre_norm", [DEPTH, D])
    post_norm = dram("post_norm", [DEPTH, D])
    w_in = dram("w_in", [DEPTH, D, IN_W])
    w_out = dram("w_out", [DEPTH, D, D])
    ssd_conv_w = dram("ssd_conv_w", [DEPTH, 4, 1024])
    ssd_conv_b = dram("ssd_conv_b", [DEPTH, 1024])
    ssd_dt_bias = dram("ssd_dt_bias", [DEPTH, 8])
    ssd_a_log = dram("ssd_a_log", [DEPTH, 8])
    ssd_d = dram("ssd_d", [DEPTH, 8])
    ssd_norm = dram("ssd_norm", [DEPTH, 512])
    swa_sinks = dram("swa_sinks", [DEPTH, 8])
    ret_norm = dram("ret_norm", [DEPTH, 512])
    gdn_conv_w = dram("gdn_conv_w", [DEPTH, 4, 1536])
    gdn_dt_bias = dram("gdn_dt_bias", [DEPTH, 4])
    gdn_a_log = dram("gdn_a_log", [DEPTH, 4])
    gdn_norm = dram("gdn_norm", [DEPTH, 128])

    EO = "ExternalOutput"
    yout = dram("yout", [NT, D], kind=EO)
    o_p_ssd = dram("o_p_ssd", [DEPTH, 8, 128, 64], kind=EO)
    o_p_ssdc = dram("o_p_ssdc", [DEPTH, 3, 1024], kind=EO)
    o_p_k = dram("o_p_k", [DEPTH, 128, 128], kind=EO)
    o_p_v = dram("o_p_v", [DEPTH, 128, 128], kind=EO)
    o_p_ret = dram("o_p_ret", [DEPTH, 4, 64, 128], kind=EO)
    o_p_gdn = dram("o_p_gdn", [DEPTH, 4, 128, 128], kind=EO)
    o_p_gdnc = dram("o_p_gdnc", [DEPTH, 3, 1536], kind=EO)
    o_s_ssd = dram("o_s_ssd", [DEPTH, NSS, 8, 128, 64], kind=EO)
    o_s_ssdc = dram("o_s_ssdc", [DEPTH, NSS, 3, 1024], kind=EO)
    o_s_k = dram("o_s_k", [DEPTH, NSS, 128, 128], kind=EO)
    o_s_v = dram("o_s_v", [DEPTH, NSS, 128, 128], kind=EO)
    o_s_ret = dram("o_s_ret", [DEPTH, NSS, 4, 64, 128], kind=EO)
    o_s_gdn = dram("o_s_gdn", [DEPTH, NSS, 4, 128, 128], kind=EO)
    o_s_gdnc = dram("o_s_gdnc", [DEPTH, NSS, 3, 1536], kind=EO)
    xscr = dram("xscr", [NT, D], kind="Internal")
    wbf_in = dram("wbf_in", [DEPTH, D, IN_W], BF16, kind="Internal")
    wbf_out = dram("wbf_out", [DEPTH, D, D], BF16, kind="Internal")
    ydbg = dram("ydbg", [NT, D], BF16, kind=EO) if dbg else None

    def sb(name, shape, dt=F32):
        return Buf(es.enter_context(nc.sbuf_tensor(name, list(shape), dt)), name)

    def ps(name, shape, dt=F32):
        return Buf(es.enter_context(nc.psum_tensor(name, list(shape), dt)), name, excl=True)

    DRAMBUF = {}

    def dbuf(ap_name):
        if ap_name not in DRAMBUF:
            DRAMBUF[ap_name] = Buf(None, ap_name)
        return DRAMBUF[ap_name]

    def dma(out, in_, r=(), w=(), eng="pool", **kw):
        P.op(eng, lambda: P.E[eng].dma_start(out=out, in_=in_, **kw), r=r, w=w, dma=True)

    def V(fn, r=(), w=()):
        P.op("dve", fn, r=r, w=w)

    def A(fn, r=(), w=()):
        P.op("act", fn, r=r, w=w)

    def G(fn, r=(), w=()):
        P.op("pool", fn, r=r, w=w)

    def T(fn, r=(), w=()):
        P.op("pe", fn, r=r, w=w)

    def mm(out, lhsT, rhs, start, stop, r, w):
        T(lambda: nc.tensor.matmul(out, lhsT=lhsT, rhs=rhs, start=start, stop=stop), r=r, w=w)

    def tr(out, in_, ident, r, w):
        T(lambda: nc.tensor.transpose(out, in_, ident), r=r, w=w)

    def act(out, in_, func, r, w, bias=None, scale=None, accum_out=None):
        kw = {}
        if bias is not None:
            kw["bias"] = bias
        if scale is not None:
            kw["scale"] = scale
        if accum_out is not None:
            kw["accum_out"] = accum_out
        A(lambda: nc.scalar.activation(out=out, in_=in_, func=func, **kw), r=r, w=w)

    def tt(out, in0, in1, op, r, w, eng="dve"):
        e = nc.vector if eng == "dve" else nc.gpsimd
        P.op(eng, lambda: e.tensor_tensor(out=out, in0=in0, in1=in1, op=op), r=r, w=w)

    def ts(out, in0, s1, op0, r, w, s2=None, op1=None, eng="dve"):
        e = nc.vector if eng == "dve" else nc.gpsimd
        if op1 is None:
            P.op(eng, lambda: e.tensor_scalar(out=out, in0=in0, scalar1=s1, scalar2=None, op0=op0), r=r, w=w)
        else:
            P.op(eng, lambda: e.tensor_scalar(out=out, in0=in0, scalar1=s1, scalar2=s2, op0=op0, op1=op1), r=r, w=w)

    def stt(out, in0, scalar, in1, op0, op1, r, w):
        V(lambda: nc.vector.scalar_tensor_tensor(out=out, in0=in0, scalar=scalar, in1=in1, op0=op0, op1=op1), r=r, w=w)

    def cp(out, in_, r, w, eng="dve"):
        if eng == "act":
            A(lambda: nc.scalar.activation(out=out, in_=in_, func=AF.Copy), r=r, w=w)
        else:
            e = nc.vector if eng == "dve" else nc.gpsimd
            P.op(eng, lambda: e.tensor_copy(out=out, in_=in_), r=r, w=w)

    def memset(buf, ap, val, eng="pool"):
        e = nc.vector if eng == "dve" else nc.gpsimd
        P.op(eng, lambda: e.memset(ap, val), w=[buf])

    def bc(ap, shape):
        return ap.to_broadcast(list(shape))

    ident_f = sb("ident_f", [128, 128])
    ident_b = sb("ident_b", [128, 128], BF16)
    Umat = sb("Umat", [128, 128])
    NEGU = sb("NEGU", [128, 128])
    SUm = sb("SUm", [128, 128])
    NEGown = sb("NEGown", [128, 128], BF16)
    NEGprev = sb("NEGprev", [128, 128], BF16)
    NEGmeta = sb("NEGmeta", [128, 128], BF16)
    ones_b = sb("ones_b", [128, 128], BF16)
    f1 = sb("f1", [128, 512])
    f2 = sb("f2", [128, 512])
    C_f1 = sb("C_f1", [128, 512])
    zeros_f = View(f1, f1[:, 0:128])
    ones_f = View(f1, f1[:, 128:256])
    tmpc = View(f1, f1[:, 256:384])
    tmpc2 = View(f1, f1[:, 384:512])
    dmat = View(f2, f2[:, 0:128])
    iota_fr = View(f2, f2[:, 128:256])
    iota_q = sb("iota_q", [128, 1])

    memset(zeros_f, zeros_f[:], 0.0)
    memset(ones_f, ones_f[:], 1.0)
    memset(ones_b, ones_b[:], 1.0)
    G(lambda: nc.gpsimd.iota(iota_q[:], pattern=[[0, 1]], base=0, channel_multiplier=1,
                             allow_small_or_imprecise_dtypes=True), w=[iota_q])
    G(lambda: nc.gpsimd.iota(iota_fr[:], pattern=[[1, 128]], base=0, channel_multiplier=0,
                             allow_small_or_imprecise_dtypes=True), w=[iota_fr])

    def aff(dst, src, fill, base, cm, step, op):
        G(lambda: nc.gpsimd.affine_select(out=dst[:], in_=src[:], pattern=[[step, 128]], compare_op=op,
                                          fill=fill, base=base, channel_multiplier=cm), r=[src], w=[dst])

    aff(ident_f, ones_f, 0.0, 0, 1, -1, ALU.is_equal)
    cp(ident_b[:], ident_f[:], r=[ident_f], w=[ident_b], eng="pool")
    aff(Umat, ones_f, 0.0, 0, -1, 1, ALU.is_ge)
    aff(NEGU, zeros_f, NEG, 0, -1, 1, ALU.is_ge)
    aff(SUm, ones_f, 0.0, 0, -1, 1, ALU.is_gt)
    cp(NEGown[:], NEGU[:], r=[NEGU], w=[NEGown], eng="pool")
    aff(tmpc, zeros_f, NEG, 0, 1, -1, ALU.is_ge)
    cp(NEGprev[:], tmpc[:], r=[tmpc], w=[NEGprev], eng="pool")
    aff(tmpc2, zeros_f, NEG, 112, 1, -1, ALU.is_ge)
    cp(NEGmeta[:], tmpc2[:], r=[tmpc2], w=[NEGmeta], eng="pool")

    lg = [math.log1p(-2.0 ** (-5 - h)) for h in range(4)]
    RDEC = sb("RDEC", [128, 4, 128])
    GPOW = sb("GPOW", [128, 4])
    RW = {L: sb("RW%d" % L, [128, 4]) for L in (LS, NMETA, 128)}
    GL = {L: sb("GL%d" % L, [64, 4]) for L in (LS, NMETA, 128)}
    ts(dmat[:], iota_fr[:], iota_q[:, 0:1], ALU.subtract, r=[iota_fr, iota_q], w=[dmat])
    ip1 = sb("ip1", [128, 1])
    ts(ip1[:], iota_q[:], 1.0, ALU.add, r=[iota_q], w=[ip1])
    for h in range(4):
        t0 = View(C_f1, C_f1[:, h * 128:(h + 1) * 128])
        ts(t0[:], dmat[:], lg[h], ALU.mult, r=[dmat], w=[t0], s2=NEGU[:, 0:1] if False else None)
        tt(t0[:], t0[:], NEGU[:], ALU.add, r=[t0, NEGU], w=[t0])
        act(RDEC[:, h, :], t0[:], AF.Exp, r=[t0], w=[RDEC])
        act(GPOW[:, h:h + 1], ip1[:], AF.Exp, r=[ip1], w=[GPOW], scale=lg[h])
        for L in RW:
            tq = sb("rwt%d_%d" % (h, L), [128, 1])
            ts(tq[:], iota_q[:], -1.0, ALU.mult, r=[iota_q], w=[tq], s2=float(L - 1), op1=ALU.add)
            act(RW[L][:, h:h + 1], tq[:], AF.Exp, r=[tq], w=[RW[L]], scale=lg[h])
            memset(GL[L], GL[L][:, h:h + 1], math.exp(lg[h] * L), eng="dve")

    slopes = [2.0 ** (-(h + 1)) for h in range(8)]
    SLQ = sb("SLQ", [128, 8])
    for h in range(8):
        ts(SLQ[:, h:h + 1], iota_q[:], slopes[h], ALU.mult, r=[iota_q], w=[SLQ])


    pchunks, schunks = make_chunks()
    blocks = [[pchunks[0]] + schunks]
    for b0 in range(1, 17, cpb):
        blocks.append(pchunks[b0:b0 + cpb])
    if n_blocks is not None:
        blocks = blocks[:n_blocks]
    BT = 128 * cpb
    NSLOT = max(5, cpb)
    WG = 256

    xt = [sb("xt%d" % i, [128, D]) for i in range(2)]
    hb = sb("hb", [128, D], BF16)
    hT = sb("hT", [128, KC, BT], BF16)
    wb = [sb("wb%d" % i, [128, KC, WG], BF16) for i in range(2)]
    SLOTA = sb("SLOTA", [128, 4096], BF16)
    SLOTB = sb("SLOTB", [128, 4096], BF16)
    wb.append(View(SLOTA, SLOTA[:, :].rearrange("p (k n) -> p k n", k=KC, n=WG)))
    wb.append(View(SLOTB, SLOTB[:, :].rearrange("p (k n) -> p k n", k=KC, n=WG)))
    assert NSLOT == 5 and WG == 256
    Gs = [sb("Gs%d" % i, [128, 4, 512], BF16) for i in range(2)]
    Gs.append(View(SLOTA, SLOTA[:, 0:2048].rearrange("p (a b) -> p a b", a=4, b=512)))
    Gs.append(View(SLOTA, SLOTA[:, 2048:4096].rearrange("p (a b) -> p a b", a=4, b=512)))
    Gs.append(View(SLOTB, SLOTB[:, 0:2048].rearrange("p (a b) -> p a b", a=4, b=512)))
    SM = [sb("SM%d" % i, [128, 16]) for i in range(NSLOT)]
    KV = [sb("KV%d" % i, [128, 256]) for i in range(2)]
    KV.append(View(SLOTB, SLOTB[:, 3584:4096].bitcast(F32)))
    KV += [sb("KV%d" % i, [128, 256]) for i in range(3, NSLOT)]
    VAUG = [sb("VAUG%d" % i, [128, 2, 72], BF16) for i in range(NSLOT)]
    VR = [sb("VR%d" % i, [128, 4, 128], BF16) for i in range(2)]
    for i in range(3):
        VR.append(View(SLOTB, SLOTB[:, 2048 + 512 * i:2560 + 512 * i].rearrange("p (a b) -> p a b", a=4, b=128)))
    KR = [sb("KR%d" % i, [128, 4, 64], BF16) for i in range(NSLOT)]
    XBC = sb("XBC", [128, 8, BT], BF16)
    XBCt = [Buf(XBC.t[:, ft_, :], "XBC%d" % ft_) for ft_ in range(8)]
    QA = sb("QA", [65, 8, BT], BF16)
    KA = sb("KA", [65, 2, BT], BF16)
    QR = sb("QR", [64, 4, BT], BF16)
    KRF = sb("KRF", [64, 4, BT], BF16)
    QKVD = sb("QKVD", [128, 12, BT], BF16)
    QKVDt = [Buf(QKVD.t[:, ft_, :], "QKVD%d" % ft_) for ft_ in range(12)]
    ST = [sb("ST%d" % i, [128, BT + 4]) for i in range(2)]
    ACC = [sb("ACC%d" % i, [128, BT]) for i in range(2)]
    carS = sb("carS", [128, 8, 3])
    carG = sb("carG", [128, 12, 3])
    cstSs = [sb("cstS%d" % i, [128, 8, NSS, 3]) for i in range(2)]
    cstGs = [sb("cstG%d" % i, [128, 12, NSS, 3]) for i in range(2)]
    csoS = sb("csoS", [128, 8, NSS, 3])
    csoG = sb("csoG", [128, 12, NSS, 3])
    Yb = [sb("Y%d" % i, [128, D], BF16) for i in range(2)]
    ssq = sb("ssq", [128, 8])
    lnv = sb("lnv", [128, 8])
    rstd = sb("rstd", [128, 8])
    smalls = {}
    for _n in ("E0_ssq", "E1_ssq", "E_tot", "E_lnv", "E_rstd", "F0_ssq", "F1_ssq", "F_tot", "F_lnv", "F0_rstd", "F1_rstd"):
        smalls[_n] = sb(_n, [128, 8])
    for _p in "ACD":
        for _n in ("ssq", "lnv", "rstd", "negcum", "ecum", "ecl", "dtr", "dtv", "lav"):
            smalls[_p + "_" + _n] = sb(_p + "_" + _n, [128, 8])
    sqj = sb("sqj", [128, 512], BF16)
    gTs = [sb("gT%d" % i, [128, KC]) for i in range(2)]
    postg = sb("postg", [128, D])
    ssdn = sb("ssdn", [128, 512])
    retn = sb("retn", [128, 512])
    gdnn = sb("gdnn", [128, 4, 128])
    cwSs = [sb("cwS%d" % i, [128, 8, 4]) for i in range(2)]
    cbSs = [sb("cbS%d" % i, [128, 8]) for i in range(2)]
    cwGs = [sb("cwG%d" % i, [128, 12, 4]) for i in range(2)]
    dtbS = sb("dtbS", [128, 8])
    AnS = sb("AnS", [128, 8])
    Dss = sb("Dss", [128, 8])
    ESQ = sb("ESQ", [128, 8])
    dtbG = sb("dtbG", [128, 4])
    AnG = sb("AnG", [128, 4])
    Sssd = [sb("Sssd%d" % i, [128, 8, 64]) for i in range(2)]
    Sssdb = [sb("Sssdb%d" % i, [128, 8, 64], BF16) for i in range(2)]
    Sret = [sb("Sret%d" % i, [64, 4, 128]) for i in range(2)]
    Sretb = [sb("Sretb%d" % i, [64, 4, 128], BF16) for i in range(2)]
    Sgdn = [sb("Sgdn%d" % i, [128, 4, 128]) for i in range(2)]
    Sgdnb = [sb("Sgdnb%d" % i, [128, 4, 128], BF16) for i in range(2)]
    PKA = sb("PKA", [65, 2, 128], BF16)
    PVA = sb("PVA", [128, 2, 72], BF16)
    PKAm = sb("PKAm", [65, 2, NMETA], BF16)
    PVAm = sb("PVAm", [128, 2, 72], BF16)
    cKb = sb("cKb", [128, 128], BF16)
    junkA = sb("junkA", [128, 512], BF16)
    junkC = sb("junkC", [128, 512], BF16)
    junkD = sb("junkD", [128, 512], BF16)
    C_ng = sb("C_ng", [128, 512])
    D_ng = sb("D_ng", [128, 512])
    decT = sb("decT", [128, 8, 128])
    negcum = sb("negcum", [128, 8])
    ecum = sb("ecum", [128, 8])
    ecl = sb("ecl", [128, 8])
    dtr = sb("dtr", [128, 8])
    dtv = sb("dtv", [128, 8])
    lav = sb("lav", [128, 8])
    MT = sb("MT", [128, 8, 128], BF16)
    xs_tm = sb("xs_tm", [128, 512], BF16)
    xdt = sb("xdt", [128, 512], BF16)
    xdtw = sb("xdtw", [128, 512], BF16)
    Btm = sb("Btm", [128, 256], BF16)
    D_f1 = sb("D_f1", [128, 512])
    D_f3 = sb("D_f3", [128, 512])
    B_f2 = sb("B_f2", [128, 512])
    cK = View(B_f2, B_f2[:, 0:256])
    C_MT = sb("C_MT", [128, 4, 128], BF16)
    D_decT = sb("D_decT", [128, 4, 128])
    kw = sb("kw", [128, 4, 64], BF16)
    beta = sb("beta", [128, 4])
    nbeta = sb("nbeta", [128, 4])
    Pm = [sb("Pm%d" % i, [128, 4, 128]) for i in range(2)]
    PTm = [sb("PTm%d" % i, [128, 4, 128]) for i in range(2)]
    Rm = [sb("Rm%d" % i, [128, 4, 128]) for i in range(2)]
    QKd = sb("QKd", [128, 4, 128], BF16)
    Vtm = sb("Vtm", [128, 512], BF16)
    knw = sb("knw", [128, 512], BF16)
    vnew = sb("vnew", [128, 512], BF16)
    PTs = [sb("PTs%d" % i, [128, 4, 128], BF16) for i in range(4)]
    den = sb("den", [128, 8])

    banks = [ps("bank%d" % i, [128, 512]) for i in range(8)]
    bctr = [0]

    def nb():
        b = banks[bctr[0] % 8]
        bctr[0] += 1
        return b

    def bfv(bk):
        return bk.t[:].bitcast(BF16)

    def v3(ap, a, b):
        return ap.rearrange("p (a b) -> p a b", a=a, b=b)

    for h in range(8):
        memset(QA, QA[64:65, h, :], 8.0 * slopes[h])
    G(lambda: nc.gpsimd.iota(PKA[64:65, :, :], pattern=[[0, 2], [1, 128]], base=-128, channel_multiplier=0,
                             allow_small_or_imprecise_dtypes=True), w=[PKA])
    G(lambda: nc.gpsimd.iota(PKAm[64:65, :, :], pattern=[[0, 2], [1, NMETA]], base=-NMETA, channel_multiplier=0,
                             allow_small_or_imprecise_dtypes=True), w=[PKAm])
    for i in range(NSLOT):
        memset(VAUG[i], VAUG[i][:, :, 64:65], 1.0)
    memset(PVA, PVA[:, :, 64:65], 1.0)
    memset(PVAm, PVAm[:, :, 64:65], 1.0)

    xbufs = {}

    def xbuf(bi):
        if bi not in xbufs:
            xbufs[bi] = Buf(None, "x%d" % bi)
        return xbufs[bi]

    wview_in = [wbf_in[l].rearrange("(kc p) n -> p kc n", p=128) for l in range(DEPTH)]
    wview_out = [wbf_out[l].rearrange("(kc p) n -> p kc n", p=128) for l in range(DEPTH)]
    wscr_bufs = {l: [Buf(None, "wscr%d_%d" % (l, i)) for i in range(5)] for l in range(DEPTH)}
    wctr = [0]
    nwb = [2]
    cur_layer = [0]

    def convert_weights(l, piece=None):
        for i in range(4):
            if piece is None or piece == i:
                dma(wbf_in[l, i * 512:(i + 1) * 512, :], w_in[l, i * 512:(i + 1) * 512, :], w=[wscr_bufs[l][i]],
                    eng="pool")
        if piece is None or piece == 4:
            dma(wbf_out[l], w_out[l], w=[wscr_bufs[l][4]], eng="pool")

    def load_w(view, c0, width, off=0, buf=None):
        if buf is None:
            buf = wb[wctr[0] % nwb[0]]
            wctr[0] += 1
        dma(buf[:, :, off:off + width], view[:, :, c0:c0 + width], r=wscr_bufs[cur_layer[0]], w=[buf], eng="sp")
        return buf

    def rms_stats(src_ap, L, n, scale, col=0):
        pass

    def chk(tag):
        if stop == tag:
            raise StopBuild()

    XA = [[f1, f2, C_f1, D_f1], [D_f3, B_f2, C_ng, D_ng]]
    a1_done = set()

    def tiles_of(bi2):
        if bi2 == 0:
            return [dict(row0=0, L=NMETA, tok0=0), dict(row0=NPT, L=NSS * LS, tok0=NMETA)]
        return [dict(row0=ch_["row0"], L=128, tok0=128 * i_) for i_, ch_ in enumerate(blocks[bi2])]

    def stage_a1(l2, bi2):
        if (l2, bi2) in a1_done:
            return
        a1_done.add((l2, bi2))
        xs2 = xin if l2 == 0 else xscr
        F_tot, F_lnv = smalls["F_tot"], smalls["F_lnv"]
        for ti, tl in enumerate(tiles_of(bi2)):
            L, r0 = tl["L"], tl["row0"]
            xa = XA[ti % 2]
            F_ssq, F_rstd = smalls["F%d_ssq" % (ti % 2)], smalls["F%d_rstd" % (ti % 2)]
            for c in range(4):
                dma(xa[c][0:L, :], xs2[r0:r0 + L, c * 512:(c + 1) * 512], r=[xbuf(bi2)], w=[xa[c]])
                act(junkC[0:L, :], xa[c][0:L, :], AF.Square, r=[xa[c]], w=[junkC, F_ssq], accum_out=F_ssq[0:L, c:c + 1])
            V(lambda L=L, F_ssq=F_ssq: nc.vector.tensor_reduce(out=F_tot[0:L, 0:1], in_=F_ssq[0:L, 0:4], axis=AX.X,
                                                               op=ALU.add), r=[F_ssq], w=[F_tot])
            act(F_lnv[0:L, 0:1], F_tot[0:L, 0:1], AF.Ln, r=[F_tot], w=[F_lnv], scale=1.0 / D, bias=EPS)
            act(F_rstd[0:L, 0:1], F_lnv[0:L, 0:1], AF.Exp, r=[F_lnv], w=[F_rstd], scale=-0.5)

    pending_epi = []

    def flush_epi():
        while pending_epi:
            pending_epi.pop(0)()

    parts_of = {}

    def multi_load(parent, pairs):
        parts = []
        for i, (o_, i_) in enumerate(pairs):
            pb = parent if i == 0 else Buf(None, "part")
            dma(o_, i_, w=[pb], eng="sp", allow_slow_non_contiguous=True)
            if i > 0:
                parts.append(pb)
        parts_of[id(parent)] = parts

    def RD(parent):
        return [parent] + parts_of.get(id(parent), [])

    def store_fm_rows(src_fn, srcbuf, nft, rows, dst2d, stg):
        for f0 in range(0, nft, 4):
            bk = nb()
            for j in range(4):
                tr(bk[0:rows, j * 128:(j + 1) * 128], src_fn(f0 + j), ident_f[:, :], r=[srcbuf, ident_f], w=[bk])
            cp(stg[0:rows, f0 * 128:(f0 + 4) * 128], bk[0:rows, 0:512], r=[bk], w=[stg])
        dma(dst2d, stg[0:rows, 0:nft * 128], r=[stg], w=[Buf(None, "fmrows")])

    def load_slow_params(l):
        p = l % 2
        multi_load(gTs[p], [(gTs[p][:], pre_norm[l].rearrange("(kc p) -> p kc", p=128))])
        multi_load(cwSs[p], [(cwSs[p][:, :, k_], ssd_conv_w[l, k_].rearrange("(ft p) -> p ft", p=128)) for k_ in range(4)])
        multi_load(cwGs[p], [(cwGs[p][:, :, k_], gdn_conv_w[l, k_].rearrange("(ft p) -> p ft", p=128)) for k_ in range(4)])
        multi_load(cbSs[p], [(cbSs[p][:], ssd_conv_b[l].rearrange("(ft p) -> p ft", p=128))])
        multi_load(cstSs[p], [(cstSs[p][:, :, s_, t_], st_ssdc[l, s_, t_].rearrange("(ft p) -> p ft", p=128))
                              for s_ in range(NSS) for t_ in range(3)])
        multi_load(cstGs[p], [(cstGs[p][:, :, s_, t_], st_gdnc[l, s_, t_].rearrange("(ft p) -> p ft", p=128))
                              for s_ in range(NSS) for t_ in range(3)])

    try:
        for l in range(n_layers):
            chk('const')
            cur_layer[0] = l
            if l == 0:
                convert_weights(0)
            xsrc = xin if l == 0 else xscr
            xdst = yout if l == n_layers - 1 else xscr
            if l == 0:
                load_slow_params(0)
            gT, cwS, cbS, cwG, cstS, cstG = [b[l % 2] for b in (gTs, cwSs, cbSs, cwGs, cstSs, cstGs)]
            dma(postg[:], bc(post_norm[l:l + 1, :], [128, D]), w=[postg])
            dma(ssdn[:], bc(ssd_norm[l:l + 1, :], [128, 512]), w=[ssdn])
            dma(retn[:], bc(ret_norm[l:l + 1, :], [128, 512]), w=[retn])
            for h in range(4):
                dma(gdnn[:, h, :], bc(gdn_norm[l:l + 1, :], [128, 128]), w=[gdnn])
            dma(dtbS[:], bc(ssd_dt_bias[l:l + 1, :], [128, 8]), w=[dtbS])
            dma(AnS[:], bc(ssd_a_log[l:l + 1, :], [128, 8]), w=[AnS])
            dma(Dss[:], bc(ssd_d[l:l + 1, :], [128, 8]), w=[Dss])
            dma(ESQ[:], bc(swa_sinks[l:l + 1, :], [128, 8]), w=[ESQ])
            dma(dtbG[:], bc(gdn_dt_bias[l:l + 1, :], [128, 4]), w=[dtbG])
            dma(AnG[:], bc(gdn_a_log[l:l + 1, :], [128, 4]), w=[AnG])
            act(AnS[:], AnS[:], AF.Exp, r=[AnS], w=[AnS])
            ts(AnS[:], AnS[:], -1.0, ALU.mult, r=[AnS], w=[AnS])
            act(AnG[:], AnG[:], AF.Exp, r=[AnG], w=[AnG])
            ts(AnG[:], AnG[:], -1.0, ALU.mult, r=[AnG], w=[AnG])
            tt(ESQ[:], ESQ[:], SLQ[:], ALU.add, r=[ESQ, SLQ], w=[ESQ])
            act(ESQ[:], ESQ[:], AF.Exp, r=[ESQ], w=[ESQ])
            memset(Sssd[0], Sssd[0][:], 0.0)
            memset(Sssdb[0], Sssdb[0][:], 0.0)
            memset(Sret[0], Sret[0][:], 0.0)
            memset(Sretb[0], Sretb[0][:], 0.0)
            memset(Sgdn[0], Sgdn[0][:], 0.0)
            memset(Sgdnb[0], Sgdnb[0][:], 0.0)
            memset(carS, carS[:], 0.0)
            memset(carG, carG[:], 0.0)

            for bi, blk in enumerate(blocks):
                is0 = (bi == 0)
                nwb[0] = 2 if is0 else 4
                tok = 0
                for si, ch in enumerate(blk):
                    ch["tok0"] = tok
                    ch["slot"] = si
                    tok += ch["L"]
                nbt = tok
                if is0:
                    tm_tiles = [dict(row0=0, L=NMETA, tok0=0), dict(row0=NPT, L=NSS * LS, tok0=NMETA)]
                else:
                    tm_tiles = [dict(row0=ch["row0"], L=128, tok0=ch["tok0"]) for ch in blk]
                if is0:
                    G(lambda: nc.gpsimd.iota(KA[64:65, :, 0:NMETA], pattern=[[0, 2], [1, NMETA]], base=0,
                                             channel_multiplier=0, allow_small_or_imprecise_dtypes=True), w=[KA])
                    G(lambda: nc.gpsimd.iota(KA[64:65, :, NMETA:NMETA + 16], pattern=[[0, 2], [0, NSS], [1, LS]], base=0,
                                             channel_multiplier=0, allow_small_or_imprecise_dtypes=True), w=[KA])
                elif bi == 1:
                    G(lambda: nc.gpsimd.iota(KA[64:65, :, :], pattern=[[0, 2], [0, cpb], [1, 128]], base=0,
                                             channel_multiplier=0, allow_small_or_imprecise_dtypes=True), w=[KA])

                chk('params')
                stage_a1(l, bi)
                for ti, tl in enumerate(tm_tiles):
                    L, r0, t0 = tl["L"], tl["row0"], tl["tok0"]
                    xa = XA[ti % 2]
                    F_rstd = smalls["F%d_rstd" % (ti % 2)]
                    for c in range(4):
                        ts(hb[0:L, c * 512:(c + 1) * 512], xa[c][0:L, :], F_rstd[0:L, 0:1], ALU.mult, r=[xa[c], F_rstd],
                           w=[hb])
                    for q in range(4):
                        bk = nb()
                        bv = bfv(bk)
                        for j in range(4):
                            kc = 4 * q + j
                            tr(bv[:, j * 128:j * 128 + L], hb[0:L, kc * 128:(kc + 1) * 128], ident_b[0:L, 0:L],
                               r=[hb, ident_b], w=[bk])
                        tt(hT[:, 4 * q:4 * q + 4, t0:t0 + L], v3(bv[:, 0:512], 4, 128)[:, :, 0:L],
                           bc(gT[:, 4 * q:4 * q + 4].unsqueeze(2), [128, 4, L]), ALU.mult, r=[bk] + RD(gT), w=[hT])
                flush_epi()

                def decay(la, L, H, decT_, negcum_, ecum_, ecl_):
                    bk = nb()
                    mm(bk[0:L, 0:H], Umat[0:L, 0:L], la[0:L, 0:H], True, True, r=[Umat, la], w=[bk])
                    ts(negcum_[0:L, 0:H], bk[0:L, 0:H], -1.0, ALU.mult, r=[bk], w=[negcum_])
                    act(ecum_[0:L, 0:H], bk[0:L, 0:H], AF.Exp, r=[bk], w=[ecum_])
                    yield
                    for hq in range(H // 4):
                        bk = nb()
                        for hh in range(4):
                            h = 4 * hq + hh
                            o = bk[:, hh * 128:hh * 128 + L]
                            mm(o, bc(la[0:L, h:h + 1], [L, 128]), Umat[0:L, 0:L], True, False, r=[la, Umat], w=[bk])
                            mm(o, ident_f[0:L, :], NEGU[0:L, 0:L], False, True, r=[ident_f, NEGU], w=[bk])
                        yield
                        for hh in range(4):
                            h = 4 * hq + hh
                            act(decT_[0:L, h, 0:L], bk[0:L, hh * 128:hh * 128 + L], AF.Exp, r=[bk, negcum_], w=[decT_],
                                bias=negcum_[0:L, h:h + 1])
                        act(ecl_[:, 4 * hq:4 * hq + 4], v3(bk[:, 0:512], 4, 128)[:, :, L - 1], AF.Exp, r=[bk], w=[ecl_])
                        yield

                def softplus_la(dst, src_ap, src_bufs, dtb, An, L, H, dtr_, keep_dt=None):
                    tt(dtr_[0:L, 0:H], src_ap, dtb[0:L, 0:H], ALU.add, r=src_bufs + [dtb], w=[dtr_])
                    act(dtr_[0:L, 0:H], dtr_[0:L, 0:H], AF.Exp, r=[dtr_], w=[dtr_])
                    tgt = keep_dt if keep_dt is not None else dtr_
                    act(tgt[0:L, 0:H], dtr_[0:L, 0:H], AF.Ln, r=[dtr_], w=[tgt], bias=1.0)
                    tt(dst[0:L, 0:H], tgt[0:L, 0:H], An[0:L, 0:H], ALU.mult, r=[tgt, An], w=[dst])

                def head_rmsnorm_gate(o_buf, junk_, ssq_, lnv_, rstd_, Yc, L, nh, hd, ng_, ycols):
                    n = nh * hd
                    for h in range(nh):
                        act(junk_[0:L, h * hd:(h + 1) * hd], o_buf[0:L, h * hd:(h + 1) * hd], AF.Square, r=[o_buf],
                            w=[junk_, ssq_], accum_out=ssq_[0:L, h:h + 1])
                    act(lnv_[0:L, 0:nh], ssq_[0:L, 0:nh], AF.Ln, r=[ssq_], w=[lnv_], scale=1.0 / hd, bias=EPS)
                    act(rstd_[0:L, 0:nh], lnv_[0:L, 0:nh], AF.Exp, r=[lnv_], w=[rstd_], scale=-0.5)
                    yield
                    tt(v3(o_buf[0:L, 0:n], nh, hd), v3(o_buf[0:L, 0:n], nh, hd), bc(rstd_[0:L, 0:nh].unsqueeze(2), [L, nh, hd]),
                       ALU.mult, r=[o_buf, rstd_], w=[o_buf])
                    tt(Yc[0:L, ycols:ycols + n], o_buf[0:L, 0:n], ng_[0:L, 0:n], ALU.mult, r=[o_buf, ng_], w=[Yc])

                def chunk_ctx(ch):
                    sid = 0 if ch["kind"] == "p" else 1
                    return ch["L"], ch["tok0"], ch["slot"], sid, ch.get("seq", None), Gs[ch["slot"]], Yb[ch["slot"] % 2]

                def ssd_thread():
                    A = lambda n: smalls["A_" + n]
                    ssq_, lnv_, rstd_, negcum_, ecum_, ecl_, dtr_, dtv_, lav_ = [A(n) for n in (
                        "ssq", "lnv", "rstd", "negcum", "ecum", "ecl", "dtr", "dtv", "lav")]
                    for ch in blk:
                        L, t0, slot, sid, seq, G_, Yc = chunk_ctx(ch)
                        while slot >= 2 and not fin.get(slot - 2):
                            yield
                        S1, S1b = Sssd[sid], Sssdb[sid]
                        if sid == 1:
                            dma(S1[:], st_ssd[l, seq].rearrange("h n e -> n h e"), w=[S1])
                            cp(S1b[:], S1[:], r=[S1], w=[S1b], eng="act")
                        softplus_la(lav_, SM[slot][0:L, 0:8], [SM[slot]], dtbS, AnS, L, 8, dtr_, keep_dt=dtv_)
                        yield
                        yield from decay(lav_, L, 8, decT, negcum_, ecum_, ecl_)
                        yield ("wait_proj",)
                        bk = nb()
                        bv = bfv(bk)
                        for ft in range(4):
                            tr(bv[0:L, ft * 128:(ft + 1) * 128], XBCt[ft][:, t0:t0 + L], ident_b[:, :], r=[XBCt[ft], ident_b], w=[bk])
                        yield
                        cp(xs_tm[0:L, :], bv[0:L, 0:512], r=[bk], w=[xs_tm], eng="act")
                        tt(v3(xdt[0:L, :], 8, 64), v3(bv[0:L, 0:512], 8, 64), bc(dtv_[0:L, 0:8].unsqueeze(2), [L, 8, 64]),
                           ALU.mult, r=[bk, dtv_], w=[xdt])
                        bk = nb()
                        bv = bfv(bk)
                        for g in range(2):
                            tr(bv[0:L, g * 128:(g + 1) * 128], XBCt[4 + g][:, t0:t0 + L], ident_b[:, :], r=[XBCt[4 + g], ident_b], w=[bk])
                        yield
                        cp(Btm[0:L, :], bv[0:L, 0:256], r=[bk], w=[Btm], eng="act")
                        bk = nb()
                        for g in range(2):
                            mm(bk[0:L, g * 128:g * 128 + L], XBCt[4 + g][:, t0:t0 + L], XBCt[6 + g][:, t0:t0 + L], True, True,
                               r=[XBCt[4 + g], XBCt[6 + g]], w=[bk])
                        yield
                        for g in range(2):
                            tt(MT[0:L, 4 * g:4 * g + 4, 0:L], bc(bk[0:L, g * 128:g * 128 + L].unsqueeze(1), [L, 4, L]),
                               decT[0:L, 4 * g:4 * g + 4, 0:L], ALU.mult, r=[bk, decT], w=[MT])
                        yield
                        bki = nb()
                        for h in range(8):
                            mm(bki[0:L, h * 64:(h + 1) * 64], MT[0:L, h, 0:L], xdt[0:L, h * 64:(h + 1) * 64], True, True,
                               r=[MT, xdt], w=[bki])
                        bks = nb()
                        for h in range(8):
                            mm(bks[0:L, h * 64:(h + 1) * 64], XBCt[6 + h // 4][:, t0:t0 + L], S1b[:, h, :], True, True,
                               r=[XBCt[6 + h // 4], S1b], w=[bks])
                        yield
                        tt(v3(f1[0:L, :], 8, 64), v3(bks[0:L, :], 8, 64), bc(ecum_[0:L, 0:8].unsqueeze(2), [L, 8, 64]), ALU.mult,
                           r=[bks, ecum_], w=[f1])
                        tt(f1[0:L, :], bki[0:L, :], f1[0:L, :], ALU.add, r=[bki, f1], w=[f1])
                        tt(v3(f2[0:L, :], 8, 64), v3(xs_tm[0:L, :], 8, 64), bc(Dss[0:L, 0:8].unsqueeze(2), [L, 8, 64]), ALU.mult,
                           r=[xs_tm, Dss], w=[f2], eng="pool")
                        yield
                        tt(f1[0:L, :], f1[0:L, :], f2[0:L, :], ALU.add, r=[f1, f2], w=[f1])
                        tt(f1[0:L, :], f1[0:L, :], G_[0:L, 0, :], ALU.mult, r=[f1, G_], w=[f1])
                        for g in range(2):
                            act(junkA[0:L, g * 256:(g + 1) * 256], f1[0:L, g * 256:(g + 1) * 256], AF.Square, r=[f1],
                                w=[junkA, ssq_], accum_out=ssq_[0:L, g:g + 1])
                        yield
                        act(lnv_[0:L, 0:2], ssq_[0:L, 0:2], AF.Ln, r=[ssq_], w=[lnv_], scale=1.0 / 256, bias=EPS)
                        act(rstd_[0:L, 0:2], lnv_[0:L, 0:2], AF.Exp, r=[lnv_], w=[rstd_], scale=-0.5)
                        yield
                        tt(v3(f2[0:L, :], 2, 256), v3(f1[0:L, :], 2, 256), bc(rstd_[0:L, 0:2].unsqueeze(2), [L, 2, 256]),
                           ALU.mult, r=[f1, rstd_], w=[f2])
                        tt(Yc[0:L, 0:512], f2[0:L, :], ssdn[0:L, :], ALU.mult, r=[f2, ssdn], w=[Yc])
                        yield
                        tt(v3(xdtw[0:L, :], 8, 64), v3(xdt[0:L, :], 8, 64), bc(decT[0:L, 0:8, L - 1:L], [L, 8, 64]), ALU.mult,
                           r=[xdt, decT], w=[xdtw])
                        bkn = nb()
                        for h in range(8):
                            mm(bkn[:, h * 64:(h + 1) * 64], Btm[0:L, (h // 4) * 128:(h // 4 + 1) * 128],
                               xdtw[0:L, h * 64:(h + 1) * 64], True, True, r=[Btm, xdtw], w=[bkn])
                        tt(S1[:], S1[:], bc(ecl_[:, 0:8].unsqueeze(2), [128, 8, 64]), ALU.mult, r=[S1, ecl_], w=[S1])
                        yield
                        tt(S1[:], v3(bkn[:, :], 8, 64), S1[:], ALU.add, r=[bkn, S1], w=[S1])
                        cp(S1b[:], S1[:], r=[S1], w=[S1b], eng="act")
                        if sid == 1:
                            dma(o_s_ssd[l, seq].rearrange("h n e -> n h e"), S1[:], r=[S1], w=[dbuf("o_s_ssd")])
                        yield ("done", ch["slot"])

                def ret_thread():
                    A = lambda n: smalls["C_" + n]
                    ssq_, lnv_, rstd_ = A("ssq"), A("lnv"), A("rstd")
                    for ch in blk:
                        L, t0, slot, sid, seq, G_, Yc = chunk_ctx(ch)
                        while slot >= 2 and not fin.get(slot - 2):
                            yield
                        S2, S2b = Sret[sid], Sretb[sid]
                        tt(C_ng[0:L, :], retn[0:L, :], G_[0:L, 2, :], ALU.mult, r=[retn, G_], w=[C_ng], eng="pool")
                        yield ("wait_proj",)
                        if sid == 1:
                            dma(S2[:], st_ret[l, seq].rearrange("h d e -> d h e"), w=[S2])
                            cp(S2b[:], S2[:], r=[S2], w=[S2b], eng="act")
                        bk = nb()
                        for h in range(4):
                            mm(bk[0:L, h * 128:h * 128 + L], KRF[0:64, h, t0:t0 + L], QR[0:64, h, t0:t0 + L], True, True,
                               r=[KRF, QR], w=[bk])
                        yield
                        tt(C_MT[0:L, 0:4, 0:L], v3(bk[0:L, :], 4, 128)[:, :, 0:L], RDEC[0:L, :, 0:L], ALU.mult, r=[bk, RDEC],
                           w=[C_MT])
                        yield
                        bki = nb()
                        for h in range(4):
                            mm(bki[0:L, h * 128:(h + 1) * 128], C_MT[0:L, h, 0:L], VR[slot][0:L, h, :], True, True,
                               r=[C_MT, VR[slot]], w=[bki])
                        bks = nb()
                        for h in range(4):
                            mm(bks[0:L, h * 128:(h + 1) * 128], QR[0:64, h, t0:t0 + L], S2b[:, h, :], True, True,
                               r=[QR, S2b], w=[bks])
                        yield
                        tt(v3(C_f1[0:L, :], 4, 128), v3(bks[0:L, :], 4, 128), bc(GPOW[0:L, 0:4].unsqueeze(2), [L, 4, 128]),
                           ALU.mult, r=[bks, GPOW], w=[C_f1])
                        tt(C_f1[0:L, :], bki[0:L, :], C_f1[0:L, :], ALU.add, r=[bki, C_f1], w=[C_f1])
                        yield
                        yield from head_rmsnorm_gate(C_f1, junkC, ssq_, lnv_, rstd_, Yc, L, 4, 128, C_ng, 1024)
                        yield
                        tt(kw[0:L, :, :], KR[slot][0:L, :, :], bc(RW[L][0:L, 0:4].unsqueeze(2), [L, 4, 64]), ALU.mult,
                           r=[KR[slot], RW[L]], w=[kw], eng="pool")
                        bkn = nb()
                        for h in range(4):
                            mm(bkn[0:64, h * 128:(h + 1) * 128], kw[0:L, h, :], VR[slot][0:L, h, :], True, True,
                               r=[kw, VR[slot]], w=[bkn])
                        tt(S2[:], S2[:], bc(GL[L][:, 0:4].unsqueeze(2), [64, 4, 128]), ALU.mult, r=[S2, GL[L]], w=[S2])
                        yield
                        tt(S2[:], v3(bkn[0:64, :], 4, 128), S2[:], ALU.add, r=[bkn, S2], w=[S2])
                        cp(S2b[:], S2[:], r=[S2], w=[S2b], eng="act")
                        if sid == 1:
                            dma(o_s_ret[l, seq].rearrange("h d e -> d h e"), S2[:], r=[S2], w=[dbuf("o_s_ret")])
                        yield ("done", ch["slot"])

                def gdn_thread():
                    A = lambda n: smalls["D_" + n]
                    ssq_, lnv_, rstd_, negcum_, ecum_, ecl_, dtr_, lav_ = [A(n) for n in (
                        "ssq", "lnv", "rstd", "negcum", "ecum", "ecl", "dtr", "lav")]
                    M1 = Pm[1]
                    for ch in blk:
                        L, t0, slot, sid, seq, G_, Yc = chunk_ctx(ch)
                        while slot >= 2 and not fin.get(slot - 2):
                            yield
                        S3, S3b = Sgdn[sid], Sgdnb[sid]
                        tt(D_ng[0:L, :], gdnn[0:L, :, :].rearrange("p a b -> p (a b)"), G_[0:L, 3, :], ALU.mult, r=[gdnn, G_],
                           w=[D_ng], eng="pool")
                        if sid == 1:
                            dma(S3[:], st_gdn[l, seq].rearrange("h d e -> d h e"), w=[S3])
                            cp(S3b[:], S3[:], r=[S3], w=[S3b], eng="act")
                        act(beta[0:L, :], SM[slot][0:L, 8:12], AF.Exp, r=[SM[slot]], w=[beta], scale=-1.0)
                        ts(beta[0:L, :], beta[0:L, :], 1.0, ALU.add, r=[beta], w=[beta])
                        V(lambda L=L: nc.vector.reciprocal(out=beta[0:L, :], in_=beta[0:L, :]), r=[beta], w=[beta])
                        ts(nbeta[0:L, :], beta[0:L, :], -1.0, ALU.mult, r=[beta], w=[nbeta])
                        yield
                        softplus_la(lav_, SM[slot][0:L, 12:16], [SM[slot]], dtbG, AnG, L, 4, dtr_)
                        yield
                        yield from decay(lav_, L, 4, D_decT, negcum_, ecum_, ecl_)
                        yield ("wait_proj",)
                        bk = nb()
                        bv = bfv(bk)
                        for h in range(4):
                            tr(bv[0:L, h * 128:(h + 1) * 128], QKVDt[4 + h][:, t0:t0 + L], ident_b[:, :], r=[QKVDt[4 + h], ident_b],
                               w=[bk])
                        yield
                        tt(v3(knw[0:L, :], 4, 128), v3(bv[0:L, 0:512], 4, 128), bc(D_decT[0:L, 0:4, L - 1:L], [L, 4, 128]),
                           ALU.mult, r=[bk, D_decT], w=[knw])
                        bk = nb()
                        bv = bfv(bk)
                        for h in range(4):
                            tr(bv[0:L, h * 128:(h + 1) * 128], QKVDt[8 + h][:, t0:t0 + L], ident_b[:, :], r=[QKVDt[8 + h], ident_b],
                               w=[bk])
                        yield
                        cp(Vtm[0:L, :], bv[0:L, 0:512], r=[bk], w=[Vtm], eng="act")
                        bkg = nb()
                        bkq = nb()
                        for h in range(4):
                            mm(bkg[0:L, h * 128:h * 128 + L], QKVDt[4 + h][:, t0:t0 + L], QKVDt[4 + h][:, t0:t0 + L], True, True,
                               r=[QKVDt[4 + h]], w=[bkg])
                        for h in range(4):
                            mm(bkq[0:L, h * 128:h * 128 + L], QKVDt[4 + h][:, t0:t0 + L], QKVDt[h][:, t0:t0 + L], True, True,
                               r=[QKVDt[4 + h], QKVDt[h]], w=[bkq])
                        yield
                        tt(M1[0:L, :, 0:L], v3(bkg[0:L, :], 4, 128)[:, :, 0:L], D_decT[0:L, 0:4, 0:L], ALU.mult,
                           r=[bkg, D_decT], w=[M1])
                        tt(QKd[0:L, :, 0:L], v3(bkq[0:L, :], 4, 128)[:, :, 0:L], D_decT[0:L, 0:4, 0:L], ALU.mult,
                           r=[bkq, D_decT], w=[QKd])
                        yield
                        tt(M1[0:L, :, 0:L], M1[0:L, :, 0:L], bc(nbeta[0:L, 0:4].unsqueeze(2), [L, 4, L]), ALU.mult,
                           r=[M1, nbeta], w=[M1])
                        P0_, PT0_ = Pm[0], PTm[0]
                        tt(P0_[0:L, :, 0:L], M1[0:L, :, 0:L], bc(SUm[0:L, 0:L].unsqueeze(1), [L, 4, L]), ALU.mult,
                           r=[M1, SUm], w=[P0_])
                        yield
                        bk = nb()
                        for h in range(4):
                            tr(bk[0:L, h * 128:h * 128 + L], P0_[0:L, h, 0:L], ident_f[0:L, 0:L], r=[P0_, ident_f], w=[bk])
                        yield
                        cp(PT0_[0:L, :, 0:L], v3(bk[0:L, :], 4, 128)[:, :, 0:L], r=[bk], w=[PT0_], eng="act")
                        R_ = Rm[0]
                        tt(R_[0:L, :, 0:L], P0_[0:L, :, 0:L], bc(ident_f[0:L, 0:L].unsqueeze(1), [L, 4, L]), ALU.add,
                           r=[P0_, ident_f], w=[R_], eng="pool")
                        yield
                        nlev = max(1, int(math.ceil(math.log2(L))))
                        cur = 0
                        for k in range(1, nlev):
                            Pc, PTc = Pm[cur], PTm[cur]
                            Pn, PTn = Pm[1 - cur], PTm[1 - cur]
                            last = (k == nlev - 1)
                            bkt = nb()
                            for h in range(4):
                                mm(bkt[0:L, h * 128:h * 128 + L], Pc[0:L, h, 0:L], PTc[0:L, h, 0:L], True, True,
                                   r=[Pc, PTc], w=[bkt])
                            if not last:
                                bkp = nb()
                                for h in range(4):
                                    mm(bkp[0:L, h * 128:h * 128 + L], PTc[0:L, h, 0:L], Pc[0:L, h, 0:L], True, True,
                                       r=[Pc, PTc], w=[bkp])
                            yield
                            cp(PTn[0:L, :, 0:L], v3(bkt[0:L, :], 4, 128)[:, :, 0:L], r=[bkt], w=[PTn], eng="act")
                            if not last:
                                cp(Pn[0:L, :, 0:L], v3(bkp[0:L, :], 4, 128)[:, :, 0:L], r=[bkp], w=[Pn])
                            yield
                            Rc, Rn = Rm[cur], Rm[1 - cur]
                            bkr = nb()
                            for h in range(4):
                                mm(bkr[0:L, h * 128:h * 128 + L], PTn[0:L, h, 0:L], Rc[0:L, h, 0:L], True, True,
                                   r=[PTn, Rc], w=[bkr])
                            yield
                            tt(Rn[0:L, :, 0:L], v3(bkr[0:L, :], 4, 128)[:, :, 0:L], Rc[0:L, :, 0:L], ALU.add, r=[bkr, Rc],
                               w=[Rn])
                            yield
                            cur = 1 - cur
                        Rf = Rm[cur]
                        bk = nb()
                        for h in range(4):
                            mm(bk[0:L, h * 128:(h + 1) * 128], QKVDt[4 + h][:, t0:t0 + L], S3b[:, h, :], True, True,
                               r=[QKVDt[4 + h], S3b], w=[bk])
                        yield
                        tt(v3(D_f1[0:L, :], 4, 128), v3(bk[0:L, :], 4, 128), bc(ecum_[0:L, 0:4].unsqueeze(2), [L, 4, 128]),
                           ALU.mult, r=[bk, ecum_], w=[D_f1])
                        tt(D_f3[0:L, :], Vtm[0:L, :], D_f1[0:L, :], ALU.subtract, r=[Vtm, D_f1], w=[D_f3])
                        yield
                        bk = nb()
                        for h in range(4):
                            mm(bk[0:L, h * 128:(h + 1) * 128], Rf[0:L, h, 0:L], D_f3[0:L, h * 128:(h + 1) * 128], True, True,
                               r=[Rf, D_f3], w=[bk])
                        yield
                        tt(v3(vnew[0:L, :], 4, 128), v3(bk[0:L, :], 4, 128), bc(beta[0:L, 0:4].unsqueeze(2), [L, 4, 128]),
                           ALU.mult, r=[bk, beta], w=[vnew])
                        yield
                        bks = nb()
                        for h in range(4):
                            mm(bks[0:L, h * 128:(h + 1) * 128], QKVDt[h][:, t0:t0 + L], S3b[:, h, :], True, True,
                               r=[QKVDt[h], S3b], w=[bks])
                        bki = nb()
                        for h in range(4):
                            mm(bki[0:L, h * 128:(h + 1) * 128], QKd[0:L, h, 0:L], vnew[0:L, h * 128:(h + 1) * 128], True, True,
                               r=[QKd, vnew], w=[bki])
                        yield
                        tt(v3(D_f1[0:L, :], 4, 128), v3(bks[0:L, :], 4, 128), bc(ecum_[0:L, 0:4].unsqueeze(2), [L, 4, 128]),
                           ALU.mult, r=[bks, ecum_], w=[D_f1])
                        tt(D_f1[0:L, :], bki[0:L, :], D_f1[0:L, :], ALU.add, r=[bki, D_f1], w=[D_f1])
                        yield
                        yield from head_rmsnorm_gate(D_f1, junkD, ssq_, lnv_, rstd_, Yc, L, 4, 128, D_ng, 1536)
                        yield
                        bkn = nb()
                        for h in range(4):
                            mm(bkn[:, h * 128:(h + 1) * 128], knw[0:L, h * 128:(h + 1) * 128],
                               vnew[0:L, h * 128:(h + 1) * 128], True, True, r=[knw, vnew], w=[bkn])
                        tt(S3[:], S3[:], bc(ecl_[:, 0:4].unsqueeze(2), [128, 4, 128]), ALU.mult, r=[S3, ecl_], w=[S3])
                        yield
                        tt(S3[:], v3(bkn[:, :], 4, 128), S3[:], ALU.add, r=[bkn, S3], w=[S3])
                        cp(S3b[:], S3[:], r=[S3], w=[S3b], eng="act")
                        if sid == 1:
                            dma(o_s_gdn[l, seq].rearrange("h d e -> d h e"), S3[:], r=[S3], w=[dbuf("o_s_gdn")])
                        yield ("done", ch["slot"])

                def swa_thread():
                    for ch in blk:
                        L, t0, slot, sid, seq, G_, Yc = chunk_ctx(ch)
                        while slot >= 2 and not fin.get(slot - 2):
                            yield
                        if sid == 1:
                            dma(cK[:, 0:128], st_k[l, seq], w=[cK])
                            dma(cK[:, 128:256], st_v[l, seq], w=[cK])
                            cp(cKb[:, :], cK[:, 0:128], r=[cK], w=[cKb])
                            bk = nb()
                            bv = bfv(bk)
                            for g in range(2):
                                tr(bv[0:64, g * 128:(g + 1) * 128], cKb[:, g * 64:(g + 1) * 64], ident_b[:, :],
                                   r=[cKb, ident_b], w=[bk])
                            yield
                            cp(PKA[0:64, :, :], v3(bv[0:64, 0:256], 2, 128), r=[bk], w=[PKA])
                            cp(PVA[:, :, 0:64], v3(cK[:, 128:256], 2, 64), r=[cK], w=[PVA])
                            prev = (PKA, PVA, 128, NEGprev)
                        elif ch["ci"] == 0:
                            prev = None
                        elif ch["ci"] == 1:
                            prev = (PKAm, PVAm, NMETA, NEGmeta)
                        else:
                            prev = (PKA, PVA, 128, NEGprev)
                        for g in range(2):
                            tiles = []
                            if prev is not None:
                                tiles.append((prev[0], prev[0][0:65, g, 0:prev[2]], prev[1], prev[1][0:prev[2], g, 0:65],
                                              prev[2], prev[3]))
                            tiles.append((KA, KA[0:65, g, t0:t0 + L], VAUG[slot], VAUG[slot][0:L, g, 0:65], L, NEGown))
                            pts = []
                            for ti, (kbuf, kap, vbuf, vap, Lk, negm) in enumerate(tiles):
                                bk = nb()
                                for hh in range(4):
                                    h = 4 * g + hh
                                    o = bk[0:Lk, hh * 128:hh * 128 + L]
                                    mm(o, kap, QA[0:65, h, t0:t0 + L], True, False, r=[kbuf, QA], w=[bk])
                                    mm(o, ident_b[0:Lk, 0:Lk], negm[0:Lk, 0:L], False, True, r=[ident_b, negm], w=[bk])
                                yield
                                pt = PTs[2 * g + ti] if len(tiles) == 2 else PTs[2 * g + 1]
                                act(pt[0:Lk, :, 0:L], v3(bk[0:Lk, :], 4, 128)[:, :, 0:L], AF.Exp, r=[bk], w=[pt], scale=0.125)
                                pts.append((pt, vbuf, vap, Lk))
                                yield
                            bko = nb()
                            for hh in range(4):
                                for ti, (pt, vbuf, vap, Lk) in enumerate(pts):
                                    mm(bko[0:L, hh * 72:hh * 72 + 65], pt[0:Lk, hh, 0:L], vap, ti == 0, ti == len(pts) - 1,
                                       r=[pt, vbuf], w=[bko])
                            yield
                            ov = v3(bko[0:L, 0:288], 4, 72)
                            tt(den[0:L, 4 * g:4 * g + 4], ov[:, :, 64], ESQ[0:L, 4 * g:4 * g + 4], ALU.add, r=[bko, ESQ],
                               w=[den])
                            V(lambda L=L, g=g: nc.vector.reciprocal(out=den[0:L, 4 * g:4 * g + 4],
                                                                    in_=den[0:L, 4 * g:4 * g + 4]), r=[den], w=[den])
                            tt(v3(B_f2[0:L, 256 * g:256 * g + 256], 4, 64), ov[:, :, 0:64],
                               bc(den[0:L, 4 * g:4 * g + 4].unsqueeze(2), [L, 4, 64]), ALU.mult, r=[bko, den], w=[B_f2])
                            yield
                        tt(Yc[0:L, 512:1024], B_f2[0:L, :], G_[0:L, 1, :], ALU.mult, r=[B_f2, G_], w=[Yc])
                        if sid == 0:
                            if ch["ci"] == 0:
                                cp(PKAm[0:64, :, 0:L], KA[0:64, :, t0:t0 + L], r=[KA], w=[PKAm], eng="pool")
                                cp(PVAm[0:L, :, 0:64], VAUG[slot][0:L, :, 0:64], r=[VAUG[slot]], w=[PVAm], eng="pool")
                            else:
                                cp(PKA[0:64, :, 0:L], KA[0:64, :, t0:t0 + L], r=[KA], w=[PKA], eng="pool")
                                cp(PVA[0:L, :, 0:64], VAUG[slot][0:L, :, 0:64], r=[VAUG[slot]], w=[PVA], eng="pool")
                        if sid == 1:
                            dma(o_s_k[l, seq, 0:128 - LS, :], st_k[l, seq, LS:128, :], w=[dbuf("o_s_k")])
                            dma(o_s_v[l, seq, 0:128 - LS, :], st_v[l, seq, LS:128, :], w=[dbuf("o_s_v")])
                            dma(o_s_k[l, seq, 128 - LS:128, :], KV[slot][0:LS, 0:128], r=[KV[slot]], w=[dbuf("o_s_k")])
                            dma(o_s_v[l, seq, 128 - LS:128, :], KV[slot][0:LS, 128:256], r=[KV[slot]], w=[dbuf("o_s_v")])
                        elif ch["ci"] == 16:
                            dma(o_p_k[l], KV[slot][:, 0:128], r=[KV[slot]], w=[dbuf("o_p_k")])
                            dma(o_p_v[l], KV[slot][:, 128:256], r=[KV[slot]], w=[dbuf("o_p_v")])
                        yield ("done", ch["slot"])

                def finish_chunk(ch):
                    L, t0, slot, sid, seq, G_, Yc = chunk_ctx(ch)
                    if dbg:
                        dma(ydbg[ch["row0"]:ch["row0"] + L, :], Yc[0:L, :], r=[Yc], w=[dbuf("ydbg")])
                    for q in range(4):
                        bk = nb()
                        bv = bfv(bk)
                        for j in range(4):
                            kc = 4 * q + j
                            tr(bv[:, j * 128:j * 128 + L], Yc[0:L, kc * 128:(kc + 1) * 128], ident_b[0:L, 0:L],
                               r=[Yc, ident_b], w=[bk])
                        cp(hT[:, 4 * q:4 * q + 4, t0:t0 + L], v3(bv[:, 0:512], 4, 128)[:, :, 0:L], r=[bk], w=[hT],
                           eng="act" if q % 2 else "dve")

                chk('A')
                def tm_group(wbuf, off, n, evac):
                    for ch in blk:
                        L, t0, slot = ch["L"], ch["tok0"], ch["slot"]
                        bk = nb()
                        for kc in range(KC):
                            mm(bk[0:L, 0:n], hT[:, kc, t0:t0 + L], wbuf[:, kc, off:off + n], kc == 0, kc == KC - 1,
                               r=[hT, wbuf], w=[bk])
                        evac(bk, ch)

                def fm_group(wbuf, off, M, evac):
                    bk = nb()
                    for kc in range(KC):
                        mm(bk[0:M, 0:nbt], wbuf[:, kc, off:off + M], hT[:, kc, 0:nbt], kc == 0, kc == KC - 1,
                           r=[hT, wbuf], w=[bk])
                    evac(bk)

                def gate_evac(gi, half):
                    def f(bk, ch):
                        L = ch["L"]
                        act(Gs[ch["slot"]][0:L, gi, half * 256:(half + 1) * 256], bk[0:L, 0:256], AF.Silu, r=[bk],
                            w=[Gs[ch["slot"]]])
                    return f

                cctr = [0]

                def conv_evac(dst, ft, cw, cb, car, cst, cso):
                    def f(bk):
                        k = cctr[0] % 2
                        cctr[0] += 1
                        S_, A_ = ST[k], ACC[k]
                        if not is0:
                            n = nbt
                            cp(S_[:, 0:3], car[:, ft, :], r=[car], w=[S_], eng="pool")
                            cp(S_[:, 3:3 + n], bk[:, 0:n], r=[bk], w=[S_], eng="act")
                            cp(car[:, ft, :], S_[:, n:n + 3], r=[S_], w=[car], eng="pool")
                            no = n
                        else:
                            memset(S_, S_[:, 0:3], 0.0)
                            sv = v3(S_[:, 19:47], NSS, 7)
                            cp(sv[:, :, 0:3], cst[:, ft, :, :], r=RD(cst), w=[S_], eng="pool")
                            cp(S_[:, 3:19], bk[:, 0:16], r=[bk], w=[S_], eng="act")
                            cp(sv[:, :, 3:7], v3(bk[:, 16:32], NSS, LS), r=[bk], w=[S_], eng="act")
                            cp(car[:, ft, :], S_[:, 16:19], r=[S_], w=[car], eng="pool")
                            cp(cso[:, ft, :, :], sv[:, :, 4:7], r=[S_], w=[cso], eng="pool")
                            no = 44
                        ts(A_[:, 0:no], S_[:, 0:no], cw[:, ft, 0:1], ALU.mult, r=[S_] + RD(cw), w=[A_])
                        for kk in range(1, 4):
                            stt(A_[:, 0:no], S_[:, kk:kk + no], cw[:, ft, kk:kk + 1], A_[:, 0:no], ALU.mult, ALU.add,
                                r=[S_, A_] + RD(cw), w=[A_])
                        bias = cb[:, ft:ft + 1] if cb is not None else None
                        rr = [A_] + (RD(cb) if cb is not None else [])
                        if not is0:
                            act(dst[ft][:, 0:no], A_[:, 0:no], AF.Silu, r=rr, w=[dst[ft]], bias=bias)
                        else:
                            act(dst[ft][:, 0:16], A_[:, 0:16], AF.Silu, r=rr, w=[dst[ft]], bias=bias)
                            act(v3(dst[ft][:, 16:32], NSS, LS), v3(A_[:, 19:47], NSS, 7)[:, :, 0:4], AF.Silu, r=rr,
                                w=[dst[ft]], bias=bias)
                    return f

                wv = wview_in[l]
                import os as _os
                _gl = ((0, C_Z), (1, C_GA), (2, C_GR), (3, C_GD))
                if _os.environ.get('KSKIPG'):
                    _gl = ()
                if _os.environ.get('KDUPG'):
                    _gl = _gl + _gl
                for gi, c0 in _gl:
                    for half in range(2):
                        wbuf = load_w(wv, c0 + half * 256, 256)
                        tm_group(wbuf, 0, 256, gate_evac(gi, half))
                chk('B1')
                wbuf = load_w(wv, C_DT, 8)
                load_w(wv, C_BD, 8, off=8, buf=wbuf)

                def small_evac(bk, ch):
                    L = ch["L"]
                    cp(SM[ch["slot"]][0:L, :], bk[0:L, 0:16], r=[bk], w=[SM[ch["slot"]]])
                tm_group(wbuf, 0, 16, small_evac)
                fin = {}
                done_cnt = {}
                th_ssd, th_gdn, th_swa, th_ret = ssd_thread(), gdn_thread(), swa_thread(), ret_thread()
                early = [th_ssd, th_gdn]
                while early:
                    for th in list(early):
                        v = next(th)
                        if v is not None and v[0] == "wait_proj":
                            early.remove(th)
                chk('B2')
                wbuf = load_w(wv, C_KA, 256)

                def kv_evac(bk, ch):
                    L, slot = ch["L"], ch["slot"]
                    _m = _os.environ.get('KVMODE', '0')
                    if _m in ('0', '1'):
                        cp(KV[slot][0:L, :], bk[0:L, 0:256], r=[bk], w=[KV[slot]], eng="act")
                    if _m in ('0', '2'):
                        cp(VAUG[slot][0:L, :, 0:64], v3(bk[0:L, 128:256], 2, 64), r=[bk] + ([KV[slot]] if _os.environ.get('KVSER') else []), w=[VAUG[slot]])
                tm_group(wbuf, 0, 256, kv_evac)
                chk('B2a')
                for g in range(2):
                    fm_group(wbuf, g * 64, 64,
                             lambda bk, g=g: cp(KA[0:64, g, 0:nbt], bk[0:64, 0:nbt], r=[bk], w=[KA]))
                chk('B3')
                for half in range(2):
                    wbuf = load_w(wv, C_QA + half * 256, 256)
                    for hh in range(4):
                        h = half * 4 + hh
                        fm_group(wbuf, hh * 64, 64,
                                 lambda bk, h=h: cp(QA[0:64, h, 0:nbt], bk[0:64, 0:nbt], r=[bk], w=[QA],
                                                    eng="act" if h % 2 else "dve"))
                chk('B4')
                wbuf = load_w(wv, C_QR, 256)
                for h in range(4):
                    fm_group(wbuf, h * 64, 64,
                             lambda bk, h=h: cp(QR[0:64, h, 0:nbt], bk[0:64, 0:nbt], r=[bk], w=[QR]))
                wbuf = load_w(wv, C_KR, 256)
                for h in range(4):
                    fm_group(wbuf, h * 64, 64,
                             lambda bk, h=h: ts(KRF[0:64, h, 0:nbt], bk[0:64, 0:nbt], 0.125, ALU.mult, r=[bk], w=[KRF]))

                def kr_evac(bk, ch):
                    L, slot = ch["L"], ch["slot"]
                    ts(KR[slot][0:L, :, :], v3(bk[0:L, 0:256], 4, 64), 0.125, ALU.mult, r=[bk], w=[KR[slot]])
                tm_group(wbuf, 0, 256, kr_evac)
                chk('B5')
                for half in range(2):
                    wbuf = load_w(wv, C_VR + half * 256, 256)

                    def vr_evac(bk, ch, half=half):
                        L, slot = ch["L"], ch["slot"]
                        cp(VR[slot][0:L, 2 * half:2 * half + 2, :], v3(bk[0:L, 0:256], 2, 128), r=[bk], w=[VR[slot]],
                           eng="act")
                    tm_group(wbuf, 0, 256, vr_evac)
                chk('B6')
                for q in range(4):
                    wbuf = load_w(wv, C_XBC + q * 256, 256)
                    for j in range(2):
                        ft = 2 * q + j
                        fm_group(wbuf, j * 128, 128, conv_evac(XBCt, ft, cwS, cbS, carS, cstS, csoS))
                chk('B7')
                for q in range(6):
                    wbuf = load_w(wv, C_QKVD + q * 256, 256)
                    for j in range(2):
                        ft = 2 * q + j
                        fm_group(wbuf, j * 128, 128, conv_evac(QKVDt, ft, cwG, None, carG, cstG, csoG))
                chk('B8')
                for ft in range(8):
                    k = ft % 2
                    S_, A_ = ST[k], ACC[k]
                    sq_ = (sqj, junkD)[k]
                    Q_ = QKVDt[ft]
                    tt(sq_[:, 0:nbt], Q_[:, 0:nbt], Q_[:, 0:nbt], ALU.mult, r=[Q_], w=[sq_], eng="pool")
                    bk = nb()
                    mm(bk[:, 0:nbt], ones_b[:, :], sq_[:, 0:nbt], True, True, r=[ones_b, sq_], w=[bk])
                    act(A_[:, 0:nbt], bk[:, 0:nbt], AF.Ln, r=[bk], w=[A_], bias=EPS)
                    act(A_[:, 0:nbt], A_[:, 0:nbt], AF.Exp, r=[A_], w=[A_], scale=-0.5,
                        bias=(math.log(128.0 ** -0.5) if ft < 4 else 0.0))
                    tt(Q_[:, 0:nbt], Q_[:, 0:nbt], A_[:, 0:nbt], ALU.mult, r=[Q_, A_], w=[Q_])

                chk('B')
                if l + 1 < n_layers and len(blocks) >= 7:
                    for pc_ in {1: (0, 1), 2: (2,), 3: (3,), 4: (4,)}.get(bi, ()):
                        convert_weights(l + 1, piece=pc_)
                    if bi == 6:
                        load_slow_params(l + 1)
                elif l + 1 < n_layers and bi == len(blocks) - 1:
                    convert_weights(l + 1)
                    load_slow_params(l + 1)
                threads = [th_gdn, th_ssd, th_swa, th_ret]
                while threads:
                    for th in list(threads):
                        try:
                            v = next(th)
                        except StopIteration:
                            threads.remove(th)
                            continue
                        if v is not None and v[0] == "done":
                            done_cnt[v[1]] = done_cnt.get(v[1], 0) + 1
                            if done_cnt[v[1]] == 4:
                                finish_chunk(blk[v[1]])
                                fin[v[1]] = True

                chk('C')
                if bi + 1 < len(blocks):
                    stage_a1(l, bi + 1)
                elif l + 1 < n_layers:
                    stage_a1(l + 1, 0)
                wvo = wview_out[l]
                for ti, tl in enumerate(tm_tiles):
                    pass
                osb = xt[0]
                outbuf = {}
                for ti, tl in enumerate(tm_tiles):
                    outbuf[ti] = None
                E_ssq = [smalls["E0_ssq"], smalls["E1_ssq"]]
                E_tot, E_lnv, E_rstd = smalls["E_tot"], smalls["E_lnv"], smalls["E_rstd"]
                XR = [ST[0], ST[1], ACC[0], ACC[1]]
                for cg in range(D // WG):
                    wbuf = load_w(wvo, cg * WG, WG)
                    for ti, tl in enumerate(tm_tiles):
                        L, t0 = tl["L"], tl["tok0"]
                        bk = nb()
                        for kc in range(KC):
                            mm(bk[0:L, 0:WG], hT[:, kc, t0:t0 + L], wbuf[:, kc, 0:WG], kc == 0, kc == KC - 1,
                               r=[hT, wbuf], w=[bk])
                        cp(xt[ti][0:L, cg * WG:(cg + 1) * WG], bk[0:L, 0:WG], r=[bk], w=[xt[ti]],
                           eng="act" if cg % 2 else "dve")
                        act(junkA[0:L, 0:WG], xt[ti][0:L, cg * WG:(cg + 1) * WG], AF.Square, r=[xt[ti]],
                            w=[junkA, E_ssq[ti]], accum_out=E_ssq[ti][0:L, cg:cg + 1])
                def epilogue(tm_tiles=tm_tiles, xsrc=xsrc, xdst=xdst, bi=bi, E_ssq=E_ssq, XR=XR):
                    xrc = 0
                    for ti, tl in enumerate(tm_tiles):
                        L, r0 = tl["L"], tl["row0"]
                        o_ = xt[ti]
                        V(lambda L=L, ti=ti: nc.vector.tensor_reduce(out=E_tot[0:L, 0:1], in_=E_ssq[ti][0:L, 0:8], axis=AX.X,
                                                                     op=ALU.add), r=[E_ssq[ti]], w=[E_tot])
                        act(E_lnv[0:L, 0:1], E_tot[0:L, 0:1], AF.Ln, r=[E_tot], w=[E_lnv], scale=1.0 / D, bias=EPS)
                        act(E_rstd[0:L, 0:1], E_lnv[0:L, 0:1], AF.Exp, r=[E_lnv], w=[E_rstd], scale=-0.5)
                        for c in range(8):
                            c0 = c * 256
                            xb = XR[xrc % 4]
                            xrc += 1
                            dma(xb[0:L, 0:256], xsrc[r0:r0 + L, c0:c0 + 256], r=[xbuf(bi)], w=[xb])
                            stt(o_[0:L, c0:c0 + 256], o_[0:L, c0:c0 + 256], E_rstd[0:L, 0:1], postg[0:L, c0:c0 + 256],
                                ALU.mult, ALU.mult, r=[o_, E_rstd, postg], w=[o_])
                            tt(o_[0:L, c0:c0 + 256], o_[0:L, c0:c0 + 256], xb[0:L, 0:256], ALU.add, r=[o_, xb], w=[o_])
                        dma(xdst[r0:r0 + L, :], o_[0:L, :], r=[o_], w=[xbuf(bi)])
                pending_epi.append(epilogue)

            flush_epi()
            if n_blocks is None:
                dma(o_p_ssd[l].rearrange("h n e -> n h e"), Sssd[0][:], r=[Sssd[0]], w=[dbuf("o_p_ssd")])
                dma(o_p_ret[l].rearrange("h d e -> d h e"), Sret[0][:], r=[Sret[0]], w=[dbuf("o_p_ret")])
                dma(o_p_gdn[l].rearrange("h d e -> d h e"), Sgdn[0][:], r=[Sgdn[0]], w=[dbuf("o_p_gdn")])
                store_fm_rows(lambda ft: carS[:, ft, :], carS, 8, 3, o_p_ssdc[l], xt[0])
                store_fm_rows(lambda ft: carG[:, ft, :], carG, 12, 3, o_p_gdnc[l], xt[1])
            store_fm_rows(lambda ft: csoS[:, ft, :, :].rearrange("p s t -> p (s t)"), csoS, 8, NSS * 3,
                          o_s_ssdc[l].rearrange("s t c -> (s t) c"), xt[0])
            store_fm_rows(lambda ft: csoG[:, ft, :, :].rearrange("p s t -> p (s t)"), csoG, 12, NSS * 3,
                          o_s_gdnc[l].rearrange("s t c -> (s t) c"), xt[1])

    except StopBuild:
        pass

    P.emit(es)
    es.close()
    return nc, P.stats


_CACHE = {}


def _in_maps(inp):
    f = lambda a: np.ascontiguousarray(np.asarray(a, dtype=np.float32))
    maps = []
    for c in range(8):
        b = c % 4
        sl = slice(NSS * c, NSS * c + NSS)
        xin = np.concatenate([inp["meta_tokens"], inp["x_prompt"][b], inp["x_sample"][sl].reshape(NSS * LS, D)], axis=0)
        m = {
            "xin": f(xin),
            "st_ssd": f(inp["state_ssd"][:, sl]),
            "st_ssdc": f(inp["state_ssd_conv"][:, sl]),
            "st_k": f(inp["cache_swa_k"][:, sl].reshape(DEPTH, NSS, 128, 128)),
            "st_v": f(inp["cache_swa_v"][:, sl].reshape(DEPTH, NSS, 128, 128)),
            "st_ret": f(inp["state_ret"][:, sl]),
            "st_gdn": f(inp["state_gdn"][:, sl]),
            "st_gdnc": f(inp["state_gdn_conv"][:, sl]),
        }
        for k in ("pre_norm", "post_norm", "w_in", "w_out", "ssd_conv_w", "ssd_conv_b", "ssd_dt_bias", "ssd_a_log",
                  "ssd_d", "ssd_norm", "swa_sinks", "ret_norm", "gdn_conv_w", "gdn_dt_bias", "gdn_a_log", "gdn_norm"):
            m[k] = f(inp[k])
        maps.append(m)
    return maps


def kernel(**inp):
    if "nc" not in _CACHE:
        _CACHE["nc"] = build_program()[0]
    nc = _CACHE["nc"]
    res = run_bass_kernel_spmd(nc, _in_maps(inp), core_ids=list(range(8)))
    R = res.results
    B = 4
    y_prompt = np.stack([R[b]["yout"][NMETA:NPT] for b in range(B)]).astype(np.float32)
    y_sample = np.concatenate([R[c]["yout"][NPT:NT].reshape(NSS, LS, D) for c in range(8)]).astype(np.float32)

    def pst(name, shape):
        return np.stack([np.stack([R[b][name][l] for b in range(B)]) for l in range(DEPTH)]).reshape(shape).astype(np.float32)

    def sst(name, shape):
        return np.concatenate([R[c][name] for c in range(8)], axis=1).reshape(shape).astype(np.float32)

    outs = (
        y_prompt, y_sample,
        pst("o_p_ssd", (DEPTH, B, 8, 128, 64)), pst("o_p_ssdc", (DEPTH, B, 3, 1024)),
        pst("o_p_k", (DEPTH, B, 128, 2, 64)), pst("o_p_v", (DEPTH, B, 128, 2, 64)),
        pst("o_p_ret", (DEPTH, B, 4, 64, 128)), pst("o_p_gdn", (DEPTH, B, 4, 128, 128)),
        pst("o_p_gdnc", (DEPTH, B, 3, 1536)),
        sst("o_s_ssd", (DEPTH, 32, 8, 128, 64)), sst("o_s_ssdc", (DEPTH, 32, 3, 1024)),
        sst("o_s_k", (DEPTH, 32, 128, 2, 64)), sst("o_s_v", (DEPTH, 32, 128, 2, 64)),
        sst("o_s_ret", (DEPTH, 32, 4, 64, 128)), sst("o_s_gdn", (DEPTH, 32, 4, 128, 128)),
        sst("o_s_gdnc", (DEPTH, 32, 3, 1536)),
    )
    return outs
```

```python
import math
from contextlib import ExitStack

import numpy as np
import concourse.bass as bass
import concourse.mybir as mybir
from concourse.bass_utils import run_bass_kernel_spmd

F32 = mybir.dt.float32
BF16 = mybir.dt.bfloat16
I32 = mybir.dt.int32
AF = mybir.ActivationFunctionType
ALU = mybir.AluOpType
AX = mybir.AxisListType

D = 2048
KC = 16
DEPTH = 4
SEQ = 2048
NMETA = 16
NPT = SEQ + NMETA
NSS = 4
LS = 4
NT = NPT + NSS * LS
IN_W = 6416
EPS = 1e-6
NEG = -30000.0

C_Z, C_XBC, C_DT, C_QA, C_KA, C_VA, C_GA = 0, 512, 1536, 1544, 2056, 2184, 2312
C_QR, C_KR, C_VR, C_GR, C_QKVD, C_GD, C_BD, C_AD = 2824, 3080, 3336, 3848, 4360, 5896, 6408, 6412


class Buf:
    __slots__ = ("t", "lw", "rd", "rd_dma", "name", "excl")

    def __init__(self, t, name="", excl=False):
        self.t = t
        self.excl = excl
        self.lw = None
        self.rd = {}
        self.rd_dma = []
        self.name = name

    def __getitem__(self, k):
        return self.t[k]


class View:
    __slots__ = ("p", "t")

    def __init__(self, parent, ap):
        self.p = parent
        self.t = ap

    def __getitem__(self, k):
        return self.t[k]

    lw = property(lambda self: self.p.lw, lambda self, v: setattr(self.p, "lw", v))
    rd = property(lambda self: self.p.rd, lambda self, v: setattr(self.p, "rd", v))
    rd_dma = property(lambda self: self.p.rd_dma, lambda self, v: setattr(self.p, "rd_dma", v))
    excl = property(lambda self: self.p.excl)


class Prog:
    def __init__(self, nc, n_dma_sems=24):
        self.nc = nc
        self.ops = []
        self.E = {"pe": nc.tensor, "act": nc.scalar, "dve": nc.vector, "pool": nc.gpsimd, "sp": nc.sync}
        self.n_dma_sems = n_dma_sems
        self.embed_waits = True

    def op(self, eng, fn, r=(), w=(), dma=False):
        idx = len(self.ops)
        deps = set()
        for b in r:
            if b.lw is not None:
                deps.add(b.lw)
            if b.excl:
                deps.update(v for e, v in b.rd.items() if e != eng)
        for b in w:
            if b.lw is not None:
                deps.add(b.lw)
            deps.update(b.rd.values())
            deps.update(b.rd_dma)
        for b in r:
            if dma:
                b.rd_dma.append(idx)
            else:
                b.rd[eng] = idx
        for b in w:
            b.lw = idx
            b.rd = {}
            b.rd_dma = []
        deps.discard(idx)
        self.ops.append([eng, fn, deps, dma])
        return idx

    def emit(self, es):
        nc = self.nc
        ops = self.ops
        needed = set()
        for i, (eng, fn, deps, dma) in enumerate(ops):
            for d in deps:
                de, _, _, ddma = ops[d]
                if ddma:
                    continue
                if de == eng and eng == "pe":
                    continue
                needed.add(d)
        esem = {e: es.enter_context(nc.semaphore("sem_" + e)) for e in ("pe", "act", "dve", "pool")}
        dsem = [es.enter_context(nc.semaphore("dsem%d" % i)) for i in range(self.n_dma_sems)]
        dval = [0] * self.n_dma_sems
        ecount = {e: 0 for e in esem}
        sig = [None] * len(ops)
        known = {e: {} for e in self.E}
        ndma = 0
        nwait = 0
        dcnt = {}
        for i, (eng, fn, deps, dma) in enumerate(ops):
            waits = {}
            for d in deps:
                s = sig[d]
                if s is None:
                    continue
                if s[1] > waits.get(s[0], (None, 0))[1]:
                    waits[s[0]] = s
            if dma:
                half = self.n_dma_sems // 2
                base = 0 if eng == "sp" else half
                k = base + (dcnt.get(eng, 0) % half)
                dcnt[eng] = dcnt.get(eng, 0) + 1
                ndma += 1
                if dval[k] > 0:
                    key = ("d", k)
                    if dval[k] > waits.get(key, (None, 0))[1]:
                        waits[key] = (key, dval[k])
            kn = known[eng]
            todo = []
            for key, (_, val) in waits.items():
                if kn.get(key, 0) >= val:
                    continue
                sem = esem[key] if isinstance(key, str) else dsem[key[1]]
                todo.append((sem, val))
                kn[key] = val
                nwait += 1
            embed = None
            if todo and eng != "pe" and self.embed_waits:
                embed = todo.pop()
            for sem, val in todo:
                self.E[eng].wait_ge(sem, val)
            ins = fn()
            if embed is not None:
                ins._wait_ge(embed[0], embed[1])
            if dma:
                dval[k] += 16
                ins.then_inc(dsem[k], 16)
                sig[i] = (("d", k), dval[k])
            elif i in needed:
                ecount[eng] += 1
                ins.then_inc(esem[eng], 1)
                sig[i] = (eng, ecount[eng])
        for k in range(self.n_dma_sems):
            if dval[k] > 0:
                nc.sync.wait_ge(dsem[k], dval[k])
        for e in esem:
            if ecount[e] > 0:
                nc.sync.wait_ge(esem[e], ecount[e])
        self.stats = dict(n_ops=len(ops), n_wait=nwait, n_dma=ndma, counts=dict(ecount))


def make_chunks():
    chunks = [dict(kind="p", L=NMETA, row0=0, ci=0)]
    for c in range(1, 17):
        chunks.append(dict(kind="p", L=128, row0=NMETA + 128 * (c - 1), ci=c))
    samples = [dict(kind="s", L=LS, row0=NPT + LS * s, seq=s) for s in range(NSS)]
    return chunks, samples


class StopBuild(Exception):
    pass


def build_program(n_layers=DEPTH, n_blocks=None, cpb=2, dbg=False, stop=None):
    nc = bass.Bass("TRN2", target_bir_lowering=False)
    es = ExitStack()
    P = Prog(nc)

    def dram(name, shape, dt=F32, kind="ExternalInput"):
        return nc.dram_tensor(name, list(shape), dt, kind=kind).ap()

    xin = dram("xin", [NT, D])
    st_ssd = dram("st_ssd", [DEPTH, NSS, 8, 128, 64])
    st_ssdc = dram("st_ssdc", [DEPTH, NSS, 3, 1024])
    st_k = dram("st_k", [DEPTH, NSS, 128, 128])
    st_v = dram("st_v", [DEPTH, NSS, 128, 128])
    st_ret = dram("st_ret", [DEPTH, NSS, 4, 64, 128])
    st_gdn = dram("st_gdn", [DEPTH, NSS, 4, 128, 128])
    st_gdnc = dram("st_gdnc", [DEPTH, NSS, 3, 1536])
    pre_norm = dram("pre_norm", [DEPTH, D])
    post_norm = dram("post_norm", [DEPTH, D])
    w_in = dram("w_in", [DEPTH, D, IN_W])
    w_out = dram("w_out", [DEPTH, D, D])
    ssd_conv_w = dram("ssd_conv_w", [DEPTH, 4, 1024])
    ssd_conv_b = dram("ssd_conv_b", [DEPTH, 1024])
    ssd_dt_bias = dram("ssd_dt_bias", [DEPTH, 8])
    ssd_a_log = dram("ssd_a_log", [DEPTH, 8])
    ssd_d = dram("ssd_d", [DEPTH, 8])
    ssd_norm = dram("ssd_norm", [DEPTH, 512])
    swa_sinks = dram("swa_sinks", [DEPTH, 8])
    ret_norm = dram("ret_norm", [DEPTH, 512])
    gdn_conv_w = dram("gdn_conv_w", [DEPTH, 4, 1536])
    gdn_dt_bias = dram("gdn_dt_bias", [DEPTH, 4])
    gdn_a_log = dram("gdn_a_log", [DEPTH, 4])
    gdn_norm = dram("gdn_norm", [DEPTH, 128])

    EO = "ExternalOutput"
    yout = dram("yout", [NT, D], kind=EO)
    o_p_ssd = dram("o_p_ssd", [DEPTH, 8, 128, 64], kind=EO)
    o_p_ssdc = dram("o_p_ssdc", [DEPTH, 3, 1024], kind=EO)
    o_p_k = dram("o_p_k", [DEPTH, 128, 128], kind=EO)
    o_p_v = dram("o_p_v", [DEPTH, 128, 128], kind=EO)
    o_p_ret = dram("o_p_ret", [DEPTH, 4, 64, 128], kind=EO)
    o_p_gdn = dram("o_p_gdn", [DEPTH, 4, 128, 128], kind=EO)
    o_p_gdnc = dram("o_p_gdnc", [DEPTH, 3, 1536], kind=EO)
    o_s_ssd = dram("o_s_ssd", [DEPTH, NSS, 8, 128, 64], kind=EO)
    o_s_ssdc = dram("o_s_ssdc", [DEPTH, NSS, 3, 1024], kind=EO)
    o_s_k = dram("o_s_k", [DEPTH, NSS, 128, 128], kind=EO)
    o_s_v = dram("o_s_v", [DEPTH, NSS, 128, 128], kind=EO)
    o_s_ret = dram("o_s_ret", [DEPTH, NSS, 4, 64, 128], kind=EO)
    o_s_gdn = dram("o_s_gdn", [DEPTH, NSS, 4, 128, 128], kind=EO)
    o_s_gdnc = dram("o_s_gdnc", [DEPTH, NSS, 3, 1536], kind=EO)
    xscr = dram("xscr", [NT, D], kind="Internal")
    wbf_in = dram("wbf_in", [DEPTH, D, IN_W], BF16, kind="Internal")
    wbf_out = dram("wbf_out", [DEPTH, D, D], BF16, kind="Internal")
    ydbg = dram("ydbg", [NT, D], BF16, kind=EO) if dbg else None

    def sb(name, shape, dt=F32):
        return Buf(es.enter_context(nc.sbuf_tensor(name, list(shape), dt)), name)

    def ps(name, shape, dt=F32):
        return Buf(es.enter_context(nc.psum_tensor(name, list(shape), dt)), name, excl=True)

    DRAMBUF = {}

    def dbuf(ap_name):
        if ap_name not in DRAMBUF:
            DRAMBUF[ap_name] = Buf(None, ap_name)
        return DRAMBUF[ap_name]

    def dma(out, in_, r=(), w=(), eng="pool", **kw):
        P.op(eng, lambda: P.E[eng].dma_start(out=out, in_=in_, **kw), r=r, w=w, dma=True)

    def V(fn, r=(), w=()):
        P.op("dve", fn, r=r, w=w)

    def A(fn, r=(), w=()):
        P.op("act", fn, r=r, w=w)

    def G(fn, r=(), w=()):
        P.op("pool", fn, r=r, w=w)

    def T(fn, r=(), w=()):
        P.op("pe", fn, r=r, w=w)

    def mm(out, lhsT, rhs, start, stop, r, w):
        T(lambda: nc.tensor.matmul(out, lhsT=lhsT, rhs=rhs, start=start, stop=stop), r=r, w=w)

    def tr(out, in_, ident, r, w):
        T(lambda: nc.tensor.transpose(out, in_, ident), r=r, w=w)

    def act(out, in_, func, r, w, bias=None, scale=None, accum_out=None):
        kw = {}
        if bias is not None:
            kw["bias"] = bias
        if scale is not None:
            kw["scale"] = scale
        if accum_out is not None:
            kw["accum_out"] = accum_out
        A(lambda: nc.scalar.activation(out=out, in_=in_, func=func, **kw), r=r, w=w)

    def tt(out, in0, in1, op, r, w, eng="dve"):
        e = nc.vector if eng == "dve" else nc.gpsimd
        P.op(eng, lambda: e.tensor_tensor(out=out, in0=in0, in1=in1, op=op), r=r, w=w)

    def ts(out, in0, s1, op0, r, w, s2=None, op1=None, eng="dve"):
        e = nc.vector if eng == "dve" else nc.gpsimd
        if op1 is None:
            P.op(eng, lambda: e.tensor_scalar(out=out, in0=in0, scalar1=s1, scalar2=None, op0=op0), r=r, w=w)
        else:
            P.op(eng, lambda: e.tensor_scalar(out=out, in0=in0, scalar1=s1, scalar2=s2, op0=op0, op1=op1), r=r, w=w)

    def stt(out, in0, scalar, in1, op0, op1, r, w):
        V(lambda: nc.vector.scalar_tensor_tensor(out=out, in0=in0, scalar=scalar, in1=in1, op0=op0, op1=op1), r=r, w=w)

    def cp(out, in_, r, w, eng="dve"):
        if eng == "act":
            A(lambda: nc.scalar.activation(out=out, in_=in_, func=AF.Copy), r=r, w=w)
        else:
            e = nc.vector if eng == "dve" else nc.gpsimd
            P.op(eng, lambda: e.tensor_copy(out=out, in_=in_), r=r, w=w)

    def memset(buf, ap, val, eng="pool"):
        e = nc.vector if eng == "dve" else nc.gpsimd
        P.op(eng, lambda: e.memset(ap, val), w=[buf])

    def bc(ap, shape):
        return ap.to_broadcast(list(shape))

    ident_f = sb("ident_f", [128, 128])
    ident_b = sb("ident_b", [128, 128], BF16)
    Umat = sb("Umat", [128, 128])
    NEGU = sb("NEGU", [128, 128])
    SUm = sb("SUm", [128, 128])
    NEGown = sb("NEGown", [128, 128], BF16)
    NEGprev = sb("NEGprev", [128, 128], BF16)
    NEGmeta = sb("NEGmeta", [128, 128], BF16)
    ones_b = sb("ones_b", [128, 128], BF16)
    f1 = sb("f1", [128, 512])
    f2 = sb("f2", [128, 512])
    C_f1 = sb("C_f1", [128, 512])
    zeros_f = View(f1, f1[:, 0:128])
    ones_f = View(f1, f1[:, 128:256])
    tmpc = View(f1, f1[:, 256:384])
    tmpc2 = View(f1, f1[:, 384:512])
    dmat = View(f2, f2[:, 0:128])
    iota_fr = View(f2, f2[:, 128:256])
    iota_q = sb("iota_q", [128, 1])

    memset(zeros_f, zeros_f[:], 0.0)
    memset(ones_f, ones_f[:], 1.0)
    memset(ones_b, ones_b[:], 1.0)
    G(lambda: nc.gpsimd.iota(iota_q[:], pattern=[[0, 1]], base=0, channel_multiplier=1,
                             allow_small_or_imprecise_dtypes=True), w=[iota_q])
    G(lambda: nc.gpsimd.iota(iota_fr[:], pattern=[[1, 128]], base=0, channel_multiplier=0,
                             allow_small_or_imprecise_dtypes=True), w=[iota_fr])

    def aff(dst, src, fill, base, cm, step, op):
        G(lambda: nc.gpsimd.affine_select(out=dst[:], in_=src[:], pattern=[[step, 128]], compare_op=op,
                                          fill=fill, base=base, channel_multiplier=cm), r=[src], w=[dst])

    aff(ident_f, ones_f, 0.0, 0, 1, -1, ALU.is_equal)
    cp(ident_b[:], ident_f[:], r=[ident_f], w=[ident_b], eng="pool")
    aff(Umat, ones_f, 0.0, 0, -1, 1, ALU.is_ge)
    aff(NEGU, zeros_f, NEG, 0, -1, 1, ALU.is_ge)
    aff(SUm, ones_f, 0.0, 0, -1, 1, ALU.is_gt)
    cp(NEGown[:], NEGU[:], r=[NEGU], w=[NEGown], eng="pool")
    aff(tmpc, zeros_f, NEG, 0, 1, -1, ALU.is_ge)
    cp(NEGprev[:], tmpc[:], r=[tmpc], w=[NEGprev], eng="pool")
    aff(tmpc2, zeros_f, NEG, 112, 1, -1, ALU.is_ge)
    cp(NEGmeta[:], tmpc2[:], r=[tmpc2], w=[NEGmeta], eng="pool")

    lg = [math.log1p(-2.0 ** (-5 - h)) for h in range(4)]
    RDEC = sb("RDEC", [128, 4, 128])
    GPOW = sb("GPOW", [128, 4])
    RW = {L: sb("RW%d" % L, [128, 4]) for L in (LS, NMETA, 128)}
    GL = {L: sb("GL%d" % L, [64, 4]) for L in (LS, NMETA, 128)}
    ts(dmat[:], iota_fr[:], iota_q[:, 0:1], ALU.subtract, r=[iota_fr, iota_q], w=[dmat])
    ip1 = sb("ip1", [128, 1])
    ts(ip1[:], iota_q[:], 1.0, ALU.add, r=[iota_q], w=[ip1])
    for h in range(4):
        t0 = View(C_f1, C_f1[:, h * 128:(h + 1) * 128])
        ts(t0[:], dmat[:], lg[h], ALU.mult, r=[dmat], w=[t0], s2=NEGU[:, 0:1] if False else None)
        tt(t0[:], t0[:], NEGU[:], ALU.add, r=[t0, NEGU], w=[t0])
        act(RDEC[:, h, :], t0[:], AF.Exp, r=[t0], w=[RDEC])
        act(GPOW[:, h:h + 1], ip1[:], AF.Exp, r=[ip1], w=[GPOW], scale=lg[h])
        for L in RW:
            tq = sb("rwt%d_%d" % (h, L), [128, 1])
            ts(tq[:], iota_q[:], -1.0, ALU.mult, r=[iota_q], w=[tq], s2=float(L - 1), op1=ALU.add)
            act(RW[L][:, h:h + 1], tq[:], AF.Exp, r=[tq], w=[RW[L]], scale=lg[h])
            memset(GL[L], GL[L][:, h:h + 1], math.exp(lg[h] * L), eng="dve")

    slopes = [2.0 ** (-(h + 1)) for h in range(8)]
    SLQ = sb("SLQ", [128, 8])
    for h in range(8):
        ts(SLQ[:, h:h + 1], iota_q[:], slopes[h], ALU.mult, r=[iota_q], w=[SLQ])


    pchunks, schunks = make_chunks()
    blocks = [[pchunks[0]] + schunks]
    for b0 in range(1, 17, cpb):
        blocks.append(pchunks[b0:b0 + cpb])
    if n_blocks is not None:
        blocks = blocks[:n_blocks]
    BT = 128 * cpb
    NSLOT = max(5, cpb)
    WG = 256

    xt = [sb("xt%d" % i, [128, D]) for i in range(2)]
    hb = sb("hb", [128, D], BF16)
    hT = sb("hT", [128, KC, BT], BF16)
    wb = [sb("wb%d" % i, [128, KC, WG], BF16) for i in range(2)]
    SLOTA = sb("SLOTA", [128, 4096], BF16)
    SLOTB = sb("SLOTB", [128, 4096], BF16)
    wb.append(View(SLOTA, SLOTA[:, :].rearrange("p (k n) -> p k n", k=KC, n=WG)))
    wb.append(View(SLOTB, SLOTB[:, :].rearrange("p (k n) -> p k n", k=KC, n=WG)))
    assert NSLOT == 5 and WG == 256
    Gs = [sb("Gs%d" % i, [128, 4, 512], BF16) for i in range(2)]
    Gs.append(View(SLOTA, SLOTA[:, 0:2048].rearrange("p (a b) -> p a b", a=4, b=512)))
    Gs.append(View(SLOTA, SLOTA[:, 2048:4096].rearrange("p (a b) -> p a b", a=4, b=512)))
    Gs.append(View(SLOTB, SLOTB[:, 0:2048].rearrange("p (a b) -> p a b", a=4, b=512)))
    SM = [sb("SM%d" % i, [128, 16]) for i in range(NSLOT)]
    KV = [sb("KV%d" % i, [128, 256]) for i in range(2)]
    KV.append(View(SLOTB, SLOTB[:, 3584:4096].bitcast(F32)))
    KV += [sb("KV%d" % i, [128, 256]) for i in range(3, NSLOT)]
    VAUG = [sb("VAUG%d" % i, [128, 2, 72], BF16) for i in range(NSLOT)]
    VR = [sb("VR%d" % i, [128, 4, 128], BF16) for i in range(2)]
    for i in range(3):
        VR.append(View(SLOTB, SLOTB[:, 2048 + 512 * i:2560 + 512 * i].rearrange("p (a b) -> p a b", a=4, b=128)))
    KR = [sb("KR%d" % i, [128, 4, 64], BF16) for i in range(NSLOT)]
    XBC = sb("XBC", [128, 8, BT], BF16)
    XBCt = [Buf(XBC.t[:, ft_, :], "XBC%d" % ft_) for ft_ in range(8)]
    QA = sb("QA", [65, 8, BT], BF16)
    KA = sb("KA", [65, 2, BT], BF16)
    QR = sb("QR", [64, 4, BT], BF16)
    KRF = sb("KRF", [64, 4, BT], BF16)
    QKVD = sb("QKVD", [128, 12, BT], BF16)
    QKVDt = [Buf(QKVD.t[:, ft_, :], "QKVD%d" % ft_) for ft_ in range(12)]
    ST = [sb("ST%d" % i, [128, BT + 4]) for i in range(2)]
    ACC = [sb("ACC%d" % i, [128, BT]) for i in range(2)]
    carS = sb("carS", [128, 8, 3])
    carG = sb("carG", [128, 12, 3])
    cstSs = [sb("cstS%d" % i, [128, 8, NSS, 3]) for i in range(2)]
    cstGs = [sb("cstG%d" % i, [128, 12, NSS, 3]) for i in range(2)]
    csoS = sb("csoS", [128, 8, NSS, 3])
    csoG = sb("csoG", [128, 12, NSS, 3])
    Yb = [sb("Y%d" % i, [128, D], BF16) for i in range(2)]
    ssq = sb("ssq", [128, 8])
    lnv = sb("lnv", [128, 8])
    rstd = sb("rstd", [128, 8])
    smalls = {}
    for _n in ("E0_ssq", "E1_ssq", "E_tot", "E_lnv", "E_rstd", "F0_ssq", "F1_ssq", "F_tot", "F_lnv", "F0_rstd", "F1_rstd"):
        smalls[_n] = sb(_n, [128, 8])
    for _p in "ACD":
        for _n in ("ssq", "lnv", "rstd", "negcum", "ecum", "ecl", "dtr", "dtv", "lav"):
            smalls[_p + "_" + _n] = sb(_p + "_" + _n, [128, 8])
    sqj = sb("sqj", [128, 512], BF16)
    gTs = [sb("gT%d" % i, [128, KC]) for i in range(2)]
    postg = sb("postg", [128, D])
    ssdn = sb("ssdn", [128, 512])
    retn = sb("retn", [128, 512])
    gdnn = sb("gdnn", [128, 4, 128])
    cwSs = [sb("cwS%d" % i, [128, 8, 4]) for i in range(2)]
    cbSs = [sb("cbS%d" % i, [128, 8]) for i in range(2)]
    cwGs = [sb("cwG%d" % i, [128, 12, 4]) for i in range(2)]
    dtbS = sb("dtbS", [128, 8])
    AnS = sb("AnS", [128, 8])
    Dss = sb("Dss", [128, 8])
    ESQ = sb("ESQ", [128, 8])
    dtbG = sb("dtbG", [128, 4])
    AnG = sb("AnG", [128, 4])
    Sssd = [sb("Sssd%d" % i, [128, 8, 64]) for i in range(2)]
    Sssdb = [sb("Sssdb%d" % i, [128, 8, 64], BF16) for i in range(2)]
    Sret = [sb("Sret%d" % i, [64, 4, 128]) for i in range(2)]
    Sretb = [sb("Sretb%d" % i, [64, 4, 128], BF16) for i in range(2)]
    Sgdn = [sb("Sgdn%d" % i, [128, 4, 128]) for i in range(2)]
    Sgdnb = [sb("Sgdnb%d" % i, [128, 4, 128], BF16) for i in range(2)]
    PKA = sb("PKA", [65, 2, 128], BF16)
    PVA = sb("PVA", [128, 2, 72], BF16)
    PKAm = sb("PKAm", [65, 2, NMETA], BF16)
    PVAm = sb("PVAm", [128, 2, 72], BF16)
    cKb = sb("cKb", [128, 128], BF16)
    junkA = sb("junkA", [128, 512], BF16)
    junkC = sb("junkC", [128, 512], BF16)
    junkD = sb("junkD", [128, 512], BF16)
    C_ng = sb("C_ng", [128, 512])
    D_ng = sb("D_ng", [128, 512])
    decT = sb("decT", [128, 8, 128])
    negcum = sb("negcum", [128, 8])
    ecum = sb("ecum", [128, 8])
    ecl = sb("ecl", [128, 8])
    dtr = sb("dtr", [128, 8])
    dtv = sb("dtv", [128, 8])
    lav = sb("lav", [128, 8])
    MT = sb("MT", [128, 8, 128], BF16)
    xs_tm = sb("xs_tm", [128, 512], BF16)
    xdt = sb("xdt", [128, 512], BF16)
    xdtw = sb("xdtw", [128, 512], BF16)
    Btm = sb("Btm", [128, 256], BF16)
    D_f1 = sb("D_f1", [128, 512])
    D_f3 = sb("D_f3", [128, 512])
    B_f2 = sb("B_f2", [128, 512])
    cK = View(B_f2, B_f2[:, 0:256])
    C_MT = sb("C_MT", [128, 4, 128], BF16)
    D_decT = sb("D_decT", [128, 4, 128])
    kw = sb("kw", [128, 4, 64], BF16)
    beta = sb("beta", [128, 4])
    nbeta = sb("nbeta", [128, 4])
    Pm = [sb("Pm%d" % i, [128, 4, 128]) for i in range(2)]
    PTm = [sb("PTm%d" % i, [128, 4, 128]) for i in range(2)]
    Rm = [sb("Rm%d" % i, [128, 4, 128]) for i in range(2)]
    QKd = sb("QKd", [128, 4, 128], BF16)
    Vtm = sb("Vtm", [128, 512], BF16)
    knw = sb("knw", [128, 512], BF16)
    vnew = sb("vnew", [128, 512], BF16)
    PTs = [sb("PTs%d" % i, [128, 4, 128], BF16) for i in range(4)]
    den = sb("den", [128, 8])

    banks = [ps("bank%d" % i, [128, 512]) for i in range(8)]
    bctr = [0]

    def nb():
        b = banks[bctr[0] % 8]
        bctr[0] += 1
        return b

    def bfv(bk):
        return bk.t[:].bitcast(BF16)

    def v3(ap, a, b):
        return ap.rearrange("p (a b) -> p a b", a=a, b=b)

    for h in range(8):
        memset(QA, QA[64:65, h, :], 8.0 * slopes[h])
    G(lambda: nc.gpsimd.iota(PKA[64:65, :, :], pattern=[[0, 2], [1, 128]], base=-128, channel_multiplier=0,
                             allow_small_or_imprecise_dtypes=True), w=[PKA])
    G(lambda: nc.gpsimd.iota(PKAm[64:65, :, :], pattern=[[0, 2], [1, NMETA]], base=-NMETA, channel_multiplier=0,
                             allow_small_or_imprecise_dtypes=True), w=[PKAm])
    for i in range(NSLOT):
        memset(VAUG[i], VAUG[i][:, :, 64:65], 1.0)
    memset(PVA, PVA[:, :, 64:65], 1.0)
    memset(PVAm, PVAm[:, :, 64:65], 1.0)

    xbufs = {}

    def xbuf(bi):
        if bi not in xbufs:
            xbufs[bi] = Buf(None, "x%d" % bi)
        return xbufs[bi]

    wview_in = [wbf_in[l].rearrange("(kc p) n -> p kc n", p=128) for l in range(DEPTH)]
    wview_out = [wbf_out[l].rearrange("(kc p) n -> p kc n", p=128) for l in range(DEPTH)]
    wscr_bufs = {l: [Buf(None, "wscr%d_%d" % (l, i)) for i in range(5)] for l in range(DEPTH)}
    wctr = [0]
    nwb = [2]
    cur_layer = [0]

    def convert_weights(l, piece=None):
        for i in range(4):
            if piece is None or piece == i:
                dma(wbf_in[l, i * 512:(i + 1) * 512, :], w_in[l, i * 512:(i + 1) * 512, :], w=[wscr_bufs[l][i]],
                    eng="pool")
        if piece is None or piece == 4:
            dma(wbf_out[l], w_out[l], w=[wscr_bufs[l][4]], eng="pool")

    def load_w(view, c0, width, off=0, buf=None):
        if buf is None:
            buf = wb[wctr[0] % nwb[0]]
            wctr[0] += 1
        dma(buf[:, :, off:off + width], view[:, :, c0:c0 + width], r=wscr_bufs[cur_layer[0]], w=[buf], eng="sp")
        return buf

    def rms_stats(src_ap, L, n, scale, col=0):
        pass

    def chk(tag):
        if stop == tag:
            raise StopBuild()

    XA = [[f1, f2, C_f1, D_f1], [D_f3, B_f2, C_ng, D_ng]]
    a1_done = set()

    def tiles_of(bi2):
        if bi2 == 0:
            return [dict(row0=0, L=NMETA, tok0=0), dict(row0=NPT, L=NSS * LS, tok0=NMETA)]
        return [dict(row0=ch_["row0"], L=128, tok0=128 * i_) for i_, ch_ in enumerate(blocks[bi2])]

    def stage_a1(l2, bi2):
        if (l2, bi2) in a1_done:
            return
        a1_done.add((l2, bi2))
        xs2 = xin if l2 == 0 else xscr
        F_tot, F_lnv = smalls["F_tot"], smalls["F_lnv"]
        for ti, tl in enumerate(tiles_of(bi2)):
            L, r0 = tl["L"], tl["row0"]
            xa = XA[ti % 2]
            F_ssq, F_rstd = smalls["F%d_ssq" % (ti % 2)], smalls["F%d_rstd" % (ti % 2)]
            for c in range(4):
                dma(xa[c][0:L, :], xs2[r0:r0 + L, c * 512:(c + 1) * 512], r=[xbuf(bi2)], w=[xa[c]])
                act(junkC[0:L, :], xa[c][0:L, :], AF.Square, r=[xa[c]], w=[junkC, F_ssq], accum_out=F_ssq[0:L, c:c + 1])
            V(lambda L=L, F_ssq=F_ssq: nc.vector.tensor_reduce(out=F_tot[0:L, 0:1], in_=F_ssq[0:L, 0:4], axis=AX.X,
                                                               op=ALU.add), r=[F_ssq], w=[F_tot])
            act(F_lnv[0:L, 0:1], F_tot[0:L, 0:1], AF.Ln, r=[F_tot], w=[F_lnv], scale=1.0 / D, bias=EPS)
            act(F_rstd[0:L, 0:1], F_lnv[0:L, 0:1], AF.Exp, r=[F_lnv], w=[F_rstd], scale=-0.5)

    pending_epi = []

    def flush_epi():
        while pending_epi:
            pending_epi.pop(0)()

    parts_of = {}

    def multi_load(parent, pairs):
        parts = []
        for i, (o_, i_) in enumerate(pairs):
            pb = parent if i == 0 else Buf(None, "part")
            dma(o_, i_, w=[pb], eng="sp", allow_slow_non_contiguous=True)
            if i > 0:
                parts.append(pb)
        parts_of[id(parent)] = parts

    def RD(parent):
        return [parent] + parts_of.get(id(parent), [])

    def store_fm_rows(src_fn, srcbuf, nft, rows, dst2d, stg):
        for f0 in range(0, nft, 4):
            bk = nb()
            for j in range(4):
                tr(bk[0:rows, j * 128:(j + 1) * 128], src_fn(f0 + j), ident_f[:, :], r=[srcbuf, ident_f], w=[bk])
            cp(stg[0:rows, f0 * 128:(f0 + 4) * 128], bk[0:rows, 0:512], r=[bk], w=[stg])
        dma(dst2d, stg[0:rows, 0:nft * 128], r=[stg], w=[Buf(None, "fmrows")])

    def load_slow_params(l):
        p = l % 2
        multi_load(gTs[p], [(gTs[p][:], pre_norm[l].rearrange("(kc p) -> p kc", p=128))])
        multi_load(cwSs[p], [(cwSs[p][:, :, k_], ssd_conv_w[l, k_].rearrange("(ft p) -> p ft", p=128)) for k_ in range(4)])
        multi_load(cwGs[p], [(cwGs[p][:, :, k_], gdn_conv_w[l, k_].rearrange("(ft p) -> p ft", p=128)) for k_ in range(4)])
        multi_load(cbSs[p], [(cbSs[p][:], ssd_conv_b[l].rearrange("(ft p) -> p ft", p=128))])
        multi_load(cstSs[p], [(cstSs[p][:, :, s_, t_], st_ssdc[l, s_, t_].rearrange("(ft p) -> p ft", p=128))
                              for s_ in range(NSS) for t_ in range(3)])
        multi_load(cstGs[p], [(cstGs[p][:, :, s_, t_], st_gdnc[l, s_, t_].rearrange("(ft p) -> p ft", p=128))
                              for s_ in range(NSS) for t_ in range(3)])

    try:
        for l in range(n_layers):
            chk('const')
            cur_layer[0] = l
            if l == 0:
                convert_weights(0)
            xsrc = xin if l == 0 else xscr
            xdst = yout if l == n_layers - 1 else xscr
            if l == 0:
                load_slow_params(0)
            gT, cwS, cbS, cwG, cstS, cstG = [b[l % 2] for b in (gTs, cwSs, cbSs, cwGs, cstSs, cstGs)]
            dma(postg[:], bc(post_norm[l:l + 1, :], [128, D]), w=[postg])
            dma(ssdn[:], bc(ssd_norm[l:l + 1, :], [128, 512]), w=[ssdn])
            dma(retn[:], bc(ret_norm[l:l + 1, :], [128, 512]), w=[retn])
            for h in range(4):
                dma(gdnn[:, h, :], bc(gdn_norm[l:l + 1, :], [128, 128]), w=[gdnn])
            dma(dtbS[:], bc(ssd_dt_bias[l:l + 1, :], [128, 8]), w=[dtbS])
            dma(AnS[:], bc(ssd_a_log[l:l + 1, :], [128, 8]), w=[AnS])
            dma(Dss[:], bc(ssd_d[l:l + 1, :], [128, 8]), w=[Dss])
            dma(ESQ[:], bc(swa_sinks[l:l + 1, :], [128, 8]), w=[ESQ])
            dma(dtbG[:], bc(gdn_dt_bias[l:l + 1, :], [128, 4]), w=[dtbG])
            dma(AnG[:], bc(gdn_a_log[l:l + 1, :], [128, 4]), w=[AnG])
            act(AnS[:], AnS[:], AF.Exp, r=[AnS], w=[AnS])
            ts(AnS[:], AnS[:], -1.0, ALU.mult, r=[AnS], w=[AnS])
            act(AnG[:], AnG[:], AF.Exp, r=[AnG], w=[AnG])
            ts(AnG[:], AnG[:], -1.0, ALU.mult, r=[AnG], w=[AnG])
            tt(ESQ[:], ESQ[:], SLQ[:], ALU.add, r=[ESQ, SLQ], w=[ESQ])
            act(ESQ[:], ESQ[:], AF.Exp, r=[ESQ], w=[ESQ])
            memset(Sssd[0], Sssd[0][:], 0.0)
            memset(Sssdb[0], Sssdb[0][:], 0.0)
            memset(Sret[0], Sret[0][:], 0.0)
            memset(Sretb[0], Sretb[0][:], 0.0)
            memset(Sgdn[0], Sgdn[0][:], 0.0)
            memset(Sgdnb[0], Sgdnb[0][:], 0.0)
            memset(carS, carS[:], 0.0)
            memset(carG, carG[:], 0.0)

            for bi, blk in enumerate(blocks):
                is0 = (bi == 0)
                nwb[0] = 2 if is0 else 4
                tok = 0
                for si, ch in enumerate(blk):
                    ch["tok0"] = tok
                    ch["slot"] = si
                    tok += ch["L"]
                nbt = tok
                if is0:
                    tm_tiles = [dict(row0=0, L=NMETA, tok0=0), dict(row0=NPT, L=NSS * LS, tok0=NMETA)]
                else:
                    tm_tiles = [dict(row0=ch["row0"], L=128, tok0=ch["tok0"]) for ch in blk]
                if is0:
                    G(lambda: nc.gpsimd.iota(KA[64:65, :, 0:NMETA], pattern=[[0, 2], [1, NMETA]], base=0,
                                             channel_multiplier=0, allow_small_or_imprecise_dtypes=True), w=[KA])
                    G(lambda: nc.gpsimd.iota(KA[64:65, :, NMETA:NMETA + 16], pattern=[[0, 2], [0, NSS], [1, LS]], base=0,
                                             channel_multiplier=0, allow_small_or_imprecise_dtypes=True), w=[KA])
                elif bi == 1:
                    G(lambda: nc.gpsimd.iota(KA[64:65, :, :], pattern=[[0, 2], [0, cpb], [1, 128]], base=0,
                                             channel_multiplier=0, allow_small_or_imprecise_dtypes=True), w=[KA])

                chk('params')
                stage_a1(l, bi)
                for ti, tl in enumerate(tm_tiles):
                    L, r0, t0 = tl["L"], tl["row0"], tl["tok0"]
                    xa = XA[ti % 2]
                    F_rstd = smalls["F%d_rstd" % (ti % 2)]
                    for c in range(4):
                        ts(hb[0:L, c * 512:(c + 1) * 512], xa[c][0:L, :], F_rstd[0:L, 0:1], ALU.mult, r=[xa[c], F_rstd],
                           w=[hb])
                    for q in range(4):
                        bk = nb()
                        bv = bfv(bk)
                        for j in range(4):
                            kc = 4 * q + j
                            tr(bv[:, j * 128:j * 128 + L], hb[0:L, kc * 128:(kc + 1) * 128], ident_b[0:L, 0:L],
                               r=[hb, ident_b], w=[bk])
                        tt(hT[:, 4 * q:4 * q + 4, t0:t0 + L], v3(bv[:, 0:512], 4, 128)[:, :, 0:L],
                           bc(gT[:, 4 * q:4 * q + 4].unsqueeze(2), [128, 4, L]), ALU.mult, r=[bk] + RD(gT), w=[hT])
                flush_epi()

                def decay(la, L, H, decT_, negcum_, ecum_, ecl_):
                    bk = nb()
                    mm(bk[0:L, 0:H], Umat[0:L, 0:L], la[0:L, 0:H], True, True, r=[Umat, la], w=[bk])
                    ts(negcum_[0:L, 0:H], bk[0:L, 0:H], -1.0, ALU.mult, r=[bk], w=[negcum_])
                    act(ecum_[0:L, 0:H], bk[0:L, 0:H], AF.Exp, r=[bk], w=[ecum_])
                    yield
                    for hq in range(H // 4):
                        bk = nb()
                        for hh in range(4):
                            h = 4 * hq + hh
                            o = bk[:, hh * 128:hh * 128 + L]
                            mm(o, bc(la[0:L, h:h + 1], [L, 128]), Umat[0:L, 0:L], True, False, r=[la, Umat], w=[bk])
                            mm(o, ident_f[0:L, :], NEGU[0:L, 0:L], False, True, r=[ident_f, NEGU], w=[bk])
                        yield
                        for hh in range(4):
                            h = 4 * hq + hh
                            act(decT_[0:L, h, 0:L], bk[0:L, hh * 128:hh * 128 + L], AF.Exp, r=[bk, negcum_], w=[decT_],
                                bias=negcum_[0:L, h:h + 1])
                        act(ecl_[:, 4 * hq:4 * hq + 4], v3(bk[:, 0:512], 4, 128)[:, :, L - 1], AF.Exp, r=[bk], w=[ecl_])
                        yield

                def softplus_la(dst, src_ap, src_bufs, dtb, An, L, H, dtr_, keep_dt=None):
                    tt(dtr_[0:L, 0:H], src_ap, dtb[0:L, 0:H], ALU.add, r=src_bufs + [dtb], w=[dtr_])
                    act(dtr_[0:L, 0:H], dtr_[0:L, 0:H], AF.Exp, r=[dtr_], w=[dtr_])
                    tgt = keep_dt if keep_dt is not None else dtr_
                    act(tgt[0:L, 0:H], dtr_[0:L, 0:H], AF.Ln, r=[dtr_], w=[tgt], bias=1.0)
                    tt(dst[0:L, 0:H], tgt[0:L, 0:H], An[0:L, 0:H], ALU.mult, r=[tgt, An], w=[dst])

                def head_rmsnorm_gate(o_buf, junk_, ssq_, lnv_, rstd_, Yc, L, nh, hd, ng_, ycols):
                    n = nh * hd
                    for h in range(nh):
                        act(junk_[0:L, h * hd:(h + 1) * hd], o_buf[0:L, h * hd:(h + 1) * hd], AF.Square, r=[o_buf],
                            w=[junk_, ssq_], accum_out=ssq_[0:L, h:h + 1])
                    act(lnv_[0:L, 0:nh], ssq_[0:L, 0:nh], AF.Ln, r=[ssq_], w=[lnv_], scale=1.0 / hd, bias=EPS)
                    act(rstd_[0:L, 0:nh], lnv_[0:L, 0:nh], AF.Exp, r=[lnv_], w=[rstd_], scale=-0.5)
                    yield
                    tt(v3(o_buf[0:L, 0:n], nh, hd), v3(o_buf[0:L, 0:n], nh, hd), bc(rstd_[0:L, 0:nh].unsqueeze(2), [L, nh, hd]),
                       ALU.mult, r=[o_buf, rstd_], w=[o_buf])
                    tt(Yc[0:L, ycols:ycols + n], o_buf[0:L, 0:n], ng_[0:L, 0:n], ALU.mult, r=[o_buf, ng_], w=[Yc])

                def chunk_ctx(ch):
                    sid = 0 if ch["kind"] == "p" else 1
                    return ch["L"], ch["tok0"], ch["slot"], sid, ch.get("seq", None), Gs[ch["slot"]], Yb[ch["slot"] % 2]

                def ssd_thread():
                    A = lambda n: smalls["A_" + n]
                    ssq_, lnv_, rstd_, negcum_, ecum_, ecl_, dtr_, dtv_, lav_ = [A(n) for n in (
                        "ssq", "lnv", "rstd", "negcum", "ecum", "ecl", "dtr", "dtv", "lav")]
                    for ch in blk:
                        L, t0, slot, sid, seq, G_, Yc = chunk_ctx(ch)
                        while slot >= 2 and not fin.get(slot - 2):
                            yield
                        S1, S1b = Sssd[sid], Sssdb[sid]
                        if sid == 1:
                            dma(S1[:], st_ssd[l, seq].rearrange("h n e -> n h e"), w=[S1])
                            cp(S1b[:], S1[:], r=[S1], w=[S1b], eng="act")
                        softplus_la(lav_, SM[slot][0:L, 0:8], [SM[slot]], dtbS, AnS, L, 8, dtr_, keep_dt=dtv_)
                        yield
                        yield from decay(lav_, L, 8, decT, negcum_, ecum_, ecl_)
                        yield ("wait_proj",)
                        bk = nb()
                        bv = bfv(bk)
                        for ft in range(4):
                            tr(bv[0:L, ft * 128:(ft + 1) * 128], XBCt[ft][:, t0:t0 + L], ident_b[:, :], r=[XBCt[ft], ident_b], w=[bk])
                        yield
                        cp(xs_tm[0:L, :], bv[0:L, 0:512], r=[bk], w=[xs_tm], eng="act")
                        tt(v3(xdt[0:L, :], 8, 64), v3(bv[0:L, 0:512], 8, 64), bc(dtv_[0:L, 0:8].unsqueeze(2), [L, 8, 64]),
                           ALU.mult, r=[bk, dtv_], w=[xdt])
                        bk = nb()
                        bv = bfv(bk)
                        for g in range(2):
                            tr(bv[0:L, g * 128:(g + 1) * 128], XBCt[4 + g][:, t0:t0 + L], ident_b[:, :], r=[XBCt[4 + g], ident_b], w=[bk])
                        yield
                        cp(Btm[0:L, :], bv[0:L, 0:256], r=[bk], w=[Btm], eng="act")
                        bk = nb()
                        for g in range(2):
                            mm(bk[0:L, g * 128:g * 128 + L], XBCt[4 + g][:, t0:t0 + L], XBCt[6 + g][:, t0:t0 + L], True, True,
                               r=[XBCt[4 + g], XBCt[6 + g]], w=[bk])
                        yield
                        for g in range(2):
                            tt(MT[0:L, 4 * g:4 * g + 4, 0:L], bc(bk[0:L, g * 128:g * 128 + L].unsqueeze(1), [L, 4, L]),
                               decT[0:L, 4 * g:4 * g + 4, 0:L], ALU.mult, r=[bk, decT], w=[MT])
                        yield
                        bki = nb()
                        for h in range(8):
                            mm(bki[0:L, h * 64:(h + 1) * 64], MT[0:L, h, 0:L], xdt[0:L, h * 64:(h + 1) * 64], True, True,
                               r=[MT, xdt], w=[bki])
                        bks = nb()
                        for h in range(8):
                            mm(bks[0:L, h * 64:(h + 1) * 64], XBCt[6 + h // 4][:, t0:t0 + L], S1b[:, h, :], True, True,
                               r=[XBCt[6 + h // 4], S1b], w=[bks])
                        yield
                        tt(v3(f1[0:L, :], 8, 64), v3(bks[0:L, :], 8, 64), bc(ecum_[0:L, 0:8].unsqueeze(2), [L, 8, 64]), ALU.mult,
                           r=[bks, ecum_], w=[f1])
                        tt(f1[0:L, :], bki[0:L, :], f1[0:L, :], ALU.add, r=[bki, f1], w=[f1])
                        tt(v3(f2[0:L, :], 8, 64), v3(xs_tm[0:L, :], 8, 64), bc(Dss[0:L, 0:8].unsqueeze(2), [L, 8, 64]), ALU.mult,
                           r=[xs_tm, Dss], w=[f2], eng="pool")
                        yield
                        tt(f1[0:L, :], f1[0:L, :], f2[0:L, :], ALU.add, r=[f1, f2], w=[f1])
                        tt(f1[0:L, :], f1[0:L, :], G_[0:L, 0, :], ALU.mult, r=[f1, G_], w=[f1])
                        for g in range(2):
                            act(junkA[0:L, g * 256:(g + 1) * 256], f1[0:L, g * 256:(g + 1) * 256], AF.Square, r=[f1],
                                w=[junkA, ssq_], accum_out=ssq_[0:L, g:g + 1])
                        yield
                        act(lnv_[0:L, 0:2], ssq_[0:L, 0:2], AF.Ln, r=[ssq_], w=[lnv_], scale=1.0 / 256, bias=EPS)
                        act(rstd_[0:L, 0:2], lnv_[0:L, 0:2], AF.Exp, r=[lnv_], w=[rstd_], scale=-0.5)
                        yield
                        tt(v3(f2[0:L, :], 2, 256), v3(f1[0:L, :], 2, 256), bc(rstd_[0:L, 0:2].unsqueeze(2), [L, 2, 256]),
                           ALU.mult, r=[f1, rstd_], w=[f2])
                        tt(Yc[0:L, 0:512], f2[0:L, :], ssdn[0:L, :], ALU.mult, r=[f2, ssdn], w=[Yc])
                        yield
                        tt(v3(xdtw[0:L, :], 8, 64), v3(xdt[0:L, :], 8, 64), bc(decT[0:L, 0:8, L - 1:L], [L, 8, 64]), ALU.mult,
                           r=[xdt, decT], w=[xdtw])
                        bkn = nb()
                        for h in range(8):
                            mm(bkn[:, h * 64:(h + 1) * 64], Btm[0:L, (h // 4) * 128:(h // 4 + 1) * 128],
                               xdtw[0:L, h * 64:(h + 1) * 64], True, True, r=[Btm, xdtw], w=[bkn])
                        tt(S1[:], S1[:], bc(ecl_[:, 0:8].unsqueeze(2), [128, 8, 64]), ALU.mult, r=[S1, ecl_], w=[S1])
                        yield
                        tt(S1[:], v3(bkn[:, :], 8, 64), S1[:], ALU.add, r=[bkn, S1], w=[S1])
                        cp(S1b[:], S1[:], r=[S1], w=[S1b], eng="act")
                        if sid == 1:
                            dma(o_s_ssd[l, seq].rearrange("h n e -> n h e"), S1[:], r=[S1], w=[dbuf("o_s_ssd")])
                        yield ("done", ch["slot"])

                def ret_thread():
                    A = lambda n: smalls["C_" + n]
                    ssq_, lnv_, rstd_ = A("ssq"), A("lnv"), A("rstd")
                    for ch in blk:
                        L, t0, slot, sid, seq, G_, Yc = chunk_ctx(ch)
                        while slot >= 2 and not fin.get(slot - 2):
                            yield
                        S2, S2b = Sret[sid], Sretb[sid]
                        tt(C_ng[0:L, :], retn[0:L, :], G_[0:L, 2, :], ALU.mult, r=[retn, G_], w=[C_ng], eng="pool")
                        yield ("wait_proj",)
                        if sid == 1:
                            dma(S2[:], st_ret[l, seq].rearrange("h d e -> d h e"), w=[S2])
                            cp(S2b[:], S2[:], r=[S2], w=[S2b], eng="act")
                        bk = nb()
                        for h in range(4):
                            mm(bk[0:L, h * 128:h * 128 + L], KRF[0:64, h, t0:t0 + L], QR[0:64, h, t0:t0 + L], True, True,
                               r=[KRF, QR], w=[bk])
                        yield
                        tt(C_MT[0:L, 0:4, 0:L], v3(bk[0:L, :], 4, 128)[:, :, 0:L], RDEC[0:L, :, 0:L], ALU.mult, r=[bk, RDEC],
                           w=[C_MT])
                        yield
                        bki = nb()
                        for h in range(4):
                            mm(bki[0:L, h * 128:(h + 1) * 128], C_MT[0:L, h, 0:L], VR[slot][0:L, h, :], True, True,
                               r=[C_MT, VR[slot]], w=[bki])
                        bks = nb()
                        for h in range(4):
                            mm(bks[0:L, h * 128:(h + 1) * 128], QR[0:64, h, t0:t0 + L], S2b[:, h, :], True, True,
                               r=[QR, S2b], w=[bks])
                        yield
                        tt(v3(C_f1[0:L, :], 4, 128), v3(bks[0:L, :], 4, 128), bc(GPOW[0:L, 0:4].unsqueeze(2), [L, 4, 128]),
                           ALU.mult, r=[bks, GPOW], w=[C_f1])
                        tt(C_f1[0:L, :], bki[0:L, :], C_f1[0:L, :], ALU.add, r=[bki, C_f1], w=[C_f1])
                        yield
                        yield from head_rmsnorm_gate(C_f1, junkC, ssq_, lnv_, rstd_, Yc, L, 4, 128, C_ng, 1024)
                        yield
                        tt(kw[0:L, :, :], KR[slot][0:L, :, :], bc(RW[L][0:L, 0:4].unsqueeze(2), [L, 4, 64]), ALU.mult,
                           r=[KR[slot], RW[L]], w=[kw], eng="pool")
                        bkn = nb()
                        for h in range(4):
                            mm(bkn[0:64, h * 128:(h + 1) * 128], kw[0:L, h, :], VR[slot][0:L, h, :], True, True,
                               r=[kw, VR[slot]], w=[bkn])
                        tt(S2[:], S2[:], bc(GL[L][:, 0:4].unsqueeze(2), [64, 4, 128]), ALU.mult, r=[S2, GL[L]], w=[S2])
                        yield
                        tt(S2[:], v3(bkn[0:64, :], 4, 128), S2[:], ALU.add, r=[bkn, S2], w=[S2])
                        cp(S2b[:], S2[:], r=[S2], w=[S2b], eng="act")
                        if sid == 1:
                            dma(o_s_ret[l, seq].rearrange("h d e -> d h e"), S2[:], r=[S2], w=[dbuf("o_s_ret")])
                        yield ("done", ch["slot"])

                def gdn_thread():
                    A = lambda n: smalls["D_" + n]
                    ssq_, lnv_, rstd_, negcum_, ecum_, ecl_, dtr_, lav_ = [A(n) for n in (
                        "ssq", "lnv", "rstd", "negcum", "ecum", "ecl", "dtr", "lav")]
                    M1 = Pm[1]
                    for ch in blk:
                        L, t0, slot, sid, seq, G_, Yc = chunk_ctx(ch)
                        while slot >= 2 and not fin.get(slot - 2):
                            yield
                        S3, S3b = Sgdn[sid], Sgdnb[sid]
                        tt(D_ng[0:L, :], gdnn[0:L, :, :].rearrange("p a b -> p (a b)"), G_[0:L, 3, :], ALU.mult, r=[gdnn, G_],
                           w=[D_ng], eng="pool")
                        if sid == 1:
                            dma(S3[:], st_gdn[l, seq].rearrange("h d e -> d h e"), w=[S3])
                            cp(S3b[:], S3[:], r=[S3], w=[S3b], eng="act")
                        act(beta[0:L, :], SM[slot][0:L, 8:12], AF.Exp, r=[SM[slot]], w=[beta], scale=-1.0)
                        ts(beta[0:L, :], beta[0:L, :], 1.0, ALU.add, r=[beta], w=[beta])
                        V(lambda L=L: nc.vector.reciprocal(out=beta[0:L, :], in_=beta[0:L, :]), r=[beta], w=[beta])
                        ts(nbeta[0:L, :], beta[0:L, :], -1.0, ALU.mult, r=[beta], w=[nbeta])
                        yield
                        softplus_la(lav_, SM[slot][0:L, 12:16], [SM[slot]], dtbG, AnG, L, 4, dtr_)
                        yield
                        yield from decay(lav_, L, 4, D_decT, negcum_, ecum_, ecl_)
                        yield ("wait_proj",)
                        bk = nb()
                        bv = bfv(bk)
                        for h in range(4):
                            tr(bv[0:L, h * 128:(h + 1) * 128], QKVDt[4 + h][:, t0:t0 + L], ident_b[:, :], r=[QKVDt[4 + h], ident_b],
                               w=[bk])
                        yield
                        tt(v3(knw[0:L, :], 4, 128), v3(bv[0:L, 0:512], 4, 128), bc(D_decT[0:L, 0:4, L - 1:L], [L, 4, 128]),
                           ALU.mult, r=[bk, D_decT], w=[knw])
                        bk = nb()
                        bv = bfv(bk)
                        for h in range(4):
                            tr(bv[0:L, h * 128:(h + 1) * 128], QKVDt[8 + h][:, t0:t0 + L], ident_b[:, :], r=[QKVDt[8 + h], ident_b],
                               w=[bk])
                        yield
                        cp(Vtm[0:L, :], bv[0:L, 0:512], r=[bk], w=[Vtm], eng="act")
                        bkg = nb()
                        bkq = nb()
                        for h in range(4):
                            mm(bkg[0:L, h * 128:h * 128 + L], QKVDt[4 + h][:, t0:t0 + L], QKVDt[4 + h][:, t0:t0 + L], True, True,
                               r=[QKVDt[4 + h]], w=[bkg])
                        for h in range(4):
                            mm(bkq[0:L, h * 128:h * 128 + L], QKVDt[4 + h][:, t0:t0 + L], QKVDt[h][:, t0:t0 + L], True, True,
                               r=[QKVDt[4 + h], QKVDt[h]], w=[bkq])
                        yield
                        tt(M1[0:L, :, 0:L], v3(bkg[0:L, :], 4, 128)[:, :, 0:L], D_decT[0:L, 0:4, 0:L], ALU.mult,
                           r=[bkg, D_decT], w=[M1])
                        tt(QKd[0:L, :, 0:L], v3(bkq[0:L, :], 4, 128)[:, :, 0:L], D_decT[0:L, 0:4, 0:L], ALU.mult,
                           r=[bkq, D_decT], w=[QKd])
                        yield
                        tt(M1[0:L, :, 0:L], M1[0:L, :, 0:L], bc(nbeta[0:L, 0:4].unsqueeze(2), [L, 4, L]), ALU.mult,
                           r=[M1, nbeta], w=[M1])
                        P0_, PT0_ = Pm[0], PTm[0]
                        tt(P0_[0:L, :, 0:L], M1[0:L, :, 0:L], bc(SUm[0:L, 0:L].unsqueeze(1), [L, 4, L]), ALU.mult,
                           r=[M1, SUm], w=[P0_])
                        yield
                        bk = nb()
                        for h in range(4):
                            tr(bk[0:L, h * 128:h * 128 + L], P0_[0:L, h, 0:L], ident_f[0:L, 0:L], r=[P0_, ident_f], w=[bk])
                        yield
                        cp(PT0_[0:L, :, 0:L], v3(bk[0:L, :], 4, 128)[:, :, 0:L], r=[bk], w=[PT0_], eng="act")
                        R_ = Rm[0]
                        tt(R_[0:L, :, 0:L], P0_[0:L, :, 0:L], bc(ident_f[0:L, 0:L].unsqueeze(1), [L, 4, L]), ALU.add,
                           r=[P0_, ident_f], w=[R_], eng="pool")
                        yield
                        nlev = max(1, int(math.ceil(math.log2(L))))
                        cur = 0
                        for k in range(1, nlev):
                            Pc, PTc = Pm[cur], PTm[cur]
                            Pn, PTn = Pm[1 - cur], PTm[1 - cur]
                            last = (k == nlev - 1)
                            bkt = nb()
                            for h in range(4):
                                mm(bkt[0:L, h * 128:h * 128 + L], Pc[0:L, h, 0:L], PTc[0:L, h, 0:L], True, True,
                                   r=[Pc, PTc], w=[bkt])
                            if not last:
                                bkp = nb()
                                for h in range(4):
                                    mm(bkp[0:L, h * 128:h * 128 + L], PTc[0:L, h, 0:L], Pc[0:L, h, 0:L], True, True,
                                       r=[Pc, PTc], w=[bkp])
                            yield
                            cp(PTn[0:L, :, 0:L], v3(bkt[0:L, :], 4, 128)[:, :, 0:L], r=[bkt], w=[PTn], eng="act")
                            if not last:
                                cp(Pn[0:L, :, 0:L], v3(bkp[0:L, :], 4, 128)[:, :, 0:L], r=[bkp], w=[Pn])
                            yield
                            Rc, Rn = Rm[cur], Rm[1 - cur]
                            bkr = nb()
                            for h in range(4):
                                mm(bkr[0:L, h * 128:h * 128 + L], PTn[0:L, h, 0:L], Rc[0:L, h, 0:L], True, True,
                                   r=[PTn, Rc], w=[bkr])
                            yield
                            tt(Rn[0:L, :, 0:L], v3(bkr[0:L, :], 4, 128)[:, :, 0:L], Rc[0:L, :, 0:L], ALU.add, r=[bkr, Rc],
                               w=[Rn])
                            yield
                            cur = 1 - cur
                        Rf = Rm[cur]
                        bk = nb()
                        for h in range(4):
                            mm(bk[0:L, h * 128:(h + 1) * 128], QKVDt[4 + h][:, t0:t0 + L], S3b[:, h, :], True, True,
                               r=[QKVDt[4 + h], S3b], w=[bk])
                        yield
                        tt(v3(D_f1[0:L, :], 4, 128), v3(bk[0:L, :], 4, 128), bc(ecum_[0:L, 0:4].unsqueeze(2), [L, 4, 128]),
                           ALU.mult, r=[bk, ecum_], w=[D_f1])
                        tt(D_f3[0:L, :], Vtm[0:L, :], D_f1[0:L, :], ALU.subtract, r=[Vtm, D_f1], w=[D_f3])
                        yield
                        bk = nb()
                        for h in range(4):
                            mm(bk[0:L, h * 128:(h + 1) * 128], Rf[0:L, h, 0:L], D_f3[0:L, h * 128:(h + 1) * 128], True, True,
                               r=[Rf, D_f3], w=[bk])
                        yield
                        tt(v3(vnew[0:L, :], 4, 128), v3(bk[0:L, :], 4, 128), bc(beta[0:L, 0:4].unsqueeze(2), [L, 4, 128]),
                           ALU.mult, r=[bk, beta], w=[vnew])
                        yield
                        bks = nb()
                        for h in range(4):
                            mm(bks[0:L, h * 128:(h + 1) * 128], QKVDt[h][:, t0:t0 + L], S3b[:, h, :], True, True,
                               r=[QKVDt[h], S3b], w=[bks])
                        bki = nb()
                        for h in range(4):
                            mm(bki[0:L, h * 128:(h + 1) * 128], QKd[0:L, h, 0:L], vnew[0:L, h * 128:(h + 1) * 128], True, True,
                               r=[QKd, vnew], w=[bki])
                        yield
                        tt(v3(D_f1[0:L, :], 4, 128), v3(bks[0:L, :], 4, 128), bc(ecum_[0:L, 0:4].unsqueeze(2), [L, 4, 128]),
                           ALU.mult, r=[bks, ecum_], w=[D_f1])
                        tt(D_f1[0:L, :], bki[0:L, :], D_f1[0:L, :], ALU.add, r=[bki, D_f1], w=[D_f1])
                        yield
                        yield from head_rmsnorm_gate(D_f1, junkD, ssq_, lnv_, rstd_, Yc, L, 4, 128, D_ng, 1536)
                        yield
                        bkn = nb()
                        for h in range(4):
                            mm(bkn[:, h * 128:(h + 1) * 128], knw[0:L, h * 128:(h + 1) * 128],
                               vnew[0:L, h * 128:(h + 1) * 128], True, True, r=[knw, vnew], w=[bkn])
                        tt(S3[:], S3[:], bc(ecl_[:, 0:4].unsqueeze(2), [128, 4, 128]), ALU.mult, r=[S3, ecl_], w=[S3])
                        yield
                        tt(S3[:], v3(bkn[:, :], 4, 128), S3[:], ALU.add, r=[bkn, S3], w=[S3])
                        cp(S3b[:], S3[:], r=[S3], w=[S3b], eng="act")
                        if sid == 1:
                            dma(o_s_gdn[l, seq].rearrange("h d e -> d h e"), S3[:], r=[S3], w=[dbuf("o_s_gdn")])
                        yield ("done", ch["slot"])

                def swa_thread():
                    for ch in blk:
                        L, t0, slot, sid, seq, G_, Yc = chunk_ctx(ch)
                        while slot >= 2 and not fin.get(slot - 2):
                            yield
                        if sid == 1:
                            dma(cK[:, 0:128], st_k[l, seq], w=[cK])
                            dma(cK[:, 128:256], st_v[l, seq], w=[cK])
                            cp(cKb[:, :], cK[:, 0:128], r=[cK], w=[cKb])
                            bk = nb()
                            bv = bfv(bk)
                            for g in range(2):
                                tr(bv[0:64, g * 128:(g + 1) * 128], cKb[:, g * 64:(g + 1) * 64], ident_b[:, :],
                                   r=[cKb, ident_b], w=[bk])
                            yield
                            cp(PKA[0:64, :, :], v3(bv[0:64, 0:256], 2, 128), r=[bk], w=[PKA])
                            cp(PVA[:, :, 0:64], v3(cK[:, 128:256], 2, 64), r=[cK], w=[PVA])
                            prev = (PKA, PVA, 128, NEGprev)
                        elif ch["ci"] == 0:
                            prev = None
                        elif ch["ci"] == 1:
                            prev = (PKAm, PVAm, NMETA, NEGmeta)
                        else:
                            prev = (PKA, PVA, 128, NEGprev)
                        for g in range(2):
                            tiles = []
                            if prev is not None:
                                tiles.append((prev[0], prev[0][0:65, g, 0:prev[2]], prev[1], prev[1][0:prev[2], g, 0:65],
                                              prev[2], prev[3]))
                            tiles.append((KA, KA[0:65, g, t0:t0 + L], VAUG[slot], VAUG[slot][0:L, g, 0:65], L, NEGown))
                            pts = []
                            for ti, (kbuf, kap, vbuf, vap, Lk, negm) in enumerate(tiles):
                                bk = nb()
                                for hh in range(4):
                                    h = 4 * g + hh
                                    o = bk[0:Lk, hh * 128:hh * 128 + L]
                                    mm(o, kap, QA[0:65, h, t0:t0 + L], True, False, r=[kbuf, QA], w=[bk])
                                    mm(o, ident_b[0:Lk, 0:Lk], negm[0:Lk, 0:L], False, True, r=[ident_b, negm], w=[bk])
                                yield
                                pt = PTs[2 * g + ti] if len(tiles) == 2 else PTs[2 * g + 1]
                                act(pt[0:Lk, :, 0:L], v3(bk[0:Lk, :], 4, 128)[:, :, 0:L], AF.Exp, r=[bk], w=[pt], scale=0.125)
                                pts.append((pt, vbuf, vap, Lk))
                                yield
                            bko = nb()
                            for hh in range(4):
                                for ti, (pt, vbuf, vap, Lk) in enumerate(pts):
                                    mm(bko[0:L, hh * 72:hh * 72 + 65], pt[0:Lk, hh, 0:L], vap, ti == 0, ti == len(pts) - 1,
                                       r=[pt, vbuf], w=[bko])
                            yield
                            ov = v3(bko[0:L, 0:288], 4, 72)
                            tt(den[0:L, 4 * g:4 * g + 4], ov[:, :, 64], ESQ[0:L, 4 * g:4 * g + 4], ALU.add, r=[bko, ESQ],
                               w=[den])
                            V(lambda L=L, g=g: nc.vector.reciprocal(out=den[0:L, 4 * g:4 * g + 4],
                                                                    in_=den[0:L, 4 * g:4 * g + 4]), r=[den], w=[den])
                            tt(v3(B_f2[0:L, 256 * g:256 * g + 256], 4, 64), ov[:, :, 0:64],
                               bc(den[0:L, 4 * g:4 * g + 4].unsqueeze(2), [L, 4, 64]), ALU.mult, r=[bko, den], w=[B_f2])
                            yield
                        tt(Yc[0:L, 512:1024], B_f2[0:L, :], G_[0:L, 1, :], ALU.mult, r=[B_f2, G_], w=[Yc])
                        if sid == 0:
                            if ch["ci"] == 0:
                                cp(PKAm[0:64, :, 0:L], KA[0:64, :, t0:t0 + L], r=[KA], w=[PKAm], eng="pool")
                                cp(PVAm[0:L, :, 0:64], VAUG[slot][0:L, :, 0:64], r=[VAUG[slot]], w=[PVAm], eng="pool")
                            else:
                                cp(PKA[0:64, :, 0:L], KA[0:64, :, t0:t0 + L], r=[KA], w=[PKA], eng="pool")
                                cp(PVA[0:L, :, 0:64], VAUG[slot][0:L, :, 0:64], r=[VAUG[slot]], w=[PVA], eng="pool")
                        if sid == 1:
                            dma(o_s_k[l, seq, 0:128 - LS, :], st_k[l, seq, LS:128, :], w=[dbuf("o_s_k")])
                            dma(o_s_v[l, seq, 0:128 - LS, :], st_v[l, seq, LS:128, :], w=[dbuf("o_s_v")])
                            dma(o_s_k[l, seq, 128 - LS:128, :], KV[slot][0:LS, 0:128], r=[KV[slot]], w=[dbuf("o_s_k")])
                            dma(o_s_v[l, seq, 128 - LS:128, :], KV[slot][0:LS, 128:256], r=[KV[slot]], w=[dbuf("o_s_v")])
                        elif ch["ci"] == 16:
                            dma(o_p_k[l], KV[slot][:, 0:128], r=[KV[slot]], w=[dbuf("o_p_k")])
                            dma(o_p_v[l], KV[slot][:, 128:256], r=[KV[slot]], w=[dbuf("o_p_v")])
                        yield ("done", ch["slot"])

                def finish_chunk(ch):
                    L, t0, slot, sid, seq, G_, Yc = chunk_ctx(ch)
                    if dbg:
                        dma(ydbg[ch["row0"]:ch["row0"] + L, :], Yc[0:L, :], r=[Yc], w=[dbuf("ydbg")])
                    for q in range(4):
                        bk = nb()
                        bv = bfv(bk)
                        for j in range(4):
                            kc = 4 * q + j
                            tr(bv[:, j * 128:j * 128 + L], Yc[0:L, kc * 128:(kc + 1) * 128], ident_b[0:L, 0:L],
                               r=[Yc, ident_b], w=[bk])
                        cp(hT[:, 4 * q:4 * q + 4, t0:t0 + L], v3(bv[:, 0:512], 4, 128)[:, :, 0:L], r=[bk], w=[hT],
                           eng="act" if q % 2 else "dve")

                chk('A')
                def tm_group(wbuf, off, n, evac):
                    for ch in blk:
                        L, t0, slot = ch["L"], ch["tok0"], ch["slot"]
                        bk = nb()
                        for kc in range(KC):
                            mm(bk[0:L, 0:n], hT[:, kc, t0:t0 + L], wbuf[:, kc, off:off + n], kc == 0, kc == KC - 1,
                               r=[hT, wbuf], w=[bk])
                        evac(bk, ch)

                def fm_group(wbuf, off, M, evac):
                    bk = nb()
                    for kc in range(KC):
                        mm(bk[0:M, 0:nbt], wbuf[:, kc, off:off + M], hT[:, kc, 0:nbt], kc == 0, kc == KC - 1,
                           r=[hT, wbuf], w=[bk])
                    evac(bk)

                def gate_evac(gi, half):
                    def f(bk, ch):
                        L = ch["L"]
                        act(Gs[ch["slot"]][0:L, gi, half * 256:(half + 1) * 256], bk[0:L, 0:256], AF.Silu, r=[bk],
                            w=[Gs[ch["slot"]]])
                    return f

                cctr = [0]

                def conv_evac(dst, ft, cw, cb, car, cst, cso):
                    def f(bk):
                        k = cctr[0] % 2
                        cctr[0] += 1
                        S_, A_ = ST[k], ACC[k]
                        if not is0:
                            n = nbt
                            cp(S_[:, 0:3], car[:, ft, :], r=[car], w=[S_], eng="pool")
                            cp(S_[:, 3:3 + n], bk[:, 0:n], r=[bk], w=[S_], eng="act")
                            cp(car[:, ft, :], S_[:, n:n + 3], r=[S_], w=[car], eng="pool")
                            no = n
                        else:
                            memset(S_, S_[:, 0:3], 0.0)
                            sv = v3(S_[:, 19:47], NSS, 7)
                            cp(sv[:, :, 0:3], cst[:, ft, :, :], r=RD(cst), w=[S_], eng="pool")
                            cp(S_[:, 3:19], bk[:, 0:16], r=[bk], w=[S_], eng="act")
                            cp(sv[:, :, 3:7], v3(bk[:, 16:32], NSS, LS), r=[bk], w=[S_], eng="act")
                            cp(car[:, ft, :], S_[:, 16:19], r=[S_], w=[car], eng="pool")
                            cp(cso[:, ft, :, :], sv[:, :, 4:7], r=[S_], w=[cso], eng="pool")
                            no = 44
                        ts(A_[:, 0:no], S_[:, 0:no], cw[:, ft, 0:1], ALU.mult, r=[S_] + RD(cw), w=[A_])
                        for kk in range(1, 4):
                            stt(A_[:, 0:no], S_[:, kk:kk + no], cw[:, ft, kk:kk + 1], A_[:, 0:no], ALU.mult, ALU.add,
                                r=[S_, A_] + RD(cw), w=[A_])
                        bias = cb[:, ft:ft + 1] if cb is not None else None
                        rr = [A_] + (RD(cb) if cb is not None else [])
                        if not is0:
                            act(dst[ft][:, 0:no], A_[:, 0:no], AF.Silu, r=rr, w=[dst[ft]], bias=bias)
                        else:
                            act(dst[ft][:, 0:16], A_[:, 0:16], AF.Silu, r=rr, w=[dst[ft]], bias=bias)
                            act(v3(dst[ft][:, 16:32], NSS, LS), v3(A_[:, 19:47], NSS, 7)[:, :, 0:4], AF.Silu, r=rr,
                                w=[dst[ft]], bias=bias)
                    return f

                wv = wview_in[l]
                import os as _os
                _gl = ((0, C_Z), (1, C_GA), (2, C_GR), (3, C_GD))
                if _os.environ.get('KSKIPG'):
                    _gl = ()
                if _os.environ.get('KDUPG'):
                    _gl = _gl + _gl
                for gi, c0 in _gl:
                    for half in range(2):
                        wbuf = load_w(wv, c0 + half * 256, 256)
                        tm_group(wbuf, 0, 256, gate_evac(gi, half))
                chk('B1')
                wbuf = load_w(wv, C_DT, 8)
                load_w(wv, C_BD, 8, off=8, buf=wbuf)

                def small_evac(bk, ch):
                    L = ch["L"]
                    cp(SM[ch["slot"]][0:L, :], bk[0:L, 0:16], r=[bk], w=[SM[ch["slot"]]])
                tm_group(wbuf, 0, 16, small_evac)
                fin = {}
                done_cnt = {}
                th_ssd, th_gdn, th_swa, th_ret = ssd_thread(), gdn_thread(), swa_thread(), ret_thread()
                early = [th_ssd, th_gdn]
                while early:
                    for th in list(early):
                        v = next(th)
                        if v is not None and v[0] == "wait_proj":
                            early.remove(th)
                chk('B2')
                wbuf = load_w(wv, C_KA, 256)

                def kv_evac(bk, ch):
                    L, slot = ch["L"], ch["slot"]
                    _m = _os.environ.get('KVMODE', '0')
                    if _m in ('0', '1'):
                        cp(KV[slot][0:L, :], bk[0:L, 0:256], r=[bk], w=[KV[slot]], eng="act")
                    if _m in ('0', '2'):
                        cp(VAUG[slot][0:L, :, 0:64], v3(bk[0:L, 128:256], 2, 64), r=[bk] + ([KV[slot]] if _os.environ.get('KVSER') else []), w=[VAUG[slot]])
                tm_group(wbuf, 0, 256, kv_evac)
                chk('B2a')
                for g in range(2):
                    fm_group(wbuf, g * 64, 64,
                             lambda bk, g=g: cp(KA[0:64, g, 0:nbt], bk[0:64, 0:nbt], r=[bk], w=[KA]))
                chk('B3')
                for half in range(2):
                    wbuf = load_w(wv, C_QA + half * 256, 256)
                    for hh in range(4):
                        h = half * 4 + hh
                        fm_group(wbuf, hh * 64, 64,
                                 lambda bk, h=h: cp(QA[0:64, h, 0:nbt], bk[0:64, 0:nbt], r=[bk], w=[QA],
                                                    eng="act" if h % 2 else "dve"))
                chk('B4')
                wbuf = load_w(wv, C_QR, 256)
                for h in range(4):
                    fm_group(wbuf, h * 64, 64,
                             lambda bk, h=h: cp(QR[0:64, h, 0:nbt], bk[0:64, 0:nbt], r=[bk], w=[QR]))
                wbuf = load_w(wv, C_KR, 256)
                for h in range(4):
                    fm_group(wbuf, h * 64, 64,
                             lambda bk, h=h: ts(KRF[0:64, h, 0:nbt], bk[0:64, 0:nbt], 0.125, ALU.mult, r=[bk], w=[KRF]))

                def kr_evac(bk, ch):
                    L, slot = ch["L"], ch["slot"]
                    ts(KR[slot][0:L, :, :], v3(bk[0:L, 0:256], 4, 64), 0.125, ALU.mult, r=[bk], w=[KR[slot]])
                tm_group(wbuf, 0, 256, kr_evac)
                chk('B5')
                for half in range(2):
                    wbuf = load_w(wv, C_VR + half * 256, 256)

                    def vr_evac(bk, ch, half=half):
                        L, slot = ch["L"], ch["slot"]
                        cp(VR[slot][0:L, 2 * half:2 * half + 2, :], v3(bk[0:L, 0:256], 2, 128), r=[bk], w=[VR[slot]],
                           eng="act")
                    tm_group(wbuf, 0, 256, vr_evac)
                chk('B6')
                for q in range(4):
                    wbuf = load_w(wv, C_XBC + q * 256, 256)
                    for j in range(2):
                        ft = 2 * q + j
                        fm_group(wbuf, j * 128, 128, conv_evac(XBCt, ft, cwS, cbS, carS, cstS, csoS))
                chk('B7')
                for q in range(6):
                    wbuf = load_w(wv, C_QKVD + q * 256, 256)
                    for j in range(2):
                        ft = 2 * q + j
                        fm_group(wbuf, j * 128, 128, conv_evac(QKVDt, ft, cwG, None, carG, cstG, csoG))
                chk('B8')
                for ft in range(8):
                    k = ft % 2
                    S_, A_ = ST[k], ACC[k]
                    sq_ = (sqj, junkD)[k]
                    Q_ = QKVDt[ft]
                    tt(sq_[:, 0:nbt], Q_[:, 0:nbt], Q_[:, 0:nbt], ALU.mult, r=[Q_], w=[sq_], eng="pool")
                    bk = nb()
                    mm(bk[:, 0:nbt], ones_b[:, :], sq_[:, 0:nbt], True, True, r=[ones_b, sq_], w=[bk])
                    act(A_[:, 0:nbt], bk[:, 0:nbt], AF.Ln, r=[bk], w=[A_], bias=EPS)
                    act(A_[:, 0:nbt], A_[:, 0:nbt], AF.Exp, r=[A_], w=[A_], scale=-0.5,
                        bias=(math.log(128.0 ** -0.5) if ft < 4 else 0.0))
                    tt(Q_[:, 0:nbt], Q_[:, 0:nbt], A_[:, 0:nbt], ALU.mult, r=[Q_, A_], w=[Q_])

                chk('B')
                if l + 1 < n_layers and len(blocks) >= 7:
                    for pc_ in {1: (0, 1), 2: (2,), 3: (3,), 4: (4,)}.get(bi, ()):
                        convert_weights(l + 1, piece=pc_)
                    if bi == 6:
                        load_slow_params(l + 1)
                elif l + 1 < n_layers and bi == len(blocks) - 1:
                    convert_weights(l + 1)
                    load_slow_params(l + 1)
                threads = [th_gdn, th_ssd, th_gdn, th_swa, th_ret]
                while threads:
                    for th in list(threads):
                        if th not in threads:
                            continue
                        try:
                            v = next(th)
                        except StopIteration:
                            while th in threads:
                                threads.remove(th)
                            continue
                        if v is not None and v[0] == "done":
                            done_cnt[v[1]] = done_cnt.get(v[1], 0) + 1
                            if done_cnt[v[1]] == 4:
                                finish_chunk(blk[v[1]])
                                fin[v[1]] = True

                chk('C')
                if bi + 1 < len(blocks):
                    stage_a1(l, bi + 1)
                elif l + 1 < n_layers:
                    stage_a1(l + 1, 0)
                wvo = wview_out[l]
                for ti, tl in enumerate(tm_tiles):
                    pass
                osb = xt[0]
                outbuf = {}
                for ti, tl in enumerate(tm_tiles):
                    outbuf[ti] = None
                E_ssq = [smalls["E0_ssq"], smalls["E1_ssq"]]
                E_tot, E_lnv, E_rstd = smalls["E_tot"], smalls["E_lnv"], smalls["E_rstd"]
                XR = [ST[0], ST[1], ACC[0], ACC[1]]
                for cg in range(D // WG):
                    wbuf = load_w(wvo, cg * WG, WG)
                    for ti, tl in enumerate(tm_tiles):
                        L, t0 = tl["L"], tl["tok0"]
                        bk = nb()
                        for kc in range(KC):
                            mm(bk[0:L, 0:WG], hT[:, kc, t0:t0 + L], wbuf[:, kc, 0:WG], kc == 0, kc == KC - 1,
                               r=[hT, wbuf], w=[bk])
                        cp(xt[ti][0:L, cg * WG:(cg + 1) * WG], bk[0:L, 0:WG], r=[bk], w=[xt[ti]],
                           eng="act" if cg % 2 else "dve")
                        act(junkA[0:L, 0:WG], xt[ti][0:L, cg * WG:(cg + 1) * WG], AF.Square, r=[xt[ti]],
                            w=[junkA, E_ssq[ti]], accum_out=E_ssq[ti][0:L, cg:cg + 1])
                def epilogue(tm_tiles=tm_tiles, xsrc=xsrc, xdst=xdst, bi=bi, E_ssq=E_ssq, XR=XR):
                    xrc = 0
                    for ti, tl in enumerate(tm_tiles):
                        L, r0 = tl["L"], tl["row0"]
                        o_ = xt[ti]
                        V(lambda L=L, ti=ti: nc.vector.tensor_reduce(out=E_tot[0:L, 0:1], in_=E_ssq[ti][0:L, 0:8], axis=AX.X,
                                                                     op=ALU.add), r=[E_ssq[ti]], w=[E_tot])
                        act(E_lnv[0:L, 0:1], E_tot[0:L, 0:1], AF.Ln, r=[E_tot], w=[E_lnv], scale=1.0 / D, bias=EPS)
                        act(E_rstd[0:L, 0:1], E_lnv[0:L, 0:1], AF.Exp, r=[E_lnv], w=[E_rstd], scale=-0.5)
                        for c in range(8):
                            c0 = c * 256
                            xb = XR[xrc % 4]
                            xrc += 1
                            dma(xb[0:L, 0:256], xsrc[r0:r0 + L, c0:c0 + 256], r=[xbuf(bi)], w=[xb])
                            stt(o_[0:L, c0:c0 + 256], o_[0:L, c0:c0 + 256], E_rstd[0:L, 0:1], postg[0:L, c0:c0 + 256],
                                ALU.mult, ALU.mult, r=[o_, E_rstd, postg], w=[o_])
                            tt(o_[0:L, c0:c0 + 256], o_[0:L, c0:c0 + 256], xb[0:L, 0:256], ALU.add, r=[o_, xb], w=[o_])
                        dma(xdst[r0:r0 + L, :], o_[0:L, :], r=[o_], w=[xbuf(bi)])
                pending_epi.append(epilogue)

            flush_epi()
            if n_blocks is None:
                dma(o_p_ssd[l].rearrange("h n e -> n h e"), Sssd[0][:], r=[Sssd[0]], w=[dbuf("o_p_ssd")])
                dma(o_p_ret[l].rearrange("h d e -> d h e"), Sret[0][:], r=[Sret[0]], w=[dbuf("o_p_ret")])
                dma(o_p_gdn[l].rearrange("h d e -> d h e"), Sgdn[0][:], r=[Sgdn[0]], w=[dbuf("o_p_gdn")])
                store_fm_rows(lambda ft: carS[:, ft, :], carS, 8, 3, o_p_ssdc[l], xt[0])
                store_fm_rows(lambda ft: carG[:, ft, :], carG, 12, 3, o_p_gdnc[l], xt[1])
            store_fm_rows(lambda ft: csoS[:, ft, :, :].rearrange("p s t -> p (s t)"), csoS, 8, NSS * 3,
                          o_s_ssdc[l].rearrange("s t c -> (s t) c"), xt[0])
            store_fm_rows(lambda ft: csoG[:, ft, :, :].rearrange("p s t -> p (s t)"), csoG, 12, NSS * 3,
                          o_s_gdnc[l].rearrange("s t c -> (s t) c"), xt[1])

    except StopBuild:
        pass

    P.emit(es)
    es.close()
    return nc, P.stats


_CACHE = {}


def _in_maps(inp):
    f = lambda a: np.ascontiguousarray(np.asarray(a, dtype=np.float32))
    maps = []
    for c in range(8):
        b = c % 4
        sl = slice(NSS * c, NSS * c + NSS)
        xin = np.concatenate([inp["meta_tokens"], inp["x_prompt"][b], inp["x_sample"][sl].reshape(NSS * LS, D)], axis=0)
        m = {
            "xin": f(xin),
            "st_ssd": f(inp["state_ssd"][:, sl]),
            "st_ssdc": f(inp["state_ssd_conv"][:, sl]),
            "st_k": f(inp["cache_swa_k"][:, sl].reshape(DEPTH, NSS, 128, 128)),
            "st_v": f(inp["cache_swa_v"][:, sl].reshape(DEPTH, NSS, 128, 128)),
            "st_ret": f(inp["state_ret"][:, sl]),
            "st_gdn": f(inp["state_gdn"][:, sl]),
            "st_gdnc": f(inp["state_gdn_conv"][:, sl]),
        }
        for k in ("pre_norm", "post_norm", "w_in", "w_out", "ssd_conv_w", "ssd_conv_b", "ssd_dt_bias", "ssd_a_log",
                  "ssd_d", "ssd_norm", "swa_sinks", "ret_norm", "gdn_conv_w", "gdn_dt_bias", "gdn_a_log", "gdn_norm"):
            m[k] = f(inp[k])
        maps.append(m)
    return maps


def kernel(**inp):
    if "nc" not in _CACHE:
        _CACHE["nc"] = build_program()[0]
    nc = _CACHE["nc"]
    res = run_bass_kernel_spmd(nc, _in_maps(inp), core_ids=list(range(8)))
    R = res.results
    B = 4
    y_prompt = np.stack([R[b]["yout"][NMETA:NPT] for b in range(B)]).astype(np.float32)
    y_sample = np.concatenate([R[c]["yout"][NPT:NT].reshape(NSS, LS, D) for c in range(8)]).astype(np.float32)

    def pst(name, shape):
        return np.stack([np.stack([R[b][name][l] for b in range(B)]) for l in range(DEPTH)]).reshape(shape).astype(np.float32)

    def sst(name, shape):
        return np.concatenate([R[c][name] for c in range(8)], axis=1).reshape(shape).astype(np.float32)

    outs = (
        y_prompt, y_sample,
        pst("o_p_ssd", (DEPTH, B, 8, 128, 64)), pst("o_p_ssdc", (DEPTH, B, 3, 1024)),
        pst("o_p_k", (DEPTH, B, 128, 2, 64)), pst("o_p_v", (DEPTH, B, 128, 2, 64)),
        pst("o_p_ret", (DEPTH, B, 4, 64, 128)), pst("o_p_gdn", (DEPTH, B, 4, 128, 128)),
        pst("o_p_gdnc", (DEPTH, B, 3, 1536)),
        sst("o_s_ssd", (DEPTH, 32, 8, 128, 64)), sst("o_s_ssdc", (DEPTH, 32, 3, 1024)),
        sst("o_s_k", (DEPTH, 32, 128, 2, 64)), sst("o_s_v", (DEPTH, 32, 128, 2, 64)),
        sst("o_s_ret", (DEPTH, 32, 4, 64, 128)), sst("o_s_gdn", (DEPTH, 32, 4, 128, 128)),
        sst("o_s_gdnc", (DEPTH, 32, 3, 1536)),
    )
    return outs
```

```python
import math
from contextlib import ExitStack

import numpy as np
import concourse.bass as bass
import concourse.mybir as mybir
from concourse.bass_utils import run_bass_kernel_spmd

F32 = mybir.dt.float32
BF16 = mybir.dt.bfloat16
I32 = mybir.dt.int32
AF = mybir.ActivationFunctionType
ALU = mybir.AluOpType
AX = mybir.AxisListType

D = 2048
KC = 16
DEPTH = 4
SEQ = 2048
NMETA = 16
NPT = SEQ + NMETA
NSS = 4
LS = 4
NT = NPT + NSS * LS
IN_W = 6416
EPS = 1e-6
NEG = -30000.0

C_Z, C_XBC, C_DT, C_QA, C_KA, C_VA, C_GA = 0, 512, 1536, 1544, 2056, 2184, 2312
C_QR, C_KR, C_VR, C_GR, C_QKVD, C_GD, C_BD, C_AD = 2824, 3080, 3336, 3848, 4360, 5896, 6408, 6412


class Buf:
    __slots__ = ("t", "lw", "rd", "rd_dma", "name", "excl")

    def __init__(self, t, name="", excl=False):
        self.t = t
        self.excl = excl
        self.lw = None
        self.rd = {}
        self.rd_dma = []
        self.name = name

    def __getitem__(self, k):
        return self.t[k]


class View:
    __slots__ = ("p", "t")

    def __init__(self, parent, ap):
        self.p = parent
        self.t = ap

    def __getitem__(self, k):
        return self.t[k]

    lw = property(lambda self: self.p.lw, lambda self, v: setattr(self.p, "lw", v))
    rd = property(lambda self: self.p.rd, lambda self, v: setattr(self.p, "rd", v))
    rd_dma = property(lambda self: self.p.rd_dma, lambda self, v: setattr(self.p, "rd_dma", v))
    excl = property(lambda self: self.p.excl)


class Prog:
    def __init__(self, nc, n_dma_sems=24):
        self.nc = nc
        self.ops = []
        self.E = {"pe": nc.tensor, "act": nc.scalar, "dve": nc.vector, "pool": nc.gpsimd, "sp": nc.sync}
        self.n_dma_sems = n_dma_sems
        self.embed_waits = True

    def op(self, eng, fn, r=(), w=(), dma=False):
        idx = len(self.ops)
        deps = set()
        for b in r:
            if b.lw is not None:
                deps.add(b.lw)
            if b.excl:
                deps.update(v for e, v in b.rd.items() if e != eng)
        for b in w:
            if b.lw is not None:
                deps.add(b.lw)
            deps.update(b.rd.values())
            deps.update(b.rd_dma)
        for b in r:
            if dma:
                b.rd_dma.append(idx)
            else:
                b.rd[eng] = idx
        for b in w:
            b.lw = idx
            b.rd = {}
            b.rd_dma = []
        deps.discard(idx)
        self.ops.append([eng, fn, deps, dma])
        return idx

    def emit(self, es):
        nc = self.nc
        ops = self.ops
        needed = set()
        for i, (eng, fn, deps, dma) in enumerate(ops):
            for d in deps:
                de, _, _, ddma = ops[d]
                if ddma:
                    continue
                if de == eng and eng == "pe":
                    continue
                needed.add(d)
        esem = {e: es.enter_context(nc.semaphore("sem_" + e)) for e in ("pe", "act", "dve", "pool")}
        dsem = [es.enter_context(nc.semaphore("dsem%d" % i)) for i in range(self.n_dma_sems)]
        dval = [0] * self.n_dma_sems
        ecount = {e: 0 for e in esem}
        sig = [None] * len(ops)
        known = {e: {} for e in self.E}
        ndma = 0
        nwait = 0
        dcnt = {}
        for i, (eng, fn, deps, dma) in enumerate(ops):
            waits = {}
            for d in deps:
                s = sig[d]
                if s is None:
                    continue
                if s[1] > waits.get(s[0], (None, 0))[1]:
                    waits[s[0]] = s
            if dma:
                half = self.n_dma_sems // 2
                base = 0 if eng == "sp" else half
                k = base + (dcnt.get(eng, 0) % half)
                dcnt[eng] = dcnt.get(eng, 0) + 1
                ndma += 1
                if dval[k] > 0:
                    key = ("d", k)
                    if dval[k] > waits.get(key, (None, 0))[1]:
                        waits[key] = (key, dval[k])
            kn = known[eng]
            todo = []
            for key, (_, val) in waits.items():
                if kn.get(key, 0) >= val:
                    continue
                sem = esem[key] if isinstance(key, str) else dsem[key[1]]
                todo.append((sem, val))
                kn[key] = val
                nwait += 1
            embed = None
            if todo and eng != "pe" and self.embed_waits:
                embed = todo.pop()
            for sem, val in todo:
                self.E[eng].wait_ge(sem, val)
            ins = fn()
            if embed is not None:
                ins._wait_ge(embed[0], embed[1])
            if dma:
                dval[k] += 16
                ins.then_inc(dsem[k], 16)
                sig[i] = (("d", k), dval[k])
            elif i in needed:
                ecount[eng] += 1
                ins.then_inc(esem[eng], 1)
                sig[i] = (eng, ecount[eng])
        for k in range(self.n_dma_sems):
            if dval[k] > 0:
                nc.sync.wait_ge(dsem[k], dval[k])
        for e in esem:
            if ecount[e] > 0:
                nc.sync.wait_ge(esem[e], ecount[e])
        self.stats = dict(n_ops=len(ops), n_wait=nwait, n_dma=ndma, counts=dict(ecount))


def make_chunks():
    chunks = [dict(kind="p", L=NMETA, row0=0, ci=0)]
    for c in range(1, 17):
        chunks.append(dict(kind="p", L=128, row0=NMETA + 128 * (c - 1), ci=c))
    samples = [dict(kind="s", L=LS, row0=NPT + LS * s, seq=s) for s in range(NSS)]
    return chunks, samples


class StopBuild(Exception):
    pass


def build_program(n_layers=DEPTH, n_blocks=None, cpb=2, dbg=False, stop=None):
    nc = bass.Bass("TRN2", target_bir_lowering=False)
    es = ExitStack()
    P = Prog(nc)

    def dram(name, shape, dt=F32, kind="ExternalInput"):
        return nc.dram_tensor(name, list(shape), dt, kind=kind).ap()

    xin = dram("xin", [NT, D])
    st_ssd = dram("st_ssd", [DEPTH, NSS, 8, 128, 64])
    st_ssdc = dram("st_ssdc", [DEPTH, NSS, 3, 1024])
    st_k = dram("st_k", [DEPTH, NSS, 128, 128])
    st_v = dram("st_v", [DEPTH, NSS, 128, 128])
    st_ret = dram("st_ret", [DEPTH, NSS, 4, 64, 128])
    st_gdn = dram("st_gdn", [DEPTH, NSS, 4, 128, 128])
    st_gdnc = dram("st_gdnc", [DEPTH, NSS, 3, 1536])
    pre_norm = dram("pre_norm", [DEPTH, D])
    post_norm = dram("post_norm", [DEPTH, D])
    w_in = dram("w_in", [DEPTH, D, IN_W])
    w_out = dram("w_out", [DEPTH, D, D])
    ssd_conv_w = dram("ssd_conv_w", [DEPTH, 4, 1024])
    ssd_conv_b = dram("ssd_conv_b", [DEPTH, 1024])
    ssd_dt_bias = dram("ssd_dt_bias", [DEPTH, 8])
    ssd_a_log = dram("ssd_a_log", [DEPTH, 8])
    ssd_d = dram("ssd_d", [DEPTH, 8])
    ssd_norm = dram("ssd_norm", [DEPTH, 512])
    swa_sinks = dram("swa_sinks", [DEPTH, 8])
    ret_norm = dram("ret_norm", [DEPTH, 512])
    gdn_conv_w = dram("gdn_conv_w", [DEPTH, 4, 1536])
    gdn_dt_bias = dram("gdn_dt_bias", [DEPTH, 4])
    gdn_a_log = dram("gdn_a_log", [DEPTH, 4])
    gdn_norm = dram("gdn_norm", [DEPTH, 128])

    EO = "ExternalOutput"
    yout = dram("yout", [NT, D], kind=EO)
    o_p_ssd = dram("o_p_ssd", [DEPTH, 8, 128, 64], kind=EO)
    o_p_ssdc = dram("o_p_ssdc", [DEPTH, 3, 1024], kind=EO)
    o_p_k = dram("o_p_k", [DEPTH, 128, 128], kind=EO)
    o_p_v = dram("o_p_v", [DEPTH, 128, 128], kind=EO)
    o_p_ret = dram("o_p_ret", [DEPTH, 4, 64, 128], kind=EO)
    o_p_gdn = dram("o_p_gdn", [DEPTH, 4, 128, 128], kind=EO)
    o_p_gdnc = dram("o_p_gdnc", [DEPTH, 3, 1536], kind=EO)
    o_s_ssd = dram("o_s_ssd", [DEPTH, NSS, 8, 128, 64], kind=EO)
    o_s_ssdc = dram("o_s_ssdc", [DEPTH, NSS, 3, 1024], kind=EO)
    o_s_k = dram("o_s_k", [DEPTH, NSS, 128, 128], kind=EO)
    o_s_v = dram("o_s_v", [DEPTH, NSS, 128, 128], kind=EO)
    o_s_ret = dram("o_s_ret", [DEPTH, NSS, 4, 64, 128], kind=EO)
    o_s_gdn = dram("o_s_gdn", [DEPTH, NSS, 4, 128, 128], kind=EO)
    o_s_gdnc = dram("o_s_gdnc", [DEPTH, NSS, 3, 1536], kind=EO)
    xscr = dram("xscr", [NT, D], kind="Internal")
    wbf_in = dram("wbf_in", [DEPTH, D, IN_W], BF16, kind="Internal")
    wbf_out = dram("wbf_out", [DEPTH, D, D], BF16, kind="Internal")
    ydbg = dram("ydbg", [NT, D], BF16, kind=EO) if dbg else None

    def sb(name, shape, dt=F32):
        return Buf(es.enter_context(nc.sbuf_tensor(name, list(shape), dt)), name)

    def ps(name, shape, dt=F32):
        return Buf(es.enter_context(nc.psum_tensor(name, list(shape), dt)), name, excl=True)

    DRAMBUF = {}

    def dbuf(ap_name):
        if ap_name not in DRAMBUF:
            DRAMBUF[ap_name] = Buf(None, ap_name)
        return DRAMBUF[ap_name]

    def dma(out, in_, r=(), w=(), eng="pool", **kw):
        P.op(eng, lambda: P.E[eng].dma_start(out=out, in_=in_, **kw), r=r, w=w, dma=True)

    def V(fn, r=(), w=()):
        P.op("dve", fn, r=r, w=w)

    def A(fn, r=(), w=()):
        P.op("act", fn, r=r, w=w)

    def G(fn, r=(), w=()):
        P.op("pool", fn, r=r, w=w)

    def T(fn, r=(), w=()):
        P.op("pe", fn, r=r, w=w)

    def mm(out, lhsT, rhs, start, stop, r, w):
        T(lambda: nc.tensor.matmul(out, lhsT=lhsT, rhs=rhs, start=start, stop=stop), r=r, w=w)

    def tr(out, in_, ident, r, w):
        T(lambda: nc.tensor.transpose(out, in_, ident), r=r, w=w)

    def act(out, in_, func, r, w, bias=None, scale=None, accum_out=None):
        kw = {}
        if bias is not None:
            kw["bias"] = bias
        if scale is not None:
            kw["scale"] = scale
        if accum_out is not None:
            kw["accum_out"] = accum_out
        A(lambda: nc.scalar.activation(out=out, in_=in_, func=func, **kw), r=r, w=w)

    def tt(out, in0, in1, op, r, w, eng="dve"):
        e = nc.vector if eng == "dve" else nc.gpsimd
        P.op(eng, lambda: e.tensor_tensor(out=out, in0=in0, in1=in1, op=op), r=r, w=w)

    def ts(out, in0, s1, op0, r, w, s2=None, op1=None, eng="dve"):
        e = nc.vector if eng == "dve" else nc.gpsimd
        if op1 is None:
            P.op(eng, lambda: e.tensor_scalar(out=out, in0=in0, scalar1=s1, scalar2=None, op0=op0), r=r, w=w)
        else:
            P.op(eng, lambda: e.tensor_scalar(out=out, in0=in0, scalar1=s1, scalar2=s2, op0=op0, op1=op1), r=r, w=w)

    def stt(out, in0, scalar, in1, op0, op1, r, w):
        V(lambda: nc.vector.scalar_tensor_tensor(out=out, in0=in0, scalar=scalar, in1=in1, op0=op0, op1=op1), r=r, w=w)

    def cp(out, in_, r, w, eng="dve"):
        if eng == "act":
            A(lambda: nc.scalar.activation(out=out, in_=in_, func=AF.Copy), r=r, w=w)
        else:
            e = nc.vector if eng == "dve" else nc.gpsimd
            P.op(eng, lambda: e.tensor_copy(out=out, in_=in_), r=r, w=w)

    def memset(buf, ap, val, eng="pool"):
        e = nc.vector if eng == "dve" else nc.gpsimd
        P.op(eng, lambda: e.memset(ap, val), w=[buf])

    def bc(ap, shape):
        return ap.to_broadcast(list(shape))

    ident_f = sb("ident_f", [128, 128])
    ident_b = sb("ident_b", [128, 128], BF16)
    Umat = sb("Umat", [128, 128])
    NEGU = sb("NEGU", [128, 128])
    SUm = sb("SUm", [128, 128])
    NEGown = sb("NEGown", [128, 128], BF16)
    NEGprev = sb("NEGprev", [128, 128], BF16)
    NEGmeta = sb("NEGmeta", [128, 128], BF16)
    ones_b = sb("ones_b", [128, 128], BF16)
    f1 = sb("f1", [128, 512])
    f2 = sb("f2", [128, 512])
    C_f1 = sb("C_f1", [128, 512])
    zeros_f = View(f1, f1[:, 0:128])
    ones_f = View(f1, f1[:, 128:256])
    tmpc = View(f1, f1[:, 256:384])
    tmpc2 = View(f1, f1[:, 384:512])
    dmat = View(f2, f2[:, 0:128])
    iota_fr = View(f2, f2[:, 128:256])
    iota_q = sb("iota_q", [128, 1])

    memset(zeros_f, zeros_f[:], 0.0)
    memset(ones_f, ones_f[:], 1.0)
    memset(ones_b, ones_b[:], 1.0)
    G(lambda: nc.gpsimd.iota(iota_q[:], pattern=[[0, 1]], base=0, channel_multiplier=1,
                             allow_small_or_imprecise_dtypes=True), w=[iota_q])
    G(lambda: nc.gpsimd.iota(iota_fr[:], pattern=[[1, 128]], base=0, channel_multiplier=0,
                             allow_small_or_imprecise_dtypes=True), w=[iota_fr])

    def aff(dst, src, fill, base, cm, step, op):
        G(lambda: nc.gpsimd.affine_select(out=dst[:], in_=src[:], pattern=[[step, 128]], compare_op=op,
                                          fill=fill, base=base, channel_multiplier=cm), r=[src], w=[dst])

    aff(ident_f, ones_f, 0.0, 0, 1, -1, ALU.is_equal)
    cp(ident_b[:], ident_f[:], r=[ident_f], w=[ident_b], eng="pool")
    aff(Umat, ones_f, 0.0, 0, -1, 1, ALU.is_ge)
    aff(NEGU, zeros_f, NEG, 0, -1, 1, ALU.is_ge)
    aff(SUm, ones_f, 0.0, 0, -1, 1, ALU.is_gt)
    cp(NEGown[:], NEGU[:], r=[NEGU], w=[NEGown], eng="pool")
    aff(tmpc, zeros_f, NEG, 0, 1, -1, ALU.is_ge)
    cp(NEGprev[:], tmpc[:], r=[tmpc], w=[NEGprev], eng="pool")
    aff(tmpc2, zeros_f, NEG, 112, 1, -1, ALU.is_ge)
    cp(NEGmeta[:], tmpc2[:], r=[tmpc2], w=[NEGmeta], eng="pool")

    lg = [math.log1p(-2.0 ** (-5 - h)) for h in range(4)]
    RDEC = sb("RDEC", [128, 4, 128])
    GPOW = sb("GPOW", [128, 4])
    RW = {L: sb("RW%d" % L, [128, 4]) for L in (LS, NMETA, 128)}
    GL = {L: sb("GL%d" % L, [64, 4]) for L in (LS, NMETA, 128)}
    ts(dmat[:], iota_fr[:], iota_q[:, 0:1], ALU.subtract, r=[iota_fr, iota_q], w=[dmat])
    ip1 = sb("ip1", [128, 1])
    ts(ip1[:], iota_q[:], 1.0, ALU.add, r=[iota_q], w=[ip1])
    for h in range(4):
        t0 = View(C_f1, C_f1[:, h * 128:(h + 1) * 128])
        ts(t0[:], dmat[:], lg[h], ALU.mult, r=[dmat], w=[t0], s2=NEGU[:, 0:1] if False else None)
        tt(t0[:], t0[:], NEGU[:], ALU.add, r=[t0, NEGU], w=[t0])
        act(RDEC[:, h, :], t0[:], AF.Exp, r=[t0], w=[RDEC])
        act(GPOW[:, h:h + 1], ip1[:], AF.Exp, r=[ip1], w=[GPOW], scale=lg[h])
        for L in RW:
            tq = sb("rwt%d_%d" % (h, L), [128, 1])
            ts(tq[:], iota_q[:], -1.0, ALU.mult, r=[iota_q], w=[tq], s2=float(L - 1), op1=ALU.add)
            act(RW[L][:, h:h + 1], tq[:], AF.Exp, r=[tq], w=[RW[L]], scale=lg[h])
            memset(GL[L], GL[L][:, h:h + 1], math.exp(lg[h] * L), eng="dve")

    slopes = [2.0 ** (-(h + 1)) for h in range(8)]
    SLQ = sb("SLQ", [128, 8])
    for h in range(8):
        ts(SLQ[:, h:h + 1], iota_q[:], slopes[h], ALU.mult, r=[iota_q], w=[SLQ])


    pchunks, schunks = make_chunks()
    blocks = [[pchunks[0]] + schunks]
    for b0 in range(1, 17, cpb):
        blocks.append(pchunks[b0:b0 + cpb])
    if n_blocks is not None:
        blocks = blocks[:n_blocks]
    BT = 128 * cpb
    NSLOT = max(5, cpb)
    WG = 256

    xt = [sb("xt%d" % i, [128, D]) for i in range(2)]
    hb = sb("hb", [128, D], BF16)
    hT = sb("hT", [128, KC, BT], BF16)
    wb = [sb("wb%d" % i, [128, KC, WG], BF16) for i in range(2)]
    SLOTA = sb("SLOTA", [128, 4096], BF16)
    SLOTB = sb("SLOTB", [128, 4096], BF16)
    wb.append(View(SLOTA, SLOTA[:, :].rearrange("p (k n) -> p k n", k=KC, n=WG)))
    wb.append(View(SLOTB, SLOTB[:, :].rearrange("p (k n) -> p k n", k=KC, n=WG)))
    assert NSLOT == 5 and WG == 256
    Gs = [sb("Gs%d" % i, [128, 4, 512], BF16) for i in range(2)]
    Gs.append(View(SLOTA, SLOTA[:, 0:2048].rearrange("p (a b) -> p a b", a=4, b=512)))
    Gs.append(View(SLOTA, SLOTA[:, 2048:4096].rearrange("p (a b) -> p a b", a=4, b=512)))
    Gs.append(View(SLOTB, SLOTB[:, 0:2048].rearrange("p (a b) -> p a b", a=4, b=512)))
    SM = [sb("SM%d" % i, [128, 16]) for i in range(NSLOT)]
    KV = [sb("KV%d" % i, [128, 256]) for i in range(2)]
    KV.append(View(SLOTB, SLOTB[:, 3584:4096].bitcast(F32)))
    KV += [sb("KV%d" % i, [128, 256]) for i in range(3, NSLOT)]
    VAUG = [sb("VAUG%d" % i, [128, 2, 72], BF16) for i in range(NSLOT)]
    VR = [sb("VR%d" % i, [128, 4, 128], BF16) for i in range(2)]
    for i in range(3):
        VR.append(View(SLOTB, SLOTB[:, 2048 + 512 * i:2560 + 512 * i].rearrange("p (a b) -> p a b", a=4, b=128)))
    KR = [sb("KR%d" % i, [128, 4, 64], BF16) for i in range(NSLOT)]
    XBC = sb("XBC", [128, 8, BT], BF16)
    XBCt = [Buf(XBC.t[:, ft_, :], "XBC%d" % ft_) for ft_ in range(8)]
    QA = sb("QA", [65, 8, BT], BF16)
    KA = sb("KA", [65, 2, BT], BF16)
    QR = sb("QR", [64, 4, BT], BF16)
    KRF = sb("KRF", [64, 4, BT], BF16)
    QKVD = sb("QKVD", [128, 12, BT], BF16)
    QKVDt = [Buf(QKVD.t[:, ft_, :], "QKVD%d" % ft_) for ft_ in range(12)]
    ST = [sb("ST%d" % i, [128, BT + 4]) for i in range(2)]
    ACC = [sb("ACC%d" % i, [128, BT]) for i in range(2)]
    carS = sb("carS", [128, 8, 3])
    carG = sb("carG", [128, 12, 3])
    cstSs = [sb("cstS%d" % i, [128, 8, NSS, 3]) for i in range(2)]
    cstGs = [sb("cstG%d" % i, [128, 12, NSS, 3]) for i in range(2)]
    csoS = sb("csoS", [128, 8, NSS, 3])
    csoG = sb("csoG", [128, 12, NSS, 3])
    Yb = [sb("Y%d" % i, [128, D], BF16) for i in range(2)]
    ssq = sb("ssq", [128, 8])
    lnv = sb("lnv", [128, 8])
    rstd = sb("rstd", [128, 8])
    smalls = {}
    for _n in ("E0_ssq", "E1_ssq", "E_tot", "E_lnv", "E_rstd", "F0_ssq", "F1_ssq", "F_tot", "F_lnv", "F0_rstd", "F1_rstd"):
        smalls[_n] = sb(_n, [128, 8])
    for _p in "ACD":
        for _n in ("ssq", "lnv", "rstd", "negcum", "ecum", "ecl", "dtr", "dtv", "lav"):
            smalls[_p + "_" + _n] = sb(_p + "_" + _n, [128, 8])
    sqj = sb("sqj", [128, 512], BF16)
    gTs = [sb("gT%d" % i, [128, KC]) for i in range(2)]
    postg = sb("postg", [128, D])
    ssdn = sb("ssdn", [128, 512])
    retn = sb("retn", [128, 512])
    gdnn = sb("gdnn", [128, 4, 128])
    cwSs = [sb("cwS%d" % i, [128, 8, 4]) for i in range(2)]
    cbSs = [sb("cbS%d" % i, [128, 8]) for i in range(2)]
    cwGs = [sb("cwG%d" % i, [128, 12, 4]) for i in range(2)]
    dtbS = sb("dtbS", [128, 8])
    AnS = sb("AnS", [128, 8])
    Dss = sb("Dss", [128, 8])
    ESQ = sb("ESQ", [128, 8])
    dtbG = sb("dtbG", [128, 4])
    AnG = sb("AnG", [128, 4])
    Sssd = [sb("Sssd%d" % i, [128, 8, 64]) for i in range(2)]
    Sssdb = [sb("Sssdb%d" % i, [128, 8, 64], BF16) for i in range(2)]
    Sret = [sb("Sret%d" % i, [64, 4, 128]) for i in range(2)]
    Sretb = [sb("Sretb%d" % i, [64, 4, 128], BF16) for i in range(2)]
    Sgdn = [sb("Sgdn%d" % i, [128, 4, 128]) for i in range(2)]
    Sgdnb = [sb("Sgdnb%d" % i, [128, 4, 128], BF16) for i in range(2)]
    PKA = sb("PKA", [65, 2, 128], BF16)
    PVA = sb("PVA", [128, 2, 72], BF16)
    PKAm = sb("PKAm", [65, 2, NMETA], BF16)
    PVAm = sb("PVAm", [128, 2, 72], BF16)
    cKb = sb("cKb", [128, 128], BF16)
    junkA = sb("junkA", [128, 512], BF16)
    junkC = sb("junkC", [128, 512], BF16)
    junkD = sb("junkD", [128, 512], BF16)
    C_ng = sb("C_ng", [128, 512])
    D_ng = sb("D_ng", [128, 512])
    decT = sb("decT", [128, 8, 128])
    negcum = sb("negcum", [128, 8])
    ecum = sb("ecum", [128, 8])
    ecl = sb("ecl", [128, 8])
    dtr = sb("dtr", [128, 8])
    dtv = sb("dtv", [128, 8])
    lav = sb("lav", [128, 8])
    MT = sb("MT", [128, 8, 128], BF16)
    xs_tm = sb("xs_tm", [128, 512], BF16)
    xdt = sb("xdt", [128, 512], BF16)
    xdtw = sb("xdtw", [128, 512], BF16)
    Btm = sb("Btm", [128, 256], BF16)
    D_f1 = sb("D_f1", [128, 512])
    D_f3 = sb("D_f3", [128, 512])
    B_f2 = sb("B_f2", [128, 512])
    cK = View(B_f2, B_f2[:, 0:256])
    C_MT = sb("C_MT", [128, 4, 128], BF16)
    D_decT = sb("D_decT", [128, 4, 128])
    kw = sb("kw", [128, 4, 64], BF16)
    beta = sb("beta", [128, 4])
    nbeta = sb("nbeta", [128, 4])
    Pm = [sb("Pm%d" % i, [128, 4, 128]) for i in range(2)]
    PTm = [sb("PTm%d" % i, [128, 4, 128]) for i in range(2)]
    Rm = [sb("Rm%d" % i, [128, 4, 128]) for i in range(2)]
    QKd = sb("QKd", [128, 4, 128], BF16)
    Vtm = sb("Vtm", [128, 512], BF16)
    knw = sb("knw", [128, 512], BF16)
    vnew = sb("vnew", [128, 512], BF16)
    PTs = [sb("PTs%d" % i, [128, 4, 128], BF16) for i in range(4)]
    den = sb("den", [128, 8])

    banks = [ps("bank%d" % i, [128, 512]) for i in range(8)]
    bctr = [0]

    def nb():
        b = banks[bctr[0] % 8]
        bctr[0] += 1
        return b

    def bfv(bk):
        return bk.t[:].bitcast(BF16)

    def v3(ap, a, b):
        return ap.rearrange("p (a b) -> p a b", a=a, b=b)

    for h in range(8):
        memset(QA, QA[64:65, h, :], 8.0 * slopes[h])
    G(lambda: nc.gpsimd.iota(PKA[64:65, :, :], pattern=[[0, 2], [1, 128]], base=-128, channel_multiplier=0,
                             allow_small_or_imprecise_dtypes=True), w=[PKA])
    G(lambda: nc.gpsimd.iota(PKAm[64:65, :, :], pattern=[[0, 2], [1, NMETA]], base=-NMETA, channel_multiplier=0,
                             allow_small_or_imprecise_dtypes=True), w=[PKAm])
    for i in range(NSLOT):
        memset(VAUG[i], VAUG[i][:, :, 64:65], 1.0)
    memset(PVA, PVA[:, :, 64:65], 1.0)
    memset(PVAm, PVAm[:, :, 64:65], 1.0)

    xbufs = {}

    def xbuf(bi):
        if bi not in xbufs:
            xbufs[bi] = Buf(None, "x%d" % bi)
        return xbufs[bi]

    wview_in = [wbf_in[l].rearrange("(kc p) n -> p kc n", p=128) for l in range(DEPTH)]
    wview_out = [wbf_out[l].rearrange("(kc p) n -> p kc n", p=128) for l in range(DEPTH)]
    wscr_bufs = {l: [Buf(None, "wscr%d_%d" % (l, i)) for i in range(5)] for l in range(DEPTH)}
    wctr = [0]
    nwb = [2]
    cur_layer = [0]

    def convert_weights(l, piece=None):
        for i in range(4):
            if piece is None or piece == i:
                dma(wbf_in[l, i * 512:(i + 1) * 512, :], w_in[l, i * 512:(i + 1) * 512, :], w=[wscr_bufs[l][i]],
                    eng="pool")
        if piece is None or piece == 4:
            dma(wbf_out[l], w_out[l], w=[wscr_bufs[l][4]], eng="pool")

    def load_w(view, c0, width, off=0, buf=None):
        if buf is None:
            buf = wb[wctr[0] % nwb[0]]
            wctr[0] += 1
        dma(buf[:, :, off:off + width], view[:, :, c0:c0 + width], r=wscr_bufs[cur_layer[0]], w=[buf], eng="sp")
        return buf

    def rms_stats(src_ap, L, n, scale, col=0):
        pass

    def chk(tag):
        if stop == tag:
            raise StopBuild()

    XA = [[f1, f2, C_f1, D_f1], [D_f3, B_f2, C_ng, D_ng]]
    a1_done = set()

    def tiles_of(bi2):
        if bi2 == 0:
            return [dict(row0=0, L=NMETA, tok0=0), dict(row0=NPT, L=NSS * LS, tok0=NMETA)]
        return [dict(row0=ch_["row0"], L=128, tok0=128 * i_) for i_, ch_ in enumerate(blocks[bi2])]

    def stage_a1(l2, bi2):
        if (l2, bi2) in a1_done:
            return
        a1_done.add((l2, bi2))
        xs2 = xin if l2 == 0 else xscr
        F_tot, F_lnv = smalls["F_tot"], smalls["F_lnv"]
        for ti, tl in enumerate(tiles_of(bi2)):
            L, r0 = tl["L"], tl["row0"]
            xa = XA[ti % 2]
            F_ssq, F_rstd = smalls["F%d_ssq" % (ti % 2)], smalls["F%d_rstd" % (ti % 2)]
            for c in range(4):
                dma(xa[c][0:L, :], xs2[r0:r0 + L, c * 512:(c + 1) * 512], r=[xbuf(bi2)], w=[xa[c]])
                act(junkC[0:L, :], xa[c][0:L, :], AF.Square, r=[xa[c]], w=[junkC, F_ssq], accum_out=F_ssq[0:L, c:c + 1])
            V(lambda L=L, F_ssq=F_ssq: nc.vector.tensor_reduce(out=F_tot[0:L, 0:1], in_=F_ssq[0:L, 0:4], axis=AX.X,
                                                               op=ALU.add), r=[F_ssq], w=[F_tot])
            act(F_lnv[0:L, 0:1], F_tot[0:L, 0:1], AF.Ln, r=[F_tot], w=[F_lnv], scale=1.0 / D, bias=EPS)
            act(F_rstd[0:L, 0:1], F_lnv[0:L, 0:1], AF.Exp, r=[F_lnv], w=[F_rstd], scale=-0.5)

    pending_epi = []

    def flush_epi():
        while pending_epi:
            pending_epi.pop(0)()

    parts_of = {}

    def multi_load(parent, pairs):
        parts = []
        for i, (o_, i_) in enumerate(pairs):
            pb = parent if i == 0 else Buf(None, "part")
            dma(o_, i_, w=[pb], eng="sp", allow_slow_non_contiguous=True)
            if i > 0:
                parts.append(pb)
        parts_of[id(parent)] = parts

    def RD(parent):
        return [parent] + parts_of.get(id(parent), [])

    def store_fm_rows(src_fn, srcbuf, nft, rows, dst2d, stg):
        for f0 in range(0, nft, 4):
            bk = nb()
            for j in range(4):
                tr(bk[0:rows, j * 128:(j + 1) * 128], src_fn(f0 + j), ident_f[:, :], r=[srcbuf, ident_f], w=[bk])
            cp(stg[0:rows, f0 * 128:(f0 + 4) * 128], bk[0:rows, 0:512], r=[bk], w=[stg])
        dma(dst2d, stg[0:rows, 0:nft * 128], r=[stg], w=[Buf(None, "fmrows")])

    def load_slow_params(l):
        p = l % 2
        multi_load(gTs[p], [(gTs[p][:], pre_norm[l].rearrange("(kc p) -> p kc", p=128))])
        multi_load(cwSs[p], [(cwSs[p][:, :, k_], ssd_conv_w[l, k_].rearrange("(ft p) -> p ft", p=128)) for k_ in range(4)])
        multi_load(cwGs[p], [(cwGs[p][:, :, k_], gdn_conv_w[l, k_].rearrange("(ft p) -> p ft", p=128)) for k_ in range(4)])
        multi_load(cbSs[p], [(cbSs[p][:], ssd_conv_b[l].rearrange("(ft p) -> p ft", p=128))])
        multi_load(cstSs[p], [(cstSs[p][:, :, s_, t_], st_ssdc[l, s_, t_].rearrange("(ft p) -> p ft", p=128))
                              for s_ in range(NSS) for t_ in range(3)])
        multi_load(cstGs[p], [(cstGs[p][:, :, s_, t_], st_gdnc[l, s_, t_].rearrange("(ft p) -> p ft", p=128))
                              for s_ in range(NSS) for t_ in range(3)])

    try:
        for l in range(n_layers):
            chk('const')
            cur_layer[0] = l
            if l == 0:
                convert_weights(0)
            xsrc = xin if l == 0 else xscr
            xdst = yout if l == n_layers - 1 else xscr
            if l == 0:
                load_slow_params(0)
            gT, cwS, cbS, cwG, cstS, cstG = [b[l % 2] for b in (gTs, cwSs, cbSs, cwGs, cstSs, cstGs)]
            dma(postg[:], bc(post_norm[l:l + 1, :], [128, D]), w=[postg])
            dma(ssdn[:], bc(ssd_norm[l:l + 1, :], [128, 512]), w=[ssdn])
            dma(retn[:], bc(ret_norm[l:l + 1, :], [128, 512]), w=[retn])
            for h in range(4):
                dma(gdnn[:, h, :], bc(gdn_norm[l:l + 1, :], [128, 128]), w=[gdnn])
            dma(dtbS[:], bc(ssd_dt_bias[l:l + 1, :], [128, 8]), w=[dtbS])
            dma(AnS[:], bc(ssd_a_log[l:l + 1, :], [128, 8]), w=[AnS])
            dma(Dss[:], bc(ssd_d[l:l + 1, :], [128, 8]), w=[Dss])
            dma(ESQ[:], bc(swa_sinks[l:l + 1, :], [128, 8]), w=[ESQ])
            dma(dtbG[:], bc(gdn_dt_bias[l:l + 1, :], [128, 4]), w=[dtbG])
            dma(AnG[:], bc(gdn_a_log[l:l + 1, :], [128, 4]), w=[AnG])
            act(AnS[:], AnS[:], AF.Exp, r=[AnS], w=[AnS])
            ts(AnS[:], AnS[:], -1.0, ALU.mult, r=[AnS], w=[AnS])
            act(AnG[:], AnG[:], AF.Exp, r=[AnG], w=[AnG])
            ts(AnG[:], AnG[:], -1.0, ALU.mult, r=[AnG], w=[AnG])
            tt(ESQ[:], ESQ[:], SLQ[:], ALU.add, r=[ESQ, SLQ], w=[ESQ])
            act(ESQ[:], ESQ[:], AF.Exp, r=[ESQ], w=[ESQ])
            memset(Sssd[0], Sssd[0][:], 0.0)
            memset(Sssdb[0], Sssdb[0][:], 0.0)
            memset(Sret[0], Sret[0][:], 0.0)
            memset(Sretb[0], Sretb[0][:], 0.0)
            memset(Sgdn[0], Sgdn[0][:], 0.0)
            memset(Sgdnb[0], Sgdnb[0][:], 0.0)
            memset(carS, carS[:], 0.0)
            memset(carG, carG[:], 0.0)

            for bi, blk in enumerate(blocks):
                is0 = (bi == 0)
                nwb[0] = 2 if is0 else 4
                tok = 0
                for si, ch in enumerate(blk):
                    ch["tok0"] = tok
                    ch["slot"] = si
                    tok += ch["L"]
                nbt = tok
                if is0:
                    tm_tiles = [dict(row0=0, L=NMETA, tok0=0), dict(row0=NPT, L=NSS * LS, tok0=NMETA)]
                else:
                    tm_tiles = [dict(row0=ch["row0"], L=128, tok0=ch["tok0"]) for ch in blk]
                if is0:
                    G(lambda: nc.gpsimd.iota(KA[64:65, :, 0:NMETA], pattern=[[0, 2], [1, NMETA]], base=0,
                                             channel_multiplier=0, allow_small_or_imprecise_dtypes=True), w=[KA])
                    G(lambda: nc.gpsimd.iota(KA[64:65, :, NMETA:NMETA + 16], pattern=[[0, 2], [0, NSS], [1, LS]], base=0,
                                             channel_multiplier=0, allow_small_or_imprecise_dtypes=True), w=[KA])
                elif bi == 1:
                    G(lambda: nc.gpsimd.iota(KA[64:65, :, :], pattern=[[0, 2], [0, cpb], [1, 128]], base=0,
                                             channel_multiplier=0, allow_small_or_imprecise_dtypes=True), w=[KA])

                chk('params')
                stage_a1(l, bi)
                for ti, tl in enumerate(tm_tiles):
                    L, r0, t0 = tl["L"], tl["row0"], tl["tok0"]
                    xa = XA[ti % 2]
                    F_rstd = smalls["F%d_rstd" % (ti % 2)]
                    for c in range(4):
                        ts(hb[0:L, c * 512:(c + 1) * 512], xa[c][0:L, :], F_rstd[0:L, 0:1], ALU.mult, r=[xa[c], F_rstd],
                           w=[hb])
                    for q in range(4):
                        bk = nb()
                        bv = bfv(bk)
                        for j in range(4):
                            kc = 4 * q + j
                            tr(bv[:, j * 128:j * 128 + L], hb[0:L, kc * 128:(kc + 1) * 128], ident_b[0:L, 0:L],
                               r=[hb, ident_b], w=[bk])
                        tt(hT[:, 4 * q:4 * q + 4, t0:t0 + L], v3(bv[:, 0:512], 4, 128)[:, :, 0:L],
                           bc(gT[:, 4 * q:4 * q + 4].unsqueeze(2), [128, 4, L]), ALU.mult, r=[bk] + RD(gT), w=[hT])
                flush_epi()

                def decay(la, L, H, decT_, negcum_, ecum_, ecl_):
                    bk = nb()
                    mm(bk[0:L, 0:H], Umat[0:L, 0:L], la[0:L, 0:H], True, True, r=[Umat, la], w=[bk])
                    ts(negcum_[0:L, 0:H], bk[0:L, 0:H], -1.0, ALU.mult, r=[bk], w=[negcum_])
                    act(ecum_[0:L, 0:H], bk[0:L, 0:H], AF.Exp, r=[bk], w=[ecum_])
                    yield
                    for hq in range(H // 4):
                        bk = nb()
                        for hh in range(4):
                            h = 4 * hq + hh
                            o = bk[:, hh * 128:hh * 128 + L]
                            mm(o, bc(la[0:L, h:h + 1], [L, 128]), Umat[0:L, 0:L], True, False, r=[la, Umat], w=[bk])
                            mm(o, ident_f[0:L, :], NEGU[0:L, 0:L], False, True, r=[ident_f, NEGU], w=[bk])
                        yield
                        for hh in range(4):
                            h = 4 * hq + hh
                            act(decT_[0:L, h, 0:L], bk[0:L, hh * 128:hh * 128 + L], AF.Exp, r=[bk, negcum_], w=[decT_],
                                bias=negcum_[0:L, h:h + 1])
                        act(ecl_[:, 4 * hq:4 * hq + 4], v3(bk[:, 0:512], 4, 128)[:, :, L - 1], AF.Exp, r=[bk], w=[ecl_])
                        yield

                def softplus_la(dst, src_ap, src_bufs, dtb, An, L, H, dtr_, keep_dt=None):
                    tt(dtr_[0:L, 0:H], src_ap, dtb[0:L, 0:H], ALU.add, r=src_bufs + [dtb], w=[dtr_])
                    act(dtr_[0:L, 0:H], dtr_[0:L, 0:H], AF.Exp, r=[dtr_], w=[dtr_])
                    tgt = keep_dt if keep_dt is not None else dtr_
                    act(tgt[0:L, 0:H], dtr_[0:L, 0:H], AF.Ln, r=[dtr_], w=[tgt], bias=1.0)
                    tt(dst[0:L, 0:H], tgt[0:L, 0:H], An[0:L, 0:H], ALU.mult, r=[tgt, An], w=[dst])

                def head_rmsnorm_gate(o_buf, junk_, ssq_, lnv_, rstd_, Yc, L, nh, hd, ng_, ycols):
                    n = nh * hd
                    for h in range(nh):
                        act(junk_[0:L, h * hd:(h + 1) * hd], o_buf[0:L, h * hd:(h + 1) * hd], AF.Square, r=[o_buf],
                            w=[junk_, ssq_], accum_out=ssq_[0:L, h:h + 1])
                    act(lnv_[0:L, 0:nh], ssq_[0:L, 0:nh], AF.Ln, r=[ssq_], w=[lnv_], scale=1.0 / hd, bias=EPS)
                    act(rstd_[0:L, 0:nh], lnv_[0:L, 0:nh], AF.Exp, r=[lnv_], w=[rstd_], scale=-0.5)
                    yield
                    tt(v3(o_buf[0:L, 0:n], nh, hd), v3(o_buf[0:L, 0:n], nh, hd), bc(rstd_[0:L, 0:nh].unsqueeze(2), [L, nh, hd]),
                       ALU.mult, r=[o_buf, rstd_], w=[o_buf])
                    tt(Yc[0:L, ycols:ycols + n], o_buf[0:L, 0:n], ng_[0:L, 0:n], ALU.mult, r=[o_buf, ng_], w=[Yc])

                def chunk_ctx(ch):
                    sid = 0 if ch["kind"] == "p" else 1
                    return ch["L"], ch["tok0"], ch["slot"], sid, ch.get("seq", None), Gs[ch["slot"]], Yb[ch["slot"] % 2]

                def ssd_thread():
                    A = lambda n: smalls["A_" + n]
                    ssq_, lnv_, rstd_, negcum_, ecum_, ecl_, dtr_, dtv_, lav_ = [A(n) for n in (
                        "ssq", "lnv", "rstd", "negcum", "ecum", "ecl", "dtr", "dtv", "lav")]
                    for ch in blk:
                        L, t0, slot, sid, seq, G_, Yc = chunk_ctx(ch)
                        while slot >= 2 and not fin.get(slot - 2):
                            yield
                        S1, S1b = Sssd[sid], Sssdb[sid]
                        if sid == 1:
                            dma(S1[:], st_ssd[l, seq].rearrange("h n e -> n h e"), w=[S1])
                            cp(S1b[:], S1[:], r=[S1], w=[S1b], eng="act")
                        softplus_la(lav_, SM[slot][0:L, 0:8], [SM[slot]], dtbS, AnS, L, 8, dtr_, keep_dt=dtv_)
                        yield
                        yield from decay(lav_, L, 8, decT, negcum_, ecum_, ecl_)
                        yield ("wait_proj",)
                        bk = nb()
                        bv = bfv(bk)
                        for ft in range(4):
                            tr(bv[0:L, ft * 128:(ft + 1) * 128], XBCt[ft][:, t0:t0 + L], ident_b[:, :], r=[XBCt[ft], ident_b], w=[bk])
                        yield
                        cp(xs_tm[0:L, :], bv[0:L, 0:512], r=[bk], w=[xs_tm], eng="act")
                        tt(v3(xdt[0:L, :], 8, 64), v3(bv[0:L, 0:512], 8, 64), bc(dtv_[0:L, 0:8].unsqueeze(2), [L, 8, 64]),
                           ALU.mult, r=[bk, dtv_], w=[xdt])
                        bk = nb()
                        bv = bfv(bk)
                        for g in range(2):
                            tr(bv[0:L, g * 128:(g + 1) * 128], XBCt[4 + g][:, t0:t0 + L], ident_b[:, :], r=[XBCt[4 + g], ident_b], w=[bk])
                        yield
                        cp(Btm[0:L, :], bv[0:L, 0:256], r=[bk], w=[Btm], eng="act")
                        bk = nb()
                        for g in range(2):
                            mm(bk[0:L, g * 128:g * 128 + L], XBCt[4 + g][:, t0:t0 + L], XBCt[6 + g][:, t0:t0 + L], True, True,
                               r=[XBCt[4 + g], XBCt[6 + g]], w=[bk])
                        yield
                        for g in range(2):
                            tt(MT[0:L, 4 * g:4 * g + 4, 0:L], bc(bk[0:L, g * 128:g * 128 + L].unsqueeze(1), [L, 4, L]),
                               decT[0:L, 4 * g:4 * g + 4, 0:L], ALU.mult, r=[bk, decT], w=[MT])
                        yield
                        bki = nb()
                        for h in range(8):
                            mm(bki[0:L, h * 64:(h + 1) * 64], MT[0:L, h, 0:L], xdt[0:L, h * 64:(h + 1) * 64], True, True,
                               r=[MT, xdt], w=[bki])
                        bks = nb()
                        for h in range(8):
                            mm(bks[0:L, h * 64:(h + 1) * 64], XBCt[6 + h // 4][:, t0:t0 + L], S1b[:, h, :], True, True,
                               r=[XBCt[6 + h // 4], S1b], w=[bks])
                        yield
                        tt(v3(f1[0:L, :], 8, 64), v3(bks[0:L, :], 8, 64), bc(ecum_[0:L, 0:8].unsqueeze(2), [L, 8, 64]), ALU.mult,
                           r=[bks, ecum_], w=[f1])
                        tt(f1[0:L, :], bki[0:L, :], f1[0:L, :], ALU.add, r=[bki, f1], w=[f1])
                        tt(v3(f2[0:L, :], 8, 64), v3(xs_tm[0:L, :], 8, 64), bc(Dss[0:L, 0:8].unsqueeze(2), [L, 8, 64]), ALU.mult,
                           r=[xs_tm, Dss], w=[f2], eng="pool")
                        yield
                        tt(f1[0:L, :], f1[0:L, :], f2[0:L, :], ALU.add, r=[f1, f2], w=[f1])
                        tt(f1[0:L, :], f1[0:L, :], G_[0:L, 0, :], ALU.mult, r=[f1, G_], w=[f1])
                        for g in range(2):
                            act(junkA[0:L, g * 256:(g + 1) * 256], f1[0:L, g * 256:(g + 1) * 256], AF.Square, r=[f1],
                                w=[junkA, ssq_], accum_out=ssq_[0:L, g:g + 1])
                        yield
                        act(lnv_[0:L, 0:2], ssq_[0:L, 0:2], AF.Ln, r=[ssq_], w=[lnv_], scale=1.0 / 256, bias=EPS)
                        act(rstd_[0:L, 0:2], lnv_[0:L, 0:2], AF.Exp, r=[lnv_], w=[rstd_], scale=-0.5)
                        yield
                        tt(v3(f2[0:L, :], 2, 256), v3(f1[0:L, :], 2, 256), bc(rstd_[0:L, 0:2].unsqueeze(2), [L, 2, 256]),
                           ALU.mult, r=[f1, rstd_], w=[f2])
                        tt(Yc[0:L, 0:512], f2[0:L, :], ssdn[0:L, :], ALU.mult, r=[f2, ssdn], w=[Yc])
                        yield
                        tt(v3(xdtw[0:L, :], 8, 64), v3(xdt[0:L, :], 8, 64), bc(decT[0:L, 0:8, L - 1:L], [L, 8, 64]), ALU.mult,
                           r=[xdt, decT], w=[xdtw])
                        bkn = nb()
                        for h in range(8):
                            mm(bkn[:, h * 64:(h + 1) * 64], Btm[0:L, (h // 4) * 128:(h // 4 + 1) * 128],
                               xdtw[0:L, h * 64:(h + 1) * 64], True, True, r=[Btm, xdtw], w=[bkn])
                        tt(S1[:], S1[:], bc(ecl_[:, 0:8].unsqueeze(2), [128, 8, 64]), ALU.mult, r=[S1, ecl_], w=[S1])
                        yield
                        tt(S1[:], v3(bkn[:, :], 8, 64), S1[:], ALU.add, r=[bkn, S1], w=[S1])
                        cp(S1b[:], S1[:], r=[S1], w=[S1b], eng="act")
                        if sid == 1:
                            dma(o_s_ssd[l, seq].rearrange("h n e -> n h e"), S1[:], r=[S1], w=[dbuf("o_s_ssd")])
                        yield ("done", ch["slot"])

                def ret_thread():
                    A = lambda n: smalls["C_" + n]
                    ssq_, lnv_, rstd_ = A("ssq"), A("lnv"), A("rstd")
                    for ch in blk:
                        L, t0, slot, sid, seq, G_, Yc = chunk_ctx(ch)
                        while slot >= 2 and not fin.get(slot - 2):
                            yield
                        S2, S2b = Sret[sid], Sretb[sid]
                        tt(C_ng[0:L, :], retn[0:L, :], G_[0:L, 2, :], ALU.mult, r=[retn, G_], w=[C_ng], eng="pool")
                        yield ("wait_proj",)
                        if sid == 1:
                            dma(S2[:], st_ret[l, seq].rearrange("h d e -> d h e"), w=[S2])
                            cp(S2b[:], S2[:], r=[S2], w=[S2b], eng="act")
                        bk = nb()
                        for h in range(4):
                            mm(bk[0:L, h * 128:h * 128 + L], KRF[0:64, h, t0:t0 + L], QR[0:64, h, t0:t0 + L], True, True,
                               r=[KRF, QR], w=[bk])
                        yield
                        tt(C_MT[0:L, 0:4, 0:L], v3(bk[0:L, :], 4, 128)[:, :, 0:L], RDEC[0:L, :, 0:L], ALU.mult, r=[bk, RDEC],
                           w=[C_MT])
                        yield
                        bki = nb()
                        for h in range(4):
                            mm(bki[0:L, h * 128:(h + 1) * 128], C_MT[0:L, h, 0:L], VR[slot][0:L, h, :], True, True,
                               r=[C_MT, VR[slot]], w=[bki])
                        bks = nb()
                        for h in range(4):
                            mm(bks[0:L, h * 128:(h + 1) * 128], QR[0:64, h, t0:t0 + L], S2b[:, h, :], True, True,
                               r=[QR, S2b], w=[bks])
                        yield
                        tt(v3(C_f1[0:L, :], 4, 128), v3(bks[0:L, :], 4, 128), bc(GPOW[0:L, 0:4].unsqueeze(2), [L, 4, 128]),
                           ALU.mult, r=[bks, GPOW], w=[C_f1])
                        tt(C_f1[0:L, :], bki[0:L, :], C_f1[0:L, :], ALU.add, r=[bki, C_f1], w=[C_f1])
                        yield
                        yield from head_rmsnorm_gate(C_f1, junkC, ssq_, lnv_, rstd_, Yc, L, 4, 128, C_ng, 1024)
                        yield
                        tt(kw[0:L, :, :], KR[slot][0:L, :, :], bc(RW[L][0:L, 0:4].unsqueeze(2), [L, 4, 64]), ALU.mult,
                           r=[KR[slot], RW[L]], w=[kw], eng="pool")
                        bkn = nb()
                        for h in range(4):
                            mm(bkn[0:64, h * 128:(h + 1) * 128], kw[0:L, h, :], VR[slot][0:L, h, :], True, True,
                               r=[kw, VR[slot]], w=[bkn])
                        tt(S2[:], S2[:], bc(GL[L][:, 0:4].unsqueeze(2), [64, 4, 128]), ALU.mult, r=[S2, GL[L]], w=[S2])
                        yield
                        tt(S2[:], v3(bkn[0:64, :], 4, 128), S2[:], ALU.add, r=[bkn, S2], w=[S2])
                        cp(S2b[:], S2[:], r=[S2], w=[S2b], eng="act")
                        if sid == 1:
                            dma(o_s_ret[l, seq].rearrange("h d e -> d h e"), S2[:], r=[S2], w=[dbuf("o_s_ret")])
                        yield ("done", ch["slot"])

                def gdn_thread():
                    A = lambda n: smalls["D_" + n]
                    ssq_, lnv_, rstd_, negcum_, ecum_, ecl_, dtr_, lav_ = [A(n) for n in (
                        "ssq", "lnv", "rstd", "negcum", "ecum", "ecl", "dtr", "lav")]
                    M1 = Pm[1]
                    for ch in blk:
                        L, t0, slot, sid, seq, G_, Yc = chunk_ctx(ch)
                        while slot >= 2 and not fin.get(slot - 2):
                            yield
                        S3, S3b = Sgdn[sid], Sgdnb[sid]
                        tt(D_ng[0:L, :], gdnn[0:L, :, :].rearrange("p a b -> p (a b)"), G_[0:L, 3, :], ALU.mult, r=[gdnn, G_],
                           w=[D_ng], eng="pool")
                        if sid == 1:
                            dma(S3[:], st_gdn[l, seq].rearrange("h d e -> d h e"), w=[S3])
                            cp(S3b[:], S3[:], r=[S3], w=[S3b], eng="act")
                        act(beta[0:L, :], SM[slot][0:L, 8:12], AF.Exp, r=[SM[slot]], w=[beta], scale=-1.0)
                        ts(beta[0:L, :], beta[0:L, :], 1.0, ALU.add, r=[beta], w=[beta])
                        V(lambda L=L: nc.vector.reciprocal(out=beta[0:L, :], in_=beta[0:L, :]), r=[beta], w=[beta])
                        ts(nbeta[0:L, :], beta[0:L, :], -1.0, ALU.mult, r=[beta], w=[nbeta])
                        yield
                        softplus_la(lav_, SM[slot][0:L, 12:16], [SM[slot]], dtbG, AnG, L, 4, dtr_)
                        yield
                        yield from decay(lav_, L, 4, D_decT, negcum_, ecum_, ecl_)
                        yield ("wait_proj",)
                        bk = nb()
                        bv = bfv(bk)
                        for h in range(4):
                            tr(bv[0:L, h * 128:(h + 1) * 128], QKVDt[4 + h][:, t0:t0 + L], ident_b[:, :], r=[QKVDt[4 + h], ident_b],
                               w=[bk])
                        yield
                        tt(v3(knw[0:L, :], 4, 128), v3(bv[0:L, 0:512], 4, 128), bc(D_decT[0:L, 0:4, L - 1:L], [L, 4, 128]),
                           ALU.mult, r=[bk, D_decT], w=[knw])
                        bk = nb()
                        bv = bfv(bk)
                        for h in range(4):
                            tr(bv[0:L, h * 128:(h + 1) * 128], QKVDt[8 + h][:, t0:t0 + L], ident_b[:, :], r=[QKVDt[8 + h], ident_b],
                               w=[bk])
                        yield
                        cp(Vtm[0:L, :], bv[0:L, 0:512], r=[bk], w=[Vtm], eng="act")
                        bkg = nb()
                        bkq = nb()
                        for h in range(4):
                            mm(bkg[0:L, h * 128:h * 128 + L], QKVDt[4 + h][:, t0:t0 + L], QKVDt[4 + h][:, t0:t0 + L], True, True,
                               r=[QKVDt[4 + h]], w=[bkg])
                        for h in range(4):
                            mm(bkq[0:L, h * 128:h * 128 + L], QKVDt[4 + h][:, t0:t0 + L], QKVDt[h][:, t0:t0 + L], True, True,
                               r=[QKVDt[4 + h], QKVDt[h]], w=[bkq])
                        yield
                        tt(M1[0:L, :, 0:L], v3(bkg[0:L, :], 4, 128)[:, :, 0:L], D_decT[0:L, 0:4, 0:L], ALU.mult,
                           r=[bkg, D_decT], w=[M1])
                        tt(QKd[0:L, :, 0:L], v3(bkq[0:L, :], 4, 128)[:, :, 0:L], D_decT[0:L, 0:4, 0:L], ALU.mult,
                           r=[bkq, D_decT], w=[QKd])
                        yield
                        tt(M1[0:L, :, 0:L], M1[0:L, :, 0:L], bc(nbeta[0:L, 0:4].unsqueeze(2), [L, 4, L]), ALU.mult,
                           r=[M1, nbeta], w=[M1])
                        P0_, PT0_ = Pm[0], PTm[0]
                        tt(P0_[0:L, :, 0:L], M1[0:L, :, 0:L], bc(SUm[0:L, 0:L].unsqueeze(1), [L, 4, L]), ALU.mult,
                           r=[M1, SUm], w=[P0_])
                        yield
                        bk = nb()
                        for h in range(4):
                            tr(bk[0:L, h * 128:h * 128 + L], P0_[0:L, h, 0:L], ident_f[0:L, 0:L], r=[P0_, ident_f], w=[bk])
                        yield
                        cp(PT0_[0:L, :, 0:L], v3(bk[0:L, :], 4, 128)[:, :, 0:L], r=[bk], w=[PT0_], eng="act")
                        R_ = Rm[0]
                        tt(R_[0:L, :, 0:L], P0_[0:L, :, 0:L], bc(ident_f[0:L, 0:L].unsqueeze(1), [L, 4, L]), ALU.add,
                           r=[P0_, ident_f], w=[R_], eng="pool")
                        yield
                        nlev = max(1, int(math.ceil(math.log2(L))))
                        cur = 0
                        for k in range(1, nlev):
                            Pc, PTc = Pm[cur], PTm[cur]
                            Pn, PTn = Pm[1 - cur], PTm[1 - cur]
                            last = (k == nlev - 1)
                            bkt = nb()
                            for h in range(4):
                                mm(bkt[0:L, h * 128:h * 128 + L], Pc[0:L, h, 0:L], PTc[0:L, h, 0:L], True, True,
                                   r=[Pc, PTc], w=[bkt])
                            if not last:
                                bkp = nb()
                                for h in range(4):
                                    mm(bkp[0:L, h * 128:h * 128 + L], PTc[0:L, h, 0:L], Pc[0:L, h, 0:L], True, True,
                                       r=[Pc, PTc], w=[bkp])
                            yield
                            cp(PTn[0:L, :, 0:L], v3(bkt[0:L, :], 4, 128)[:, :, 0:L], r=[bkt], w=[PTn], eng="act")
                            if not last:
                                cp(Pn[0:L, :, 0:L], v3(bkp[0:L, :], 4, 128)[:, :, 0:L], r=[bkp], w=[Pn])
                            yield
                            Rc, Rn = Rm[cur], Rm[1 - cur]
                            bkr = nb()
                            for h in range(4):
                                mm(bkr[0:L, h * 128:h * 128 + L], PTn[0:L, h, 0:L], Rc[0:L, h, 0:L], True, True,
                                   r=[PTn, Rc], w=[bkr])
                            yield
                            tt(Rn[0:L, :, 0:L], v3(bkr[0:L, :], 4, 128)[:, :, 0:L], Rc[0:L, :, 0:L], ALU.add, r=[bkr, Rc],
                               w=[Rn])
                            yield
                            cur = 1 - cur
                        Rf = Rm[cur]
                        bk = nb()
                        for h in range(4):
                            mm(bk[0:L, h * 128:(h + 1) * 128], QKVDt[4 + h][:, t0:t0 + L], S3b[:, h, :], True, True,
                               r=[QKVDt[4 + h], S3b], w=[bk])
                        yield
                        tt(v3(D_f1[0:L, :], 4, 128), v3(bk[0:L, :], 4, 128), bc(ecum_[0:L, 0:4].unsqueeze(2), [L, 4, 128]),
                           ALU.mult, r=[bk, ecum_], w=[D_f1])
                        tt(D_f3[0:L, :], Vtm[0:L, :], D_f1[0:L, :], ALU.subtract, r=[Vtm, D_f1], w=[D_f3])
                        yield
                        bk = nb()
                        for h in range(4):
                            mm(bk[0:L, h * 128:(h + 1) * 128], Rf[0:L, h, 0:L], D_f3[0:L, h * 128:(h + 1) * 128], True, True,
                               r=[Rf, D_f3], w=[bk])
                        yield
                        tt(v3(vnew[0:L, :], 4, 128), v3(bk[0:L, :], 4, 128), bc(beta[0:L, 0:4].unsqueeze(2), [L, 4, 128]),
                           ALU.mult, r=[bk, beta], w=[vnew])
                        yield
                        bks = nb()
                        for h in range(4):
                            mm(bks[0:L, h * 128:(h + 1) * 128], QKVDt[h][:, t0:t0 + L], S3b[:, h, :], True, True,
                               r=[QKVDt[h], S3b], w=[bks])
                        bki = nb()
                        for h in range(4):
                            mm(bki[0:L, h * 128:(h + 1) * 128], QKd[0:L, h, 0:L], vnew[0:L, h * 128:(h + 1) * 128], True, True,
                               r=[QKd, vnew], w=[bki])
                        yield
                        tt(v3(D_f1[0:L, :], 4, 128), v3(bks[0:L, :], 4, 128), bc(ecum_[0:L, 0:4].unsqueeze(2), [L, 4, 128]),
                           ALU.mult, r=[bks, ecum_], w=[D_f1])
                        tt(D_f1[0:L, :], bki[0:L, :], D_f1[0:L, :], ALU.add, r=[bki, D_f1], w=[D_f1])
                        yield
                        yield from head_rmsnorm_gate(D_f1, junkD, ssq_, lnv_, rstd_, Yc, L, 4, 128, D_ng, 1536)
                        yield
                        bkn = nb()
                        for h in range(4):
                            mm(bkn[:, h * 128:(h + 1) * 128], knw[0:L, h * 128:(h + 1) * 128],
                               vnew[0:L, h * 128:(h + 1) * 128], True, True, r=[knw, vnew], w=[bkn])
                        tt(S3[:], S3[:], bc(ecl_[:, 0:4].unsqueeze(2), [128, 4, 128]), ALU.mult, r=[S3, ecl_], w=[S3])
                        yield
                        tt(S3[:], v3(bkn[:, :], 4, 128), S3[:], ALU.add, r=[bkn, S3], w=[S3])
                        cp(S3b[:], S3[:], r=[S3], w=[S3b], eng="act")
                        if sid == 1:
                            dma(o_s_gdn[l, seq].rearrange("h d e -> d h e"), S3[:], r=[S3], w=[dbuf("o_s_gdn")])
                        yield ("done", ch["slot"])

                def swa_thread():
                    for ch in blk:
                        L, t0, slot, sid, seq, G_, Yc = chunk_ctx(ch)
                        while slot >= 2 and not fin.get(slot - 2):
                            yield
                        if sid == 1:
                            dma(cK[:, 0:128], st_k[l, seq], w=[cK])
                            dma(cK[:, 128:256], st_v[l, seq], w=[cK])
                            cp(cKb[:, :], cK[:, 0:128], r=[cK], w=[cKb])
                            bk = nb()
                            bv = bfv(bk)
                            for g in range(2):
                                tr(bv[0:64, g * 128:(g + 1) * 128], cKb[:, g * 64:(g + 1) * 64], ident_b[:, :],
                                   r=[cKb, ident_b], w=[bk])
                            yield
                            cp(PKA[0:64, :, :], v3(bv[0:64, 0:256], 2, 128), r=[bk], w=[PKA])
                            cp(PVA[:, :, 0:64], v3(cK[:, 128:256], 2, 64), r=[cK], w=[PVA])
                            prev = (PKA, PVA, 128, NEGprev)
                        elif ch["ci"] == 0:
                            prev = None
                        elif ch["ci"] == 1:
                            prev = (PKAm, PVAm, NMETA, NEGmeta)
                        else:
                            prev = (PKA, PVA, 128, NEGprev)
                        for g in range(2):
                            tiles = []
                            if prev is not None:
                                tiles.append((prev[0], prev[0][0:65, g, 0:prev[2]], prev[1], prev[1][0:prev[2], g, 0:65],
                                              prev[2], prev[3]))
                            tiles.append((KA, KA[0:65, g, t0:t0 + L], VAUG[slot], VAUG[slot][0:L, g, 0:65], L, NEGown))
                            pts = []
                            for ti, (kbuf, kap, vbuf, vap, Lk, negm) in enumerate(tiles):
                                bk = nb()
                                for hh in range(4):
                                    h = 4 * g + hh
                                    o = bk[0:Lk, hh * 128:hh * 128 + L]
                                    mm(o, kap, QA[0:65, h, t0:t0 + L], True, False, r=[kbuf, QA], w=[bk])
                                    mm(o, ident_b[0:Lk, 0:Lk], negm[0:Lk, 0:L], False, True, r=[ident_b, negm], w=[bk])
                                yield
                                pt = PTs[2 * g + ti] if len(tiles) == 2 else PTs[2 * g + 1]
                                act(pt[0:Lk, :, 0:L], v3(bk[0:Lk, :], 4, 128)[:, :, 0:L], AF.Exp, r=[bk], w=[pt], scale=0.125)
                                pts.append((pt, vbuf, vap, Lk))
                                yield
                            bko = nb()
                            for hh in range(4):
                                for ti, (pt, vbuf, vap, Lk) in enumerate(pts):
                                    mm(bko[0:L, hh * 72:hh * 72 + 65], pt[0:Lk, hh, 0:L], vap, ti == 0, ti == len(pts) - 1,
                                       r=[pt, vbuf], w=[bko])
                            yield
                            ov = v3(bko[0:L, 0:288], 4, 72)
                            tt(den[0:L, 4 * g:4 * g + 4], ov[:, :, 64], ESQ[0:L, 4 * g:4 * g + 4], ALU.add, r=[bko, ESQ],
                               w=[den])
                            V(lambda L=L, g=g: nc.vector.reciprocal(out=den[0:L, 4 * g:4 * g + 4],
                                                                    in_=den[0:L, 4 * g:4 * g + 4]), r=[den], w=[den])
                            tt(v3(B_f2[0:L, 256 * g:256 * g + 256], 4, 64), ov[:, :, 0:64],
                               bc(den[0:L, 4 * g:4 * g + 4].unsqueeze(2), [L, 4, 64]), ALU.mult, r=[bko, den], w=[B_f2])
                            yield
                        tt(Yc[0:L, 512:1024], B_f2[0:L, :], G_[0:L, 1, :], ALU.mult, r=[B_f2, G_], w=[Yc])
                        if sid == 0:
                            if ch["ci"] == 0:
                                cp(PKAm[0:64, :, 0:L], KA[0:64, :, t0:t0 + L], r=[KA], w=[PKAm], eng="pool")
                                cp(PVAm[0:L, :, 0:64], VAUG[slot][0:L, :, 0:64], r=[VAUG[slot]], w=[PVAm], eng="pool")
                            else:
                                cp(PKA[0:64, :, 0:L], KA[0:64, :, t0:t0 + L], r=[KA], w=[PKA], eng="pool")
                                cp(PVA[0:L, :, 0:64], VAUG[slot][0:L, :, 0:64], r=[VAUG[slot]], w=[PVA], eng="pool")
                        if sid == 1:
                            dma(o_s_k[l, seq, 0:128 - LS, :], st_k[l, seq, LS:128, :], w=[dbuf("o_s_k")])
                            dma(o_s_v[l, seq, 0:128 - LS, :], st_v[l, seq, LS:128, :], w=[dbuf("o_s_v")])
                            dma(o_s_k[l, seq, 128 - LS:128, :], KV[slot][0:LS, 0:128], r=[KV[slot]], w=[dbuf("o_s_k")])
                            dma(o_s_v[l, seq, 128 - LS:128, :], KV[slot][0:LS, 128:256], r=[KV[slot]], w=[dbuf("o_s_v")])
                        elif ch["ci"] == 16:
                            dma(o_p_k[l], KV[slot][:, 0:128], r=[KV[slot]], w=[dbuf("o_p_k")])
                            dma(o_p_v[l], KV[slot][:, 128:256], r=[KV[slot]], w=[dbuf("o_p_v")])
                        yield ("done", ch["slot"])

                def finish_chunk(ch):
                    L, t0, slot, sid, seq, G_, Yc = chunk_ctx(ch)
                    if dbg:
                        dma(ydbg[ch["row0"]:ch["row0"] + L, :], Yc[0:L, :], r=[Yc], w=[dbuf("ydbg")])
                    for q in range(4):
                        bk = nb()
                        bv = bfv(bk)
                        for j in range(4):
                            kc = 4 * q + j
                            tr(bv[:, j * 128:j * 128 + L], Yc[0:L, kc * 128:(kc + 1) * 128], ident_b[0:L, 0:L],
                               r=[Yc, ident_b], w=[bk])
                        cp(hT[:, 4 * q:4 * q + 4, t0:t0 + L], v3(bv[:, 0:512], 4, 128)[:, :, 0:L], r=[bk], w=[hT],
                           eng="act" if q % 2 else "dve")

                chk('A')
                def tm_group(wbuf, off, n, evac):
                    for ch in blk:
                        L, t0, slot = ch["L"], ch["tok0"], ch["slot"]
                        bk = nb()
                        for kc in range(KC):
                            mm(bk[0:L, 0:n], hT[:, kc, t0:t0 + L], wbuf[:, kc, off:off + n], kc == 0, kc == KC - 1,
                               r=[hT, wbuf], w=[bk])
                        evac(bk, ch)

                def fm_group(wbuf, off, M, evac):
                    bk = nb()
                    for kc in range(KC):
                        mm(bk[0:M, 0:nbt], wbuf[:, kc, off:off + M], hT[:, kc, 0:nbt], kc == 0, kc == KC - 1,
                           r=[hT, wbuf], w=[bk])
                    evac(bk)

                def gate_evac(gi, half):
                    def f(bk, ch):
                        L = ch["L"]
                        act(Gs[ch["slot"]][0:L, gi, half * 256:(half + 1) * 256], bk[0:L, 0:256], AF.Silu, r=[bk],
                            w=[Gs[ch["slot"]]])
                    return f

                cctr = [0]

                def conv_evac(dst, ft, cw, cb, car, cst, cso):
                    def f(bk):
                        k = cctr[0] % 2
                        cctr[0] += 1
                        S_, A_ = ST[k], ACC[k]
                        if not is0:
                            n = nbt
                            cp(S_[:, 0:3], car[:, ft, :], r=[car], w=[S_], eng="pool")
                            cp(S_[:, 3:3 + n], bk[:, 0:n], r=[bk], w=[S_], eng="act")
                            cp(car[:, ft, :], S_[:, n:n + 3], r=[S_], w=[car], eng="pool")
                            no = n
                        else:
                            memset(S_, S_[:, 0:3], 0.0)
                            sv = v3(S_[:, 19:47], NSS, 7)
                            cp(sv[:, :, 0:3], cst[:, ft, :, :], r=RD(cst), w=[S_], eng="pool")
                            cp(S_[:, 3:19], bk[:, 0:16], r=[bk], w=[S_], eng="act")
                            cp(sv[:, :, 3:7], v3(bk[:, 16:32], NSS, LS), r=[bk], w=[S_], eng="act")
                            cp(car[:, ft, :], S_[:, 16:19], r=[S_], w=[car], eng="pool")
                            cp(cso[:, ft, :, :], sv[:, :, 4:7], r=[S_], w=[cso], eng="pool")
                            no = 44
                        ts(A_[:, 0:no], S_[:, 0:no], cw[:, ft, 0:1], ALU.mult, r=[S_] + RD(cw), w=[A_])
                        for kk in range(1, 4):
                            stt(A_[:, 0:no], S_[:, kk:kk + no], cw[:, ft, kk:kk + 1], A_[:, 0:no], ALU.mult, ALU.add,
                                r=[S_, A_] + RD(cw), w=[A_])
                        bias = cb[:, ft:ft + 1] if cb is not None else None
                        rr = [A_] + (RD(cb) if cb is not None else [])
                        if not is0:
                            act(dst[ft][:, 0:no], A_[:, 0:no], AF.Silu, r=rr, w=[dst[ft]], bias=bias)
                        else:
                            act(dst[ft][:, 0:16], A_[:, 0:16], AF.Silu, r=rr, w=[dst[ft]], bias=bias)
                            act(v3(dst[ft][:, 16:32], NSS, LS), v3(A_[:, 19:47], NSS, 7)[:, :, 0:4], AF.Silu, r=rr,
                                w=[dst[ft]], bias=bias)
                    return f

                wv = wview_in[l]
                import os as _os
                _gl = ((0, C_Z), (1, C_GA), (2, C_GR), (3, C_GD))
                if _os.environ.get('KSKIPG'):
                    _gl = ()
                if _os.environ.get('KDUPG'):
                    _gl = _gl + _gl
                for gi, c0 in _gl:
                    for half in range(2):
                        wbuf = load_w(wv, c0 + half * 256, 256)
                        tm_group(wbuf, 0, 256, gate_evac(gi, half))
                chk('B1')
                wbuf = load_w(wv, C_DT, 8)
                load_w(wv, C_BD, 8, off=8, buf=wbuf)

                def small_evac(bk, ch):
                    L = ch["L"]
                    cp(SM[ch["slot"]][0:L, :], bk[0:L, 0:16], r=[bk], w=[SM[ch["slot"]]])
                tm_group(wbuf, 0, 16, small_evac)
                fin = {}
                done_cnt = {}
                th_ssd, th_gdn, th_swa, th_ret = ssd_thread(), gdn_thread(), swa_thread(), ret_thread()
                early = [th_gdn, th_ssd]
                while early:
                    for th in list(early):
                        v = next(th)
                        if v is not None and v[0] == "wait_proj":
                            early.remove(th)
                chk('B2')
                wbuf = load_w(wv, C_KA, 256)

                def kv_evac(bk, ch):
                    L, slot = ch["L"], ch["slot"]
                    _m = _os.environ.get('KVMODE', '0')
                    if _m in ('0', '1'):
                        cp(KV[slot][0:L, :], bk[0:L, 0:256], r=[bk], w=[KV[slot]], eng="act")
                    if _m in ('0', '2'):
                        cp(VAUG[slot][0:L, :, 0:64], v3(bk[0:L, 128:256], 2, 64), r=[bk] + ([KV[slot]] if _os.environ.get('KVSER') else []), w=[VAUG[slot]])
                tm_group(wbuf, 0, 256, kv_evac)
                chk('B2a')
                for g in range(2):
                    fm_group(wbuf, g * 64, 64,
                             lambda bk, g=g: cp(KA[0:64, g, 0:nbt], bk[0:64, 0:nbt], r=[bk], w=[KA]))
                chk('B3')
                for half in range(2):
                    wbuf = load_w(wv, C_QA + half * 256, 256)
                    for hh in range(4):
                        h = half * 4 + hh
                        fm_group(wbuf, hh * 64, 64,
                                 lambda bk, h=h: cp(QA[0:64, h, 0:nbt], bk[0:64, 0:nbt], r=[bk], w=[QA],
                                                    eng="act" if h % 2 else "dve"))
                chk('B4')
                wbuf = load_w(wv, C_QR, 256)
                for h in range(4):
                    fm_group(wbuf, h * 64, 64,
                             lambda bk, h=h: cp(QR[0:64, h, 0:nbt], bk[0:64, 0:nbt], r=[bk], w=[QR]))
                wbuf = load_w(wv, C_KR, 256)
                for h in range(4):
                    fm_group(wbuf, h * 64, 64,
                             lambda bk, h=h: ts(KRF[0:64, h, 0:nbt], bk[0:64, 0:nbt], 0.125, ALU.mult, r=[bk], w=[KRF]))

                def kr_evac(bk, ch):
                    L, slot = ch["L"], ch["slot"]
                    ts(KR[slot][0:L, :, :], v3(bk[0:L, 0:256], 4, 64), 0.125, ALU.mult, r=[bk], w=[KR[slot]])
                tm_group(wbuf, 0, 256, kr_evac)
                chk('B5')
                for half in range(2):
                    wbuf = load_w(wv, C_VR + half * 256, 256)

                    def vr_evac(bk, ch, half=half):
                        L, slot = ch["L"], ch["slot"]
                        cp(VR[slot][0:L, 2 * half:2 * half + 2, :], v3(bk[0:L, 0:256], 2, 128), r=[bk], w=[VR[slot]],
                           eng="act")
                    tm_group(wbuf, 0, 256, vr_evac)
                chk('B6')
                for q in range(4):
                    wbuf = load_w(wv, C_XBC + q * 256, 256)
                    for j in range(2):
                        ft = 2 * q + j
                        fm_group(wbuf, j * 128, 128, conv_evac(XBCt, ft, cwS, cbS, carS, cstS, csoS))
                chk('B7')
                for q in range(6):
                    wbuf = load_w(wv, C_QKVD + q * 256, 256)
                    for j in range(2):
                        ft = 2 * q + j
                        fm_group(wbuf, j * 128, 128, conv_evac(QKVDt, ft, cwG, None, carG, cstG, csoG))
                chk('B8')
                for ft in range(8):
                    k = ft % 2
                    S_, A_ = ST[k], ACC[k]
                    sq_ = (sqj, junkD)[k]
                    Q_ = QKVDt[ft]
                    tt(sq_[:, 0:nbt], Q_[:, 0:nbt], Q_[:, 0:nbt], ALU.mult, r=[Q_], w=[sq_], eng="pool")
                    bk = nb()
                    mm(bk[:, 0:nbt], ones_b[:, :], sq_[:, 0:nbt], True, True, r=[ones_b, sq_], w=[bk])
                    act(A_[:, 0:nbt], bk[:, 0:nbt], AF.Ln, r=[bk], w=[A_], bias=EPS)
                    act(A_[:, 0:nbt], A_[:, 0:nbt], AF.Exp, r=[A_], w=[A_], scale=-0.5,
                        bias=(math.log(128.0 ** -0.5) if ft < 4 else 0.0))
                    tt(Q_[:, 0:nbt], Q_[:, 0:nbt], A_[:, 0:nbt], ALU.mult, r=[Q_, A_], w=[Q_])

                chk('B')
                if l + 1 < n_layers and len(blocks) >= 7:
                    for pc_ in {1: (0, 1), 2: (2,), 3: (3,), 4: (4,)}.get(bi, ()):
                        convert_weights(l + 1, piece=pc_)
                    if bi == 6:
                        load_slow_params(l + 1)
                elif l + 1 < n_layers and bi == len(blocks) - 1:
                    convert_weights(l + 1)
                    load_slow_params(l + 1)
                threads = [th_gdn, th_ssd, th_ret, th_swa]
                while threads:
                    for th in list(threads):
                        try:
                            v = next(th)
                        except StopIteration:
                            threads.remove(th)
                            continue
                        if v is not None and v[0] == "done":
                            done_cnt[v[1]] = done_cnt.get(v[1], 0) + 1
                            if done_cnt[v[1]] == 4:
                                finish_chunk(blk[v[1]])
                                fin[v[1]] = True

                chk('C')
                if bi + 1 < len(blocks):
                    stage_a1(l, bi + 1)
                elif l + 1 < n_layers:
                    stage_a1(l + 1, 0)
                wvo = wview_out[l]
                for ti, tl in enumerate(tm_tiles):
                    pass
                osb = xt[0]
                outbuf = {}
                for ti, tl in enumerate(tm_tiles):
                    outbuf[ti] = None
                E_ssq = [smalls["E0_ssq"], smalls["E1_ssq"]]
                E_tot, E_lnv, E_rstd = smalls["E_tot"], smalls["E_lnv"], smalls["E_rstd"]
                XR = [ST[0], ST[1], ACC[0], ACC[1]]
                for cg in range(D // WG):
                    wbuf = load_w(wvo, cg * WG, WG)
                    for ti, tl in enumerate(tm_tiles):
                        L, t0 = tl["L"], tl["tok0"]
                        bk = nb()
                        for kc in range(KC):
                            mm(bk[0:L, 0:WG], hT[:, kc, t0:t0 + L], wbuf[:, kc, 0:WG], kc == 0, kc == KC - 1,
                               r=[hT, wbuf], w=[bk])
                        cp(xt[ti][0:L, cg * WG:(cg + 1) * WG], bk[0:L, 0:WG], r=[bk], w=[xt[ti]],
                           eng="act" if cg % 2 else "dve")
                        act(junkA[0:L, 0:WG], xt[ti][0:L, cg * WG:(cg + 1) * WG], AF.Square, r=[xt[ti]],
                            w=[junkA, E_ssq[ti]], accum_out=E_ssq[ti][0:L, cg:cg + 1])
                def epilogue(tm_tiles=tm_tiles, xsrc=xsrc, xdst=xdst, bi=bi, E_ssq=E_ssq, XR=XR):
                    xrc = 0
                    for ti, tl in enumerate(tm_tiles):
                        L, r0 = tl["L"], tl["row0"]
                        o_ = xt[ti]
                        V(lambda L=L, ti=ti: nc.vector.tensor_reduce(out=E_tot[0:L, 0:1], in_=E_ssq[ti][0:L, 0:8], axis=AX.X,
                                                                     op=ALU.add), r=[E_ssq[ti]], w=[E_tot])
                        act(E_lnv[0:L, 0:1], E_tot[0:L, 0:1], AF.Ln, r=[E_tot], w=[E_lnv], scale=1.0 / D, bias=EPS)
                        act(E_rstd[0:L, 0:1], E_lnv[0:L, 0:1], AF.Exp, r=[E_lnv], w=[E_rstd], scale=-0.5)
                        for c in range(8):
                            c0 = c * 256
                            xb = XR[xrc % 4]
                            xrc += 1
                            dma(xb[0:L, 0:256], xsrc[r0:r0 + L, c0:c0 + 256], r=[xbuf(bi)], w=[xb])
                            stt(o_[0:L, c0:c0 + 256], o_[0:L, c0:c0 + 256], E_rstd[0:L, 0:1], postg[0:L, c0:c0 + 256],
                                ALU.mult, ALU.mult, r=[o_, E_rstd, postg], w=[o_])
                            tt(o_[0:L, c0:c0 + 256], o_[0:L, c0:c0 + 256], xb[0:L, 0:256], ALU.add, r=[o_, xb], w=[o_])
                        dma(xdst[r0:r0 + L, :], o_[0:L, :], r=[o_], w=[xbuf(bi)])
                pending_epi.append(epilogue)

            flush_epi()
            if n_blocks is None:
                dma(o_p_ssd[l].rearrange("h n e -> n h e"), Sssd[0][:], r=[Sssd[0]], w=[dbuf("o_p_ssd")])
                dma(o_p_ret[l].rearrange("h d e -> d h e"), Sret[0][:], r=[Sret[0]], w=[dbuf("o_p_ret")])
                dma(o_p_gdn[l].rearrange("h d e -> d h e"), Sgdn[0][:], r=[Sgdn[0]], w=[dbuf("o_p_gdn")])
                store_fm_rows(lambda ft: carS[:, ft, :], carS, 8, 3, o_p_ssdc[l], xt[0])
                store_fm_rows(lambda ft: carG[:, ft, :], carG, 12, 3, o_p_gdnc[l], xt[1])
            store_fm_rows(lambda ft: csoS[:, ft, :, :].rearrange("p s t -> p (s t)"), csoS, 8, NSS * 3,
                          o_s_ssdc[l].rearrange("s t c -> (s t) c"), xt[0])
            store_fm_rows(lambda ft: csoG[:, ft, :, :].rearrange("p s t -> p (s t)"), csoG, 12, NSS * 3,
                          o_s_gdnc[l].rearrange("s t c -> (s t) c"), xt[1])

    except StopBuild:
        pass

    P.emit(es)
    es.close()
    return nc, P.stats


_CACHE = {}


def _in_maps(inp):
    f = lambda a: np.ascontiguousarray(np.asarray(a, dtype=np.float32))
    maps = []
    for c in range(8):
        b = c % 4
        sl = slice(NSS * c, NSS * c + NSS)
        xin = np.concatenate([inp["meta_tokens"], inp["x_prompt"][b], inp["x_sample"][sl].reshape(NSS * LS, D)], axis=0)
        m = {
            "xin": f(xin),
            "st_ssd": f(inp["state_ssd"][:, sl]),
            "st_ssdc": f(inp["state_ssd_conv"][:, sl]),
            "st_k": f(inp["cache_swa_k"][:, sl].reshape(DEPTH, NSS, 128, 128)),
            "st_v": f(inp["cache_swa_v"][:, sl].reshape(DEPTH, NSS, 128, 128)),
            "st_ret": f(inp["state_ret"][:, sl]),
            "st_gdn": f(inp["state_gdn"][:, sl]),
            "st_gdnc": f(inp["state_gdn_conv"][:, sl]),
        }
        for k in ("pre_norm", "post_norm", "w_in", "w_out", "ssd_conv_w", "ssd_conv_b", "ssd_dt_bias", "ssd_a_log",
                  "ssd_d", "ssd_norm", "swa_sinks", "ret_norm", "gdn_conv_w", "gdn_dt_bias", "gdn_a_log", "gdn_norm"):
            m[k] = f(inp[k])
        maps.append(m)
    return maps


def kernel(**inp):
    if "nc" not in _CACHE:
        _CACHE["nc"] = build_program()[0]
    nc = _CACHE["nc"]
    res = run_bass_kernel_spmd(nc, _in_maps(inp), core_ids=list(range(8)))
    R = res.results
    B = 4
    y_prompt = np.stack([R[b]["yout"][NMETA:NPT] for b in range(B)]).astype(np.float32)
    y_sample = np.concatenate([R[c]["yout"][NPT:NT].reshape(NSS, LS, D) for c in range(8)]).astype(np.float32)

    def pst(name, shape):
        return np.stack([np.stack([R[b][name][l] for b in range(B)]) for l in range(DEPTH)]).reshape(shape).astype(np.float32)

    def sst(name, shape):
        return np.concatenate([R[c][name] for c in range(8)], axis=1).reshape(shape).astype(np.float32)

    outs = (
        y_prompt, y_sample,
        pst("o_p_ssd", (DEPTH, B, 8, 128, 64)), pst("o_p_ssdc", (DEPTH, B, 3, 1024)),
        pst("o_p_k", (DEPTH, B, 128, 2, 64)), pst("o_p_v", (DEPTH, B, 128, 2, 64)),
        pst("o_p_ret", (DEPTH, B, 4, 64, 128)), pst("o_p_gdn", (DEPTH, B, 4, 128, 128)),
        pst("o_p_gdnc", (DEPTH, B, 3, 1536)),
        sst("o_s_ssd", (DEPTH, 32, 8, 128, 64)), sst("o_s_ssdc", (DEPTH, 32, 3, 1024)),
        sst("o_s_k", (DEPTH, 32, 128, 2, 64)), sst("o_s_v", (DEPTH, 32, 128, 2, 64)),
        sst("o_s_ret", (DEPTH, 32, 4, 64, 128)), sst("o_s_gdn", (DEPTH, 32, 4, 128, 128)),
        sst("o_s_gdnc", (DEPTH, 32, 3, 1536)),
    )
    return outs
```
